# Optimizing a Trainium2 kernel written in Bass

```python
import math
import jax, jax.numpy as jnp
from jax import lax
import numpy as np

D_MODEL = 2048
BATCH = 2
SEQ = 4096
DEPTH = 2
DEC_BATCH = 32
DEC_SEQ = 1
PAST_LEN = 16384
PAGE_SIZE = 128

N_META = 16
ATTN_WIDTH = D_MODEL // 2
HEAD_DIM = 128
N_HEADS = ATTN_WIDTH // HEAD_DIM
N_KV_HEADS = N_HEADS // 4
GQA_GROUP = N_HEADS // N_KV_HEADS
KV_WIDTH = N_KV_HEADS * HEAD_DIM
WINDOW = 128
BLOCK = 128
ROT_DIM = HEAD_DIM // 4
ROPE_THETA = 500000.0
SSM_WIDTH = D_MODEL // 4
SSM_GROUP_SIZE = 16
SSM_GROUPS = SSM_WIDTH // SSM_GROUP_SIZE
SSM_STATE = 64
POOL_WIDTH = D_MODEL // 4
POOL_WINDOWS = (2, 4, 8, 16)
POOL_GROUP = POOL_WIDTH // len(POOL_WINDOWS)
POOL_BUF = max(POOL_WINDOWS) - 1
MIX_WIDTH = ATTN_WIDTH + SSM_WIDTH + POOL_WIDTH
IN_WIDTH = ATTN_WIDTH + 2 * KV_WIDTH + SSM_WIDTH + POOL_WIDTH
D_FF = 4 * D_MODEL
EPS = 1e-6

kernel_name = "hymba_swa_s5_pool_decoder_step"


def rms_norm(x, g):
    xf = x.astype(jnp.float32)
    y = xf * lax.rsqrt(jnp.mean(xf * xf, axis=-1, keepdims=True) + EPS)
    return (y * g.astype(jnp.float32)).astype(x.dtype)


def rope(x, pos):
    half = ROT_DIM // 2
    inv = ROPE_THETA ** (-jnp.arange(0, ROT_DIM, 2, dtype=jnp.float32) / ROT_DIM)
    ang = pos.astype(jnp.float32)[:, None] * inv
    cos = jnp.cos(ang)[:, None, :]
    sin = jnp.sin(ang)[:, None, :]
    xf = x.astype(jnp.float32)
    x1, x2 = xf[..., :half], xf[..., half:ROT_DIM]
    out = jnp.concatenate([x1 * cos - x2 * sin, x2 * cos + x1 * sin, xf[..., ROT_DIM:]], axis=-1)
    return out.astype(x.dtype)


def sink_attend(q, k, v, mask, sink):
    s = jnp.einsum('...qkgd,...skd->...kgqs', q, k).astype(jnp.float32) * (HEAD_DIM ** -0.5)
    s = jnp.where(mask[..., None, None, :, :], s, -jnp.inf)
    sk = sink.astype(jnp.float32)[:, :, None, None]
    m = jnp.maximum(jnp.max(s, axis=-1, keepdims=True), sk)
    p = jnp.exp(s - m)
    denom = jnp.sum(p, axis=-1, keepdims=True) + jnp.exp(sk - m)
    return jnp.einsum('...kgqs,...skd->...qkgd', (p / denom).astype(v.dtype), v)


def attn_prompt(q, k, v, sink):
    bsz, L = q.shape[0], q.shape[1]
    front = (-N_META) % BLOCK
    back = (-(front + L)) % BLOCK
    pad = lambda t: jnp.pad(t, ((0, 0), (front, back), (0, 0), (0, 0)))
    qp, kp, vp = pad(q), pad(k), pad(v)
    Lp = front + L + back
    nb = Lp // BLOCK
    qb = qp.reshape(bsz, nb, BLOCK, N_KV_HEADS, GQA_GROUP, HEAD_DIM)
    kb = kp.reshape(bsz, nb, BLOCK, N_KV_HEADS, HEAD_DIM)
    vb = vp.reshape(bsz, nb, BLOCK, N_KV_HEADS, HEAD_DIM)

    def band(t):
        prev = jnp.pad(t, ((0, 0), (1, 0), (0, 0), (0, 0), (0, 0)))[:, :-1]
        return jnp.concatenate([prev, t], axis=2)

    qpos = (jnp.arange(Lp) - front).reshape(nb, BLOCK)
    kpos = qpos[:, :1] - BLOCK + jnp.arange(2 * BLOCK)[None, :]
    diff = qpos[:, :, None] - kpos[:, None, :]
    mask = (diff >= 0) & (diff <= WINDOW) & (kpos[:, None, :] >= 0)
    o = sink_attend(qb, band(kb), band(vb), mask[None], sink)
    return o.reshape(bsz, Lp, ATTN_WIDTH)[:, front:front + L]


def attn_sample(q, k, v, k_buf, v_buf, sink):
    bsz, T = q.shape[0], q.shape[1]
    wb = k_buf.shape[1]
    kk = jnp.concatenate([k_buf.astype(k.dtype), k], axis=1)
    vv = jnp.concatenate([v_buf.astype(v.dtype), v], axis=1)
    qpos = PAST_LEN + jnp.arange(T)
    kpos = PAST_LEN - wb + jnp.arange(wb + T)
    diff = qpos[:, None] - kpos[None, :]
    mask = (diff >= 0) & (diff <= WINDOW)
    o = sink_attend(q.reshape(bsz, T, N_KV_HEADS, GQA_GROUP, HEAD_DIM), kk, vv, mask[None], sink)
    return o.reshape(bsz, T, ATTN_WIDTH), kk[:, -wb:], vv[:, -wb:]


def _complex_combine(e1, e2):
    a1r, a1i, b1r, b1i = e1
    a2r, a2i, b2r, b2i = e2
    return (a2r * a1r - a2i * a1i, a2r * a1i + a2i * a1r,
            a2r * b1r - a2i * b1i + b2r, a2r * b1i + a2i * b1r + b2i)


def ssm_mix(u, A_re, A_im, log_dt, B_re, B_im, C_re, C_im, D_skip, w_glu, b_glu, h0=None):
    bsz, T = u.shape[0], u.shape[1]
    uf = u.astype(jnp.float32)
    ug = uf.reshape(bsz, T, SSM_GROUPS, SSM_GROUP_SIZE)
    dt = jnp.exp(log_dt.astype(jnp.float32))[:, None]
    ar, ai = A_re.astype(jnp.float32), A_im.astype(jnp.float32)
    mag = jnp.exp(dt * ar)
    abr, abi = mag * jnp.cos(dt * ai), mag * jnp.sin(dt * ai)
    den = ar * ar + ai * ai
    fr = ((abr - 1.0) * ar + abi * ai) / den
    fi = (abi * ar - (abr - 1.0) * ai) / den
    br, bi = B_re.astype(jnp.float32), B_im.astype(jnp.float32)
    bbr = fr[..., None] * br - fi[..., None] * bi
    bbi = fr[..., None] * bi + fi[..., None] * br
    xr = jnp.einsum('btgc,gnc->btgn', ug, bbr)
    xi = jnp.einsum('btgc,gnc->btgn', ug, bbi)
    car, cai, hr, hi = lax.associative_scan(
        _complex_combine,
        (jnp.broadcast_to(abr, xr.shape), jnp.broadcast_to(abi, xr.shape), xr, xi), axis=1)
    if h0 is not None:
        h0r = h0[0].astype(jnp.float32)[:, None]
        h0i = h0[1].astype(jnp.float32)[:, None]
        hr, hi = hr + car * h0r - cai * h0i, hi + car * h0i + cai * h0r
    y = (jnp.einsum('gcn,btgn->btgc', C_re.astype(jnp.float32), hr)
         - jnp.einsum('gcn,btgn->btgc', C_im.astype(jnp.float32), hi))
    y = y.reshape(bsz, T, SSM_WIDTH) + D_skip.astype(jnp.float32) * uf
    z = jax.nn.gelu(y)
    out = z * jax.nn.sigmoid(z @ w_glu.astype(jnp.float32) + b_glu.astype(jnp.float32))
    return out.astype(u.dtype), hr[:, -1], hi[:, -1]


def pool_mix(xp, w_pool, pool_scale, prev=None):
    T = xp.shape[1]
    ext = xp if prev is None else jnp.concatenate([prev.astype(xp.dtype), xp], axis=1)
    n_prev = ext.shape[1] - T
    cs = jnp.pad(jnp.cumsum(ext.astype(jnp.float32), axis=1), ((0, 0), (1, 0), (0, 0)))
    hi = n_prev + jnp.arange(T) + 1
    outs = []
    for g, w in enumerate(POOL_WINDOWS):
        lo = jnp.maximum(hi - w, 0)
        sl = slice(g * POOL_GROUP, (g + 1) * POOL_GROUP)
        csg = cs[..., sl]
        mean = (csg[:, hi] - csg[:, lo]) / (hi - lo).astype(jnp.float32)[:, None]
        outs.append(jnp.einsum('btc,cd->btd', mean - xp[..., sl].astype(jnp.float32),
                               w_pool[g].astype(jnp.float32)))
    y = jnp.concatenate(outs, axis=-1) * pool_scale.astype(jnp.float32)
    return y.astype(xp.dtype), ext[:, -POOL_BUF:]


def trunk_layer(x, pos, lp, past):
    bsz, T = x.shape[0], x.shape[1]
    h = rms_norm(x, lp['g_mix'])
    proj = h @ lp['w_in']
    o1 = ATTN_WIDTH
    o2 = o1 + KV_WIDTH
    o3 = o2 + KV_WIDTH
    o4 = o3 + SSM_WIDTH
    q = proj[..., :o1].reshape(bsz, T, N_HEADS, HEAD_DIM)
    k = proj[..., o1:o2].reshape(bsz, T, N_KV_HEADS, HEAD_DIM)
    v = proj[..., o2:o3].reshape(bsz, T, N_KV_HEADS, HEAD_DIM)
    u = proj[..., o3:o4]
    xpool = proj[..., o4:]
    q = rope(rms_norm(q, lp['g_q']), pos)
    k = rope(rms_norm(k, lp['g_k']), pos)
    sink = lp['sinks'].reshape(N_KV_HEADS, GQA_GROUP)
    ssm_args = (lp['A_re'], lp['A_im'], lp['log_dt'], lp['B_re'], lp['B_im'], lp['C_re'],
                lp['C_im'], lp['D_skip'], lp['w_glu'], lp['b_glu'])
    if past is None:
        a = attn_prompt(q, k, v, sink)
        nw = min(WINDOW, T)
        new_k, new_v = k[:, -nw:], v[:, -nw:]
        s, hr, hi = ssm_mix(u, *ssm_args)
        pl, pbuf = pool_mix(xpool, lp['w_pool'], lp['pool_scale'])
    else:
        k_buf, v_buf, h_re, h_im, p_buf = past
        a, new_k, new_v = attn_sample(q, k, v, k_buf, v_buf, sink)
        s, hr, hi = ssm_mix(u, *ssm_args, h0=(h_re, h_im))
        pl, pbuf = pool_mix(xpool, lp['w_pool'], lp['pool_scale'], p_buf)
    mix = jnp.concatenate([rms_norm(a, lp['g_out_attn']), rms_norm(s, lp['g_out_ssm']),
                           rms_norm(pl, lp['g_out_pool'])], axis=-1)
    x = x + mix @ lp['w_out']
    h2 = rms_norm(x, lp['g_ffn'])
    x = x + jnp.square(jax.nn.relu(h2 @ lp['w_ff1'])) @ lp['w_ff2']
    return x, (new_k, new_v, hr, hi, pbuf)


def setup_inputs(seed: int = 0) -> dict:
    key = jax.random.key(seed)
    ks = jax.random.split(key, 40)
    nrm = lambda i, shape: jax.random.normal(ks[i], shape, jnp.float32)
    win_buf = min(WINDOW, PAST_LEN)
    return {
        'x_prompt': nrm(0, (BATCH, SEQ, D_MODEL)),
        'x_sample': nrm(1, (DEC_BATCH, DEC_SEQ, D_MODEL)),
        'cache_k': nrm(2, (DEPTH, DEC_BATCH, win_buf, N_KV_HEADS, HEAD_DIM)),
        'cache_v': nrm(3, (DEPTH, DEC_BATCH, win_buf, N_KV_HEADS, HEAD_DIM)),
        'state_ssm_re': 0.1 * nrm(4, (DEPTH, DEC_BATCH, SSM_GROUPS, SSM_STATE)),
        'state_ssm_im': 0.1 * nrm(5, (DEPTH, DEC_BATCH, SSM_GROUPS, SSM_STATE)),
        'state_pool': nrm(6, (DEPTH, DEC_BATCH, POOL_BUF, POOL_WIDTH)),
        'meta_tokens': nrm(7, (N_META, D_MODEL)),
        'g_mix': 1.0 + 0.01 * nrm(8, (DEPTH, D_MODEL)),
        'w_in': nrm(9, (DEPTH, D_MODEL, IN_WIDTH)) * D_MODEL ** -0.5,
        'g_q': 1.0 + 0.01 * nrm(10, (DEPTH, HEAD_DIM)),
        'g_k': 1.0 + 0.01 * nrm(11, (DEPTH, HEAD_DIM)),
        'sinks': nrm(12, (DEPTH, N_HEADS)),
        'A_re': -0.5 + 0.01 * nrm(13, (DEPTH, SSM_GROUPS, SSM_STATE)),
        'A_im': jnp.pi * jnp.arange(SSM_STATE, dtype=jnp.float32) + 0.01 * nrm(14, (DEPTH, SSM_GROUPS, SSM_STATE)),
        'log_dt': jax.random.uniform(ks[15], (DEPTH, SSM_GROUPS), jnp.float32, math.log(1e-3), math.log(1e-1)),
        'B_re': nrm(16, (DEPTH, SSM_GROUPS, SSM_STATE, SSM_GROUP_SIZE)) * (2 * SSM_GROUP_SIZE) ** -0.5,
        'B_im': nrm(17, (DEPTH, SSM_GROUPS, SSM_STATE, SSM_GROUP_SIZE)) * (2 * SSM_GROUP_SIZE) ** -0.5,
        'C_re': nrm(18, (DEPTH, SSM_GROUPS, SSM_GROUP_SIZE, SSM_STATE)) * (2 * SSM_STATE) ** -0.5,
        'C_im': nrm(19, (DEPTH, SSM_GROUPS, SSM_GROUP_SIZE, SSM_STATE)) * (2 * SSM_STATE) ** -0.5,
        'D_skip': nrm(20, (DEPTH, SSM_WIDTH)),
        'w_glu': nrm(21, (DEPTH, SSM_WIDTH, SSM_WIDTH)) * SSM_WIDTH ** -0.5,
        'b_glu': 0.01 * nrm(22, (DEPTH, SSM_WIDTH)),
        'w_pool': nrm(23, (DEPTH, len(POOL_WINDOWS), POOL_GROUP, POOL_GROUP)) * POOL_GROUP ** -0.5,
        'pool_scale': 1.0 + 0.1 * nrm(24, (DEPTH, POOL_WIDTH)),
        'g_out_attn': 1.0 + 0.01 * nrm(25, (DEPTH, ATTN_WIDTH)),
        'g_out_ssm': 1.0 + 0.01 * nrm(26, (DEPTH, SSM_WIDTH)),
        'g_out_pool': 1.0 + 0.01 * nrm(27, (DEPTH, POOL_WIDTH)),
        'w_out': nrm(28, (DEPTH, MIX_WIDTH, D_MODEL)) * MIX_WIDTH ** -0.5,
        'g_ffn': 1.0 + 0.01 * nrm(29, (DEPTH, D_MODEL)),
        'w_ff1': nrm(30, (DEPTH, D_MODEL, D_FF)) * D_MODEL ** -0.5,
        'w_ff2': nrm(31, (DEPTH, D_FF, D_MODEL)) * D_FF ** -0.5,
    }


def reference(x_prompt, x_sample, cache_k, cache_v, state_ssm_re, state_ssm_im, state_pool,
              meta_tokens, g_mix, w_in, g_q, g_k, sinks, A_re, A_im, log_dt, B_re, B_im,
              C_re, C_im, D_skip, w_glu, b_glu, w_pool, pool_scale, g_out_attn, g_out_ssm,
              g_out_pool, w_out, g_ffn, w_ff1, w_ff2):
    meta = jnp.broadcast_to(meta_tokens.astype(x_prompt.dtype)[None],
                            (x_prompt.shape[0], N_META, D_MODEL))
    xp = jnp.concatenate([meta, x_prompt], axis=1)
    xs = x_sample
    pos_p = jnp.arange(xp.shape[1])
    pos_s = PAST_LEN + jnp.arange(xs.shape[1])
    states_p, states_s = [], []
    for l in range(DEPTH):
        lp = dict(g_mix=g_mix[l], w_in=w_in[l], g_q=g_q[l], g_k=g_k[l], sinks=sinks[l],
                  A_re=A_re[l], A_im=A_im[l], log_dt=log_dt[l], B_re=B_re[l], B_im=B_im[l],
                  C_re=C_re[l], C_im=C_im[l], D_skip=D_skip[l], w_glu=w_glu[l], b_glu=b_glu[l],
                  w_pool=w_pool[l], pool_scale=pool_scale[l], g_out_attn=g_out_attn[l],
                  g_out_ssm=g_out_ssm[l], g_out_pool=g_out_pool[l], w_out=w_out[l],
                  g_ffn=g_ffn[l], w_ff1=w_ff1[l], w_ff2=w_ff2[l])
        xp, st_p = trunk_layer(xp, pos_p, lp, None)
        xs, st_s = trunk_layer(xs, pos_s, lp, (cache_k[l], cache_v[l], state_ssm_re[l],
                                               state_ssm_im[l], state_pool[l]))
        states_p.append(st_p)
        states_s.append(st_s)
    np_ = [jnp.stack(s) for s in zip(*states_p)]
    ns_ = [jnp.stack(s) for s in zip(*states_s)]
    y_prompt = xp[:, N_META:]
    return (y_prompt, xs, np_[0], np_[1], np_[2], np_[3], np_[4],
            ns_[0], ns_[1], ns_[2], ns_[3], ns_[4])
```

```python
import numpy as np
import concourse.bass as bass
import concourse.mybir as mybir
from concourse.bass_utils import run_bass_kernel_spmd

F32 = mybir.dt.float32
BF16 = mybir.dt.bfloat16
I32 = mybir.dt.int32
ALU = mybir.AluOpType
AF = mybir.ActivationFunctionType

D = 2048
NQ = 1024
NKV = 256
SSMW = 512
POOLW = 512
INW = 2560
DFF = 8192
NCORES = 8
CL = 64
EPS = 1e-6
NEG = -30000.0
PI = float(np.pi)


class Sched:
    ENGS = ["pe", "act", "dve", "pool", "sp"]

    def __init__(self, nc, n_sp_slots=60, n_pool_slots=24):
        self.nc = nc
        self.ops = []
        self.by_eng = {e: [] for e in self.ENGS}
        self.last_w = {}
        self.readers = {}
        self.nslots = {"sp": n_sp_slots, "pool": n_pool_slots}
        self.ndma = {"sp": 0, "pool": 0}

    def op(self, eng, fn, reads=(), writes=(), dma=False, cc=False):
        oid = len(self.ops)
        deps = set()
        for r in reads:
            w = self.last_w.get(r)
            if w is not None:
                deps.add(w)
        for r in writes:
            w = self.last_w.get(r)
            if w is not None:
                deps.add(w)
            for rid in self.readers.get(r, {}).values():
                deps.add(rid)
        o = dict(id=oid, eng=eng, fn=fn, deps=deps, dma=dma, marked=False)
        if cc:
            o["dma"] = True
            dma = True
            self.ncc = getattr(self, "ncc", 0) + 1
            o["q"] = "cc"
            o["slot"] = 0
            o["val"] = self.ncc
        elif dma:
            k = self.ndma[eng]
            self.ndma[eng] += 1
            o["q"] = eng
            o["slot"] = k % self.nslots[eng]
            o["val"] = 16 * (k // self.nslots[eng] + 1)
        self.ops.append(o)
        self.by_eng[eng].append(o)
        for r in reads:
            self.readers.setdefault(r, {})[("d", oid) if dma else eng] = oid
        for r in writes:
            self.last_w[r] = oid
            self.readers[r] = {}
        return oid

    def fence(self, pred, newkeys, eng="sp", extra_reads=()):
        keys = [k for k in set(list(self.last_w.keys()) + list(self.readers.keys())) if pred(k)]
        if getattr(self, "nofence", False):
            return None
        return self.op(eng, lambda e: e.nop(), reads=list(extra_reads), writes=keys + list(newkeys))

    def emit(self):
        nc = self.nc
        ops = self.ops
        for o in ops:
            for p in o["deps"]:
                po = ops[p]
                if po["dma"]:
                    continue
                if po["eng"] == "pe" and o["eng"] == "pe":
                    continue
                po["marked"] = True
        cum = {}
        cnt = {e: 0 for e in self.ENGS}
        for e in self.ENGS:
            for o in self.by_eng[e]:
                if o["marked"] and not o["dma"]:
                    cnt[e] += 1
                o["cum"] = cnt[e]
        esem = {e: nc.alloc_semaphore("s_" + e) for e in self.ENGS}
        dsem = {q: [nc.alloc_semaphore(f"d_{q}_{i}") for i in range(self.nslots[q])] for q in ("sp", "pool")}
        dsem["cc"] = [nc.alloc_semaphore("d_cc")]
        handles = {"pe": nc.tensor, "act": nc.scalar, "dve": nc.vector, "pool": nc.gpsimd, "sp": nc.sync}

        def run(eng, e):
            waited = {}
            for o in self.by_eng[eng]:
                need = {}
                for p in o["deps"]:
                    po = ops[p]
                    if po["dma"]:
                        key = ("d", po["q"], po["slot"])
                        v = po["val"]
                    else:
                        if po["eng"] == "pe" and eng == "pe":
                            continue
                        key = ("e", po["eng"])
                        v = po["cum"]
                    if v > need.get(key, 0):
                        need[key] = v
                if o["dma"] and o["q"] != "cc":
                    if o["val"] > 16:
                        key = ("d", o["q"], o["slot"])
                        need[key] = max(need.get(key, 0), o["val"] - 16)
                for key, v in need.items():
                    if waited.get(key, 0) >= v:
                        continue
                    waited[key] = v
                    sem = esem[key[1]] if key[0] == "e" else dsem[key[1]][key[2]]
                    e.wait_ge(sem, v)
                ins = o["fn"](e)
                if o["dma"]:
                    ins.then_inc(dsem[o["q"]][o["slot"]], 1 if o["q"] == "cc" else 16)
                elif o["marked"]:
                    ins.then_inc(esem[eng], 1)

        with nc.Block() as block:
            @block.tensor
            def _(e):
                run("pe", e)

            @block.scalar
            def _(e):
                run("act", e)

            @block.vector
            def _(e):
                run("dve", e)

            @block.gpsimd
            def _(e):
                run("pool", e)

            @block.sync
            def _(e):
                run("sp", e)


def col_groups(T):
    n = (T + 511) // 512
    nb = (T - 32) // 128
    per = [nb // n + (1 if i < nb % n else 0) for i in range(n)]
    gs = []
    c = 0
    for i, p in enumerate(per):
        w = p * 128 + (32 if i == n - 1 else 0)
        gs.append((c, c + w))
        c += w
    assert c == T
    return gs


def build(NB, NL=2, dbg=(), stop=None, ffw=True, NQ=4):
    T = NB * 128 + 32
    L = NB * 128
    E0 = NB * 128
    NBLK = NB + 1
    CG = col_groups(T)
    NCH = 2 * NB
    NSQ = int(np.log2(NCH))
    assert 2 ** NSQ == NCH
    nc = bass.Bass("TRN2", target_bir_lowering=False)
    nc.allow_low_precision("bf16 matmul operands by design (reference tolerance measured for bf16)")
    S = Sched(nc)
    S.nofence = 'nofence' in dbg
    A = nc.alloc_sbuf_tensor

    def din(name, shape, dt=F32):
        return nc.dram_tensor(name, list(shape), dt, kind="ExternalInput")

    def dout(name, shape, dt=F32):
        return nc.dram_tensor(name, list(shape), dt, kind="ExternalOutput")

    TT = NQ * L + 32
    x_in = din("x_in", [TT, D])
    w_in = din("w_in", [NL, D, INW]); w_out = din("w_out", [NL, D, D])
    if ffw:
        w_ff1 = din("w_ff1", [NL, D, DFF]); w_ff2 = din("w_ff2", [NL, DFF, D])
    g_mix = din("g_mix", [2, D]); g_ffn = din("g_ffn", [2, D])
    g_q = din("g_q", [2, 128]); g_k = din("g_k", [2, 128]); sinks = din("sinks", [2, 8])
    A_re = din("A_re", [2, 2048]); A_im = din("A_im", [2, 2048]); log_dt = din("log_dt", [2, 32])
    B_re = din("B_re", [2, 2048, 16]); B_im = din("B_im", [2, 2048, 16])
    C_re = din("C_re", [2, 32, 16, 64]); C_im = din("C_im", [2, 32, 16, 64])
    D_skip = din("D_skip", [2, 512]); w_glu = din("w_glu", [2, 512, 512]); b_glu = din("b_glu", [2, 512])
    w_pool = din("w_pool", [2, 4, 128, 128]); pool_scale = din("pool_scale", [2, 512])
    g_oa = din("g_out_attn", [2, 1024]); g_os = din("g_out_ssm", [2, 512]); g_op = din("g_out_pool", [2, 512])
    cache_k = din("cache_k", [2, 4, 128, 256]); cache_v = din("cache_v", [2, 4, 128, 256])
    st_re = din("st_re", [2, 64, 128]); st_im = din("st_im", [2, 64, 128])
    st_pool = din("st_pool", [2, 4, 15, 512])
    c_rope = din("c_rope", [NQ, 2, 32, T])
    c_mask = din("c_mask", [3, 128, 512])
    c_maskE = din("c_maskE", [32, 128])
    c_maskS = din("c_maskS", [4, 32, 4])
    c_prot = din("c_prot", [128, 32]); c_ident = din("c_ident", [128, 128])
    c_sel = din("c_sel", [128, 24])
    c_jtab = din("c_jtab", [128, 16 * (CL + 1)])
    c_pinv = din("c_pinv", [128, 4 * 16])
    c_psel = din("c_psel", [16, 4])

    xs = nc.dram_tensor("xs_scratch", [TT, D], F32)
    y_out = dout("y_out", [TT, D])
    o_pk = dout("o_pk", [2, 128, 256]); o_pv = dout("o_pv", [2, 128, 256])
    o_pssm = dout("o_pssm", [2, 2, 16, 128]); o_ppool = dout("o_ppool", [2, 15, 512])
    o_sk = dout("o_sk", [2, 4, 128, 256]); o_sv = dout("o_sv", [2, 4, 128, 256])
    o_sssm = dout("o_sssm", [2, 2, 64, 128]); o_spool = dout("o_spool", [2, 4, 15, 512])
    XW = 640
    xch_in = nc.dram_tensor("xch_in", [128, XW], F32)
    xch_out = nc.dram_tensor("xch_out", [NCORES * 128, XW], F32)

    actT = A("actT", [128, 16, T], BF16)
    PROJW = 8 * T + 2 * T + NBLK * 256 + 4 * T
    HIDW = 16 * T
    projbuf = A("projbuf", [128, max(PROJW, HIDW)], BF16)
    o_ = 0
    qT = projbuf[:, o_:o_ + 8 * T].rearrange("p (h t) -> p h t", h=8); o_ += 8 * T
    kT = projbuf[:, o_:o_ + 2 * T].rearrange("p (h t) -> p h t", h=2); o_ += 2 * T
    vbf = projbuf[:, o_:o_ + NBLK * 256].rearrange("p (b c) -> p b c", b=NBLK); o_ += NBLK * 256
    uT = projbuf[:, o_:o_ + 4 * T].rearrange("p (h t) -> p h t", h=4); o_ += 4 * T
    hidT = projbuf[:, 0:HIDW].rearrange("p (m t) -> p m t", m=16)
    xpT = A("xpT", [128, 4, 16 + T], BF16)
    wbuf = [A(f"wbuf{i}", [128, 16, 512], BF16) for i in range(2)]
    xblk = A("xblk", [128, D], F32)
    xn = A("xn", [128, D], BF16)
    ident = A("ident", [128, 128], BF16); identf = A("identf", [128, 128], F32)
    ones_b = A("ones_b", [128, 4, 128], BF16)
    prot = A("prot", [128, 32], BF16)
    rope = A("rope", [32, 2, T], F32)
    masks = A("masks", [128, 3, 512], BF16)
    maskE = A("maskE", [32, 128], BF16); maskS = A("maskS", [32, 4, 4], BF16)
    sel = A("sel", [128, 24], F32)
    vecs = A("vecs", [128, 48], F32)
    gvec = A("gvec", [128, 16], F32)
    qraw = A("qraw", [128, 512], F32); sqb = A("sqb", [128, 512], BF16)
    rtmp = A("rtmp", [128, 512], F32); rtmp2 = A("rtmp2", [128, 512], F32)
    small = A("small", [128, 16], F32)
    relu_t = [A(f"relu_t{i}", [128, 512], BF16) for i in range(2)]
    cosT = A("cosT", [128, 16, CL + 1], F32); sinT = A("sinT", [128, 16, CL + 1], F32)
    Dk = A("Dk", [128, 16, CL], F32); Dk16 = A("Dk16", [128, 16, 16], F32)
    sm = A("sm", [128, 28, 16], F32)
    BbT = A("BbT", [128, 32, 128], BF16)
    CTp_r = A("CTp_r", [128, 16, 128], F32); CTp_ni = A("CTp_ni", [128, 16, 128], F32)
    scur = A("scur", [128, 32], F32)
    Gst = A("Gst", [128, NCH + 1, 32], F32)
    sst = A("sst", [128, NCH + 2, 32], F32)
    esink = A("esink", [1, 8, 128], BF16); sinkrow = A("sinkrow", [1, 16], F32)
    hkL = [A(f"hk{i}", [128, 2, 128], BF16) for i in range(2)]; hvL = [A(f"hv{i}", [128, 256], BF16) for i in range(2)]
    phL = [A(f"ph{i}", [128, 4, 16], BF16) for i in range(2)]; sfin = A("sfin", [128, 2, 32], F32)
    kctok = A("kctok", [128, 256], BF16); kcT = A("kcT", [128, 128], BF16); vc = A("vc", [128, 256], BF16)
    es = A("es", [128, 8], BF16)
    h0s = A("h0s", [128, 2, 64], F32); hs = A("hs", [128, 2, 64], F32)
    sttok = A("sttok", [64, 2, 128], F32)
    stp = A("stp", [16, 512], BF16); psel = A("psel", [16, 4], BF16)
    wglu = A("wglu", [128, 4, 512], BF16); wpool = A("wpool", [128, 4, 128], BF16)
    pmeta = A("pmeta", [128, 4, 32], F32); pmeta2 = A("pmeta2", [128, 4, 32], F32); pinvE = A("pinvE", [128, 4, 16], F32)
    outst = A("outst", [128, 512], F32)

    scr = wbuf[1][:, :, :].rearrange("p a b -> p (a b)").bitcast(F32)
    z_re = scr[:, 0:1024]; z_im = scr[:, 1024:2048]; k_re = scr[:, 2048:3072]; k_im = scr[:, 3072:4096]
    hE_re = rtmp[:, :].rearrange("p (a b) -> p a b", a=16)
    hE_im = rtmp2[:, :].rearrange("p (a b) -> p a b", a=16)
    xst = xblk[:, 0:XW]; stage = xblk[:, 640:640 + 576]; acc = xblk[:, 1280:1280 + 576]
    ebuf = [xn[:, i * 1024:(i + 1) * 1024].rearrange("p (a b) -> p a b", a=2) for i in range(2)]
    XBK = [("xio", j) for j in range(4)]
    xio = [xblk[:, j * 512:(j + 1) * 512] for j in range(4)]

    if "psep" in dbg:
        psA = [nc.alloc_psum_tensor(f"psA{i}", [128, 512], F32) for i in range(4)]
        psX = None
    else:
        psX = nc.alloc_psum_tensor("psX", [128, 4, 512], F32)
        psA = [psX[:, i, :] for i in range(4)]
    psB = [nc.alloc_psum_tensor(f"psB{i}", [128, 512], F32) for i in range(3)]
    psT = nc.alloc_psum_tensor("psT", [128, 1024], BF16)
    psT_alt = psB[2][:, :].bitcast(BF16)

    S._taps = []
    QS = {"q": 0, "l": 0}

    def dma(q, out, in_, reads, writes):
        return S.op(q, lambda e: e.dma_start(out=out, in_=in_), reads=reads, writes=writes, dma=True)

    def dma_slow(q, out, in_, reads, writes):
        return S.op(q, lambda e: e.dma_start(out=out, in_=in_, allow_slow_non_contiguous=True),
                    reads=reads, writes=writes, dma=True)

    def dma_tp(q, out, src_flat, nt, reads, writes):
        v = src_flat.rearrange("(t p) -> p t", p=128)
        for t0 in range(0, nt, 4):
            t1 = min(nt, t0 + 4)
            dma_slow(q, out[:, t0:t1], v[:, t0:t1], reads, writes)

    def tap(name, ap, dt, reads):
        if name not in dbg:
            return
        t = dout("dbg_%s_%d%d" % (name, QS["q"], QS["l"]), list(ap.shape), dt)
        full = t.ap()
        S.op("sp", lambda e: e.dma_start(out=full, in_=ap), reads=reads, writes=[("out", "dbg", name, QS["q"], QS["l"])], dma=True)

    def V(fn, reads, writes):
        return S.op("dve", fn, reads, writes)

    def G(fn, reads, writes):
        return S.op("pool", fn, reads, writes)

    def ACT(fn, reads, writes):
        return S.op("act", fn, reads, writes)

    def PE(fn, reads, writes):
        return S.op("pe", fn, reads, writes)

    def tt(eng, out, a, b, op, reads, writes):
        return S.op(eng, lambda e: e.tensor_tensor(out, a, b, op), reads, writes)

    def mm(out, lhsT, rhs, start, stop, reads, writes):
        return S.op("pe", lambda e: e.matmul(out, lhsT, rhs, start=start, stop=stop), reads, writes)

    def tr(out, in_, idn, reads, writes):
        return S.op("pe", lambda e: e.transpose(out, in_, idn), reads, writes)

    def act(out, in_, func, reads, writes, **kw):
        return S.op("act", lambda e: e.activation(out, in_, func, **kw), reads, writes)

    def cp(eng, out, in_, reads, writes):
        if eng == "act":
            return S.op("act", lambda e: e.copy(out, in_), reads, writes)
        return S.op(eng, lambda e: e.tensor_copy(out, in_), reads, writes)

    def ts(eng, out, in0, s1, s2, op0, op1, reads, writes):
        if op1 is None:
            return S.op(eng, lambda e: e.tensor_scalar(out, in0, s1, 1.0, op0, ALU.mult), reads, writes)
        return S.op(eng, lambda e: e.tensor_scalar(out, in0, s1, s2, op0, op1), reads, writes)

    def stt(out, in0, sc, in1, op0, op1, reads, writes):
        return S.op("dve", lambda e: e.scalar_tensor_tensor(out, in0, sc, in1, op0, op1), reads, writes)

    def ms(eng, out, val, reads, writes):
        return S.op(eng, lambda e: e.memset(out, val), reads, writes)

    def rcp(out, in_, reads, writes):
        return S.op("dve", lambda e: e.reciprocal(out, in_), reads, writes)

    def scan(out, d0, d1, reads, writes):
        return S.op("dve", lambda e: e.tensor_tensor_scan(out, d0, d1, 0.0, ALU.mult, ALU.add), reads, writes)

    dma("pool", ident[:, :], c_ident[:, :], [], ["ident"])
    dma("sp", identf[:, :], c_ident[:, :], [], ["identf"])
    dma("pool", prot[:, :], c_prot[:, :], [], ["prot"])
    for i in range(3):
        dma("pool", masks[:, i, :], c_mask[i, :, :], [], ["masks"])
    dma("pool", maskE[:, :], c_maskE[:, :], [], ["maskE"])
    if "noconst" not in dbg:
        for s_ in range(4):
            dma("pool", maskS[:, s_, :], c_maskS[s_, :, :], [], ["maskS"])
        dma("pool", psel[:, :], c_psel[:, :], [], ["psel"])
    dma("sp", sel[:, :], c_sel[:, :], [], ["sel"])
    dma("sp", pinvE[:, :, :], c_pinv[:, :].rearrange("p (a b) -> p a b", a=4), [], ["pinvE"])
    for i, v in enumerate([1.0 / 128, 1.0 / 1024, 1.0 / 512, 1.0]):
        G(lambda e, i=i, v=v: e.memset(ones_b[:, i, :], v), [], ["ones"])
    G(lambda e: e.memset(stp[:, :], 0.0), [], ["stp"])

    def load_w(src_ap, i):
        key = f"W{i}"
        S.op("pool", lambda e: e.dma_start(out=wbuf[i][:, :, :], in_=src_ap), reads=[], writes=[key], dma=True)
        return key

    def w_tile(wt, l, rows0, col0):
        return wt[l, rows0:rows0 + 2048, col0:col0 + 512].rearrange("(c p) m -> p c m", p=128)

    def drow(blk):
        return QS["q"] * L + blk * 128 if blk < NB else NQ * L

    def xdk(blk):
        return ("xd", QS["q"], blk) if blk < NB else ("xd", "E")

    def actT_reads(c0, c1):
        return [("actT", b) for b in range(NBLK) if b * 128 < c1 and min((b + 1) * 128, T) > c0]

    def blk_of_cols(c0, c1):
        return [b for b in range(NBLK) if b * 128 < c1 and min((b + 1) * 128, T) > c0]

    def norm_to_actT(xsrc, gsrc, l):
        dma_tp("sp", gvec, gsrc[l, :], 16, [], ["gvec"])
        for blk in range(NBLK):
            P = 128 if blk < NB else 32
            r0 = blk * 128
            dr = drow(blk)
            dma("sp", xblk[0:P, :], xsrc[dr:dr + P, :], [xdk(blk)], XBK)
            V(lambda e, P=P: e.memset(small[0:P, 0:1], 0.0), [], ["small0"])
            ACT(lambda e, P=P: e.activation(xn[0:P, :], xblk[0:P, :], AF.Square, accum_out=small[0:P, 0:1]),
                XBK + ["small0"], ["xn", "small0"])
            ACT(lambda e, P=P: e.activation(small[0:P, 1:2], small[0:P, 0:1], AF.Ln, bias=EPS, scale=1.0 / D),
                ["small0"], ["small1"])
            ACT(lambda e, P=P: e.activation(small[0:P, 2:3], small[0:P, 1:2], AF.Exp, scale=-0.5),
                ["small1"], ["small2"])
            if "n2" in dbg:
                V(lambda e, P=P: e.tensor_scalar(xn[0:P, :], xblk[0:P, :], small[0:P, 2:3], 1.0, ALU.mult, ALU.mult),
                  XBK + ["small2", "xn"], ["xn"])
            elif "n3" in dbg:
                ACT(lambda e, P=P: e.activation(xn[0:P, :], xblk[0:P, :], AF.Copy, scale=small[0:P, 2:3]),
                    XBK + ["small2", "xn"], ["xn"])
            elif "n4" in dbg:
                pass
            else:
                V(lambda e, P=P: e.tensor_scalar(xn[0:P, :], xblk[0:P, :], small[0:P, 2:3], 1.0, ALU.mult, ALU.mult),
                  XBK + ["small2", "xn"], ["xn"])
            for c4 in range(4 if "n1" not in dbg else 0):
                pk = ("psT" if c4 % 2 == 0 else "psB2")
                pst = psT[:, 0:512] if c4 % 2 == 0 else psT_alt[:, 0:512]
                for i in range(4):
                    kc = c4 * 4 + i
                    PE(lambda e, P=P, kc=kc, i=i, pst=pst: e.transpose(pst[:, i * 128:i * 128 + P],
                                                                       xn[0:P, kc * 128:(kc + 1) * 128], ident[0:P, 0:P]),
                       ["xn", "ident"], [pk])
                for i in range(4):
                    kc = c4 * 4 + i
                    src = pst[:, i * 128:i * 128 + P]
                    dst = actT[:, kc, r0:r0 + P]
                    if "gplain" in dbg:
                        cp("act" if i % 2 == 0 else "dve", dst, src, [pk], [("actT", blk)])
                    elif (i % 2 == 0 or "gact" in dbg) and "gdve" not in dbg:
                        ACT(lambda e, src=src, dst=dst, kc=kc: e.activation(dst, src, AF.Copy, scale=gvec[:, kc:kc + 1]),
                            [pk, "gvec"], [("actT", blk)])
                    else:
                        V(lambda e, src=src, dst=dst, kc=kc: e.tensor_scalar(dst, src, gvec[:, kc:kc + 1], 1.0, ALU.mult, ALU.mult),
                          [pk, "gvec"], [("actT", blk)])

    def load_vecs(l):
        dma_slow("sp", vecs[:, 0:1], g_q[l, :].rearrange("(p o) -> p o", o=1), [], ["vecs"])
        dma_slow("sp", vecs[:, 1:2], g_k[l, :].rearrange("(p o) -> p o", o=1), [], ["vecs"])
        dma_tp("sp", vecs[:, 2:10], g_oa[l, :], 8, [], ["vecs"])
        for j, src in enumerate([g_os, g_op, D_skip, b_glu, pool_scale]):
            dma_slow("sp", vecs[:, 10 + 4 * j:14 + 4 * j], src[l, :].rearrange("(t p) -> p t", p=128), [], ["vecs"])
        V(lambda e: e.tensor_scalar(vecs[:, 30:34], vecs[:, 22:26], -1.0, 1.0, ALU.mult, ALU.mult), ["vecs"], ["vecs"])
        if "novecx" in dbg:
            return
        dma("pool", wglu[:, :, :], w_glu[l, :, :].rearrange("(c p) m -> p c m", p=128), [], ["wglu"])
        dma("pool", wpool[:, :, :], w_pool[l, :, :, :].rearrange("g c d -> c g d"), [], ["wpool"])
        dma("sp", sinkrow[0:1, 0:8], sinks[l:l + 1, :], [], ["sinkrow"])
        ACT(lambda e: e.activation(sinkrow[0:1, 8:16], sinkrow[0:1, 0:8], AF.Exp), ["sinkrow"], ["sinkrow"])
        V(lambda e: e.tensor_copy(esink[0:1, :, :], sinkrow[0:1, 8:16].unsqueeze(2).to_broadcast([1, 8, 128])),
          ["sinkrow"], ["esink"])

    def qk_finish(ps, n, c0, dst, gcol, wkey, fkey):
        ACT(lambda e: e.copy(qraw[:, 0:n], ps[:, 0:n]), [wkey], ["qraw"])
        G(lambda e: e.tensor_tensor(sqb[:, 0:n], qraw[:, 0:n], qraw[:, 0:n], ALU.mult), ["qraw"], ["sqb"])
        PE(lambda e: e.matmul(psB[0][:, 0:n], ones_b[:, 0, :], sqb[:, 0:n], start=True, stop=True),
           ["sqb", "ones"], ["psB0"])
        ACT(lambda e: e.activation(rtmp[:, 0:n], psB[0][:, 0:n], AF.Ln, bias=EPS, scale=1.0), ["psB0"], ["rtmp"])
        ACT(lambda e: e.activation(rtmp[:, 0:n], rtmp[:, 0:n], AF.Exp, scale=-0.5), ["rtmp"], ["rtmp"])
        V(lambda e: e.scalar_tensor_tensor(dst, qraw[:, 0:n], vecs[:, gcol:gcol + 1], rtmp[:, 0:n], ALU.mult, ALU.mult),
          ["qraw", "rtmp", "vecs"], [wkey + "_d"])
        PE(lambda e: e.matmul(psB[1][0:32, 0:n], prot[:, :], dst, start=True, stop=True), [wkey + "_d", "prot"], ["psB1"])
        V(lambda e: e.tensor_tensor(rtmp2[0:32, 0:n], psB[1][0:32, 0:n], rope[:, 1, c0:c0 + n], ALU.mult),
          ["psB1", "rope"], ["rtmp2"])
        V(lambda e: e.tensor_tensor(rtmp[0:32, 0:n], dst[0:32], rope[:, 0, c0:c0 + n], ALU.mult),
          [wkey + "_d", "rope", "rtmp"], ["rtmp"])
        V(lambda e: e.tensor_tensor(dst[0:32], rtmp[0:32, 0:n], rtmp2[0:32, 0:n], ALU.add),
          ["rtmp", "rtmp2", wkey + "_d"], [wkey + "_d", fkey])

    def w_in_phase(l):
        order = [3, 4, 0, 1, 2]
        for oi, ti in enumerate(order):
            bi = oi % 2
            wkey = load_w(w_tile(w_in, l, 0, ti * 512), bi)
            wb = wbuf[bi]
            for gi, (c0, c1) in enumerate(CG):
                n = c1 - c0
                for mt in range(4):
                    if ti == 2 and mt >= 2:
                        break
                    ps = psA[mt]
                    pk = f"psA{mt}"
                    for kc in range(16):
                        PE(lambda e, ps=ps, kc=kc, mt=mt, wb=wb, c0=c0, c1=c1, n=n:
                           e.matmul(ps[:, 0:n], wb[:, kc, mt * 128:(mt + 1) * 128], actT[:, kc, c0:c1],
                                    start=(kc == 0), stop=(kc == 15)),
                           [wkey] + actT_reads(c0, c1), [pk])
                    if ti in (0, 1):
                        h = ti * 4 + mt
                        qk_finish(ps, n, c0, qT[:, h, c0:c1], 0, pk, ("qT", h, gi))
                    elif ti == 2:
                        qk_finish(ps, n, c0, kT[:, mt, c0:c1], 1, pk, ("kT", mt, gi))
                    elif ti == 3:
                        ACT(lambda e, ps=ps, mt=mt, c0=c0, c1=c1, n=n: e.copy(uT[:, mt, c0:c1], ps[:, 0:n]),
                            [pk], [("uT", gi)])
                    else:
                        V(lambda e, ps=ps, mt=mt, c0=c0, c1=c1, n=n:
                          e.tensor_copy(xpT[:, mt, 16 + c0:16 + c1], ps[:, 0:n]), [pk], [("xpT", gi)])
            if ti == 2:
                for blk in range(NBLK):
                    P = 128 if blk < NB else 32
                    r0 = blk * 128
                    ps = psA[2 + blk % 2]
                    pk = f"psA{2 + blk % 2}"
                    for kc in range(16):
                        PE(lambda e, ps=ps, kc=kc, wb=wb, r0=r0, P=P:
                           e.matmul(ps[0:P, 0:256], actT[:, kc, r0:r0 + P], wb[:, kc, 256:512],
                                    start=(kc == 0), stop=(kc == 15)),
                           [wkey, ("actT", blk)], [pk])
                    ACT(lambda e, ps=ps, blk=blk, P=P: e.copy(vbf[0:P, blk, :], ps[0:P, 0:256]), [pk], [("vbf", blk)])

    SMI = dict(rho=0, aC_r=1, aC_i=2, aL_r=3, aL_i=4, a1_r=5, a1_i=6, f_r=7, f_i=8, are=9, aim=10, dt=11, dre=12, th=13,
               t0=14, t1=15, t2=16, t3=17)

    def smv(name):
        return sm[:, SMI[name], :]

    def ssm_prep(l):
        SK = ["z_re", "z_im", "k_re", "k_im"]
        dma_tp("sp", smv("are"), A_re[l, :], 16, [], ["sm_in"])
        dma_tp("sp", smv("aim"), A_im[l, :], 16, [], ["sm_in"])
        ld = log_dt[l, :].rearrange("(t h) -> h t", h=2)
        dma_slow("sp", sm[0:64, SMI["dt"], :], ld[0:1, :].partition_broadcast(64), [], ["sm_in"])
        dma_slow("sp", sm[64:128, SMI["dt"], :], ld[1:2, :].partition_broadcast(64), [], ["sm_in"])
        ACT(lambda e: e.activation(smv("dt"), smv("dt"), AF.Exp), ["sm_in"], ["sm_dt"])
        V(lambda e: e.tensor_tensor(smv("dre"), smv("dt"), smv("are"), ALU.mult), ["sm_dt", "sm_in"], ["sm_dre"])
        V(lambda e: e.tensor_tensor(smv("th"), smv("dt"), smv("aim"), ALU.mult), ["sm_dt", "sm_in"], ["sm_th"])
        ACT(lambda e: e.activation(smv("rho"), smv("dre"), AF.Exp), ["sm_dre"], ["sm_rho"])
        W65 = 16 * (CL + 1)
        jt = scr[:, 0:W65].rearrange("p (a b) -> p a b", a=16)
        ang = scr[:, W65:2 * W65].rearrange("p (a b) -> p a b", a=16)
        rp = scr[:, 2 * W65:3 * W65].rearrange("p (a b) -> p a b", a=16)
        tq = scr[:, 0:W65]
        angf = scr[:, W65:2 * W65]
        dma("sp", scr[:, 0:W65], c_jtab[:, :], [], SK + ["W1"])
        V(lambda e: e.tensor_tensor(ang, jt, smv("th").unsqueeze(2).to_broadcast([128, 16, CL + 1]), ALU.mult),
          SK + ["sm_th"], SK)
        V(lambda e: e.tensor_tensor(rp, jt, smv("dre").unsqueeze(2).to_broadcast([128, 16, CL + 1]), ALU.mult),
          SK + ["sm_dre"], SK)
        ACT(lambda e: e.activation(scr[:, 2 * W65:3 * W65], scr[:, 2 * W65:3 * W65], AF.Exp), SK, SK)
        tqi = tq.bitcast(I32)
        V(lambda e: e.tensor_scalar(tq, angf, 1.0 / (2 * PI), 1.0, ALU.mult, ALU.mult), SK, SK)
        V(lambda e: e.tensor_copy(tqi, tq), SK, SK)
        V(lambda e: e.tensor_copy(tq, tqi), SK, SK)
        V(lambda e: e.scalar_tensor_tensor(angf, tq, -2 * PI, angf, ALU.mult, ALU.add), SK, SK)
        V(lambda e: e.tensor_scalar(angf, angf, -PI, PI, ALU.max, ALU.min), SK, SK)
        ACT(lambda e: e.activation(sinT[:, :, :].rearrange("p a b -> p (a b)"), angf, AF.Sin), SK, ["sinT"])
        ACT(lambda e: e.activation(tq, angf, AF.Abs), SK, SK)
        ACT(lambda e: e.activation(cosT[:, :, :].rearrange("p a b -> p (a b)"), tq, AF.Sin, bias=PI / 2, scale=-1.0),
            SK, ["cosT"])
        V(lambda e: e.tensor_tensor(smv("a1_r"), rp[:, :, 1], cosT[:, :, 1], ALU.mult), SK + ["cosT"], ["sm_a1"])
        V(lambda e: e.tensor_tensor(smv("a1_i"), rp[:, :, 1], sinT[:, :, 1], ALU.mult), SK + ["sinT"], ["sm_a1"])
        V(lambda e: e.tensor_tensor(smv("aC_r"), rp[:, :, CL], cosT[:, :, CL], ALU.mult), SK + ["cosT"], ["sm_aC"])
        V(lambda e: e.tensor_tensor(smv("aC_i"), rp[:, :, CL], sinT[:, :, CL], ALU.mult), SK + ["sinT"], ["sm_aC"])
        V(lambda e: e.tensor_copy(Dk[:, :, :], smv("rho").unsqueeze(2).to_broadcast([128, 16, CL])), ["sm_rho"], ["Dk"])
        V(lambda e: e.memset(Dk[:, :, 0:1], 0.0), ["Dk"], ["Dk"])
        V(lambda e: e.tensor_copy(Dk16[:, :, :], smv("rho").unsqueeze(2).to_broadcast([128, 16, 16])), ["sm_rho"], ["Dk16"])
        V(lambda e: e.memset(Dk16[:, :, 0:1], 0.0), ["Dk16"], ["Dk16"])
        G(lambda e: e.tensor_copy(smv("aL_r"), smv("aC_r")), ["sm_aC"], ["sm_aL"])
        G(lambda e: e.tensor_copy(smv("aL_i"), smv("aC_i")), ["sm_aC"], ["sm_aL"])
        for _ in range(NSQ):
            tt("pool", smv("t0"), smv("aL_r"), smv("aL_r"), ALU.mult, ["sm_aL"], ["sm_t0"])
            tt("pool", smv("t1"), smv("aL_i"), smv("aL_i"), ALU.mult, ["sm_aL"], ["sm_t1"])
            tt("pool", smv("t2"), smv("aL_r"), smv("aL_i"), ALU.mult, ["sm_aL"], ["sm_t2"])
            tt("pool", smv("aL_r"), smv("t0"), smv("t1"), ALU.subtract, ["sm_t0", "sm_t1", "sm_aL"], ["sm_aL"])
            tt("pool", smv("aL_i"), smv("t2"), smv("t2"), ALU.add, ["sm_t2", "sm_aL"], ["sm_aL"])
        tt("dve", smv("t0"), smv("are"), smv("are"), ALU.mult, ["sm_in", "sm_t0"], ["sm_t0"])
        tt("dve", smv("t1"), smv("aim"), smv("aim"), ALU.mult, ["sm_in", "sm_t1"], ["sm_t1"])
        tt("dve", smv("t0"), smv("t0"), smv("t1"), ALU.add, ["sm_t0", "sm_t1"], ["sm_t0"])
        V(lambda e: e.reciprocal(smv("t0"), smv("t0")), ["sm_t0"], ["sm_t0"])
        V(lambda e: e.tensor_scalar(smv("t1"), smv("a1_r"), -1.0, 1.0, ALU.add, ALU.mult), ["sm_a1", "sm_t1"], ["sm_t1"])
        tt("dve", smv("t2"), smv("t1"), smv("are"), ALU.mult, ["sm_t1", "sm_in", "sm_t2"], ["sm_t2"])
        tt("dve", smv("t3"), smv("a1_i"), smv("aim"), ALU.mult, ["sm_a1", "sm_in"], ["sm_t3"])
        tt("dve", smv("t2"), smv("t2"), smv("t3"), ALU.add, ["sm_t2", "sm_t3"], ["sm_t2"])
        tt("dve", smv("f_r"), smv("t2"), smv("t0"), ALU.mult, ["sm_t2", "sm_t0"], ["sm_f"])
        tt("dve", smv("t2"), smv("a1_i"), smv("are"), ALU.mult, ["sm_a1", "sm_in", "sm_t2"], ["sm_t2"])
        tt("dve", smv("t3"), smv("t1"), smv("aim"), ALU.mult, ["sm_t1", "sm_in", "sm_t3"], ["sm_t3"])
        tt("dve", smv("t2"), smv("t2"), smv("t3"), ALU.subtract, ["sm_t2", "sm_t3"], ["sm_t2"])
        tt("dve", smv("f_i"), smv("t2"), smv("t0"), ALU.mult, ["sm_t2", "sm_t0"], ["sm_f"])
        Bs_r = z_re[:, 0:256].rearrange("p (a b) -> p a b", a=16)
        Bs_i = z_re[:, 256:512].rearrange("p (a b) -> p a b", a=16)
        Bb_r = z_re[:, 512:768].rearrange("p (a b) -> p a b", a=16)
        Bb_i = z_re[:, 768:1024].rearrange("p (a b) -> p a b", a=16)
        Bt = z_im[:, 0:256].rearrange("p (a b) -> p a b", a=16)
        Mp = [k_re.bitcast(BF16), k_im.bitcast(BF16)]
        dma("sp", Bs_r, B_re[l, :, :].rearrange("(t p) c -> p t c", p=128), [], SK)
        dma("sp", Bs_i, B_im[l, :, :].rearrange("(t p) c -> p t c", p=128), [], SK)
        fr = smv("f_r").unsqueeze(2).to_broadcast([128, 16, 16])
        fi = smv("f_i").unsqueeze(2).to_broadcast([128, 16, 16])
        tt("dve", Bb_r, Bs_r, fr, ALU.mult, SK + ["sm_f"], SK)
        tt("dve", Bt, Bs_i, fi, ALU.mult, SK + ["sm_f"], SK)
        tt("dve", Bb_r, Bb_r, Bt, ALU.subtract, SK, SK)
        tt("dve", Bb_i, Bs_i, fr, ALU.mult, SK + ["sm_f"], SK)
        tt("dve", Bt, Bs_r, fi, ALU.mult, SK + ["sm_f"], SK)
        tt("dve", Bb_i, Bb_i, Bt, ALU.add, SK, SK)
        for ri, Bb in enumerate((Bb_r, Bb_i)):
            V(lambda e, ri=ri: e.memset(Mp[ri], 0.0), SK, SK)
            for h in range(2):
                base = Mp[ri][64 * h:64 * h + 64, 0:1]
                for a in range(4):
                    dst = bass.AP(base.tensor, base.offset + 16 * h + 512 * a, [[base.ap[0][0], 64], [160, 4], [1, 16]])
                    src = Bb[64 * h:64 * h + 64, 4 * a:4 * a + 4, :]
                    cp("dve", dst, src, SK, SK)
        for ri in range(2):
            for t4 in range(4):
                pk = "psT"
                pst = psT[:, (t4 % 2) * 512:(t4 % 2 + 1) * 512]
                for i in range(4):
                    t = t4 * 4 + i
                    PE(lambda e, pst=pst, i=i, t=t, ri=ri: e.transpose(pst[:, i * 128:(i + 1) * 128],
                                                                       Mp[ri][:, t * 128:(t + 1) * 128], ident[:, :]),
                       SK + ["ident"], [pk])
                dst = BbT[:, ri * 16 + t4 * 4:ri * 16 + t4 * 4 + 4, :]
                V(lambda e, dst=dst, pst=pst: e.tensor_copy(dst, pst.rearrange("p (a b) -> p a b", a=4)),
                  [pk], ["BbT", "corr"])
        for ri, Csrc in enumerate((C_re, C_im)):
            Zf = scr[0:32, 0:2048].rearrange("p (a b) -> p a b", a=16)
            V(lambda e, Zf=Zf: e.memset(Zf, 0.0), SK, SK)
            cv_ = Csrc[l, :, :, :].rearrange("(t h) c n -> h c t n", h=2)
            dma("sp", Zf[0:16, :, 0:64], cv_[0, :, :, :], [], SK)
            dma("sp", Zf[16:32, :, 64:128], cv_[1, :, :, :], [], SK)
            CTp = CTp_r if ri == 0 else CTp_ni
            ms("dve", CTp[:, :, :], 0.0, [], ["CTp"])
            base = CTp[:, 0, 0:1]
            for t4 in range(4):
                ps = psB[2]
                for i in range(4):
                    t = t4 * 4 + i
                    tr(ps[:, i * 32:(i + 1) * 32], Zf[:, t, :], identf[0:32, 0:32], SK + ["identf"], ["psB2"])
                dst = bass.AP(base.tensor, base.offset + 512 * t4, [[base.ap[0][0], 128], [160, 4], [1, 32]])
                srcv = ps[:, 0:128].rearrange("p (a b) -> p a b", a=4)
                if ri == 0:
                    cp("dve", dst, srcv, ["psB2", "CTp"], ["CTp"])
                else:
                    ts("dve", dst, srcv, -1.0, None, ALU.mult, None, ["psB2", "CTp"], ["CTp"])

    def ssm_chunk(c):
        c0 = c * CL
        gi = [i for i, (a, b) in enumerate(CG) if a <= c0 < b][0]
        xr = psX[:, 0:2, :].rearrange("p a b -> p (a b)")
        xi = psX[:, 2:4, :].rearrange("p a b -> p (a b)")
        for ri in range(2):
            for t in range(16):
                mm(psX[:, 2 * ri + t // 8, (t % 8) * CL:(t % 8 + 1) * CL], BbT[:, ri * 16 + t, :], uT[:, t // 4, c0:c0 + CL],
                   True, True, ["BbT", ("uT", gi)], [f"psA{2 * ri}", f"psA{2 * ri + 1}"])
        C3 = cosT[:, :, 0:CL]; S3 = sinT[:, :, 0:CL]
        v3 = lambda ap: ap.rearrange("p (a b) -> p a b", a=16)
        XR = ["psA0", "psA1"]; XI = ["psA2", "psA3"]
        tt("dve", v3(z_re), v3(xr), C3, ALU.mult, XR + ["cosT", "z_re"], ["z_re"])
        tt("dve", v3(k_re), v3(xi), S3, ALU.mult, XI + ["sinT", "k_re"], ["k_re"])
        tt("dve", z_re, z_re, k_re, ALU.add, ["z_re", "k_re"], ["z_re"])
        tt("dve", v3(z_im), v3(xi), C3, ALU.mult, XI + ["cosT", "z_im"], ["z_im"])
        tt("dve", v3(k_im), v3(xr), S3, ALU.mult, XR + ["sinT", "k_im"], ["k_im"])
        tt("dve", z_im, z_im, k_im, ALU.subtract, ["z_im", "k_im"], ["z_im"])
        sr = scur[:, 0:16]; si = scur[:, 16:32]
        tt("dve", smv("t0"), smv("a1_r"), sr, ALU.mult, ["sm_a1", "scur", "sm_t0"], ["sm_t0"])
        tt("dve", smv("t1"), smv("a1_i"), si, ALU.mult, ["sm_a1", "scur", "sm_t1"], ["sm_t1"])
        tt("dve", smv("t0"), smv("t0"), smv("t1"), ALU.subtract, ["sm_t0", "sm_t1"], ["sm_t0"])
        tt("dve", v3(z_re)[:, :, 0], v3(z_re)[:, :, 0], smv("t0"), ALU.add, ["z_re", "sm_t0"], ["z_re"])
        tt("dve", smv("t2"), smv("a1_r"), si, ALU.mult, ["sm_a1", "scur", "sm_t2"], ["sm_t2"])
        tt("dve", smv("t3"), smv("a1_i"), sr, ALU.mult, ["sm_a1", "scur", "sm_t3"], ["sm_t3"])
        tt("dve", smv("t2"), smv("t2"), smv("t3"), ALU.add, ["sm_t2", "sm_t3"], ["sm_t2"])
        tt("dve", v3(z_im)[:, :, 0], v3(z_im)[:, :, 0], smv("t2"), ALU.add, ["z_im", "sm_t2"], ["z_im"])
        dk = Dk[:, :, :].rearrange("p a b -> p (a b)")
        scan(k_re, dk, z_re, ["Dk", "z_re", "k_re"], ["k_re"])
        scan(k_im, dk, z_im, ["Dk", "z_im", "k_im"], ["k_im"])
        tt("dve", v3(z_re), v3(k_re), C3, ALU.mult, ["k_re", "cosT", "z_re"], ["z_re"])
        tt("dve", v3(z_im), v3(k_im), S3, ALU.mult, ["k_im", "sinT", "z_im"], ["z_im"])
        tt("dve", z_re, z_re, z_im, ALU.subtract, ["z_re", "z_im"], ["z_re"])
        tt("dve", v3(z_im), v3(k_im), C3, ALU.mult, ["k_im", "cosT", "z_im"], ["z_im"])
        tt("dve", v3(k_re), v3(k_re), S3, ALU.mult, ["k_re", "sinT"], ["k_re"])
        tt("dve", z_im, z_im, k_re, ALU.add, ["z_im", "k_re"], ["z_im"])
        cp("dve", scur[:, 0:16], v3(z_re)[:, :, CL - 1], ["z_re", "scur"], ["scur"])
        cp("dve", scur[:, 16:32], v3(z_im)[:, :, CL - 1], ["z_im", "scur"], ["scur"])
        for ct in range(4):
            for i in range(4):
                t = ct * 4 + i
                mm(psB[2][:, ct * CL:(ct + 1) * CL], CTp_r[:, t, :], v3(z_re)[:, t, :], (i == 0), False, ["CTp", "z_re"], ["psB2"])
                mm(psB[2][:, ct * CL:(ct + 1) * CL], CTp_ni[:, t, :], v3(z_im)[:, t, :], False, (i == 3), ["CTp", "z_im"], ["psB2"])
        cp("act", actT[:, 8:12, c0:c0 + CL], psB[2][:, 0:4 * CL].rearrange("p (a b) -> p a b", a=4), ["psB2"], [("ysm", c)])

    def ssm_E(l):
        SK = ["z_re", "z_im", "k_re", "k_im"]
        giE = len(CG) - 1
        for ri in range(2):
            for t in range(16):
                PE(lambda e, ri=ri, t=t: e.matmul(psX[:, 2 * ri + t // 8, (t % 8) * CL:(t % 8) * CL + 32],
                                                  BbT[:, ri * 16 + t, :], uT[:, t // 4, E0:E0 + 32], start=True, stop=True),
                   ["BbT", ("uT", giE)], [f"psA{2 * ri}", f"psA{2 * ri + 1}"])
        xr = psX[:, 0:2, :].rearrange("p a b -> p (a b)").rearrange("p (a b) -> p a b", a=16)
        xi = psX[:, 2:4, :].rearrange("p a b -> p (a b)").rearrange("p (a b) -> p a b", a=16)
        XR = ["psA0", "psA1"]; XI = ["psA2", "psA3"]
        C3 = cosT[:, :, 0:16]; S3 = sinT[:, :, 0:16]
        m3 = lambda ap: ap[:, 0:256].rearrange("p (a b) -> p a b", a=16)
        zr = z_re[:, 0:256]; zi = z_im[:, 0:256]; kr = k_re[:, 0:256]; kim = k_im[:, 0:256]
        tt("dve", m3(z_re), xr[:, :, 0:16], C3, ALU.mult, XR + ["cosT", "z_re"], ["z_re"])
        tt("dve", m3(k_re), xi[:, :, 0:16], S3, ALU.mult, XI + ["sinT", "k_re"], ["k_re"])
        tt("dve", zr, zr, kr, ALU.add, ["z_re", "k_re"], ["z_re"])
        tt("dve", m3(z_im), xi[:, :, 0:16], C3, ALU.mult, XI + ["cosT", "z_im"], ["z_im"])
        tt("dve", m3(k_im), xr[:, :, 0:16], S3, ALU.mult, XR + ["sinT", "k_im"], ["k_im"])
        tt("dve", zi, zi, kim, ALU.subtract, ["z_im", "k_im"], ["z_im"])
        dk = Dk16[:, :, :].rearrange("p a b -> p (a b)")
        V(lambda e: e.tensor_tensor_scan(kr, dk, zr, 0.0, ALU.mult, ALU.add), ["Dk16", "z_re", "k_re"], ["k_re"])
        V(lambda e: e.tensor_tensor_scan(kim, dk, zi, 0.0, ALU.mult, ALU.add), ["Dk16", "z_im", "k_im"], ["k_im"])
        ms("dve", hE_re, 0.0, ["rtmp"], ["rtmp", "hE"])
        ms("dve", hE_im, 0.0, ["rtmp2"], ["rtmp2", "hE"])
        tt("dve", m3(z_re), m3(k_re), C3, ALU.mult, ["k_re", "cosT", "z_re"], ["z_re"])
        tt("dve", m3(z_im), m3(k_im), S3, ALU.mult, ["k_im", "sinT", "z_im"], ["z_im"])
        tt("dve", hE_re[:, :, 0:16], m3(z_re), m3(z_im), ALU.subtract, ["z_re", "z_im", "hE"], ["hE"])
        tt("dve", Gst[:, NCH, 0:16], m3(z_re)[:, :, 15], m3(z_im)[:, :, 15], ALU.subtract, ["z_re", "z_im"], [("G", NCH)])
        tt("dve", m3(z_re), m3(k_im), C3, ALU.mult, ["k_im", "cosT", "z_re"], ["z_re"])
        tt("dve", m3(z_im), m3(k_re), S3, ALU.mult, ["k_re", "sinT", "z_im"], ["z_im"])
        tt("dve", hE_im[:, :, 0:16], m3(z_re), m3(z_im), ALU.add, ["z_re", "z_im", "hE"], ["hE"])
        tt("dve", Gst[:, NCH, 16:32], m3(z_re)[:, :, 15], m3(z_im)[:, :, 15], ALU.add, ["z_re", "z_im"], [("G", NCH)])
        dma("sp", sttok[:, 0, :], st_re[l, :, :], [], ["sttok"])
        dma("sp", sttok[:, 1, :], st_im[l, :, :], [], ["sttok"])
        for ri in range(2):
            PE(lambda e, ri=ri: e.transpose(psB[2][:, ri * 64:(ri + 1) * 64], sttok[:, ri, :], identf[0:64, 0:64]),
               ["sttok", "identf"], ["psB2"])
        V(lambda e: e.tensor_copy(h0s[:, :, :], psB[2][:, 0:128].rearrange("p (a b) -> p a b", a=2)), ["psB2"], ["h0s"])
        h0r = h0s[:, 0, :].rearrange("p (s t) -> p s t", s=4); h0i = h0s[:, 1, :].rearrange("p (s t) -> p s t", s=4)
        hsr = hs[:, 0, :].rearrange("p (s t) -> p s t", s=4); hsi = hs[:, 1, :].rearrange("p (s t) -> p s t", s=4)
        a1r = smv("a1_r").unsqueeze(1).to_broadcast([128, 4, 16]); a1i = smv("a1_i").unsqueeze(1).to_broadcast([128, 4, 16])
        t0 = z_re[:, 0:64].rearrange("p (s t) -> p s t", s=4); t1 = z_im[:, 0:64].rearrange("p (s t) -> p s t", s=4)
        xrs = xr[:, :, 16:20].rearrange("p t s -> p s t"); xis = xi[:, :, 16:20].rearrange("p t s -> p s t")
        tt("dve", t0, h0r, a1r, ALU.mult, ["h0s", "sm_a1", "z_re"], ["z_re"])
        tt("dve", t1, h0i, a1i, ALU.mult, ["h0s", "sm_a1", "z_im"], ["z_im"])
        tt("dve", t0, t0, t1, ALU.subtract, ["z_re", "z_im"], ["z_re"])
        tt("dve", hsr, t0, xrs, ALU.add, ["z_re"] + XR, ["hs"])
        tt("dve", t0, h0i, a1r, ALU.mult, ["h0s", "sm_a1", "z_re"], ["z_re"])
        tt("dve", t1, h0r, a1i, ALU.mult, ["h0s", "sm_a1", "z_im"], ["z_im"])
        tt("dve", t0, t0, t1, ALU.add, ["z_re", "z_im"], ["z_re"])
        tt("dve", hsi, t0, xis, ALU.add, ["z_re"] + XI, ["hs"])
        V(lambda e: e.tensor_copy(hE_re[:, :, 16:20], hsr.rearrange("p s t -> p t s")), ["hs", "hE"], ["hE"])
        V(lambda e: e.tensor_copy(hE_im[:, :, 16:20], hsi.rearrange("p s t -> p t s")), ["hs", "hE"], ["hE"])
        for ri in range(2):
            PE(lambda e, ri=ri: e.transpose(psB[2][0:64, ri * 128:(ri + 1) * 128], hs[:, ri, :], identf[:, :]),
               ["hs", "identf"], ["psB2"])
        V(lambda e: e.tensor_copy(outst[0:64, 0:256], psB[2][0:64, 0:256]), ["psB2"], ["outst"])
        for ri in range(2):
            dma("sp", o_sssm[l, ri, :, :], outst[0:64, ri * 128:(ri + 1) * 128], ["outst"], [("out", "sssm", l, ri)])
        for ct in range(4):
            for i in range(4):
                t = ct * 4 + i
                PE(lambda e, ct=ct, t=t, i=i: e.matmul(psB[2][:, ct * 32:(ct + 1) * 32], CTp_r[:, t, :], hE_re[:, t, :],
                                                       start=(i == 0), stop=False), ["CTp", "hE", "rtmp"], ["psB2"])
                PE(lambda e, ct=ct, t=t, i=i: e.matmul(psB[2][:, ct * 32:(ct + 1) * 32], CTp_ni[:, t, :], hE_im[:, t, :],
                                                       start=False, stop=(i == 3)), ["CTp", "hE", "rtmp2"], ["psB2"])
        ACT(lambda e: e.copy(actT[:, 8:12, E0:E0 + 32], psB[2][:, 0:128].rearrange("p (a b) -> p a b", a=4)),
            ["psB2"], [("ysm", "E")])

    def cmuladd(dst_r, dst_i, a_r, a_i, s_r, s_i, g_r, g_i, rd, wr):
        tt("pool", smv("t0"), a_r, s_r, ALU.mult, rd + ["sm_t0"], ["sm_t0"])
        tt("pool", smv("t1"), a_i, s_i, ALU.mult, rd + ["sm_t1"], ["sm_t1"])
        tt("pool", smv("t2"), a_r, s_i, ALU.mult, rd + ["sm_t2"], ["sm_t2"])
        tt("pool", smv("t3"), a_i, s_r, ALU.mult, rd + ["sm_t3"], ["sm_t3"])
        tt("pool", smv("t0"), smv("t0"), smv("t1"), ALU.subtract, ["sm_t0", "sm_t1"], ["sm_t0"])
        tt("pool", smv("t2"), smv("t2"), smv("t3"), ALU.add, ["sm_t2", "sm_t3"], ["sm_t2"])
        if g_r is not None:
            tt("pool", dst_r, smv("t0"), g_r, ALU.add, rd + ["sm_t0"], wr)
            tt("pool", dst_i, smv("t2"), g_i, ALU.add, rd + ["sm_t2"], wr)
        else:
            G(lambda e: e.tensor_copy(dst_r, smv("t0")), rd + ["sm_t0"], wr)
            G(lambda e: e.tensor_copy(dst_i, smv("t2")), rd + ["sm_t2"], wr)

    def state_init(l, q):
        if q == 0:
            cp("dve", scur[:, :], Gst[:, NCH, :], [("G", NCH), "scur"], ["scur"])
        else:
            cp("dve", scur[:, :], sfin[:, l, :], [("sfin", l), "scur"], ["scur"])

    def local_state(l, q):
        giE = len(CG) - 1
        hk = hkL[l]; hv = hvL[l]
        if q == 0:
            ms("dve", hk[:, :, :], 0.0, [], [("hk", l)])
            cp("dve", hk[:, :, 0:16], kT[:, :, E0:E0 + 16], [("kT", 0, giE), ("kT", 1, giE), ("hk", l)], [("hk", l)])
            ms("dve", hv[:, :], 0.0, [], [("hv", l)])
            cp("dve", hv[0:16, :], vbf[0:16, NB, :], [("vbf", NB), ("hv", l)], [("hv", l)])
            cp("dve", xpT[:, :, 0:16], xpT[:, :, 16 + E0:16 + E0 + 16], [("xpT", giE)], [("xpT", "halo")])
        else:
            cp("dve", xpT[:, :, 0:16], phL[l][:, :, :], [("ph", l)], [("xpT", "halo")])
        cp("dve", sfin[:, l, :], scur[:, :], ["scur", ("sfin", l)], [("sfin", l)])
        for ri in range(2):
            tr(psB[2][0:16, ri * 128:(ri + 1) * 128], scur[:, ri * 16:(ri + 1) * 16], identf[:, :], ["scur", "identf"], ["psB2"])
        cp("dve", outst[0:16, 256:512], psB[2][0:16, 0:256], ["psB2", "outst2"], ["outst2"])
        for ri in range(2):
            dma("sp", o_pssm[l, ri, :, :], outst[0:16, 256 + ri * 128:256 + (ri + 1) * 128], ["outst2"], [("out", "pssm", l, ri)])

    def save_halo(l):
        giL = gi_of(L - 128)
        cp("dve", hkL[l][:, :, :], kT[:, :, L - 128:L], [("kT", 0, giL), ("kT", 1, giL), ("hk", l)], [("hk", l)])
        cp("dve", hvL[l][:, :], vbf[:, NB - 1, :], [("vbf", NB - 1), ("hv", l)], [("hv", l)])
        cp("dve", phL[l][:, :, :], xpT[:, :, 16 + L - 16:16 + L], [("xpT", gi_of(L - 16)), ("ph", l)], [("ph", l)])

    def gi_of(col):
        return [i for i, (a, b) in enumerate(CG) if a <= col < b][0]

    SC = float(128 ** -0.5)

    def attn_block(blk, step):
        r0 = blk * 128
        gi = gi_of(r0)

        def one(kvh):
            i2 = (step * 2 + kvh) % 2
            pa, pb = psA[2 * i2], psA[2 * i2 + 1]
            ka, kb_ = f"psA{2 * i2}", f"psA{2 * i2 + 1}"
            eb = ebuf[i2]; ek = ("ebuf", i2)
            qv = qT[:, 4 * kvh:4 * kvh + 4, r0:r0 + 128]
            qk_ = [("qT", 4 * kvh + g, gi) for g in range(4)]
            if blk == 0:
                l_ = QS["l"]
                kprev = hkL[l_][:, kvh, :]; vprev = hvL[l_][:, kvh * 128:(kvh + 1) * 128]
                mprev = masks[:, 2 if QS["q"] == 0 else 1, :]
                rprev = [("hk", l_)]; rvprev = [("hv", l_)]
            else:
                kprev = kT[:, kvh, r0 - 128:r0]; vprev = vbf[:, blk - 1, kvh * 128:(kvh + 1) * 128]; mprev = masks[:, 1, :]
                rprev = [("kT", kvh, gi_of(r0 - 128))]; rvprev = [("vbf", blk - 1)]
            PE(lambda e: e.matmul(pa[:, :], kprev, qv, start=True, stop=False), rprev + qk_, [ka])
            PE(lambda e: e.matmul(pa[:, :], ident[:, :], mprev, start=False, stop=True), ["ident", "masks"], [ka])
            PE(lambda e: e.matmul(pb[:, :], kT[:, kvh, r0:r0 + 128], qv, start=True, stop=False), [("kT", kvh, gi)] + qk_, [kb_])
            PE(lambda e: e.matmul(pb[:, :], ident[:, :], masks[:, 0, :], start=False, stop=True), ["ident", "masks"], [kb_])
            ACT(lambda e: e.activation(eb[:, 0, :], pa[:, :], AF.Exp, scale=SC), [ka], [ek])
            ACT(lambda e: e.activation(eb[:, 1, :], pb[:, :], AF.Exp, scale=SC), [kb_], [ek])
            PE(lambda e: e.matmul(psB[0][:, :], vprev, eb[:, 0, :], start=True, stop=False), rvprev + [ek], ["psB0"])
            PE(lambda e: e.matmul(psB[0][:, :], vbf[:, blk, kvh * 128:(kvh + 1) * 128], eb[:, 1, :], start=False, stop=True),
               [("vbf", blk), ek], ["psB0"])
            PE(lambda e: e.matmul(psB[1][:, :], ones_b[:, 3, :], eb[:, 0, :], start=True, stop=False), ["ones", ek], ["psB1"])
            PE(lambda e: e.matmul(psB[1][:, :], ones_b[:, 3, :], eb[:, 1, :], start=False, stop=False), ["ones", ek], ["psB1"])
            PE(lambda e: e.matmul(psB[1][:, :], ones_b[0:1, 3, :], esink[0:1, 4 * kvh:4 * kvh + 4, :], start=False, stop=True),
               ["ones", "esink"], ["psB1"])
            V(lambda e: e.reciprocal(qraw[:, :], psB[1][:, :]), ["psB1", "qraw"], ["qraw"])
            V(lambda e: e.tensor_tensor(qv, psB[0][:, :].rearrange("p (a b) -> p a b", a=4),
                                        qraw[:, :].rearrange("p (a b) -> p a b", a=4), ALU.mult),
              ["psB0", "qraw"], [("qT", 4 * kvh + g, gi) for g in range(4)])
        for kvh in range(2):
            one(kvh)

    def attn_E():
        giE = len(CG) - 1

        def one(kvh):
            pa = psA[2 * kvh]; ka = f"psA{2 * kvh}"
            eb = ebuf[kvh]; ek = ("ebuf", kvh)
            qv = qT[:, 4 * kvh:4 * kvh + 4, E0:E0 + 32]
            qk_ = [("qT", 4 * kvh + g, giE) for g in range(4)]
            PE(lambda e: e.matmul(pa[0:32, 0:128], kT[:, kvh, E0:E0 + 32], qv, start=True, stop=False), [("kT", kvh, giE)] + qk_, [ka])
            PE(lambda e: e.matmul(pa[0:32, 0:128], ident[0:32, 0:32], maskE[:, :], start=False, stop=True), ["ident", "maskE"], [ka])
            ACT(lambda e: e.activation(eb[0:32, 0, 0:128], pa[0:32, 0:128], AF.Exp, scale=SC), [ka], [ek])
            PE(lambda e: e.matmul(psB[0][:, 0:128], vbf[0:32, NB, kvh * 128:(kvh + 1) * 128], eb[0:32, 0, 0:128], start=True, stop=True),
               [("vbf", NB), ek], ["psB0"])
            PE(lambda e: e.matmul(psB[1][:, 0:128], ones_b[0:32, 3, :], eb[0:32, 0, 0:128], start=True, stop=False), ["ones", ek], ["psB1"])
            PE(lambda e: e.matmul(psB[1][:, 0:128], ones_b[0:1, 3, :], esink[0:1, 4 * kvh:4 * kvh + 4, 0:32], start=False, stop=True),
               ["ones", "esink"], ["psB1"])
            V(lambda e: e.reciprocal(qraw[:, 0:128], psB[1][:, 0:128]), ["psB1", "qraw"], ["qraw"])
            V(lambda e: e.tensor_tensor(qT[:, 4 * kvh:4 * kvh + 4, E0:E0 + 16],
                                        psB[0][:, 0:128].rearrange("p (a b) -> p a b", a=4)[:, :, 0:16],
                                        qraw[:, 0:128].rearrange("p (a b) -> p a b", a=4)[:, :, 0:16], ALU.mult),
              ["psB0", "qraw"], [("qTm", kvh)])
        for kvh in range(2):
            one(kvh)

    def attn_samples(l):
        giE = len(CG) - 1
        for s_ in range(4):
            dma("pool", kctok[:, :], cache_k[l, s_, :, :], [], ["kctok"])
            dma("pool", vc[:, :], cache_v[l, s_, :, :], [], ["vc"])
            dma("sp", o_sk[l, s_, 0:127, :], cache_k[l, s_, 1:128, :], [], [("out", "sk", l, s_)])
            dma("sp", o_sv[l, s_, 0:127, :], cache_v[l, s_, 1:128, :], [], [("out", "sv", l, s_)])
            dma("sp", o_spool[l, s_, 0:14, :], st_pool[l, s_, 1:15, :], [], [("out", "sp", l, s_)])
            col = E0 + 16 + s_
            for kvh in range(2):
                pa = psA[2 * kvh]; ka = f"psA{2 * kvh}"
                qs = qT[:, 4 * kvh:4 * kvh + 4, col]
                qk_ = [("qT", 4 * kvh + g, giE) for g in range(4)]
                PE(lambda e, kvh=kvh: e.transpose(psT[:, 0:128], kctok[:, kvh * 128:(kvh + 1) * 128], ident[:, :]),
                   ["kctok", "ident"], ["psT"])
                V(lambda e: e.tensor_copy(kcT[:, :], psT[:, 0:128]), ["psT"], ["kcT"])
                PE(lambda e, pa=pa, qs=qs: e.matmul(pa[:, 0:4], kcT[:, :], qs, start=True, stop=True), ["kcT"] + qk_, [ka])
                PE(lambda e, pa=pa, qs=qs, kvh=kvh: e.matmul(pa[0:32, 4:8], kT[:, kvh, E0:E0 + 32], qs, start=True, stop=False),
                   [("kT", kvh, giE)] + qk_, [ka])
                PE(lambda e, pa=pa, s_=s_: e.matmul(pa[0:32, 4:8], ident[0:32, 0:32], maskS[:, s_, :], start=False, stop=True),
                   ["ident", "maskS"], [ka])
                ACT(lambda e, pa=pa: e.activation(es[:, 0:4], pa[:, 0:4], AF.Exp, scale=SC), [ka], ["es"])
                ACT(lambda e, pa=pa: e.activation(es[0:32, 4:8], pa[0:32, 4:8], AF.Exp, scale=SC), [ka], ["es"])
                PE(lambda e, kvh=kvh: e.matmul(psB[0][:, 0:4], vc[:, kvh * 128:(kvh + 1) * 128], es[:, 0:4], start=True, stop=False),
                   ["vc", "es"], ["psB0"])
                PE(lambda e, kvh=kvh: e.matmul(psB[0][:, 0:4], vbf[0:32, NB, kvh * 128:(kvh + 1) * 128], es[0:32, 4:8],
                                               start=False, stop=True), [("vbf", NB), "es"], ["psB0"])
                PE(lambda e: e.matmul(psB[1][:, 0:4], ones_b[:, 3, :], es[:, 0:4], start=True, stop=False), ["ones", "es"], ["psB1"])
                PE(lambda e: e.matmul(psB[1][:, 0:4], ones_b[0:32, 3, :], es[0:32, 4:8], start=False, stop=False), ["ones", "es"], ["psB1"])
                PE(lambda e, kvh=kvh: e.matmul(psB[1][:, 0:4], ones_b[0:1, 3, :], esink[0:1, 4 * kvh:4 * kvh + 4, 0],
                                               start=False, stop=True), ["ones", "esink"], ["psB1"])
                V(lambda e: e.reciprocal(qraw[:, 0:4], psB[1][:, 0:4]), ["psB1", "qraw"], ["qraw"])
                V(lambda e, qs=qs: e.tensor_tensor(qs, psB[0][:, 0:4], qraw[:, 0:4], ALU.mult), ["psB0", "qraw"], [("qTs", s_, kvh)])

    def kv_outputs(l):
        giL = gi_of(L - 128); giE = len(CG) - 1
        for kvh in range(2):
            PE(lambda e, kvh=kvh: e.transpose(psT[:, 512 + kvh * 128:512 + (kvh + 1) * 128], kT[:, kvh, L - 128:L], ident[:, :]),
               [("kT", kvh, giL), "ident"], ["psT"])
        V(lambda e: e.tensor_copy(outst[:, 0:256], psT[:, 512:768]), ["psT", "outst"], ["outst"])
        dma("sp", o_pk[l, :, :], outst[:, 0:256], ["outst"], [("out", "pk", l)])
        V(lambda e: e.tensor_copy(outst[:, 256:512], vbf[:, NB - 1, :]), [("vbf", NB - 1), "outst2"], ["outst2"])
        dma("sp", o_pv[l, :, :], outst[:, 256:512], ["outst2"], [("out", "pv", l)])
        for kvh in range(2):
            PE(lambda e, kvh=kvh: e.transpose(psT[0:32, 512 + kvh * 128:512 + (kvh + 1) * 128], kT[:, kvh, E0:E0 + 32], ident[:, :]),
               [("kT", kvh, giE), "ident"], ["psT"])
        V(lambda e: e.tensor_copy(outst[0:32, 0:256], psT[0:32, 512:768]), ["psT", "outst"], ["outst"])
        V(lambda e: e.tensor_copy(outst[0:32, 256:512], vbf[0:32, NB, :]), [("vbf", NB), "outst2"], ["outst2"])
        for s_ in range(4):
            dma("sp", o_sk[l, s_, 127:128, :], outst[16 + s_:17 + s_, 0:256], ["outst"], [("out", "sk2", l, s_)])
            dma("sp", o_sv[l, s_, 127:128, :], outst[16 + s_:17 + s_, 256:512], ["outst2"], [("out", "sv2", l, s_)])

    def pool_mixer(l):
        SKA = ["z_re", "z_im"]; SKB = ["k_re", "k_im"]
        W_ = 16 + L
        pa = scr[:, 0:W_]; pb = scr[:, 2048:2048 + W_]
        giE = len(CG) - 1
        allx = [("xpT", i) for i in range(len(CG))] + [("xpT", "halo")]
        for g in range(4):
            PE(lambda e, g=g: e.transpose(psT[0:32, g * 128:(g + 1) * 128], xpT[:, g, 16 + L - 32:16 + L], ident[:, :]),
               allx + ["ident"], ["psT"])
        V(lambda e: e.tensor_copy(outst[0:32, :], psT[0:32, 0:512]), ["psT", "outst", "outst2"], ["outst", "outst2"])
        dma("sp", o_ppool[l, :, :], outst[17:32, :], ["outst", "outst2"], [("out", "ppool", l)])
        for g in range(4):
            PE(lambda e, g=g: e.transpose(psT[0:32, 512 + g * 128:512 + (g + 1) * 128], xpT[:, g, 16 + E0:16 + E0 + 32], ident[:, :]),
               allx + ["ident"], ["psT"])
        V(lambda e: e.tensor_copy(outst[32:64, :], psT[0:32, 512:1024]) if False else e.tensor_copy(outst[0:32, :], psT[0:32, 512:1024]),
          ["psT", "outst", "outst2"], ["outst", "outst2"])
        for s_ in range(4):
            dma("sp", o_spool[l, s_, 14:15, :], outst[16 + s_:17 + s_, :], ["outst", "outst2"], [("out", "sp2", l, s_)])
        for g in range(4):
            w = 2 ** (g + 1)
            src = xpT[:, g, 0:W_]
            cur = None
            sh = 1
            bufs = [pa, pb]
            bkeys = [SKA, SKB]
            bi = 0
            for k_ in range(g + 1):
                out = bufs[bi]; ok = bkeys[bi]
                if cur is None:
                    tt("pool", out[:, sh:W_], src[:, sh:W_], src[:, 0:W_ - sh], ALU.add, allx + ok, ok)
                else:
                    tt("pool", out[:, sh:W_], cur[:, sh:W_], cur[:, 0:W_ - sh], ALU.add, bkeys[1 - bi] + ok, ok)
                cur = out; ck = ok
                sh *= 2
                bi = 1 - bi
            for gi, (c0, c1) in enumerate(CG):
                c1p = min(c1, L)
                n = c1p - c0
                V(lambda e, cur=cur, c0=c0, n=n, g=g, w=w: e.scalar_tensor_tensor(
                    sqb[:, 0:n], cur[:, 16 + c0:16 + c0 + n], 1.0 / w, xpT[:, g, 16 + c0:16 + c0 + n], ALU.mult, ALU.subtract),
                  ck + allx + ["sqb"], ["sqb"])
                if gi == giE:
                    pm = pmeta[:, g, :]; pm2 = pmeta2[:, g, :]
                    V(lambda e, pm=pm: e.memset(pm, 0.0), [], ["pmeta"])
                    V(lambda e, pm=pm, g=g: e.tensor_copy(pm[:, 16:32], xpT[:, g, 16 + E0:16 + E0 + 16]), allx + ["pmeta"], ["pmeta"])
                    a_, b_ = pm, pm2
                    sh2 = 1
                    for k_ in range(g + 1):
                        V(lambda e, a_=a_, b_=b_: e.tensor_copy(b_[:, 0:16], a_[:, 0:16]), ["pmeta"], ["pmeta"])
                        V(lambda e, a_=a_, b_=b_, sh2=sh2: e.tensor_tensor(b_[:, 16:32], a_[:, 16:32], a_[:, 16 - sh2:32 - sh2], ALU.add),
                          ["pmeta"], ["pmeta"])
                        a_, b_ = b_, a_
                        sh2 *= 2
                    V(lambda e, a_=a_, g=g: e.tensor_tensor(a_[:, 16:32], a_[:, 16:32], pinvE[:, g, :], ALU.mult), ["pmeta", "pinvE"], ["pmeta"])
                    V(lambda e, a_=a_, g=g, n=n: e.tensor_tensor(sqb[:, n:n + 16], a_[:, 16:32], xpT[:, g, 16 + E0:16 + E0 + 16], ALU.subtract),
                      ["pmeta", "sqb"] + allx, ["sqb"])
                    for s_ in range(4):
                        dma("pool", stp[0:15, :], st_pool[l, s_, :, :], [], ["stp"])
                        PE(lambda e, g=g, s_=s_: e.matmul(psB[2][:, s_:s_ + 1], stp[0:16, g * 128:(g + 1) * 128], psel[0:16, g:g + 1],
                                                          start=True, stop=True), ["stp", "psel"], ["psB2"])
                    xs4 = xpT[:, g, 16 + E0 + 16:16 + E0 + 20]
                    V(lambda e, xs4=xs4: e.tensor_tensor(small[:, 4:8], psB[2][:, 0:4], xs4, ALU.add), ["psB2"] + allx, ["small4"])
                    V(lambda e, xs4=xs4, n=n, w=w: e.scalar_tensor_tensor(sqb[:, n + 16:n + 20], small[:, 4:8], 1.0 / w, xs4,
                                                                         ALU.mult, ALU.subtract), ["small4", "sqb"] + allx, ["sqb"])
                    V(lambda e, n=n: e.memset(sqb[:, n + 20:n + 32], 0.0), ["sqb"], ["sqb"])
                    n = n + 32
                PE(lambda e, g=g, n=n: e.matmul(psB[0][:, 0:n], wpool[:, g, :], sqb[:, 0:n], start=True, stop=True),
                   ["wpool", "sqb"], ["psB0"])
                ACT(lambda e, g=g, n=n, c0=c0: e.activation(actT[:, 12 + g, c0:c0 + n], psB[0][:, 0:n], AF.Copy, scale=vecs[:, 26 + g:27 + g]),
                    ["psB0", "vecs"], [("plT", g, gi)])

    GC = float(2.0 * np.sqrt(2.0 / np.pi))

    def ssm_post():
        for gi, (c0, c1) in enumerate(CG):
            n = c1 - c0
            ysk = [("ysm", c) for c in range(NCH) if c0 <= c * CL < c1] + ([("ysm", "E")] if c1 > L else [])
            for ct in range(4):
                yv = actT[:, 8 + ct, c0:c1]
                uv = uT[:, ct, c0:c1]
                V(lambda e, yv=yv, uv=uv, ct=ct, n=n: e.scalar_tensor_tensor(qraw[:, 0:n], uv, vecs[:, 18 + ct:19 + ct], yv, ALU.mult, ALU.add),
                  ysk + [("uT", gi), "vecs", "qraw"], ["qraw"])
                tt("pool", rtmp2[:, 0:n], qraw[:, 0:n], qraw[:, 0:n], ALU.mult, ["qraw", "rtmp2"], ["rtmp2"])
                G(lambda e, n=n: e.tensor_scalar(rtmp2[:, 0:n], rtmp2[:, 0:n], 0.044715, 1.0, ALU.mult, ALU.add), ["rtmp2"], ["rtmp2"])
                tt("pool", rtmp2[:, 0:n], rtmp2[:, 0:n], qraw[:, 0:n], ALU.mult, ["qraw", "rtmp2"], ["rtmp2"])
                ACT(lambda e, n=n: e.activation(rtmp2[:, 0:n], rtmp2[:, 0:n], AF.Exp, scale=-GC), ["rtmp2"], ["rtmp2"])
                G(lambda e, n=n: e.tensor_scalar(rtmp2[:, 0:n], rtmp2[:, 0:n], 1.0, 1.0, ALU.add, ALU.mult), ["rtmp2"], ["rtmp2"])
                V(lambda e, n=n: e.reciprocal(rtmp2[:, 0:n], rtmp2[:, 0:n]), ["rtmp2"], ["rtmp2"])
                V(lambda e, uv=uv, n=n: e.tensor_tensor(uv, qraw[:, 0:n], rtmp2[:, 0:n], ALU.mult), ["qraw", "rtmp2", ("uT", gi)], [("uT", gi)])
            for co in range(4):
                for ci in range(4):
                    PE(lambda e, co=co, ci=ci, c0=c0, c1=c1, n=n: e.matmul(psA[co][:, 0:n], wglu[:, ci, co * 128:(co + 1) * 128],
                                                                          uT[:, ci, c0:c1], start=(ci == 0), stop=(ci == 3)),
                       ["wglu", ("uT", gi)], [f"psA{co}"])
                ACT(lambda e, co=co, n=n: e.activation(rtmp[:, 0:n], psA[co][:, 0:n], AF.Exp, bias=vecs[:, 30 + co:31 + co], scale=-1.0),
                    [f"psA{co}", "vecs", "rtmp"], ["rtmp"])
                G(lambda e, n=n: e.tensor_scalar(rtmp[:, 0:n], rtmp[:, 0:n], 1.0, 1.0, ALU.add, ALU.mult), ["rtmp"], ["rtmp"])
                V(lambda e, n=n: e.reciprocal(rtmp[:, 0:n], rtmp[:, 0:n]), ["rtmp"], ["rtmp"])
                V(lambda e, co=co, c0=c0, c1=c1, n=n: e.tensor_tensor(actT[:, 8 + co, c0:c1], uT[:, co, c0:c1], rtmp[:, 0:n], ALU.mult),
                  ["rtmp", ("uT", gi)] + ysk, [("smT", co, gi)])

    def out_norms():
        for gi, (c0, c1) in enumerate(CG):
            n = c1 - c0
            groups = [
                ([(qT[:, h, c0:c1], [("qT", h, gi), ("qTm", h // 4)] + [("qTs", s_, h // 4) for s_ in range(4)]) for h in range(8)], 1, 2, 0),
                ([(actT[:, 8 + t, c0:c1], [("smT", t, gi)]) for t in range(4)], 2, 10, 8),
                ([(actT[:, 12 + t, c0:c1], [("plT", t, gi)]) for t in range(4)], 2, 14, 12),
            ]
            for tiles, oi, gcol, slot0 in groups:
                for i, (src, rk) in enumerate(tiles):
                    tt("pool", sqb[:, 0:n], src, src, ALU.mult, rk + ["sqb"], ["sqb"])
                    PE(lambda e, i=i, oi=oi, n=n, last=(i == len(tiles) - 1): e.matmul(psB[0][:, 0:n], ones_b[:, oi, :], sqb[:, 0:n],
                                                                                     start=(i == 0), stop=last), ["sqb", "ones"], ["psB0"])
                ACT(lambda e, n=n: e.activation(rtmp[:, 0:n], psB[0][:, 0:n], AF.Ln, bias=EPS, scale=1.0), ["psB0", "rtmp"], ["rtmp"])
                ACT(lambda e, n=n: e.activation(rtmp[:, 0:n], rtmp[:, 0:n], AF.Exp, scale=-0.5), ["rtmp"], ["rtmp"])
                for i, (src, rk) in enumerate(tiles):
                    dst = actT[:, slot0 + i, c0:c1]
                    V(lambda e, src=src, dst=dst, i=i, gcol=gcol, n=n: e.scalar_tensor_tensor(dst, src, vecs[:, gcol + i:gcol + i + 1],
                                                                                             rtmp[:, 0:n], ALU.mult, ALU.mult),
                      rk + ["rtmp", "vecs"], [("actT", b) for b in blk_of_cols(c0, c1)])

    def resid_pass(wt, l, rows0, nk, lhs_of, lhs_reads, xsrc, xdst, outkey):
        for ni in range(4):
            wkey = load_w(w_tile(wt, l, rows0, ni * 512), ni % 2)
            wb = wbuf[ni % 2]
            for blk in range(NBLK):
                P = 128 if blk < NB else 32
                r0 = blk * 128
                j = blk % 3
                ps = psB[j]; pk = f"psB{j}"
                for kc in range(nk):
                    PE(lambda e, ps=ps, kc=kc, wb=wb, r0=r0, P=P: e.matmul(ps[0:P, :], lhs_of(kc, r0, P), wb[:, kc, :],
                                                                          start=(kc == 0), stop=(kc == nk - 1)),
                       [wkey] + lhs_reads(blk), [pk])
                xj = blk % 4
                dr = drow(blk)
                dma("sp", xio[xj][0:P, :], xsrc[dr:dr + P, ni * 512:(ni + 1) * 512], [xdk(blk)], [("xio", xj)])
                V(lambda e, ps=ps, xj=xj, P=P: e.tensor_tensor(xio[xj][0:P, :], ps[0:P, :], xio[xj][0:P, :], ALU.add),
                  [pk, ("xio", xj)], [("xio", xj)])
                wr = [xdk(blk)] + ([("out", outkey, QS["q"], blk, ni)] if outkey else [])
                dma("sp", xdst[dr:dr + P, ni * 512:(ni + 1) * 512], xio[xj][0:P, :], [("xio", xj)], wr)

    def ffn_phase(l, last):
        for hq in range(4):
            for t4 in range(4):
                wkey = load_w(w_tile(w_ff1, l, 0, hq * 2048 + t4 * 512), t4 % 2)
                wb = wbuf[t4 % 2]
                for gi, (c0, c1) in enumerate(CG):
                    n = c1 - c0
                    for mt in range(4):
                        ps = psA[mt]; pk = f"psA{mt}"
                        for kc in range(16):
                            PE(lambda e, ps=ps, kc=kc, mt=mt, wb=wb, c0=c0, c1=c1, n=n:
                               e.matmul(ps[:, 0:n], wb[:, kc, mt * 128:(mt + 1) * 128], actT[:, kc, c0:c1],
                                        start=(kc == 0), stop=(kc == 15)), [wkey] + actT_reads(c0, c1), [pk])
                        rt = relu_t[mt % 2]; rk = f"relu{mt % 2}"
                        ACT(lambda e, ps=ps, rt=rt, n=n: e.activation(rt[:, 0:n], ps[:, 0:n], AF.Relu), [pk], [rk])
                        m = t4 * 4 + mt
                        tt("pool", hidT[:, m, c0:c1], rt[:, 0:n], rt[:, 0:n], ALU.mult, [rk, "hidfence"], [("hid", m, gi)])
            resid_pass(w_ff2, l, hq * 2048, 16, lambda kc, r0, P: hidT[:, kc, r0:r0 + P],
                       lambda blk: [("hid", m, gi) for m in range(16) for gi in range(len(CG))
                                    if blk in blk_of_cols(*CG[gi])],
                       xs, y_out if (last and hq == 3) else xs, "y" if (last and hq == 3) else None)

    PROJ_PRED = lambda k: isinstance(k, tuple) and k[0] in ("qT", "kT", "vbf", "uT", "qTm", "qTs", "hid")
    SKK = ["z_re", "z_im", "k_re", "k_im"]
    STG = ["w_in", "chunks", "ssmE", "kvout", "state", "attnE", "samples", "corr", "attn0", "pool", "post", "mix", "w_out", "ffn"]

    def reached(name):
        return stop in STG and STG.index(stop) < STG.index(name)

    def layer_pass(q, l, first, last):
        QS["q"] = q; QS["l"] = l
        xsrc = x_in if l == 0 else xs
        S.fence(lambda k: isinstance(k, tuple) and k[0] == "ebuf", ["xn"])
        load_vecs(l)
        norm_to_actT(xsrc, g_mix, l)
        S.fence(lambda k: k == "xn", [("ebuf", 0), ("ebuf", 1)])
        S.fence(PROJ_PRED, ["projfence"])
        S.fence(lambda k: k == "W1", SKK)
        ssm_prep(l)
        S.fence(lambda k: k in SKK, ["W1"])
        w_in_phase(l)
        if reached("chunks"): return
        S.fence(lambda k: k in ("W1",), SKK)
        ssm_E(l)
        state_init(l, q)
        nblk_done = 1
        for c in range(NCH):
            ssm_chunk(c)
            if c % 2 == 1 and nblk_done < NB:
                attn_block(nblk_done, nblk_done)
                nblk_done += 1
        while nblk_done < NB:
            attn_block(nblk_done, nblk_done)
            nblk_done += 1
        if reached("kvout"): return
        kv_outputs(l)
        if reached("state"): return
        local_state(l, q)
        if reached("attnE"): return
        attn_E()
        if reached("samples"): return
        attn_samples(l)
        tap("ysm", actT[:, 8:12, :], BF16, [k for k in S.last_w])
        if reached("attn0"): return
        attn_block(0, 1)
        tap("aT", projbuf[:, 0:8 * T], BF16, [k for k in S.last_w])
        if reached("pool"): return
        pool_mixer(l)
        save_halo(l)
        tap("ysm2", actT[:, 8:12, :], BF16, [k for k in S.last_w])
        if reached("post"): return
        ssm_post()
        tap("premix", actT[:, 8:16, :], BF16, [k for k in S.last_w])
        if reached("mix"): return
        out_norms()
        tap("mixT", actT[:, :, :], BF16, [k for k in S.last_w])
        if reached("w_out"): return
        S.fence(lambda k: k in SKK, ["W1"])
        S.fence(lambda k: isinstance(k, tuple) and k[0] == "ebuf", ["xn"])
        direct = (not ffw) and last
        resid_pass(w_out, l, 0, 16, lambda kc, r0, P: actT[:, kc, r0:r0 + P], lambda blk: [("actT", blk)],
                   xsrc, y_out if direct else xs, "y" if direct else None)
        if reached("ffn") or not ffw: return
        norm_to_actT(xs, g_ffn, l)
        S.fence(PROJ_PRED, ["hidfence"])
        ffn_phase(l, last)

    PROJ_PRED = lambda k: isinstance(k, tuple) and k[0] in ("qT", "kT", "vbf", "uT", "qTm", "qTs", "hid")
    for q in range(NQ):
        dma("sp", rope[:, 0, :], c_rope[q, 0, :, :], [], ["rope"])
        dma("sp", rope[:, 1, :], c_rope[q, 1, :, :], [], ["rope"])
        for l in range(NL):
            layer_pass(q, l, q == 0, l == NL - 1)

    S.op("sp", lambda e: e.nop(), reads=[k for k in S.last_w.keys() if isinstance(k, tuple) and k[0] == "out"], writes=[])
    S.emit()
    return nc


def _consts(NB, NQ=4):
    T = NB * 128 + 32
    E0 = NB * 128
    L = NB * 128
    inv = (np.float32(500000.0) ** (-np.arange(0, 32, 2, dtype=np.float32) / np.float32(32))).astype(np.float32)
    ropes = []
    for q in range(NQ):
        pos = np.zeros(T, np.float32)
        pos[:L] = 16 + q * L + np.arange(L)
        pos[E0:E0 + 16] = np.arange(16)
        pos[E0 + 16:E0 + 20] = 16384
        ang = (pos[:, None] * inv[None, :]).astype(np.float32)
        cos = np.cos(ang).astype(np.float32).T
        sin = np.sin(ang).astype(np.float32).T
        ropes.append(np.stack([np.concatenate([cos, cos], 0), np.concatenate([sin, sin], 0)], 0))
    c_rope = np.stack(ropes, 0).astype(np.float32)
    j = np.arange(128)[:, None]
    i = np.arange(128)[None, :]
    md = np.where(j <= i, 0.0, NEG)
    mp = np.where(j >= i, 0.0, NEG)
    mp0 = np.where((j < 16) & (j >= i - 112), 0.0, NEG)
    c_mask = np.stack([np.tile(m, (1, 4)) for m in (md, mp, mp0)], 0).astype(np.float32)
    jE = np.arange(32)[:, None]
    iE = np.arange(32)[None, :]
    mE = np.where((jE < 16) & (iE < 16) & (jE <= iE), 0.0, NEG)
    c_maskE = np.tile(mE, (1, 4)).astype(np.float32)
    c_maskS = np.full((4, 32, 4), NEG, np.float32)
    for s in range(4):
        c_maskS[s, 16 + s, :] = 0.0
    prot = np.zeros((128, 32), np.float32)
    for m in range(16):
        prot[m + 16, m] = -1.0
        prot[m, m + 16] = 1.0
    sel = np.zeros((128, 24), np.float32)
    jt = np.tile(np.arange(CL + 1, dtype=np.float32), 16)
    c_jtab = np.tile(jt[None, :], (128, 1)).astype(np.float32)
    pinv = np.zeros((4, 16), np.float32)
    for g, w in enumerate((2, 4, 8, 16)):
        pinv[g, :] = 1.0 / np.minimum(w, np.arange(16) + 1)
    c_pinv = np.tile(pinv.reshape(1, -1), (128, 1)).astype(np.float32)
    psel = np.zeros((16, 4), np.float32)
    for g, w in enumerate((2, 4, 8, 16)):
        psel[15 - (w - 1):15, g] = 1.0
    return dict(c_rope=c_rope, c_mask=c_mask, c_maskE=c_maskE, c_maskS=c_maskS, c_prot=prot,
                c_ident=np.eye(128, dtype=np.float32), c_sel=sel, c_jtab=c_jtab, c_pinv=c_pinv, c_psel=psel)


def prep_inputs(inp, NB, NL=2, ffw=True, NQ=4):
    L = NB * 128
    T = L + 32
    TT = NQ * L + 32
    cst = _consts(NB, NQ)
    f = lambda a: np.ascontiguousarray(np.asarray(a, dtype=np.float32))
    shared = dict(
        w_in=f(inp["w_in"][:NL]), w_out=f(inp["w_out"][:NL]),
        g_mix=f(inp["g_mix"]), g_ffn=f(inp["g_ffn"]), g_q=f(inp["g_q"]), g_k=f(inp["g_k"]), sinks=f(inp["sinks"]),
        A_re=f(inp["A_re"]).reshape(2, 2048), A_im=f(inp["A_im"]).reshape(2, 2048), log_dt=f(inp["log_dt"]),
        B_re=f(inp["B_re"]).reshape(2, 2048, 16), B_im=f(inp["B_im"]).reshape(2, 2048, 16),
        C_re=f(inp["C_re"]), C_im=f(inp["C_im"]), D_skip=f(inp["D_skip"]), w_glu=f(inp["w_glu"]), b_glu=f(inp["b_glu"]),
        w_pool=f(inp["w_pool"]), pool_scale=f(inp["pool_scale"]), g_out_attn=f(inp["g_out_attn"]),
        g_out_ssm=f(inp["g_out_ssm"]), g_out_pool=f(inp["g_out_pool"]),
    )
    if ffw:
        shared.update(w_ff1=f(inp["w_ff1"][:NL]), w_ff2=f(inp["w_ff2"][:NL]))
    xp = f(inp["x_prompt"]); xsm = f(inp["x_sample"]); meta = f(inp["meta_tokens"])
    ck = f(inp["cache_k"]).reshape(2, 32, 128, 256); cv = f(inp["cache_v"]).reshape(2, 32, 128, 256)
    sre = f(inp["state_ssm_re"]).reshape(2, 32, 16, 128); sim = f(inp["state_ssm_im"]).reshape(2, 32, 16, 128)
    spool = f(inp["state_pool"])
    maps = []
    for r in range(NCORES):
        b = r % 2
        x_in = np.zeros((TT, D), np.float32)
        x_in[:NQ * L] = xp[b, :NQ * L]
        x_in[NQ * L:NQ * L + 16] = meta
        x_in[NQ * L + 16:NQ * L + 20] = xsm[4 * r:4 * r + 4, 0]
        m = dict(shared)
        m.update(cst)
        m.update(x_in=x_in, cache_k=np.ascontiguousarray(ck[:, 4 * r:4 * r + 4]),
                 cache_v=np.ascontiguousarray(cv[:, 4 * r:4 * r + 4]),
                 st_re=np.ascontiguousarray(sre[:, 4 * r:4 * r + 4]).reshape(2, 64, 128),
                 st_im=np.ascontiguousarray(sim[:, 4 * r:4 * r + 4]).reshape(2, 64, 128),
                 st_pool=np.ascontiguousarray(spool[:, 4 * r:4 * r + 4]))
        maps.append(m)
    return maps


_NC_CACHE = {}


def _run(inp, n_cores=NCORES):
    SEQ = np.asarray(inp["x_prompt"]).shape[1]
    NQ = 4
    NB = SEQ // (NQ * 128)
    L = NB * 128
    key = (NB,)
    if key not in _NC_CACHE:
        _NC_CACHE[key] = build(NB, NL=2, ffw=True, NQ=NQ)
    nc = _NC_CACHE[key]
    maps = prep_inputs(inp, NB, NL=2, ffw=True, NQ=NQ)[:n_cores]
    res = run_bass_kernel_spmd(nc, maps, core_ids=list(range(n_cores)))
    R = res.results
    f32 = lambda a: np.asarray(a, dtype=np.float32)
    nb = min(2, n_cores)
    y_prompt = np.stack([f32(R[b]["y_out"])[:NQ * L] for b in range(nb)], 0)
    y_sample = np.concatenate([f32(R[r]["y_out"])[NQ * L + 16:NQ * L + 20] for r in range(n_cores)], 0)[:, None, :]
    pk = np.stack([f32(R[b]["o_pk"]).reshape(2, 128, 2, 128) for b in range(nb)], 1)
    pv = np.stack([f32(R[b]["o_pv"]).reshape(2, 128, 2, 128) for b in range(nb)], 1)
    pre = np.stack([f32(R[b]["o_pssm"])[:, 0].reshape(2, 32, 64) for b in range(nb)], 1)
    pim = np.stack([f32(R[b]["o_pssm"])[:, 1].reshape(2, 32, 64) for b in range(nb)], 1)
    ppool = np.stack([f32(R[b]["o_ppool"]) for b in range(nb)], 1)
    sk = np.concatenate([f32(R[r]["o_sk"]).reshape(2, 4, 128, 2, 128) for r in range(n_cores)], 1)
    sv = np.concatenate([f32(R[r]["o_sv"]).reshape(2, 4, 128, 2, 128) for r in range(n_cores)], 1)
    sre = np.concatenate([f32(R[r]["o_sssm"])[:, 0].reshape(2, 4, 32, 64) for r in range(n_cores)], 1)
    sim = np.concatenate([f32(R[r]["o_sssm"])[:, 1].reshape(2, 4, 32, 64) for r in range(n_cores)], 1)
    spool = np.concatenate([f32(R[r]["o_spool"]) for r in range(n_cores)], 1)
    return (y_prompt, y_sample, pk, pv, pre, pim, ppool, sk, sv, sre, sim, spool)


def kernel(**inputs):
    return _run(inputs, NCORES)
```

```python
import numpy as np
import concourse.bass as bass
import concourse.mybir as mybir
from concourse.bass_utils import run_bass_kernel_spmd

F32 = mybir.dt.float32
BF16 = mybir.dt.bfloat16
I32 = mybir.dt.int32
ALU = mybir.AluOpType
AF = mybir.ActivationFunctionType

D = 2048
NQ = 1024
NKV = 256
SSMW = 512
POOLW = 512
INW = 2560
DFF = 8192
NCORES = 8
CL = 64
EPS = 1e-6
NEG = -30000.0
PI = float(np.pi)


class Sched:
    ENGS = ["pe", "act", "dve", "pool", "sp"]

    def __init__(self, nc, n_sp_slots=60, n_pool_slots=24):
        self.nc = nc
        self.ops = []
        self.by_eng = {e: [] for e in self.ENGS}
        self.last_w = {}
        self.readers = {}
        self.nslots = {"sp": n_sp_slots, "pool": n_pool_slots}
        self.ndma = {"sp": 0, "pool": 0}

    def op(self, eng, fn, reads=(), writes=(), dma=False, cc=False):
        oid = len(self.ops)
        deps = set()
        for r in reads:
            w = self.last_w.get(r)
            if w is not None:
                deps.add(w)
        for r in writes:
            w = self.last_w.get(r)
            if w is not None:
                deps.add(w)
            for rid in self.readers.get(r, {}).values():
                deps.add(rid)
        o = dict(id=oid, eng=eng, fn=fn, deps=deps, dma=dma, marked=False)
        if cc:
            o["dma"] = True
            dma = True
            self.ncc = getattr(self, "ncc", 0) + 1
            o["q"] = "cc"
            o["slot"] = 0
            o["val"] = self.ncc
        elif dma:
            k = self.ndma[eng]
            self.ndma[eng] += 1
            o["q"] = eng
            o["slot"] = k % self.nslots[eng]
            o["val"] = 16 * (k // self.nslots[eng] + 1)
        self.ops.append(o)
        self.by_eng[eng].append(o)
        for r in reads:
            self.readers.setdefault(r, {})[("d", oid) if dma else eng] = oid
        for r in writes:
            self.last_w[r] = oid
            self.readers[r] = {}
        return oid

    def fence(self, pred, newkeys, eng="sp", extra_reads=()):
        keys = [k for k in set(list(self.last_w.keys()) + list(self.readers.keys())) if pred(k)]
        if getattr(self, "nofence", False):
            return None
        return self.op(eng, lambda e: e.nop(), reads=list(extra_reads), writes=keys + list(newkeys))

    def emit(self):
        nc = self.nc
        ops = self.ops
        for o in ops:
            for p in o["deps"]:
                po = ops[p]
                if po["dma"]:
                    continue
                if po["eng"] == "pe" and o["eng"] == "pe":
                    continue
                po["marked"] = True
        cum = {}
        cnt = {e: 0 for e in self.ENGS}
        for e in self.ENGS:
            for o in self.by_eng[e]:
                if o["marked"] and not o["dma"]:
                    cnt[e] += 1
                o["cum"] = cnt[e]
        esem = {e: nc.alloc_semaphore("s_" + e) for e in self.ENGS}
        dsem = {q: [nc.alloc_semaphore(f"d_{q}_{i}") for i in range(self.nslots[q])] for q in ("sp", "pool")}
        dsem["cc"] = [nc.alloc_semaphore("d_cc")]
        handles = {"pe": nc.tensor, "act": nc.scalar, "dve": nc.vector, "pool": nc.gpsimd, "sp": nc.sync}

        def run(eng, e):
            waited = {}
            for o in self.by_eng[eng]:
                need = {}
                for p in o["deps"]:
                    po = ops[p]
                    if po["dma"]:
                        key = ("d", po["q"], po["slot"])
                        v = po["val"]
                    else:
                        if po["eng"] == "pe" and eng == "pe":
                            continue
                        key = ("e", po["eng"])
                        v = po["cum"]
                    if v > need.get(key, 0):
                        need[key] = v
                if o["dma"] and o["q"] != "cc":
                    if o["val"] > 16:
                        key = ("d", o["q"], o["slot"])
                        need[key] = max(need.get(key, 0), o["val"] - 16)
                for key, v in need.items():
                    if waited.get(key, 0) >= v:
                        continue
                    waited[key] = v
                    sem = esem[key[1]] if key[0] == "e" else dsem[key[1]][key[2]]
                    e.wait_ge(sem, v)
                ins = o["fn"](e)
                if o["dma"]:
                    ins.then_inc(dsem[o["q"]][o["slot"]], 1 if o["q"] == "cc" else 16)
                elif o["marked"]:
                    ins.then_inc(esem[eng], 1)

        with nc.Block() as block:
            @block.tensor
            def _(e):
                run("pe", e)

            @block.scalar
            def _(e):
                run("act", e)

            @block.vector
            def _(e):
                run("dve", e)

            @block.gpsimd
            def _(e):
                run("pool", e)

            @block.sync
            def _(e):
                run("sp", e)


def col_groups(T):
    n = (T + 511) // 512
    nb = (T - 32) // 128
    per = [nb // n + (1 if i < nb % n else 0) for i in range(n)]
    gs = []
    c = 0
    for i, p in enumerate(per):
        w = p * 128 + (32 if i == n - 1 else 0)
        gs.append((c, c + w))
        c += w
    assert c == T
    return gs


def build(NB, NL=2, dbg=(), stop=None, ffw=True, NQ=4):
    T = NB * 128 + 32
    L = NB * 128
    E0 = NB * 128
    NBLK = NB + 1
    CG = col_groups(T)
    NCH = 2 * NB
    NSQ = int(np.log2(NCH))
    assert 2 ** NSQ == NCH
    nc = bass.Bass("TRN2", target_bir_lowering=False)
    nc.allow_low_precision("bf16 matmul operands by design (reference tolerance measured for bf16)")
    S = Sched(nc)
    S.nofence = 'nofence' in dbg
    A = nc.alloc_sbuf_tensor

    def din(name, shape, dt=F32):
        return nc.dram_tensor(name, list(shape), dt, kind="ExternalInput")

    def dout(name, shape, dt=F32):
        return nc.dram_tensor(name, list(shape), dt, kind="ExternalOutput")

    TT = NQ * L + 32
    x_in = din("x_in", [TT, D])
    w_in = din("w_in", [NL, D, INW]); w_out = din("w_out", [NL, D, D])
    if ffw:
        w_ff1 = din("w_ff1", [NL, D, DFF]); w_ff2 = din("w_ff2", [NL, DFF, D])
    g_mix = din("g_mix", [2, D]); g_ffn = din("g_ffn", [2, D])
    g_q = din("g_q", [2, 128]); g_k = din("g_k", [2, 128]); sinks = din("sinks", [2, 8])
    A_re = din("A_re", [2, 2048]); A_im = din("A_im", [2, 2048]); log_dt = din("log_dt", [2, 32])
    B_re = din("B_re", [2, 2048, 16]); B_im = din("B_im", [2, 2048, 16])
    C_re = din("C_re", [2, 32, 16, 64]); C_im = din("C_im", [2, 32, 16, 64])
    D_skip = din("D_skip", [2, 512]); w_glu = din("w_glu", [2, 512, 512]); b_glu = din("b_glu", [2, 512])
    w_pool = din("w_pool", [2, 4, 128, 128]); pool_scale = din("pool_scale", [2, 512])
    g_oa = din("g_out_attn", [2, 1024]); g_os = din("g_out_ssm", [2, 512]); g_op = din("g_out_pool", [2, 512])
    cache_k = din("cache_k", [2, 4, 128, 256]); cache_v = din("cache_v", [2, 4, 128, 256])
    st_re = din("st_re", [2, 64, 128]); st_im = din("st_im", [2, 64, 128])
    st_pool = din("st_pool", [2, 4, 15, 512])
    c_rope = din("c_rope", [NQ, 2, 32, T])
    c_mask = din("c_mask", [3, 128, 512])
    c_maskE = din("c_maskE", [32, 128])
    c_maskS = din("c_maskS", [4, 32, 4])
    c_prot = din("c_prot", [128, 32]); c_ident = din("c_ident", [128, 128])
    c_sel = din("c_sel", [128, 24])
    c_jtab = din("c_jtab", [128, 16 * (CL + 1)])
    c_pinv = din("c_pinv", [128, 4 * 16])
    c_psel = din("c_psel", [16, 4])

    xs = nc.dram_tensor("xs_scratch", [TT, D], F32)
    y_out = dout("y_out", [TT, D])
    o_pk = dout("o_pk", [2, 128, 256]); o_pv = dout("o_pv", [2, 128, 256])
    o_pssm = dout("o_pssm", [2, 2, 16, 128]); o_ppool = dout("o_ppool", [2, 15, 512])
    o_sk = dout("o_sk", [2, 4, 128, 256]); o_sv = dout("o_sv", [2, 4, 128, 256])
    o_sssm = dout("o_sssm", [2, 2, 64, 128]); o_spool = dout("o_spool", [2, 4, 15, 512])
    XW = 640
    xch_in = nc.dram_tensor("xch_in", [128, XW], F32)
    xch_out = nc.dram_tensor("xch_out", [NCORES * 128, XW], F32)

    actT = A("actT", [128, 16, T], BF16)
    PROJW = 8 * T + 2 * T + NBLK * 256 + 4 * T
    HIDW = 16 * T
    projbuf = A("projbuf", [128, max(PROJW, HIDW)], BF16)
    o_ = 0
    qT = projbuf[:, o_:o_ + 8 * T].rearrange("p (h t) -> p h t", h=8); o_ += 8 * T
    kT = projbuf[:, o_:o_ + 2 * T].rearrange("p (h t) -> p h t", h=2); o_ += 2 * T
    vbf = projbuf[:, o_:o_ + NBLK * 256].rearrange("p (b c) -> p b c", b=NBLK); o_ += NBLK * 256
    uT = projbuf[:, o_:o_ + 4 * T].rearrange("p (h t) -> p h t", h=4); o_ += 4 * T
    hidT = projbuf[:, 0:HIDW].rearrange("p (m t) -> p m t", m=16)
    xpT = A("xpT", [128, 4, 16 + T], BF16)
    wbuf = [A(f"wbuf{i}", [128, 16, 512], BF16) for i in range(2)]
    xblk = A("xblk", [128, D], F32)
    xn = A("xn", [128, D], BF16)
    ident = A("ident", [128, 128], BF16); identf = A("identf", [128, 128], F32)
    ones_b = A("ones_b", [128, 4, 128], BF16)
    prot = A("prot", [128, 32], BF16)
    rope = A("rope", [32, 2, T], F32)
    masks = A("masks", [128, 3, 512], BF16)
    maskE = A("maskE", [32, 128], BF16); maskS = A("maskS", [32, 4, 4], BF16)
    sel = A("sel", [128, 24], F32)
    vecs = A("vecs", [128, 48], F32)
    gvec = A("gvec", [128, 16], F32)
    qraw = A("qraw", [128, 512], F32); sqb = A("sqb", [128, 512], BF16)
    rtmp = A("rtmp", [128, 512], F32); rtmp2 = A("rtmp2", [128, 512], F32)
    small = A("small", [128, 16], F32)
    relu_t = [A(f"relu_t{i}", [128, 512], BF16) for i in range(2)]
    cosT = A("cosT", [128, 16, CL + 1], F32); sinT = A("sinT", [128, 16, CL + 1], F32)
    Dk = A("Dk", [128, 16, CL], F32); Dk16 = A("Dk16", [128, 16, 16], F32)
    sm = A("sm", [128, 28, 16], F32)
    BbT = A("BbT", [128, 32, 128], BF16)
    CTp_r = A("CTp_r", [128, 16, 128], F32); CTp_ni = A("CTp_ni", [128, 16, 128], F32)
    scur = A("scur", [128, 32], F32)
    Gst = A("Gst", [128, NCH + 1, 32], F32)
    sst = A("sst", [128, NCH + 2, 32], F32)
    esink = A("esink", [1, 8, 128], BF16); sinkrow = A("sinkrow", [1, 16], F32)
    hkL = [A(f"hk{i}", [128, 2, 128], BF16) for i in range(2)]; hvL = [A(f"hv{i}", [128, 256], BF16) for i in range(2)]
    phL = [A(f"ph{i}", [128, 4, 16], BF16) for i in range(2)]; sfin = A("sfin", [128, 2, 32], F32)
    kctok = A("kctok", [128, 256], BF16); kcT = A("kcT", [128, 128], BF16); vc = A("vc", [128, 256], BF16)
    es = A("es", [128, 8], BF16)
    h0s = A("h0s", [128, 2, 64], F32); hs = A("hs", [128, 2, 64], F32)
    sttok = A("sttok", [64, 2, 128], F32)
    stp = A("stp", [16, 512], BF16); psel = A("psel", [16, 4], BF16)
    wglu = A("wglu", [128, 4, 512], BF16); wpool = A("wpool", [128, 4, 128], BF16)
    pmeta = A("pmeta", [128, 4, 32], F32); pmeta2 = A("pmeta2", [128, 4, 32], F32); pinvE = A("pinvE", [128, 4, 16], F32)
    outst = A("outst", [128, 512], F32)

    scr = wbuf[1][:, :, :].rearrange("p a b -> p (a b)").bitcast(F32)
    z_re = scr[:, 0:1024]; z_im = scr[:, 1024:2048]; k_re = scr[:, 2048:3072]; k_im = scr[:, 3072:4096]
    hE_re = rtmp[:, :].rearrange("p (a b) -> p a b", a=16)
    hE_im = rtmp2[:, :].rearrange("p (a b) -> p a b", a=16)
    xst = xblk[:, 0:XW]; stage = xblk[:, 640:640 + 576]; acc = xblk[:, 1280:1280 + 576]
    ebuf = [xn[:, i * 1024:(i + 1) * 1024].rearrange("p (a b) -> p a b", a=2) for i in range(2)]
    XBK = [("xio", j) for j in range(4)]
    xio = [xblk[:, j * 512:(j + 1) * 512] for j in range(4)]

    if "psep" in dbg:
        psA = [nc.alloc_psum_tensor(f"psA{i}", [128, 512], F32) for i in range(4)]
        psX = None
    else:
        psX = nc.alloc_psum_tensor("psX", [128, 4, 512], F32)
        psA = [psX[:, i, :] for i in range(4)]
    psB = [nc.alloc_psum_tensor(f"psB{i}", [128, 512], F32) for i in range(3)]
    psT = nc.alloc_psum_tensor("psT", [128, 1024], BF16)
    psT_alt = psB[2][:, :].bitcast(BF16)

    S._taps = []
    QS = {"q": 0, "l": 0}

    def dma(q, out, in_, reads, writes):
        return S.op(q, lambda e: e.dma_start(out=out, in_=in_), reads=reads, writes=writes, dma=True)

    def dma_slow(q, out, in_, reads, writes):
        return S.op(q, lambda e: e.dma_start(out=out, in_=in_, allow_slow_non_contiguous=True),
                    reads=reads, writes=writes, dma=True)

    def dma_tp(q, out, src_flat, nt, reads, writes):
        v = src_flat.rearrange("(t p) -> p t", p=128)
        for t0 in range(0, nt, 4):
            t1 = min(nt, t0 + 4)
            dma_slow(q, out[:, t0:t1], v[:, t0:t1], reads, writes)

    def tap(name, ap, dt, reads):
        if name not in dbg:
            return
        t = dout("dbg_%s_%d%d" % (name, QS["q"], QS["l"]), list(ap.shape), dt)
        full = t.ap()
        S.op("sp", lambda e: e.dma_start(out=full, in_=ap), reads=reads, writes=[("out", "dbg", name, QS["q"], QS["l"])], dma=True)

    def V(fn, reads, writes):
        return S.op("dve", fn, reads, writes)

    def G(fn, reads, writes):
        return S.op("pool", fn, reads, writes)

    def ACT(fn, reads, writes):
        return S.op("act", fn, reads, writes)

    def PE(fn, reads, writes):
        return S.op("pe", fn, reads, writes)

    def tt(eng, out, a, b, op, reads, writes):
        return S.op(eng, lambda e: e.tensor_tensor(out, a, b, op), reads, writes)

    def mm(out, lhsT, rhs, start, stop, reads, writes):
        return S.op("pe", lambda e: e.matmul(out, lhsT, rhs, start=start, stop=stop), reads, writes)

    def tr(out, in_, idn, reads, writes):
        return S.op("pe", lambda e: e.transpose(out, in_, idn), reads, writes)

    def act(out, in_, func, reads, writes, **kw):
        return S.op("act", lambda e: e.activation(out, in_, func, **kw), reads, writes)

    def cp(eng, out, in_, reads, writes):
        if eng == "act":
            return S.op("act", lambda e: e.copy(out, in_), reads, writes)
        return S.op(eng, lambda e: e.tensor_copy(out, in_), reads, writes)

    def ts(eng, out, in0, s1, s2, op0, op1, reads, writes):
        if op1 is None:
            return S.op(eng, lambda e: e.tensor_scalar(out, in0, s1, 1.0, op0, ALU.mult), reads, writes)
        return S.op(eng, lambda e: e.tensor_scalar(out, in0, s1, s2, op0, op1), reads, writes)

    def stt(out, in0, sc, in1, op0, op1, reads, writes):
        return S.op("dve", lambda e: e.scalar_tensor_tensor(out, in0, sc, in1, op0, op1), reads, writes)

    def ms(eng, out, val, reads, writes):
        return S.op(eng, lambda e: e.memset(out, val), reads, writes)

    def rcp(out, in_, reads, writes):
        return S.op("dve", lambda e: e.reciprocal(out, in_), reads, writes)

    def scan(out, d0, d1, reads, writes):
        return S.op("dve", lambda e: e.tensor_tensor_scan(out, d0, d1, 0.0, ALU.mult, ALU.add), reads, writes)

    dma("pool", ident[:, :], c_ident[:, :], [], ["ident"])
    dma("sp", identf[:, :], c_ident[:, :], [], ["identf"])
    dma("pool", prot[:, :], c_prot[:, :], [], ["prot"])
    for i in range(3):
        dma("pool", masks[:, i, :], c_mask[i, :, :], [], ["masks"])
    dma("pool", maskE[:, :], c_maskE[:, :], [], ["maskE"])
    if "noconst" not in dbg:
        for s_ in range(4):
            dma("pool", maskS[:, s_, :], c_maskS[s_, :, :], [], ["maskS"])
        dma("pool", psel[:, :], c_psel[:, :], [], ["psel"])
    dma("sp", sel[:, :], c_sel[:, :], [], ["sel"])
    dma("sp", pinvE[:, :, :], c_pinv[:, :].rearrange("p (a b) -> p a b", a=4), [], ["pinvE"])
    for i, v in enumerate([1.0 / 128, 1.0 / 1024, 1.0 / 512, 1.0]):
        G(lambda e, i=i, v=v: e.memset(ones_b[:, i, :], v), [], ["ones"])
    G(lambda e: e.memset(stp[:, :], 0.0), [], ["stp"])

    def load_w(src_ap, i):
        key = f"W{i}"
        S.op("pool", lambda e: e.dma_start(out=wbuf[i][:, :, :], in_=src_ap), reads=[], writes=[key], dma=True)
        return key

    def w_tile(wt, l, rows0, col0):
        return wt[l, rows0:rows0 + 2048, col0:col0 + 512].rearrange("(c p) m -> p c m", p=128)

    def drow(blk):
        return QS["q"] * L + blk * 128 if blk < NB else NQ * L

    def xdk(blk):
        return ("xd", QS["q"], blk) if blk < NB else ("xd", "E")

    def actT_reads(c0, c1):
        return [("actT", b) for b in range(NBLK) if b * 128 < c1 and min((b + 1) * 128, T) > c0]

    def blk_of_cols(c0, c1):
        return [b for b in range(NBLK) if b * 128 < c1 and min((b + 1) * 128, T) > c0]

    def norm_to_actT(xsrc, gsrc, l):
        dma_tp("sp", gvec, gsrc[l, :], 16, [], ["gvec"])
        for blk in range(NBLK):
            P = 128 if blk < NB else 32
            r0 = blk * 128
            dr = drow(blk)
            dma("sp", xblk[0:P, :], xsrc[dr:dr + P, :], [xdk(blk)], XBK)
            V(lambda e, P=P: e.memset(small[0:P, 0:1], 0.0), [], ["small0"])
            ACT(lambda e, P=P: e.activation(xn[0:P, :], xblk[0:P, :], AF.Square, accum_out=small[0:P, 0:1]),
                XBK + ["small0"], ["xn", "small0"])
            ACT(lambda e, P=P: e.activation(small[0:P, 1:2], small[0:P, 0:1], AF.Ln, bias=EPS, scale=1.0 / D),
                ["small0"], ["small1"])
            ACT(lambda e, P=P: e.activation(small[0:P, 2:3], small[0:P, 1:2], AF.Exp, scale=-0.5),
                ["small1"], ["small2"])
            if "n2" in dbg:
                V(lambda e, P=P: e.tensor_scalar(xn[0:P, :], xblk[0:P, :], small[0:P, 2:3], 1.0, ALU.mult, ALU.mult),
                  XBK + ["small2", "xn"], ["xn"])
            elif "n3" in dbg:
                ACT(lambda e, P=P: e.activation(xn[0:P, :], xblk[0:P, :], AF.Copy, scale=small[0:P, 2:3]),
                    XBK + ["small2", "xn"], ["xn"])
            elif "n4" in dbg:
                pass
            else:
                V(lambda e, P=P: e.tensor_scalar(xn[0:P, :], xblk[0:P, :], small[0:P, 2:3], 1.0, ALU.mult, ALU.mult),
                  XBK + ["small2", "xn"], ["xn"])
            for c4 in range(4 if "n1" not in dbg else 0):
                pk = ("psT" if c4 % 2 == 0 else "psB2")
                pst = psT[:, 0:512] if c4 % 2 == 0 else psT_alt[:, 0:512]
                for i in range(4):
                    kc = c4 * 4 + i
                    PE(lambda e, P=P, kc=kc, i=i, pst=pst: e.transpose(pst[:, i * 128:i * 128 + P],
                                                                       xn[0:P, kc * 128:(kc + 1) * 128], ident[0:P, 0:P]),
                       ["xn", "ident"], [pk])
                for i in range(4):
                    kc = c4 * 4 + i
                    src = pst[:, i * 128:i * 128 + P]
                    dst = actT[:, kc, r0:r0 + P]
                    if "gplain" in dbg:
                        cp("act" if i % 2 == 0 else "dve", dst, src, [pk], [("actT", blk)])
                    elif (i % 2 == 0 or "gact" in dbg) and "gdve" not in dbg:
                        ACT(lambda e, src=src, dst=dst, kc=kc: e.activation(dst, src, AF.Copy, scale=gvec[:, kc:kc + 1]),
                            [pk, "gvec"], [("actT", blk)])
                    else:
                        V(lambda e, src=src, dst=dst, kc=kc: e.tensor_scalar(dst, src, gvec[:, kc:kc + 1], 1.0, ALU.mult, ALU.mult),
                          [pk, "gvec"], [("actT", blk)])

    def load_vecs(l):
        dma_slow("sp", vecs[:, 0:1], g_q[l, :].rearrange("(p o) -> p o", o=1), [], ["vecs"])
        dma_slow("sp", vecs[:, 1:2], g_k[l, :].rearrange("(p o) -> p o", o=1), [], ["vecs"])
        dma_tp("sp", vecs[:, 2:10], g_oa[l, :], 8, [], ["vecs"])
        for j, src in enumerate([g_os, g_op, D_skip, b_glu, pool_scale]):
            dma_slow("sp", vecs[:, 10 + 4 * j:14 + 4 * j], src[l, :].rearrange("(t p) -> p t", p=128), [], ["vecs"])
        V(lambda e: e.tensor_scalar(vecs[:, 30:34], vecs[:, 22:26], -1.0, 1.0, ALU.mult, ALU.mult), ["vecs"], ["vecs"])
        if "novecx" in dbg:
            return
        dma("pool", wglu[:, :, :], w_glu[l, :, :].rearrange("(c p) m -> p c m", p=128), [], ["wglu"])
        dma("pool", wpool[:, :, :], w_pool[l, :, :, :].rearrange("g c d -> c g d"), [], ["wpool"])
        dma("sp", sinkrow[0:1, 0:8], sinks[l:l + 1, :], [], ["sinkrow"])
        ACT(lambda e: e.activation(sinkrow[0:1, 8:16], sinkrow[0:1, 0:8], AF.Exp), ["sinkrow"], ["sinkrow"])
        V(lambda e: e.tensor_copy(esink[0:1, :, :], sinkrow[0:1, 8:16].unsqueeze(2).to_broadcast([1, 8, 128])),
          ["sinkrow"], ["esink"])

    def qk_finish(ps, n, c0, dst, gcol, wkey, fkey):
        ACT(lambda e: e.copy(qraw[:, 0:n], ps[:, 0:n]), [wkey], ["qraw"])
        G(lambda e: e.tensor_tensor(sqb[:, 0:n], qraw[:, 0:n], qraw[:, 0:n], ALU.mult), ["qraw"], ["sqb"])
        PE(lambda e: e.matmul(psB[0][:, 0:n], ones_b[:, 0, :], sqb[:, 0:n], start=True, stop=True),
           ["sqb", "ones"], ["psB0"])
        ACT(lambda e: e.activation(rtmp[:, 0:n], psB[0][:, 0:n], AF.Ln, bias=EPS, scale=1.0), ["psB0"], ["rtmp"])
        ACT(lambda e: e.activation(rtmp[:, 0:n], rtmp[:, 0:n], AF.Exp, scale=-0.5), ["rtmp"], ["rtmp"])
        V(lambda e: e.scalar_tensor_tensor(dst, qraw[:, 0:n], vecs[:, gcol:gcol + 1], rtmp[:, 0:n], ALU.mult, ALU.mult),
          ["qraw", "rtmp", "vecs"], [wkey + "_d"])
        PE(lambda e: e.matmul(psB[1][0:32, 0:n], prot[:, :], dst, start=True, stop=True), [wkey + "_d", "prot"], ["psB1"])
        V(lambda e: e.tensor_tensor(rtmp2[0:32, 0:n], psB[1][0:32, 0:n], rope[:, 1, c0:c0 + n], ALU.mult),
          ["psB1", "rope"], ["rtmp2"])
        V(lambda e: e.tensor_tensor(rtmp[0:32, 0:n], dst[0:32], rope[:, 0, c0:c0 + n], ALU.mult),
          [wkey + "_d", "rope", "rtmp"], ["rtmp"])
        V(lambda e: e.tensor_tensor(dst[0:32], rtmp[0:32, 0:n], rtmp2[0:32, 0:n], ALU.add),
          ["rtmp", "rtmp2", wkey + "_d"], [wkey + "_d", fkey])

    def w_in_phase(l):
        order = [3, 4, 0, 1, 2]
        for oi, ti in enumerate(order):
            bi = oi % 2
            wkey = load_w(w_tile(w_in, l, 0, ti * 512), bi)
            wb = wbuf[bi]
            for gi, (c0, c1) in enumerate(CG):
                n = c1 - c0
                for mt in range(4):
                    if ti == 2 and mt >= 2:
                        break
                    ps = psA[mt]
                    pk = f"psA{mt}"
                    for kc in range(16):
                        PE(lambda e, ps=ps, kc=kc, mt=mt, wb=wb, c0=c0, c1=c1, n=n:
                           e.matmul(ps[:, 0:n], wb[:, kc, mt * 128:(mt + 1) * 128], actT[:, kc, c0:c1],
                                    start=(kc == 0), stop=(kc == 15)),
                           [wkey] + actT_reads(c0, c1), [pk])
                    if ti in (0, 1):
                        h = ti * 4 + mt
                        qk_finish(ps, n, c0, qT[:, h, c0:c1], 0, pk, ("qT", h, gi))
                    elif ti == 2:
                        qk_finish(ps, n, c0, kT[:, mt, c0:c1], 1, pk, ("kT", mt, gi))
                    elif ti == 3:
                        ACT(lambda e, ps=ps, mt=mt, c0=c0, c1=c1, n=n: e.copy(uT[:, mt, c0:c1], ps[:, 0:n]),
                            [pk], [("uT", gi)])
                    else:
                        V(lambda e, ps=ps, mt=mt, c0=c0, c1=c1, n=n:
                          e.tensor_copy(xpT[:, mt, 16 + c0:16 + c1], ps[:, 0:n]), [pk], [("xpT", gi)])
            if ti == 2:
                for blk in range(NBLK):
                    P = 128 if blk < NB else 32
                    r0 = blk * 128
                    ps = psA[2 + blk % 2]
                    pk = f"psA{2 + blk % 2}"
                    for kc in range(16):
                        PE(lambda e, ps=ps, kc=kc, wb=wb, r0=r0, P=P:
                           e.matmul(ps[0:P, 0:256], actT[:, kc, r0:r0 + P], wb[:, kc, 256:512],
                                    start=(kc == 0), stop=(kc == 15)),
                           [wkey, ("actT", blk)], [pk])
                    ACT(lambda e, ps=ps, blk=blk, P=P: e.copy(vbf[0:P, blk, :], ps[0:P, 0:256]), [pk], [("vbf", blk)])

    SMI = dict(rho=0, aC_r=1, aC_i=2, aL_r=3, aL_i=4, a1_r=5, a1_i=6, f_r=7, f_i=8, are=9, aim=10, dt=11, dre=12, th=13,
               t0=14, t1=15, t2=16, t3=17)

    def smv(name):
        return sm[:, SMI[name], :]

    def ssm_prep(l):
        SK = ["z_re", "z_im", "k_re", "k_im"]
        dma_tp("sp", smv("are"), A_re[l, :], 16, [], ["sm_in"])
        dma_tp("sp", smv("aim"), A_im[l, :], 16, [], ["sm_in"])
        ld = log_dt[l, :].rearrange("(t h) -> h t", h=2)
        dma_slow("sp", sm[0:64, SMI["dt"], :], ld[0:1, :].partition_broadcast(64), [], ["sm_in"])
        dma_slow("sp", sm[64:128, SMI["dt"], :], ld[1:2, :].partition_broadcast(64), [], ["sm_in"])
        ACT(lambda e: e.activation(smv("dt"), smv("dt"), AF.Exp), ["sm_in"], ["sm_dt"])
        V(lambda e: e.tensor_tensor(smv("dre"), smv("dt"), smv("are"), ALU.mult), ["sm_dt", "sm_in"], ["sm_dre"])
        V(lambda e: e.tensor_tensor(smv("th"), smv("dt"), smv("aim"), ALU.mult), ["sm_dt", "sm_in"], ["sm_th"])
        ACT(lambda e: e.activation(smv("rho"), smv("dre"), AF.Exp), ["sm_dre"], ["sm_rho"])
        W65 = 16 * (CL + 1)
        jt = scr[:, 0:W65].rearrange("p (a b) -> p a b", a=16)
        ang = scr[:, W65:2 * W65].rearrange("p (a b) -> p a b", a=16)
        rp = scr[:, 2 * W65:3 * W65].rearrange("p (a b) -> p a b", a=16)
        tq = scr[:, 0:W65]
        angf = scr[:, W65:2 * W65]
        dma("sp", scr[:, 0:W65], c_jtab[:, :], [], SK + ["W1"])
        V(lambda e: e.tensor_tensor(ang, jt, smv("th").unsqueeze(2).to_broadcast([128, 16, CL + 1]), ALU.mult),
          SK + ["sm_th"], SK)
        V(lambda e: e.tensor_tensor(rp, jt, smv("dre").unsqueeze(2).to_broadcast([128, 16, CL + 1]), ALU.mult),
          SK + ["sm_dre"], SK)
        ACT(lambda e: e.activation(scr[:, 2 * W65:3 * W65], scr[:, 2 * W65:3 * W65], AF.Exp), SK, SK)
        tqi = tq.bitcast(I32)
        V(lambda e: e.tensor_scalar(tq, angf, 1.0 / (2 * PI), 1.0, ALU.mult, ALU.mult), SK, SK)
        V(lambda e: e.tensor_copy(tqi, tq), SK, SK)
        V(lambda e: e.tensor_copy(tq, tqi), SK, SK)
        V(lambda e: e.scalar_tensor_tensor(angf, tq, -2 * PI, angf, ALU.mult, ALU.add), SK, SK)
        V(lambda e: e.tensor_scalar(angf, angf, -PI, PI, ALU.max, ALU.min), SK, SK)
        ACT(lambda e: e.activation(sinT[:, :, :].rearrange("p a b -> p (a b)"), angf, AF.Sin), SK, ["sinT"])
        ACT(lambda e: e.activation(tq, angf, AF.Abs), SK, SK)
        ACT(lambda e: e.activation(cosT[:, :, :].rearrange("p a b -> p (a b)"), tq, AF.Sin, bias=PI / 2, scale=-1.0),
            SK, ["cosT"])
        V(lambda e: e.tensor_tensor(smv("a1_r"), rp[:, :, 1], cosT[:, :, 1], ALU.mult), SK + ["cosT"], ["sm_a1"])
        V(lambda e: e.tensor_tensor(smv("a1_i"), rp[:, :, 1], sinT[:, :, 1], ALU.mult), SK + ["sinT"], ["sm_a1"])
        V(lambda e: e.tensor_tensor(smv("aC_r"), rp[:, :, CL], cosT[:, :, CL], ALU.mult), SK + ["cosT"], ["sm_aC"])
        V(lambda e: e.tensor_tensor(smv("aC_i"), rp[:, :, CL], sinT[:, :, CL], ALU.mult), SK + ["sinT"], ["sm_aC"])
        V(lambda e: e.tensor_copy(Dk[:, :, :], smv("rho").unsqueeze(2).to_broadcast([128, 16, CL])), ["sm_rho"], ["Dk"])
        V(lambda e: e.memset(Dk[:, :, 0:1], 0.0), ["Dk"], ["Dk"])
        V(lambda e: e.tensor_copy(Dk16[:, :, :], smv("rho").unsqueeze(2).to_broadcast([128, 16, 16])), ["sm_rho"], ["Dk16"])
        V(lambda e: e.memset(Dk16[:, :, 0:1], 0.0), ["Dk16"], ["Dk16"])
        G(lambda e: e.tensor_copy(smv("aL_r"), smv("aC_r")), ["sm_aC"], ["sm_aL"])
        G(lambda e: e.tensor_copy(smv("aL_i"), smv("aC_i")), ["sm_aC"], ["sm_aL"])
        for _ in range(NSQ):
            tt("pool", smv("t0"), smv("aL_r"), smv("aL_r"), ALU.mult, ["sm_aL"], ["sm_t0"])
            tt("pool", smv("t1"), smv("aL_i"), smv("aL_i"), ALU.mult, ["sm_aL"], ["sm_t1"])
            tt("pool", smv("t2"), smv("aL_r"), smv("aL_i"), ALU.mult, ["sm_aL"], ["sm_t2"])
            tt("pool", smv("aL_r"), smv("t0"), smv("t1"), ALU.subtract, ["sm_t0", "sm_t1", "sm_aL"], ["sm_aL"])
            tt("pool", smv("aL_i"), smv("t2"), smv("t2"), ALU.add, ["sm_t2", "sm_aL"], ["sm_aL"])
        tt("dve", smv("t0"), smv("are"), smv("are"), ALU.mult, ["sm_in", "sm_t0"], ["sm_t0"])
        tt("dve", smv("t1"), smv("aim"), smv("aim"), ALU.mult, ["sm_in", "sm_t1"], ["sm_t1"])
        tt("dve", smv("t0"), smv("t0"), smv("t1"), ALU.add, ["sm_t0", "sm_t1"], ["sm_t0"])
        V(lambda e: e.reciprocal(smv("t0"), smv("t0")), ["sm_t0"], ["sm_t0"])
        V(lambda e: e.tensor_scalar(smv("t1"), smv("a1_r"), -1.0, 1.0, ALU.add, ALU.mult), ["sm_a1", "sm_t1"], ["sm_t1"])
        tt("dve", smv("t2"), smv("t1"), smv("are"), ALU.mult, ["sm_t1", "sm_in", "sm_t2"], ["sm_t2"])
        tt("dve", smv("t3"), smv("a1_i"), smv("aim"), ALU.mult, ["sm_a1", "sm_in"], ["sm_t3"])
        tt("dve", smv("t2"), smv("t2"), smv("t3"), ALU.add, ["sm_t2", "sm_t3"], ["sm_t2"])
        tt("dve", smv("f_r"), smv("t2"), smv("t0"), ALU.mult, ["sm_t2", "sm_t0"], ["sm_f"])
        tt("dve", smv("t2"), smv("a1_i"), smv("are"), ALU.mult, ["sm_a1", "sm_in", "sm_t2"], ["sm_t2"])
        tt("dve", smv("t3"), smv("t1"), smv("aim"), ALU.mult, ["sm_t1", "sm_in", "sm_t3"], ["sm_t3"])
        tt("dve", smv("t2"), smv("t2"), smv("t3"), ALU.subtract, ["sm_t2", "sm_t3"], ["sm_t2"])
        tt("dve", smv("f_i"), smv("t2"), smv("t0"), ALU.mult, ["sm_t2", "sm_t0"], ["sm_f"])
        Bs_r = z_re[:, 0:256].rearrange("p (a b) -> p a b", a=16)
        Bs_i = z_re[:, 256:512].rearrange("p (a b) -> p a b", a=16)
        Bb_r = z_re[:, 512:768].rearrange("p (a b) -> p a b", a=16)
        Bb_i = z_re[:, 768:1024].rearrange("p (a b) -> p a b", a=16)
        Bt = z_im[:, 0:256].rearrange("p (a b) -> p a b", a=16)
        Mp = [k_re.bitcast(BF16), k_im.bitcast(BF16)]
        dma("sp", Bs_r, B_re[l, :, :].rearrange("(t p) c -> p t c", p=128), [], SK)
        dma("sp", Bs_i, B_im[l, :, :].rearrange("(t p) c -> p t c", p=128), [], SK)
        fr = smv("f_r").unsqueeze(2).to_broadcast([128, 16, 16])
        fi = smv("f_i").unsqueeze(2).to_broadcast([128, 16, 16])
        tt("dve", Bb_r, Bs_r, fr, ALU.mult, SK + ["sm_f"], SK)
        tt("dve", Bt, Bs_i, fi, ALU.mult, SK + ["sm_f"], SK)
        tt("dve", Bb_r, Bb_r, Bt, ALU.subtract, SK, SK)
        tt("dve", Bb_i, Bs_i, fr, ALU.mult, SK + ["sm_f"], SK)
        tt("dve", Bt, Bs_r, fi, ALU.mult, SK + ["sm_f"], SK)
        tt("dve", Bb_i, Bb_i, Bt, ALU.add, SK, SK)
        for ri, Bb in enumerate((Bb_r, Bb_i)):
            V(lambda e, ri=ri: e.memset(Mp[ri], 0.0), SK, SK)
            for h in range(2):
                base = Mp[ri][64 * h:64 * h + 64, 0:1]
                for a in range(4):
                    dst = bass.AP(base.tensor, base.offset + 16 * h + 512 * a, [[base.ap[0][0], 64], [160, 4], [1, 16]])
                    src = Bb[64 * h:64 * h + 64, 4 * a:4 * a + 4, :]
                    cp("dve", dst, src, SK, SK)
        for ri in range(2):
            for t4 in range(4):
                pk = "psT"
                pst = psT[:, (t4 % 2) * 512:(t4 % 2 + 1) * 512]
                for i in range(4):
                    t = t4 * 4 + i
                    PE(lambda e, pst=pst, i=i, t=t, ri=ri: e.transpose(pst[:, i * 128:(i + 1) * 128],
                                                                       Mp[ri][:, t * 128:(t + 1) * 128], ident[:, :]),
                       SK + ["ident"], [pk])
                dst = BbT[:, ri * 16 + t4 * 4:ri * 16 + t4 * 4 + 4, :]
                V(lambda e, dst=dst, pst=pst: e.tensor_copy(dst, pst.rearrange("p (a b) -> p a b", a=4)),
                  [pk], ["BbT", "corr"])
        for ri, Csrc in enumerate((C_re, C_im)):
            Zf = scr[0:32, 0:2048].rearrange("p (a b) -> p a b", a=16)
            V(lambda e, Zf=Zf: e.memset(Zf, 0.0), SK, SK)
            cv_ = Csrc[l, :, :, :].rearrange("(t h) c n -> h c t n", h=2)
            dma("sp", Zf[0:16, :, 0:64], cv_[0, :, :, :], [], SK)
            dma("sp", Zf[16:32, :, 64:128], cv_[1, :, :, :], [], SK)
            CTp = CTp_r if ri == 0 else CTp_ni
            ms("dve", CTp[:, :, :], 0.0, [], ["CTp"])
            base = CTp[:, 0, 0:1]
            for t4 in range(4):
                ps = psB[2]
                for i in range(4):
                    t = t4 * 4 + i
                    tr(ps[:, i * 32:(i + 1) * 32], Zf[:, t, :], identf[0:32, 0:32], SK + ["identf"], ["psB2"])
                dst = bass.AP(base.tensor, base.offset + 512 * t4, [[base.ap[0][0], 128], [160, 4], [1, 32]])
                srcv = ps[:, 0:128].rearrange("p (a b) -> p a b", a=4)
                if ri == 0:
                    cp("dve", dst, srcv, ["psB2", "CTp"], ["CTp"])
                else:
                    ts("dve", dst, srcv, -1.0, None, ALU.mult, None, ["psB2", "CTp"], ["CTp"])

    def ssm_chunk(c):
        c0 = c * CL
        gi = [i for i, (a, b) in enumerate(CG) if a <= c0 < b][0]
        xr = psX[:, 0:2, :].rearrange("p a b -> p (a b)")
        xi = psX[:, 2:4, :].rearrange("p a b -> p (a b)")
        for ri in range(2):
            for t in range(16):
                mm(psX[:, 2 * ri + t // 8, (t % 8) * CL:(t % 8 + 1) * CL], BbT[:, ri * 16 + t, :], uT[:, t // 4, c0:c0 + CL],
                   True, True, ["BbT", ("uT", gi)], [f"psA{2 * ri}", f"psA{2 * ri + 1}"])
        C3 = cosT[:, :, 0:CL]; S3 = sinT[:, :, 0:CL]
        v3 = lambda ap: ap.rearrange("p (a b) -> p a b", a=16)
        XR = ["psA0", "psA1"]; XI = ["psA2", "psA3"]
        tt("dve", v3(z_re), v3(xr), C3, ALU.mult, XR + ["cosT", "z_re"], ["z_re"])
        tt("dve", v3(k_re), v3(xi), S3, ALU.mult, XI + ["sinT", "k_re"], ["k_re"])
        tt("dve", z_re, z_re, k_re, ALU.add, ["z_re", "k_re"], ["z_re"])
        tt("dve", v3(z_im), v3(xi), C3, ALU.mult, XI + ["cosT", "z_im"], ["z_im"])
        tt("dve", v3(k_im), v3(xr), S3, ALU.mult, XR + ["sinT", "k_im"], ["k_im"])
        tt("dve", z_im, z_im, k_im, ALU.subtract, ["z_im", "k_im"], ["z_im"])
        sr = scur[:, 0:16]; si = scur[:, 16:32]
        tt("dve", smv("t0"), smv("a1_r"), sr, ALU.mult, ["sm_a1", "scur", "sm_t0"], ["sm_t0"])
        tt("dve", smv("t1"), smv("a1_i"), si, ALU.mult, ["sm_a1", "scur", "sm_t1"], ["sm_t1"])
        tt("dve", smv("t0"), smv("t0"), smv("t1"), ALU.subtract, ["sm_t0", "sm_t1"], ["sm_t0"])
        tt("dve", v3(z_re)[:, :, 0], v3(z_re)[:, :, 0], smv("t0"), ALU.add, ["z_re", "sm_t0"], ["z_re"])
        tt("dve", smv("t2"), smv("a1_r"), si, ALU.mult, ["sm_a1", "scur", "sm_t2"], ["sm_t2"])
        tt("dve", smv("t3"), smv("a1_i"), sr, ALU.mult, ["sm_a1", "scur", "sm_t3"], ["sm_t3"])
        tt("dve", smv("t2"), smv("t2"), smv("t3"), ALU.add, ["sm_t2", "sm_t3"], ["sm_t2"])
        tt("dve", v3(z_im)[:, :, 0], v3(z_im)[:, :, 0], smv("t2"), ALU.add, ["z_im", "sm_t2"], ["z_im"])
        dk = Dk[:, :, :].rearrange("p a b -> p (a b)")
        scan(k_re, dk, z_re, ["Dk", "z_re", "k_re"], ["k_re"])
        scan(k_im, dk, z_im, ["Dk", "z_im", "k_im"], ["k_im"])
        tt("dve", v3(z_re), v3(k_re), C3, ALU.mult, ["k_re", "cosT", "z_re"], ["z_re"])
        tt("dve", v3(z_im), v3(k_im), S3, ALU.mult, ["k_im", "sinT", "z_im"], ["z_im"])
        tt("dve", z_re, z_re, z_im, ALU.subtract, ["z_re", "z_im"], ["z_re"])
        tt("dve", v3(z_im), v3(k_im), C3, ALU.mult, ["k_im", "cosT", "z_im"], ["z_im"])
        tt("dve", v3(k_re), v3(k_re), S3, ALU.mult, ["k_re", "sinT"], ["k_re"])
        tt("dve", z_im, z_im, k_re, ALU.add, ["z_im", "k_re"], ["z_im"])
        cp("dve", scur[:, 0:16], v3(z_re)[:, :, CL - 1], ["z_re", "scur"], ["scur"])
        cp("dve", scur[:, 16:32], v3(z_im)[:, :, CL - 1], ["z_im", "scur"], ["scur"])
        for ct in range(4):
            for i in range(4):
                t = ct * 4 + i
                mm(psB[2][:, ct * CL:(ct + 1) * CL], CTp_r[:, t, :], v3(z_re)[:, t, :], (i == 0), False, ["CTp", "z_re"], ["psB2"])
                mm(psB[2][:, ct * CL:(ct + 1) * CL], CTp_ni[:, t, :], v3(z_im)[:, t, :], False, (i == 3), ["CTp", "z_im"], ["psB2"])
        cp("act", actT[:, 8:12, c0:c0 + CL], psB[2][:, 0:4 * CL].rearrange("p (a b) -> p a b", a=4), ["psB2"], [("ysm", c)])

    def ssm_E(l):
        SK = ["z_re", "z_im", "k_re", "k_im"]
        giE = len(CG) - 1
        for ri in range(2):
            for t in range(16):
                PE(lambda e, ri=ri, t=t: e.matmul(psX[:, 2 * ri + t // 8, (t % 8) * CL:(t % 8) * CL + 32],
                                                  BbT[:, ri * 16 + t, :], uT[:, t // 4, E0:E0 + 32], start=True, stop=True),
                   ["BbT", ("uT", giE)], [f"psA{2 * ri}", f"psA{2 * ri + 1}"])
        xr = psX[:, 0:2, :].rearrange("p a b -> p (a b)").rearrange("p (a b) -> p a b", a=16)
        xi = psX[:, 2:4, :].rearrange("p a b -> p (a b)").rearrange("p (a b) -> p a b", a=16)
        XR = ["psA0", "psA1"]; XI = ["psA2", "psA3"]
        C3 = cosT[:, :, 0:16]; S3 = sinT[:, :, 0:16]
        m3 = lambda ap: ap[:, 0:256].rearrange("p (a b) -> p a b", a=16)
        zr = z_re[:, 0:256]; zi = z_im[:, 0:256]; kr = k_re[:, 0:256]; kim = k_im[:, 0:256]
        tt("dve", m3(z_re), xr[:, :, 0:16], C3, ALU.mult, XR + ["cosT", "z_re"], ["z_re"])
        tt("dve", m3(k_re), xi[:, :, 0:16], S3, ALU.mult, XI + ["sinT", "k_re"], ["k_re"])
        tt("dve", zr, zr, kr, ALU.add, ["z_re", "k_re"], ["z_re"])
        tt("dve", m3(z_im), xi[:, :, 0:16], C3, ALU.mult, XI + ["cosT", "z_im"], ["z_im"])
        tt("dve", m3(k_im), xr[:, :, 0:16], S3, ALU.mult, XR + ["sinT", "k_im"], ["k_im"])
        tt("dve", zi, zi, kim, ALU.subtract, ["z_im", "k_im"], ["z_im"])
        dk = Dk16[:, :, :].rearrange("p a b -> p (a b)")
        V(lambda e: e.tensor_tensor_scan(kr, dk, zr, 0.0, ALU.mult, ALU.add), ["Dk16", "z_re", "k_re"], ["k_re"])
        V(lambda e: e.tensor_tensor_scan(kim, dk, zi, 0.0, ALU.mult, ALU.add), ["Dk16", "z_im", "k_im"], ["k_im"])
        ms("dve", hE_re, 0.0, ["rtmp"], ["rtmp", "hE"])
        ms("dve", hE_im, 0.0, ["rtmp2"], ["rtmp2", "hE"])
        tt("dve", m3(z_re), m3(k_re), C3, ALU.mult, ["k_re", "cosT", "z_re"], ["z_re"])
        tt("dve", m3(z_im), m3(k_im), S3, ALU.mult, ["k_im", "sinT", "z_im"], ["z_im"])
        tt("dve", hE_re[:, :, 0:16], m3(z_re), m3(z_im), ALU.subtract, ["z_re", "z_im", "hE"], ["hE"])
        tt("dve", Gst[:, NCH, 0:16], m3(z_re)[:, :, 15], m3(z_im)[:, :, 15], ALU.subtract, ["z_re", "z_im"], [("G", NCH)])
        tt("dve", m3(z_re), m3(k_im), C3, ALU.mult, ["k_im", "cosT", "z_re"], ["z_re"])
        tt("dve", m3(z_im), m3(k_re), S3, ALU.mult, ["k_re", "sinT", "z_im"], ["z_im"])
        tt("dve", hE_im[:, :, 0:16], m3(z_re), m3(z_im), ALU.add, ["z_re", "z_im", "hE"], ["hE"])
        tt("dve", Gst[:, NCH, 16:32], m3(z_re)[:, :, 15], m3(z_im)[:, :, 15], ALU.add, ["z_re", "z_im"], [("G", NCH)])
        dma("sp", sttok[:, 0, :], st_re[l, :, :], [], ["sttok"])
        dma("sp", sttok[:, 1, :], st_im[l, :, :], [], ["sttok"])
        for ri in range(2):
            PE(lambda e, ri=ri: e.transpose(psB[2][:, ri * 64:(ri + 1) * 64], sttok[:, ri, :], identf[0:64, 0:64]),
               ["sttok", "identf"], ["psB2"])
        V(lambda e: e.tensor_copy(h0s[:, :, :], psB[2][:, 0:128].rearrange("p (a b) -> p a b", a=2)), ["psB2"], ["h0s"])
        h0r = h0s[:, 0, :].rearrange("p (s t) -> p s t", s=4); h0i = h0s[:, 1, :].rearrange("p (s t) -> p s t", s=4)
        hsr = hs[:, 0, :].rearrange("p (s t) -> p s t", s=4); hsi = hs[:, 1, :].rearrange("p (s t) -> p s t", s=4)
        a1r = smv("a1_r").unsqueeze(1).to_broadcast([128, 4, 16]); a1i = smv("a1_i").unsqueeze(1).to_broadcast([128, 4, 16])
        t0 = z_re[:, 0:64].rearrange("p (s t) -> p s t", s=4); t1 = z_im[:, 0:64].rearrange("p (s t) -> p s t", s=4)
        xrs = xr[:, :, 16:20].rearrange("p t s -> p s t"); xis = xi[:, :, 16:20].rearrange("p t s -> p s t")
        tt("dve", t0, h0r, a1r, ALU.mult, ["h0s", "sm_a1", "z_re"], ["z_re"])
        tt("dve", t1, h0i, a1i, ALU.mult, ["h0s", "sm_a1", "z_im"], ["z_im"])
        tt("dve", t0, t0, t1, ALU.subtract, ["z_re", "z_im"], ["z_re"])
        tt("dve", hsr, t0, xrs, ALU.add, ["z_re"] + XR, ["hs"])
        tt("dve", t0, h0i, a1r, ALU.mult, ["h0s", "sm_a1", "z_re"], ["z_re"])
        tt("dve", t1, h0r, a1i, ALU.mult, ["h0s", "sm_a1", "z_im"], ["z_im"])
        tt("dve", t0, t0, t1, ALU.add, ["z_re", "z_im"], ["z_re"])
        tt("dve", hsi, t0, xis, ALU.add, ["z_re"] + XI, ["hs"])
        V(lambda e: e.tensor_copy(hE_re[:, :, 16:20], hsr.rearrange("p s t -> p t s")), ["hs", "hE"], ["hE"])
        V(lambda e: e.tensor_copy(hE_im[:, :, 16:20], hsi.rearrange("p s t -> p t s")), ["hs", "hE"], ["hE"])
        for ri in range(2):
            PE(lambda e, ri=ri: e.transpose(psB[2][0:64, ri * 128:(ri + 1) * 128], hs[:, ri, :], identf[:, :]),
               ["hs", "identf"], ["psB2"])
        V(lambda e: e.tensor_copy(outst[0:64, 0:256], psB[2][0:64, 0:256]), ["psB2"], ["outst"])
        for ri in range(2 if QS["q"] == 0 else 0):
            dma("sp", o_sssm[l, ri, :, :], outst[0:64, ri * 128:(ri + 1) * 128], ["outst"], [("out", "sssm", l, ri)])
        for ct in range(4):
            for i in range(4):
                t = ct * 4 + i
                PE(lambda e, ct=ct, t=t, i=i: e.matmul(psB[2][:, ct * 32:(ct + 1) * 32], CTp_r[:, t, :], hE_re[:, t, :],
                                                       start=(i == 0), stop=False), ["CTp", "hE", "rtmp"], ["psB2"])
                PE(lambda e, ct=ct, t=t, i=i: e.matmul(psB[2][:, ct * 32:(ct + 1) * 32], CTp_ni[:, t, :], hE_im[:, t, :],
                                                       start=False, stop=(i == 3)), ["CTp", "hE", "rtmp2"], ["psB2"])
        ACT(lambda e: e.copy(actT[:, 8:12, E0:E0 + 32], psB[2][:, 0:128].rearrange("p (a b) -> p a b", a=4)),
            ["psB2"], [("ysm", "E")])

    def cmuladd(dst_r, dst_i, a_r, a_i, s_r, s_i, g_r, g_i, rd, wr):
        tt("pool", smv("t0"), a_r, s_r, ALU.mult, rd + ["sm_t0"], ["sm_t0"])
        tt("pool", smv("t1"), a_i, s_i, ALU.mult, rd + ["sm_t1"], ["sm_t1"])
        tt("pool", smv("t2"), a_r, s_i, ALU.mult, rd + ["sm_t2"], ["sm_t2"])
        tt("pool", smv("t3"), a_i, s_r, ALU.mult, rd + ["sm_t3"], ["sm_t3"])
        tt("pool", smv("t0"), smv("t0"), smv("t1"), ALU.subtract, ["sm_t0", "sm_t1"], ["sm_t0"])
        tt("pool", smv("t2"), smv("t2"), smv("t3"), ALU.add, ["sm_t2", "sm_t3"], ["sm_t2"])
        if g_r is not None:
            tt("pool", dst_r, smv("t0"), g_r, ALU.add, rd + ["sm_t0"], wr)
            tt("pool", dst_i, smv("t2"), g_i, ALU.add, rd + ["sm_t2"], wr)
        else:
            G(lambda e: e.tensor_copy(dst_r, smv("t0")), rd + ["sm_t0"], wr)
            G(lambda e: e.tensor_copy(dst_i, smv("t2")), rd + ["sm_t2"], wr)

    def state_init(l, q):
        if q == 0:
            cp("dve", scur[:, :], Gst[:, NCH, :], [("G", NCH), "scur"], ["scur"])
        else:
            cp("dve", scur[:, :], sfin[:, l, :], [("sfin", l), "scur"], ["scur"])

    def local_state(l, q):
        giE = len(CG) - 1
        hk = hkL[l]; hv = hvL[l]
        if q == 0:
            ms("dve", hk[:, :, :], 0.0, [], [("hk", l)])
            cp("dve", hk[:, :, 0:16], kT[:, :, E0:E0 + 16], [("kT", 0, giE), ("kT", 1, giE), ("hk", l)], [("hk", l)])
            ms("dve", hv[:, :], 0.0, [], [("hv", l)])
            cp("dve", hv[0:16, :], vbf[0:16, NB, :], [("vbf", NB), ("hv", l)], [("hv", l)])
            cp("dve", xpT[:, :, 0:16], xpT[:, :, 16 + E0:16 + E0 + 16], [("xpT", giE)], [("xpT", "halo")])
        else:
            cp("dve", xpT[:, :, 0:16], phL[l][:, :, :], [("ph", l)], [("xpT", "halo")])
        cp("dve", sfin[:, l, :], scur[:, :], ["scur", ("sfin", l)], [("sfin", l)])
        for ri in range(2):
            tr(psB[2][0:16, ri * 128:(ri + 1) * 128], scur[:, ri * 16:(ri + 1) * 16], identf[:, :], ["scur", "identf"], ["psB2"])
        cp("dve", outst[0:16, 256:512], psB[2][0:16, 0:256], ["psB2", "outst2"], ["outst2"])
        for ri in range(2):
            dma("sp", o_pssm[l, ri, :, :], outst[0:16, 256 + ri * 128:256 + (ri + 1) * 128], ["outst2"], [("out", "pssm", l, ri)])

    def save_halo(l):
        giL = gi_of(L - 128)
        cp("dve", hkL[l][:, :, :], kT[:, :, L - 128:L], [("kT", 0, giL), ("kT", 1, giL), ("hk", l)], [("hk", l)])
        cp("dve", hvL[l][:, :], vbf[:, NB - 1, :], [("vbf", NB - 1), ("hv", l)], [("hv", l)])
        cp("dve", phL[l][:, :, :], xpT[:, :, 16 + L - 16:16 + L], [("xpT", gi_of(L - 16)), ("ph", l)], [("ph", l)])

    def gi_of(col):
        return [i for i, (a, b) in enumerate(CG) if a <= col < b][0]

    SC = float(128 ** -0.5)

    def attn_block(blk, step):
        r0 = blk * 128
        gi = gi_of(r0)

        def one(kvh):
            i2 = (step * 2 + kvh) % 2
            pa, pb = psA[2 * i2], psA[2 * i2 + 1]
            ka, kb_ = f"psA{2 * i2}", f"psA{2 * i2 + 1}"
            eb = ebuf[i2]; ek = ("ebuf", i2)
            qv = qT[:, 4 * kvh:4 * kvh + 4, r0:r0 + 128]
            qk_ = [("qT", 4 * kvh + g, gi) for g in range(4)]
            if blk == 0:
                l_ = QS["l"]
                kprev = hkL[l_][:, kvh, :]; vprev = hvL[l_][:, kvh * 128:(kvh + 1) * 128]
                mprev = masks[:, 2 if QS["q"] == 0 else 1, :]
                rprev = [("hk", l_)]; rvprev = [("hv", l_)]
            else:
                kprev = kT[:, kvh, r0 - 128:r0]; vprev = vbf[:, blk - 1, kvh * 128:(kvh + 1) * 128]; mprev = masks[:, 1, :]
                rprev = [("kT", kvh, gi_of(r0 - 128))]; rvprev = [("vbf", blk - 1)]
            PE(lambda e: e.matmul(pa[:, :], kprev, qv, start=True, stop=False), rprev + qk_, [ka])
            PE(lambda e: e.matmul(pa[:, :], ident[:, :], mprev, start=False, stop=True), ["ident", "masks"], [ka])
            PE(lambda e: e.matmul(pb[:, :], kT[:, kvh, r0:r0 + 128], qv, start=True, stop=False), [("kT", kvh, gi)] + qk_, [kb_])
            PE(lambda e: e.matmul(pb[:, :], ident[:, :], masks[:, 0, :], start=False, stop=True), ["ident", "masks"], [kb_])
            ACT(lambda e: e.activation(eb[:, 0, :], pa[:, :], AF.Exp, scale=SC), [ka], [ek])
            ACT(lambda e: e.activation(eb[:, 1, :], pb[:, :], AF.Exp, scale=SC), [kb_], [ek])
            PE(lambda e: e.matmul(psB[0][:, :], vprev, eb[:, 0, :], start=True, stop=False), rvprev + [ek], ["psB0"])
            PE(lambda e: e.matmul(psB[0][:, :], vbf[:, blk, kvh * 128:(kvh + 1) * 128], eb[:, 1, :], start=False, stop=True),
               [("vbf", blk), ek], ["psB0"])
            PE(lambda e: e.matmul(psB[1][:, :], ones_b[:, 3, :], eb[:, 0, :], start=True, stop=False), ["ones", ek], ["psB1"])
            PE(lambda e: e.matmul(psB[1][:, :], ones_b[:, 3, :], eb[:, 1, :], start=False, stop=False), ["ones", ek], ["psB1"])
            PE(lambda e: e.matmul(psB[1][:, :], ones_b[0:1, 3, :], esink[0:1, 4 * kvh:4 * kvh + 4, :], start=False, stop=True),
               ["ones", "esink"], ["psB1"])
            V(lambda e: e.reciprocal(qraw[:, :], psB[1][:, :]), ["psB1", "qraw"], ["qraw"])
            V(lambda e: e.tensor_tensor(qv, psB[0][:, :].rearrange("p (a b) -> p a b", a=4),
                                        qraw[:, :].rearrange("p (a b) -> p a b", a=4), ALU.mult),
              ["psB0", "qraw"], [("qT", 4 * kvh + g, gi) for g in range(4)])
        for kvh in range(2):
            one(kvh)

    def attn_E():
        giE = len(CG) - 1

        def one(kvh):
            pa = psA[2 * kvh]; ka = f"psA{2 * kvh}"
            eb = ebuf[kvh]; ek = ("ebuf", kvh)
            qv = qT[:, 4 * kvh:4 * kvh + 4, E0:E0 + 32]
            qk_ = [("qT", 4 * kvh + g, giE) for g in range(4)]
            PE(lambda e: e.matmul(pa[0:32, 0:128], kT[:, kvh, E0:E0 + 32], qv, start=True, stop=False), [("kT", kvh, giE)] + qk_, [ka])
            PE(lambda e: e.matmul(pa[0:32, 0:128], ident[0:32, 0:32], maskE[:, :], start=False, stop=True), ["ident", "maskE"], [ka])
            ACT(lambda e: e.activation(eb[0:32, 0, 0:128], pa[0:32, 0:128], AF.Exp, scale=SC), [ka], [ek])
            PE(lambda e: e.matmul(psB[0][:, 0:128], vbf[0:32, NB, kvh * 128:(kvh + 1) * 128], eb[0:32, 0, 0:128], start=True, stop=True),
               [("vbf", NB), ek], ["psB0"])
            PE(lambda e: e.matmul(psB[1][:, 0:128], ones_b[0:32, 3, :], eb[0:32, 0, 0:128], start=True, stop=False), ["ones", ek], ["psB1"])
            PE(lambda e: e.matmul(psB[1][:, 0:128], ones_b[0:1, 3, :], esink[0:1, 4 * kvh:4 * kvh + 4, 0:32], start=False, stop=True),
               ["ones", "esink"], ["psB1"])
            V(lambda e: e.reciprocal(qraw[:, 0:128], psB[1][:, 0:128]), ["psB1", "qraw"], ["qraw"])
            V(lambda e: e.tensor_tensor(qT[:, 4 * kvh:4 * kvh + 4, E0:E0 + 16],
                                        psB[0][:, 0:128].rearrange("p (a b) -> p a b", a=4)[:, :, 0:16],
                                        qraw[:, 0:128].rearrange("p (a b) -> p a b", a=4)[:, :, 0:16], ALU.mult),
              ["psB0", "qraw"], [("qTm", kvh)])
        for kvh in range(2):
            one(kvh)

    def attn_samples(l):
        giE = len(CG) - 1
        for s_ in range(4):
            dma("pool", kctok[:, :], cache_k[l, s_, :, :], [], ["kctok"])
            dma("pool", vc[:, :], cache_v[l, s_, :, :], [], ["vc"])
            if QS["q"] == 0:
                dma("sp", o_sk[l, s_, 0:127, :], cache_k[l, s_, 1:128, :], [], [("out", "sk", l, s_)])
                dma("sp", o_sv[l, s_, 0:127, :], cache_v[l, s_, 1:128, :], [], [("out", "sv", l, s_)])
                dma("sp", o_spool[l, s_, 0:14, :], st_pool[l, s_, 1:15, :], [], [("out", "sp", l, s_)])
            col = E0 + 16 + s_
            for kvh in range(2):
                pa = psA[2 * kvh]; ka = f"psA{2 * kvh}"
                qs = qT[:, 4 * kvh:4 * kvh + 4, col]
                qk_ = [("qT", 4 * kvh + g, giE) for g in range(4)]
                PE(lambda e, kvh=kvh: e.transpose(psT[:, 0:128], kctok[:, kvh * 128:(kvh + 1) * 128], ident[:, :]),
                   ["kctok", "ident"], ["psT"])
                V(lambda e: e.tensor_copy(kcT[:, :], psT[:, 0:128]), ["psT"], ["kcT"])
                PE(lambda e, pa=pa, qs=qs: e.matmul(pa[:, 0:4], kcT[:, :], qs, start=True, stop=True), ["kcT"] + qk_, [ka])
                PE(lambda e, pa=pa, qs=qs, kvh=kvh: e.matmul(pa[0:32, 4:8], kT[:, kvh, E0:E0 + 32], qs, start=True, stop=False),
                   [("kT", kvh, giE)] + qk_, [ka])
                PE(lambda e, pa=pa, s_=s_: e.matmul(pa[0:32, 4:8], ident[0:32, 0:32], maskS[:, s_, :], start=False, stop=True),
                   ["ident", "maskS"], [ka])
                ACT(lambda e, pa=pa: e.activation(es[:, 0:4], pa[:, 0:4], AF.Exp, scale=SC), [ka], ["es"])
                ACT(lambda e, pa=pa: e.activation(es[0:32, 4:8], pa[0:32, 4:8], AF.Exp, scale=SC), [ka], ["es"])
                PE(lambda e, kvh=kvh: e.matmul(psB[0][:, 0:4], vc[:, kvh * 128:(kvh + 1) * 128], es[:, 0:4], start=True, stop=False),
                   ["vc", "es"], ["psB0"])
                PE(lambda e, kvh=kvh: e.matmul(psB[0][:, 0:4], vbf[0:32, NB, kvh * 128:(kvh + 1) * 128], es[0:32, 4:8],
                                               start=False, stop=True), [("vbf", NB), "es"], ["psB0"])
                PE(lambda e: e.matmul(psB[1][:, 0:4], ones_b[:, 3, :], es[:, 0:4], start=True, stop=False), ["ones", "es"], ["psB1"])
                PE(lambda e: e.matmul(psB[1][:, 0:4], ones_b[0:32, 3, :], es[0:32, 4:8], start=False, stop=False), ["ones", "es"], ["psB1"])
                PE(lambda e, kvh=kvh: e.matmul(psB[1][:, 0:4], ones_b[0:1, 3, :], esink[0:1, 4 * kvh:4 * kvh + 4, 0],
                                               start=False, stop=True), ["ones", "esink"], ["psB1"])
                V(lambda e: e.reciprocal(qraw[:, 0:4], psB[1][:, 0:4]), ["psB1", "qraw"], ["qraw"])
                V(lambda e, qs=qs: e.tensor_tensor(qs, psB[0][:, 0:4], qraw[:, 0:4], ALU.mult), ["psB0", "qraw"], [("qTs", s_, kvh)])

    def kv_outputs(l):
        giL = gi_of(L - 128); giE = len(CG) - 1
        for kvh in range(2):
            PE(lambda e, kvh=kvh: e.transpose(psT[:, 512 + kvh * 128:512 + (kvh + 1) * 128], kT[:, kvh, L - 128:L], ident[:, :]),
               [("kT", kvh, giL), "ident"], ["psT"])
        V(lambda e: e.tensor_copy(outst[:, 0:256], psT[:, 512:768]), ["psT", "outst"], ["outst"])
        dma("sp", o_pk[l, :, :], outst[:, 0:256], ["outst"], [("out", "pk", l)])
        V(lambda e: e.tensor_copy(outst[:, 256:512], vbf[:, NB - 1, :]), [("vbf", NB - 1), "outst2"], ["outst2"])
        dma("sp", o_pv[l, :, :], outst[:, 256:512], ["outst2"], [("out", "pv", l)])
        for kvh in range(2):
            PE(lambda e, kvh=kvh: e.transpose(psT[0:32, 512 + kvh * 128:512 + (kvh + 1) * 128], kT[:, kvh, E0:E0 + 32], ident[:, :]),
               [("kT", kvh, giE), "ident"], ["psT"])
        V(lambda e: e.tensor_copy(outst[0:32, 0:256], psT[0:32, 512:768]), ["psT", "outst"], ["outst"])
        V(lambda e: e.tensor_copy(outst[0:32, 256:512], vbf[0:32, NB, :]), [("vbf", NB), "outst2"], ["outst2"])
        for s_ in range(4 if QS["q"] == 0 else 0):
            dma("sp", o_sk[l, s_, 127:128, :], outst[16 + s_:17 + s_, 0:256], ["outst"], [("out", "sk2", l, s_)])
            dma("sp", o_sv[l, s_, 127:128, :], outst[16 + s_:17 + s_, 256:512], ["outst2"], [("out", "sv2", l, s_)])

    def pool_mixer(l):
        SKA = ["z_re", "z_im"]; SKB = ["k_re", "k_im"]
        W_ = 16 + L
        pa = scr[:, 0:W_]; pb = scr[:, 2048:2048 + W_]
        giE = len(CG) - 1
        allx = [("xpT", i) for i in range(len(CG))] + [("xpT", "halo")]
        for g in range(4):
            PE(lambda e, g=g: e.transpose(psT[0:32, g * 128:(g + 1) * 128], xpT[:, g, 16 + L - 32:16 + L], ident[:, :]),
               allx + ["ident"], ["psT"])
        V(lambda e: e.tensor_copy(outst[0:32, :], psT[0:32, 0:512]), ["psT", "outst", "outst2"], ["outst", "outst2"])
        dma("sp", o_ppool[l, :, :], outst[17:32, :], ["outst", "outst2"], [("out", "ppool", l)])
        for g in range(4):
            PE(lambda e, g=g: e.transpose(psT[0:32, 512 + g * 128:512 + (g + 1) * 128], xpT[:, g, 16 + E0:16 + E0 + 32], ident[:, :]),
               allx + ["ident"], ["psT"])
        V(lambda e: e.tensor_copy(outst[32:64, :], psT[0:32, 512:1024]) if False else e.tensor_copy(outst[0:32, :], psT[0:32, 512:1024]),
          ["psT", "outst", "outst2"], ["outst", "outst2"])
        for s_ in range(4 if QS["q"] == 0 else 0):
            dma("sp", o_spool[l, s_, 14:15, :], outst[16 + s_:17 + s_, :], ["outst", "outst2"], [("out", "sp2", l, s_)])
        for g in range(4):
            w = 2 ** (g + 1)
            src = xpT[:, g, 0:W_]
            cur = None
            sh = 1
            bufs = [pa, pb]
            bkeys = [SKA, SKB]
            bi = 0
            for k_ in range(g + 1):
                out = bufs[bi]; ok = bkeys[bi]
                if cur is None:
                    tt("pool", out[:, sh:W_], src[:, sh:W_], src[:, 0:W_ - sh], ALU.add, allx + ok, ok)
                else:
                    tt("pool", out[:, sh:W_], cur[:, sh:W_], cur[:, 0:W_ - sh], ALU.add, bkeys[1 - bi] + ok, ok)
                cur = out; ck = ok
                sh *= 2
                bi = 1 - bi
            for gi, (c0, c1) in enumerate(CG):
                c1p = min(c1, L)
                n = c1p - c0
                V(lambda e, cur=cur, c0=c0, n=n, g=g, w=w: e.scalar_tensor_tensor(
                    sqb[:, 0:n], cur[:, 16 + c0:16 + c0 + n], 1.0 / w, xpT[:, g, 16 + c0:16 + c0 + n], ALU.mult, ALU.subtract),
                  ck + allx + ["sqb"], ["sqb"])
                if gi == giE:
                    pm = pmeta[:, g, :]; pm2 = pmeta2[:, g, :]
                    V(lambda e, pm=pm: e.memset(pm, 0.0), [], ["pmeta"])
                    V(lambda e, pm=pm, g=g: e.tensor_copy(pm[:, 16:32], xpT[:, g, 16 + E0:16 + E0 + 16]), allx + ["pmeta"], ["pmeta"])
                    a_, b_ = pm, pm2
                    sh2 = 1
                    for k_ in range(g + 1):
                        V(lambda e, a_=a_, b_=b_: e.tensor_copy(b_[:, 0:16], a_[:, 0:16]), ["pmeta"], ["pmeta"])
                        V(lambda e, a_=a_, b_=b_, sh2=sh2: e.tensor_tensor(b_[:, 16:32], a_[:, 16:32], a_[:, 16 - sh2:32 - sh2], ALU.add),
                          ["pmeta"], ["pmeta"])
                        a_, b_ = b_, a_
                        sh2 *= 2
                    V(lambda e, a_=a_, g=g: e.tensor_tensor(a_[:, 16:32], a_[:, 16:32], pinvE[:, g, :], ALU.mult), ["pmeta", "pinvE"], ["pmeta"])
                    V(lambda e, a_=a_, g=g, n=n: e.tensor_tensor(sqb[:, n:n + 16], a_[:, 16:32], xpT[:, g, 16 + E0:16 + E0 + 16], ALU.subtract),
                      ["pmeta", "sqb"] + allx, ["sqb"])
                    for s_ in range(4):
                        dma("pool", stp[0:15, :], st_pool[l, s_, :, :], [], ["stp"])
                        PE(lambda e, g=g, s_=s_: e.matmul(psB[2][:, s_:s_ + 1], stp[0:16, g * 128:(g + 1) * 128], psel[0:16, g:g + 1],
                                                          start=True, stop=True), ["stp", "psel"], ["psB2"])
                    xs4 = xpT[:, g, 16 + E0 + 16:16 + E0 + 20]
                    V(lambda e, xs4=xs4: e.tensor_tensor(small[:, 4:8], psB[2][:, 0:4], xs4, ALU.add), ["psB2"] + allx, ["small4"])
                    V(lambda e, xs4=xs4, n=n, w=w: e.scalar_tensor_tensor(sqb[:, n + 16:n + 20], small[:, 4:8], 1.0 / w, xs4,
                                                                         ALU.mult, ALU.subtract), ["small4", "sqb"] + allx, ["sqb"])
                    V(lambda e, n=n: e.memset(sqb[:, n + 20:n + 32], 0.0), ["sqb"], ["sqb"])
                    n = n + 32
                PE(lambda e, g=g, n=n: e.matmul(psB[0][:, 0:n], wpool[:, g, :], sqb[:, 0:n], start=True, stop=True),
                   ["wpool", "sqb"], ["psB0"])
                ACT(lambda e, g=g, n=n, c0=c0: e.activation(actT[:, 12 + g, c0:c0 + n], psB[0][:, 0:n], AF.Copy, scale=vecs[:, 26 + g:27 + g]),
                    ["psB0", "vecs"], [("plT", g, gi)])

    GC = float(2.0 * np.sqrt(2.0 / np.pi))

    def ssm_post():
        for gi, (c0, c1) in enumerate(CG):
            n = c1 - c0
            ysk = [("ysm", c) for c in range(NCH) if c0 <= c * CL < c1] + ([("ysm", "E")] if c1 > L else [])
            for ct in range(4):
                yv = actT[:, 8 + ct, c0:c1]
                uv = uT[:, ct, c0:c1]
                V(lambda e, yv=yv, uv=uv, ct=ct, n=n: e.scalar_tensor_tensor(qraw[:, 0:n], uv, vecs[:, 18 + ct:19 + ct], yv, ALU.mult, ALU.add),
                  ysk + [("uT", gi), "vecs", "qraw"], ["qraw"])
                tt("pool", rtmp2[:, 0:n], qraw[:, 0:n], qraw[:, 0:n], ALU.mult, ["qraw", "rtmp2"], ["rtmp2"])
                G(lambda e, n=n: e.tensor_scalar(rtmp2[:, 0:n], rtmp2[:, 0:n], 0.044715, 1.0, ALU.mult, ALU.add), ["rtmp2"], ["rtmp2"])
                tt("pool", rtmp2[:, 0:n], rtmp2[:, 0:n], qraw[:, 0:n], ALU.mult, ["qraw", "rtmp2"], ["rtmp2"])
                ACT(lambda e, n=n: e.activation(rtmp2[:, 0:n], rtmp2[:, 0:n], AF.Exp, scale=-GC), ["rtmp2"], ["rtmp2"])
                G(lambda e, n=n: e.tensor_scalar(rtmp2[:, 0:n], rtmp2[:, 0:n], 1.0, 1.0, ALU.add, ALU.mult), ["rtmp2"], ["rtmp2"])
                V(lambda e, n=n: e.reciprocal(rtmp2[:, 0:n], rtmp2[:, 0:n]), ["rtmp2"], ["rtmp2"])
                V(lambda e, uv=uv, n=n: e.tensor_tensor(uv, qraw[:, 0:n], rtmp2[:, 0:n], ALU.mult), ["qraw", "rtmp2", ("uT", gi)], [("uT", gi)])
            for co in range(4):
                for ci in range(4):
                    PE(lambda e, co=co, ci=ci, c0=c0, c1=c1, n=n: e.matmul(psA[co][:, 0:n], wglu[:, ci, co * 128:(co + 1) * 128],
                                                                          uT[:, ci, c0:c1], start=(ci == 0), stop=(ci == 3)),
                       ["wglu", ("uT", gi)], [f"psA{co}"])
                ACT(lambda e, co=co, n=n: e.activation(rtmp[:, 0:n], psA[co][:, 0:n], AF.Exp, bias=vecs[:, 30 + co:31 + co], scale=-1.0),
                    [f"psA{co}", "vecs", "rtmp"], ["rtmp"])
                G(lambda e, n=n: e.tensor_scalar(rtmp[:, 0:n], rtmp[:, 0:n], 1.0, 1.0, ALU.add, ALU.mult), ["rtmp"], ["rtmp"])
                V(lambda e, n=n: e.reciprocal(rtmp[:, 0:n], rtmp[:, 0:n]), ["rtmp"], ["rtmp"])
                V(lambda e, co=co, c0=c0, c1=c1, n=n: e.tensor_tensor(actT[:, 8 + co, c0:c1], uT[:, co, c0:c1], rtmp[:, 0:n], ALU.mult),
                  ["rtmp", ("uT", gi)] + ysk, [("smT", co, gi)])

    def out_norms():
        for gi, (c0, c1) in enumerate(CG):
            n = c1 - c0
            groups = [
                ([(qT[:, h, c0:c1], [("qT", h, gi), ("qTm", h // 4)] + [("qTs", s_, h // 4) for s_ in range(4)]) for h in range(8)], 1, 2, 0),
                ([(actT[:, 8 + t, c0:c1], [("smT", t, gi)]) for t in range(4)], 2, 10, 8),
                ([(actT[:, 12 + t, c0:c1], [("plT", t, gi)]) for t in range(4)], 2, 14, 12),
            ]
            for tiles, oi, gcol, slot0 in groups:
                for i, (src, rk) in enumerate(tiles):
                    tt("pool", sqb[:, 0:n], src, src, ALU.mult, rk + ["sqb"], ["sqb"])
                    PE(lambda e, i=i, oi=oi, n=n, last=(i == len(tiles) - 1): e.matmul(psB[0][:, 0:n], ones_b[:, oi, :], sqb[:, 0:n],
                                                                                     start=(i == 0), stop=last), ["sqb", "ones"], ["psB0"])
                ACT(lambda e, n=n: e.activation(rtmp[:, 0:n], psB[0][:, 0:n], AF.Ln, bias=EPS, scale=1.0), ["psB0", "rtmp"], ["rtmp"])
                ACT(lambda e, n=n: e.activation(rtmp[:, 0:n], rtmp[:, 0:n], AF.Exp, scale=-0.5), ["rtmp"], ["rtmp"])
                for i, (src, rk) in enumerate(tiles):
                    dst = actT[:, slot0 + i, c0:c1]
                    V(lambda e, src=src, dst=dst, i=i, gcol=gcol, n=n: e.scalar_tensor_tensor(dst, src, vecs[:, gcol + i:gcol + i + 1],
                                                                                             rtmp[:, 0:n], ALU.mult, ALU.mult),
                      rk + ["rtmp", "vecs"], [("actT", b) for b in blk_of_cols(c0, c1)])

    def resid_pass(wt, l, rows0, nk, lhs_of, lhs_reads, xsrc, xdst, outkey):
        for ni in range(4):
            wkey = load_w(w_tile(wt, l, rows0, ni * 512), ni % 2)
            wb = wbuf[ni % 2]
            for blk in range(NBLK):
                P = 128 if blk < NB else 32
                r0 = blk * 128
                j = blk % 3
                ps = psB[j]; pk = f"psB{j}"
                for kc in range(nk):
                    PE(lambda e, ps=ps, kc=kc, wb=wb, r0=r0, P=P: e.matmul(ps[0:P, :], lhs_of(kc, r0, P), wb[:, kc, :],
                                                                          start=(kc == 0), stop=(kc == nk - 1)),
                       [wkey] + lhs_reads(blk), [pk])
                xj = blk % 4
                dr = drow(blk)
                dma("sp", xio[xj][0:P, :], xsrc[dr:dr + P, ni * 512:(ni + 1) * 512], [xdk(blk)], [("xio", xj)])
                V(lambda e, ps=ps, xj=xj, P=P: e.tensor_tensor(xio[xj][0:P, :], ps[0:P, :], xio[xj][0:P, :], ALU.add),
                  [pk, ("xio", xj)], [("xio", xj)])
                wr = [xdk(blk)] + ([("out", outkey, QS["q"], blk, ni)] if outkey else [])
                if blk < NB or QS["q"] == 0:
                    dma("sp", xdst[dr:dr + P, ni * 512:(ni + 1) * 512], xio[xj][0:P, :], [("xio", xj)], wr)

    def ffn_phase(l, last):
        for hq in range(4):
            for t4 in range(4):
                wkey = load_w(w_tile(w_ff1, l, 0, hq * 2048 + t4 * 512), t4 % 2)
                wb = wbuf[t4 % 2]
                for gi, (c0, c1) in enumerate(CG):
                    n = c1 - c0
                    for mt in range(4):
                        ps = psA[mt]; pk = f"psA{mt}"
                        for kc in range(16):
                            PE(lambda e, ps=ps, kc=kc, mt=mt, wb=wb, c0=c0, c1=c1, n=n:
                               e.matmul(ps[:, 0:n], wb[:, kc, mt * 128:(mt + 1) * 128], actT[:, kc, c0:c1],
                                        start=(kc == 0), stop=(kc == 15)), [wkey] + actT_reads(c0, c1), [pk])
                        rt = relu_t[mt % 2]; rk = f"relu{mt % 2}"
                        ACT(lambda e, ps=ps, rt=rt, n=n: e.activation(rt[:, 0:n], ps[:, 0:n], AF.Relu), [pk], [rk])
                        m = t4 * 4 + mt
                        tt("pool", hidT[:, m, c0:c1], rt[:, 0:n], rt[:, 0:n], ALU.mult, [rk, "hidfence"], [("hid", m, gi)])
            resid_pass(w_ff2, l, hq * 2048, 16, lambda kc, r0, P: hidT[:, kc, r0:r0 + P],
                       lambda blk: [("hid", m, gi) for m in range(16) for gi in range(len(CG))
                                    if blk in blk_of_cols(*CG[gi])],
                       xs, y_out if (last and hq == 3) else xs, "y" if (last and hq == 3) else None)

    PROJ_PRED = lambda k: isinstance(k, tuple) and k[0] in ("qT", "kT", "vbf", "uT", "qTm", "qTs", "hid")
    SKK = ["z_re", "z_im", "k_re", "k_im"]
    STG = ["w_in", "chunks", "ssmE", "kvout", "state", "attnE", "samples", "corr", "attn0", "pool", "post", "mix", "w_out", "ffn"]

    def reached(name):
        return stop in STG and STG.index(stop) < STG.index(name)

    def layer_pass(q, l, first, last):
        QS["q"] = q; QS["l"] = l
        xsrc = x_in if l == 0 else xs
        S.fence(lambda k: isinstance(k, tuple) and k[0] == "ebuf", ["xn"])
        norm_to_actT(xsrc, g_mix, l)
        S.fence(lambda k: k == "xn", [("ebuf", 0), ("ebuf", 1)])
        S.fence(PROJ_PRED, ["projfence"])
        w_in_phase(l)
        if reached("chunks"): return
        S.fence(lambda k: k in ("W1",), SKK)
        if q == 0:
            ssm_E(l)
        state_init(l, q)
        nblk_done = 1
        for c in range(NCH):
            ssm_chunk(c)
            if c % 2 == 1 and nblk_done < NB:
                attn_block(nblk_done, nblk_done)
                nblk_done += 1
        while nblk_done < NB:
            attn_block(nblk_done, nblk_done)
            nblk_done += 1
        if reached("kvout"): return
        kv_outputs(l)
        if reached("state"): return
        local_state(l, q)
        if reached("attnE"): return
        if q == 0:
            attn_E()
        if reached("samples"): return
        if q == 0:
            attn_samples(l)
        tap("ysm", actT[:, 8:12, :], BF16, [k for k in S.last_w])
        if reached("attn0"): return
        attn_block(0, 1)
        tap("aT", projbuf[:, 0:8 * T], BF16, [k for k in S.last_w])
        if reached("pool"): return
        pool_mixer(l)
        save_halo(l)
        tap("ysm2", actT[:, 8:12, :], BF16, [k for k in S.last_w])
        if reached("post"): return
        ssm_post()
        tap("premix", actT[:, 8:16, :], BF16, [k for k in S.last_w])
        if reached("mix"): return
        out_norms()
        tap("mixT", actT[:, :, :], BF16, [k for k in S.last_w])
        if reached("w_out"): return
        S.fence(lambda k: k in SKK, ["W1"])
        S.fence(lambda k: isinstance(k, tuple) and k[0] == "ebuf", ["xn"])
        direct = (not ffw) and last
        resid_pass(w_out, l, 0, 16, lambda kc, r0, P: actT[:, kc, r0:r0 + P], lambda blk: [("actT", blk)],
                   xsrc, y_out if direct else xs, "y" if direct else None)
        if reached("ffn") or not ffw: return
        norm_to_actT(xs, g_ffn, l)
        S.fence(PROJ_PRED, ["hidfence"])
        ffn_phase(l, last)

    PROJ_PRED = lambda k: isinstance(k, tuple) and k[0] in ("qT", "kT", "vbf", "uT", "qTm", "qTs", "hid")
    for l in range(NL):
        load_vecs(l)
        S.fence(lambda k: k == "W1", SKK)
        ssm_prep(l)
        S.fence(lambda k: k in SKK, ["W1"])
        for q in range(NQ):
            dma("sp", rope[:, 0, :], c_rope[q, 0, :, :], [], ["rope"])
            dma("sp", rope[:, 1, :], c_rope[q, 1, :, :], [], ["rope"])
            layer_pass(q, l, q == 0, l == NL - 1)

    S.op("sp", lambda e: e.nop(), reads=[k for k in S.last_w.keys() if isinstance(k, tuple) and k[0] == "out"], writes=[])
    S.emit()
    return nc


def _consts(NB, NQ=4):
    T = NB * 128 + 32
    E0 = NB * 128
    L = NB * 128
    inv = (np.float32(500000.0) ** (-np.arange(0, 32, 2, dtype=np.float32) / np.float32(32))).astype(np.float32)
    ropes = []
    for q in range(NQ):
        pos = np.zeros(T, np.float32)
        pos[:L] = 16 + q * L + np.arange(L)
        pos[E0:E0 + 16] = np.arange(16)
        pos[E0 + 16:E0 + 20] = 16384
        ang = (pos[:, None] * inv[None, :]).astype(np.float32)
        cos = np.cos(ang).astype(np.float32).T
        sin = np.sin(ang).astype(np.float32).T
        ropes.append(np.stack([np.concatenate([cos, cos], 0), np.concatenate([sin, sin], 0)], 0))
    c_rope = np.stack(ropes, 0).astype(np.float32)
    j = np.arange(128)[:, None]
    i = np.arange(128)[None, :]
    md = np.where(j <= i, 0.0, NEG)
    mp = np.where(j >= i, 0.0, NEG)
    mp0 = np.where((j < 16) & (j >= i - 112), 0.0, NEG)
    c_mask = np.stack([np.tile(m, (1, 4)) for m in (md, mp, mp0)], 0).astype(np.float32)
    jE = np.arange(32)[:, None]
    iE = np.arange(32)[None, :]
    mE = np.where((jE < 16) & (iE < 16) & (jE <= iE), 0.0, NEG)
    c_maskE = np.tile(mE, (1, 4)).astype(np.float32)
    c_maskS = np.full((4, 32, 4), NEG, np.float32)
    for s in range(4):
        c_maskS[s, 16 + s, :] = 0.0
    prot = np.zeros((128, 32), np.float32)
    for m in range(16):
        prot[m + 16, m] = -1.0
        prot[m, m + 16] = 1.0
    sel = np.zeros((128, 24), np.float32)
    jt = np.tile(np.arange(CL + 1, dtype=np.float32), 16)
    c_jtab = np.tile(jt[None, :], (128, 1)).astype(np.float32)
    pinv = np.zeros((4, 16), np.float32)
    for g, w in enumerate((2, 4, 8, 16)):
        pinv[g, :] = 1.0 / np.minimum(w, np.arange(16) + 1)
    c_pinv = np.tile(pinv.reshape(1, -1), (128, 1)).astype(np.float32)
    psel = np.zeros((16, 4), np.float32)
    for g, w in enumerate((2, 4, 8, 16)):
        psel[15 - (w - 1):15, g] = 1.0
    return dict(c_rope=c_rope, c_mask=c_mask, c_maskE=c_maskE, c_maskS=c_maskS, c_prot=prot,
                c_ident=np.eye(128, dtype=np.float32), c_sel=sel, c_jtab=c_jtab, c_pinv=c_pinv, c_psel=psel)


def prep_inputs(inp, NB, NL=2, ffw=True, NQ=4):
    L = NB * 128
    T = L + 32
    TT = NQ * L + 32
    cst = _consts(NB, NQ)
    f = lambda a: np.ascontiguousarray(np.asarray(a, dtype=np.float32))
    shared = dict(
        w_in=f(inp["w_in"][:NL]), w_out=f(inp["w_out"][:NL]),
        g_mix=f(inp["g_mix"]), g_ffn=f(inp["g_ffn"]), g_q=f(inp["g_q"]), g_k=f(inp["g_k"]), sinks=f(inp["sinks"]),
        A_re=f(inp["A_re"]).reshape(2, 2048), A_im=f(inp["A_im"]).reshape(2, 2048), log_dt=f(inp["log_dt"]),
        B_re=f(inp["B_re"]).reshape(2, 2048, 16), B_im=f(inp["B_im"]).reshape(2, 2048, 16),
        C_re=f(inp["C_re"]), C_im=f(inp["C_im"]), D_skip=f(inp["D_skip"]), w_glu=f(inp["w_glu"]), b_glu=f(inp["b_glu"]),
        w_pool=f(inp["w_pool"]), pool_scale=f(inp["pool_scale"]), g_out_attn=f(inp["g_out_attn"]),
        g_out_ssm=f(inp["g_out_ssm"]), g_out_pool=f(inp["g_out_pool"]),
    )
    if ffw:
        shared.update(w_ff1=f(inp["w_ff1"][:NL]), w_ff2=f(inp["w_ff2"][:NL]))
    xp = f(inp["x_prompt"]); xsm = f(inp["x_sample"]); meta = f(inp["meta_tokens"])
    ck = f(inp["cache_k"]).reshape(2, 32, 128, 256); cv = f(inp["cache_v"]).reshape(2, 32, 128, 256)
    sre = f(inp["state_ssm_re"]).reshape(2, 32, 16, 128); sim = f(inp["state_ssm_im"]).reshape(2, 32, 16, 128)
    spool = f(inp["state_pool"])
    maps = []
    for r in range(NCORES):
        b = r % 2
        x_in = np.zeros((TT, D), np.float32)
        x_in[:NQ * L] = xp[b, :NQ * L]
        x_in[NQ * L:NQ * L + 16] = meta
        x_in[NQ * L + 16:NQ * L + 20] = xsm[4 * r:4 * r + 4, 0]
        m = dict(shared)
        m.update(cst)
        m.update(x_in=x_in, cache_k=np.ascontiguousarray(ck[:, 4 * r:4 * r + 4]),
                 cache_v=np.ascontiguousarray(cv[:, 4 * r:4 * r + 4]),
                 st_re=np.ascontiguousarray(sre[:, 4 * r:4 * r + 4]).reshape(2, 64, 128),
                 st_im=np.ascontiguousarray(sim[:, 4 * r:4 * r + 4]).reshape(2, 64, 128),
                 st_pool=np.ascontiguousarray(spool[:, 4 * r:4 * r + 4]))
        maps.append(m)
    return maps


_NC_CACHE = {}


def _run(inp, n_cores=NCORES):
    SEQ = np.asarray(inp["x_prompt"]).shape[1]
    NQ = 4
    NB = SEQ // (NQ * 128)
    L = NB * 128
    key = (NB,)
    if key not in _NC_CACHE:
        _NC_CACHE[key] = build(NB, NL=2, ffw=True, NQ=NQ)
    nc = _NC_CACHE[key]
    maps = prep_inputs(inp, NB, NL=2, ffw=True, NQ=NQ)[:n_cores]
    res = run_bass_kernel_spmd(nc, maps, core_ids=list(range(n_cores)))
    R = res.results
    f32 = lambda a: np.asarray(a, dtype=np.float32)
    nb = min(2, n_cores)
    y_prompt = np.stack([f32(R[b]["y_out"])[:NQ * L] for b in range(nb)], 0)
    y_sample = np.concatenate([f32(R[r]["y_out"])[NQ * L + 16:NQ * L + 20] for r in range(n_cores)], 0)[:, None, :]
    pk = np.stack([f32(R[b]["o_pk"]).reshape(2, 128, 2, 128) for b in range(nb)], 1)
    pv = np.stack([f32(R[b]["o_pv"]).reshape(2, 128, 2, 128) for b in range(nb)], 1)
    pre = np.stack([f32(R[b]["o_pssm"])[:, 0].reshape(2, 32, 64) for b in range(nb)], 1)
    pim = np.stack([f32(R[b]["o_pssm"])[:, 1].reshape(2, 32, 64) for b in range(nb)], 1)
    ppool = np.stack([f32(R[b]["o_ppool"]) for b in range(nb)], 1)
    sk = np.concatenate([f32(R[r]["o_sk"]).reshape(2, 4, 128, 2, 128) for r in range(n_cores)], 1)
    sv = np.concatenate([f32(R[r]["o_sv"]).reshape(2, 4, 128, 2, 128) for r in range(n_cores)], 1)
    sre = np.concatenate([f32(R[r]["o_sssm"])[:, 0].reshape(2, 4, 32, 64) for r in range(n_cores)], 1)
    sim = np.concatenate([f32(R[r]["o_sssm"])[:, 1].reshape(2, 4, 32, 64) for r in range(n_cores)], 1)
    spool = np.concatenate([f32(R[r]["o_spool"]) for r in range(n_cores)], 1)
    return (y_prompt, y_sample, pk, pv, pre, pim, ppool, sk, sv, sre, sim, spool)


def kernel(**inputs):
    return _run(inputs, NCORES)
```

```python
import numpy as np
import concourse.bass as bass
import concourse.mybir as mybir
from concourse.bass_utils import run_bass_kernel_spmd

F32 = mybir.dt.float32
BF16 = mybir.dt.bfloat16
I32 = mybir.dt.int32
ALU = mybir.AluOpType
AF = mybir.ActivationFunctionType

D = 2048
NQ = 1024
NKV = 256
SSMW = 512
POOLW = 512
INW = 2560
DFF = 8192
NCORES = 8
CL = 64
EPS = 1e-6
NEG = -30000.0
PI = float(np.pi)


class Sched:
    ENGS = ["pe", "act", "dve", "pool", "sp"]

    def __init__(self, nc, n_sp_slots=60, n_pool_slots=24):
        self.nc = nc
        self.ops = []
        self.by_eng = {e: [] for e in self.ENGS}
        self.last_w = {}
        self.readers = {}
        self.nslots = {"sp": n_sp_slots, "pool": n_pool_slots}
        self.ndma = {"sp": 0, "pool": 0}

    def op(self, eng, fn, reads=(), writes=(), dma=False, cc=False):
        oid = len(self.ops)
        deps = set()
        for r in reads:
            w = self.last_w.get(r)
            if w is not None:
                deps.add(w)
        for r in writes:
            w = self.last_w.get(r)
            if w is not None:
                deps.add(w)
            for rid in self.readers.get(r, {}).values():
                deps.add(rid)
        o = dict(id=oid, eng=eng, fn=fn, deps=deps, dma=dma, marked=False)
        if cc:
            o["dma"] = True
            dma = True
            self.ncc = getattr(self, "ncc", 0) + 1
            o["q"] = "cc"
            o["slot"] = 0
            o["val"] = self.ncc
        elif dma:
            k = self.ndma[eng]
            self.ndma[eng] += 1
            o["q"] = eng
            o["slot"] = k % self.nslots[eng]
            o["val"] = 16 * (k // self.nslots[eng] + 1)
        self.ops.append(o)
        self.by_eng[eng].append(o)
        for r in reads:
            self.readers.setdefault(r, {})[("d", oid) if dma else eng] = oid
        for r in writes:
            self.last_w[r] = oid
            self.readers[r] = {}
        return oid

    def fence(self, pred, newkeys, eng="sp", extra_reads=()):
        keys = [k for k in set(list(self.last_w.keys()) + list(self.readers.keys())) if pred(k)]
        if getattr(self, "nofence", False):
            return None
        return self.op(eng, lambda e: e.nop(), reads=list(extra_reads), writes=keys + list(newkeys))

    def emit(self):
        nc = self.nc
        ops = self.ops
        for o in ops:
            for p in o["deps"]:
                po = ops[p]
                if po["dma"]:
                    continue
                if po["eng"] == "pe" and o["eng"] == "pe":
                    continue
                po["marked"] = True
        cum = {}
        cnt = {e: 0 for e in self.ENGS}
        for e in self.ENGS:
            for o in self.by_eng[e]:
                if o["marked"] and not o["dma"]:
                    cnt[e] += 1
                o["cum"] = cnt[e]
        esem = {e: nc.alloc_semaphore("s_" + e) for e in self.ENGS}
        dsem = {q: [nc.alloc_semaphore(f"d_{q}_{i}") for i in range(self.nslots[q])] for q in ("sp", "pool")}
        dsem["cc"] = [nc.alloc_semaphore("d_cc")]
        handles = {"pe": nc.tensor, "act": nc.scalar, "dve": nc.vector, "pool": nc.gpsimd, "sp": nc.sync}

        def run(eng, e):
            waited = {}
            for o in self.by_eng[eng]:
                need = {}
                for p in o["deps"]:
                    po = ops[p]
                    if po["dma"]:
                        key = ("d", po["q"], po["slot"])
                        v = po["val"]
                    else:
                        if po["eng"] == "pe" and eng == "pe":
                            continue
                        key = ("e", po["eng"])
                        v = po["cum"]
                    if v > need.get(key, 0):
                        need[key] = v
                if o["dma"] and o["q"] != "cc":
                    if o["val"] > 16:
                        key = ("d", o["q"], o["slot"])
                        need[key] = max(need.get(key, 0), o["val"] - 16)
                for key, v in need.items():
                    if waited.get(key, 0) >= v:
                        continue
                    waited[key] = v
                    sem = esem[key[1]] if key[0] == "e" else dsem[key[1]][key[2]]
                    e.wait_ge(sem, v)
                ins = o["fn"](e)
                if o["dma"]:
                    ins.then_inc(dsem[o["q"]][o["slot"]], 1 if o["q"] == "cc" else 16)
                elif o["marked"]:
                    ins.then_inc(esem[eng], 1)

        with nc.Block() as block:
            @block.tensor
            def _(e):
                run("pe", e)

            @block.scalar
            def _(e):
                run("act", e)

            @block.vector
            def _(e):
                run("dve", e)

            @block.gpsimd
            def _(e):
                run("pool", e)

            @block.sync
            def _(e):
                run("sp", e)


def col_groups(T):
    n = (T + 511) // 512
    nb = (T - 32) // 128
    per = [nb // n + (1 if i < nb % n else 0) for i in range(n)]
    gs = []
    c = 0
    for i, p in enumerate(per):
        w = p * 128 + (32 if i == n - 1 else 0)
        gs.append((c, c + w))
        c += w
    assert c == T
    return gs


def build(NB, NL=2, dbg=(), stop=None, ffw=True, NQ=4):
    T = NB * 128 + 32
    L = NB * 128
    E0 = NB * 128
    NBLK = NB + 1
    CG = col_groups(T)
    NCH = 2 * NB
    NSQ = int(np.log2(NCH))
    assert 2 ** NSQ == NCH
    nc = bass.Bass("TRN2", target_bir_lowering=False)
    nc.allow_low_precision("bf16 matmul operands by design (reference tolerance measured for bf16)")
    S = Sched(nc)
    S.nofence = 'nofence' in dbg
    A = nc.alloc_sbuf_tensor

    def din(name, shape, dt=F32):
        return nc.dram_tensor(name, list(shape), dt, kind="ExternalInput")

    def dout(name, shape, dt=F32):
        return nc.dram_tensor(name, list(shape), dt, kind="ExternalOutput")

    TT = NQ * L + 32
    x_in = din("x_in", [TT, D])
    w_in = din("w_in", [NL, D, INW]); w_out = din("w_out", [NL, D, D])
    if ffw:
        w_ff1 = din("w_ff1", [NL, D, DFF]); w_ff2 = din("w_ff2", [NL, DFF, D])
    g_mix = din("g_mix", [2, D]); g_ffn = din("g_ffn", [2, D])
    g_q = din("g_q", [2, 128]); g_k = din("g_k", [2, 128]); sinks = din("sinks", [2, 8])
    A_re = din("A_re", [2, 2048]); A_im = din("A_im", [2, 2048]); log_dt = din("log_dt", [2, 32])
    B_re = din("B_re", [2, 2048, 16]); B_im = din("B_im", [2, 2048, 16])
    C_re = din("C_re", [2, 32, 16, 64]); C_im = din("C_im", [2, 32, 16, 64])
    D_skip = din("D_skip", [2, 512]); w_glu = din("w_glu", [2, 512, 512]); b_glu = din("b_glu", [2, 512])
    w_pool = din("w_pool", [2, 4, 128, 128]); pool_scale = din("pool_scale", [2, 512])
    g_oa = din("g_out_attn", [2, 1024]); g_os = din("g_out_ssm", [2, 512]); g_op = din("g_out_pool", [2, 512])
    cache_k = din("cache_k", [2, 4, 128, 256]); cache_v = din("cache_v", [2, 4, 128, 256])
    st_re = din("st_re", [2, 64, 128]); st_im = din("st_im", [2, 64, 128])
    st_pool = din("st_pool", [2, 4, 15, 512])
    c_rope = din("c_rope", [NQ, 2, 32, T])
    c_mask = din("c_mask", [3, 128, 512])
    c_maskE = din("c_maskE", [32, 128])
    c_maskS = din("c_maskS", [4, 32, 4])
    c_prot = din("c_prot", [128, 32]); c_ident = din("c_ident", [128, 128])
    c_sel = din("c_sel", [128, 24])
    c_jtab = din("c_jtab", [128, 16 * (CL + 1)])
    c_pinv = din("c_pinv", [128, 4 * 16])
    c_psel = din("c_psel", [16, 4])

    xs = nc.dram_tensor("xs_scratch", [TT, D], F32)
    y_out = dout("y_out", [TT, D])
    o_pk = dout("o_pk", [2, 128, 256]); o_pv = dout("o_pv", [2, 128, 256])
    o_pssm = dout("o_pssm", [2, 2, 16, 128]); o_ppool = dout("o_ppool", [2, 15, 512])
    o_sk = dout("o_sk", [2, 4, 128, 256]); o_sv = dout("o_sv", [2, 4, 128, 256])
    o_sssm = dout("o_sssm", [2, 2, 64, 128]); o_spool = dout("o_spool", [2, 4, 15, 512])
    XW = 640
    xch_in = nc.dram_tensor("xch_in", [128, XW], F32)
    xch_out = nc.dram_tensor("xch_out", [NCORES * 128, XW], F32)

    actT = A("actT", [128, 16, T], BF16)
    PROJW = 8 * T + 2 * T + NBLK * 256 + 4 * T
    HIDW = 16 * T
    projbuf = A("projbuf", [128, max(PROJW, HIDW)], BF16)
    o_ = 0
    qT = projbuf[:, o_:o_ + 8 * T].rearrange("p (h t) -> p h t", h=8); o_ += 8 * T
    kT = projbuf[:, o_:o_ + 2 * T].rearrange("p (h t) -> p h t", h=2); o_ += 2 * T
    vbf = projbuf[:, o_:o_ + NBLK * 256].rearrange("p (b c) -> p b c", b=NBLK); o_ += NBLK * 256
    uT = projbuf[:, o_:o_ + 4 * T].rearrange("p (h t) -> p h t", h=4); o_ += 4 * T
    hidT = projbuf[:, 0:HIDW].rearrange("p (m t) -> p m t", m=16)
    xpT = A("xpT", [128, 4, 16 + T], BF16)
    wbuf = [A(f"wbuf{i}", [128, 16, 512], BF16) for i in range(2)]
    xblk = A("xblk", [128, D], F32)
    xn = A("xn", [128, D], BF16)
    ident = A("ident", [128, 128], BF16); identf = A("identf", [128, 128], F32)
    ones_b = A("ones_b", [128, 4, 128], BF16)
    prot = A("prot", [128, 32], BF16)
    rope = A("rope", [32, 2, T], F32)
    masks = A("masks", [128, 3, 512], BF16)
    maskE = A("maskE", [32, 128], BF16); maskS = A("maskS", [32, 4, 4], BF16)
    sel = A("sel", [128, 24], F32)
    vecs = A("vecs", [128, 48], F32)
    gvec = A("gvec", [128, 16], F32)
    qraw = A("qraw", [128, 512], F32); sqb = A("sqb", [128, 512], BF16)
    rtmp = A("rtmp", [128, 512], F32); rtmp2 = A("rtmp2", [128, 512], F32)
    small = A("small", [128, 16], F32)
    relu_t = [A(f"relu_t{i}", [128, 512], BF16) for i in range(2)]
    cosT = A("cosT", [128, 16, CL + 1], F32); sinT = A("sinT", [128, 16, CL + 1], F32)
    Dk = A("Dk", [128, 16, CL], F32); Dk16 = A("Dk16", [128, 16, 16], F32)
    sm = A("sm", [128, 28, 16], F32)
    BbT = A("BbT", [128, 32, 128], BF16)
    CTp_r = A("CTp_r", [128, 16, 128], F32); CTp_ni = A("CTp_ni", [128, 16, 128], F32)
    scur = A("scur", [128, 32], F32)
    Gst = A("Gst", [128, NCH + 1, 32], F32)
    sst = A("sst", [128, NCH + 2, 32], F32)
    esink = A("esink", [1, 8, 128], BF16); sinkrow = A("sinkrow", [1, 16], F32)
    hkL = [A(f"hk{i}", [128, 2, 128], BF16) for i in range(2)]; hvL = [A(f"hv{i}", [128, 256], BF16) for i in range(2)]
    phL = [A(f"ph{i}", [128, 4, 16], BF16) for i in range(2)]; sfin = A("sfin", [128, 2, 32], F32)
    kctok = A("kctok", [128, 256], BF16); kcT = A("kcT", [128, 128], BF16); vc = A("vc", [128, 256], BF16)
    es = A("es", [128, 8], BF16)
    h0s = A("h0s", [128, 2, 64], F32); hs = A("hs", [128, 2, 64], F32)
    sttok = A("sttok", [64, 2, 128], F32)
    stp = A("stp", [16, 512], BF16); psel = A("psel", [16, 4], BF16)
    wglu = A("wglu", [128, 4, 512], BF16); wpool = A("wpool", [128, 4, 128], BF16)
    pmeta = A("pmeta", [128, 4, 32], F32); pmeta2 = A("pmeta2", [128, 4, 32], F32); pinvE = A("pinvE", [128, 4, 16], F32)
    outst = A("outst", [128, 512], F32)

    scr = wbuf[1][:, :, :].rearrange("p a b -> p (a b)").bitcast(F32)
    z_re = scr[:, 0:1024]; z_im = scr[:, 1024:2048]; k_re = scr[:, 2048:3072]; k_im = scr[:, 3072:4096]
    hE_re = rtmp[:, :].rearrange("p (a b) -> p a b", a=16)
    hE_im = rtmp2[:, :].rearrange("p (a b) -> p a b", a=16)
    xst = xblk[:, 0:XW]; stage = xblk[:, 640:640 + 576]; acc = xblk[:, 1280:1280 + 576]
    ebuf = [xn[:, i * 1024:(i + 1) * 1024].rearrange("p (a b) -> p a b", a=2) for i in range(2)]
    XBK = [("xio", j) for j in range(4)]
    xio = [xblk[:, j * 512:(j + 1) * 512] for j in range(4)]

    if "psep" in dbg:
        psA = [nc.alloc_psum_tensor(f"psA{i}", [128, 512], F32) for i in range(4)]
        psX = None
    else:
        psX = nc.alloc_psum_tensor("psX", [128, 4, 512], F32)
        psA = [psX[:, i, :] for i in range(4)]
    psB = [nc.alloc_psum_tensor(f"psB{i}", [128, 512], F32) for i in range(3)]
    psT = nc.alloc_psum_tensor("psT", [128, 1024], BF16)
    psT_alt = psB[2][:, :].bitcast(BF16)

    S._taps = []
    QS = {"q": 0, "l": 0}

    def dma(q, out, in_, reads, writes):
        return S.op(q, lambda e: e.dma_start(out=out, in_=in_), reads=reads, writes=writes, dma=True)

    def dma_slow(q, out, in_, reads, writes):
        return S.op(q, lambda e: e.dma_start(out=out, in_=in_, allow_slow_non_contiguous=True),
                    reads=reads, writes=writes, dma=True)

    def dma_tp(q, out, src_flat, nt, reads, writes):
        v = src_flat.rearrange("(t p) -> p t", p=128)
        for t0 in range(0, nt, 4):
            t1 = min(nt, t0 + 4)
            dma_slow(q, out[:, t0:t1], v[:, t0:t1], reads, writes)

    def tap(name, ap, dt, reads):
        if name not in dbg:
            return
        t = dout("dbg_%s_%d%d" % (name, QS["q"], QS["l"]), list(ap.shape), dt)
        full = t.ap()
        S.op("sp", lambda e: e.dma_start(out=full, in_=ap), reads=reads, writes=[("out", "dbg", name, QS["q"], QS["l"])], dma=True)

    def V(fn, reads, writes):
        return S.op("dve", fn, reads, writes)

    def G(fn, reads, writes):
        return S.op("pool", fn, reads, writes)

    def ACT(fn, reads, writes):
        return S.op("act", fn, reads, writes)

    def PE(fn, reads, writes):
        return S.op("pe", fn, reads, writes)

    def tt(eng, out, a, b, op, reads, writes):
        return S.op(eng, lambda e: e.tensor_tensor(out, a, b, op), reads, writes)

    def mm(out, lhsT, rhs, start, stop, reads, writes):
        return S.op("pe", lambda e: e.matmul(out, lhsT, rhs, start=start, stop=stop), reads, writes)

    def tr(out, in_, idn, reads, writes):
        return S.op("pe", lambda e: e.transpose(out, in_, idn), reads, writes)

    def act(out, in_, func, reads, writes, **kw):
        return S.op("act", lambda e: e.activation(out, in_, func, **kw), reads, writes)

    def cp(eng, out, in_, reads, writes):
        if eng == "act":
            return S.op("act", lambda e: e.copy(out, in_), reads, writes)
        return S.op(eng, lambda e: e.tensor_copy(out, in_), reads, writes)

    def ts(eng, out, in0, s1, s2, op0, op1, reads, writes):
        if op1 is None:
            return S.op(eng, lambda e: e.tensor_scalar(out, in0, s1, 1.0, op0, ALU.mult), reads, writes)
        return S.op(eng, lambda e: e.tensor_scalar(out, in0, s1, s2, op0, op1), reads, writes)

    def stt(out, in0, sc, in1, op0, op1, reads, writes):
        return S.op("dve", lambda e: e.scalar_tensor_tensor(out, in0, sc, in1, op0, op1), reads, writes)

    def ms(eng, out, val, reads, writes):
        return S.op(eng, lambda e: e.memset(out, val), reads, writes)

    def rcp(out, in_, reads, writes):
        return S.op("dve", lambda e: e.reciprocal(out, in_), reads, writes)

    def scan(out, d0, d1, reads, writes):
        return S.op("dve", lambda e: e.tensor_tensor_scan(out, d0, d1, 0.0, ALU.mult, ALU.add), reads, writes)

    dma("pool", ident[:, :], c_ident[:, :], [], ["ident"])
    dma("sp", identf[:, :], c_ident[:, :], [], ["identf"])
    dma("pool", prot[:, :], c_prot[:, :], [], ["prot"])
    for i in range(3):
        dma("pool", masks[:, i, :], c_mask[i, :, :], [], ["masks"])
    dma("pool", maskE[:, :], c_maskE[:, :], [], ["maskE"])
    if "noconst" not in dbg:
        for s_ in range(4):
            dma("pool", maskS[:, s_, :], c_maskS[s_, :, :], [], ["maskS"])
        dma("pool", psel[:, :], c_psel[:, :], [], ["psel"])
    dma("sp", sel[:, :], c_sel[:, :], [], ["sel"])
    dma("sp", pinvE[:, :, :], c_pinv[:, :].rearrange("p (a b) -> p a b", a=4), [], ["pinvE"])
    for i, v in enumerate([1.0 / 128, 1.0 / 1024, 1.0 / 512, 1.0]):
        G(lambda e, i=i, v=v: e.memset(ones_b[:, i, :], v), [], ["ones"])
    G(lambda e: e.memset(stp[:, :], 0.0), [], ["stp"])

    def load_w(src_ap, i):
        key = f"W{i}"
        S.op("pool", lambda e: e.dma_start(out=wbuf[i][:, :, :], in_=src_ap), reads=[], writes=[key], dma=True)
        return key

    def w_tile(wt, l, rows0, col0):
        return wt[l, rows0:rows0 + 2048, col0:col0 + 512].rearrange("(c p) m -> p c m", p=128)

    def drow(blk):
        return QS["q"] * L + blk * 128 if blk < NB else NQ * L

    def xdk(blk):
        return ("xd", QS["q"], blk) if blk < NB else ("xd", "E")

    def actT_reads(c0, c1):
        return [("actT", b) for b in range(NBLK) if b * 128 < c1 and min((b + 1) * 128, T) > c0]

    def blk_of_cols(c0, c1):
        return [b for b in range(NBLK) if b * 128 < c1 and min((b + 1) * 128, T) > c0]

    def norm_to_actT(xsrc, gsrc, l):
        dma_tp("sp", gvec, gsrc[l, :], 16, [], ["gvec"])
        for blk in range(NBLK if QS["q"] == 0 else NB):
            P = 128 if blk < NB else 32
            r0 = blk * 128
            dr = drow(blk)
            dma("sp", xblk[0:P, :], xsrc[dr:dr + P, :], [xdk(blk)], XBK)
            V(lambda e, P=P: e.memset(small[0:P, 0:1], 0.0), [], ["small0"])
            ACT(lambda e, P=P: e.activation(xn[0:P, :], xblk[0:P, :], AF.Square, accum_out=small[0:P, 0:1]),
                XBK + ["small0"], ["xn", "small0"])
            ACT(lambda e, P=P: e.activation(small[0:P, 1:2], small[0:P, 0:1], AF.Ln, bias=EPS, scale=1.0 / D),
                ["small0"], ["small1"])
            ACT(lambda e, P=P: e.activation(small[0:P, 2:3], small[0:P, 1:2], AF.Exp, scale=-0.5),
                ["small1"], ["small2"])
            if "n2" in dbg:
                V(lambda e, P=P: e.tensor_scalar(xn[0:P, :], xblk[0:P, :], small[0:P, 2:3], 1.0, ALU.mult, ALU.mult),
                  XBK + ["small2", "xn"], ["xn"])
            elif "n3" in dbg:
                ACT(lambda e, P=P: e.activation(xn[0:P, :], xblk[0:P, :], AF.Copy, scale=small[0:P, 2:3]),
                    XBK + ["small2", "xn"], ["xn"])
            elif "n4" in dbg:
                pass
            else:
                V(lambda e, P=P: e.tensor_scalar(xn[0:P, :], xblk[0:P, :], small[0:P, 2:3], 1.0, ALU.mult, ALU.mult),
                  XBK + ["small2", "xn"], ["xn"])
            for c4 in range(4 if "n1" not in dbg else 0):
                pk = ("psT" if c4 % 2 == 0 else "psB2")
                pst = psT[:, 0:512] if c4 % 2 == 0 else psT_alt[:, 0:512]
                for i in range(4):
                    kc = c4 * 4 + i
                    PE(lambda e, P=P, kc=kc, i=i, pst=pst: e.transpose(pst[:, i * 128:i * 128 + P],
                                                                       xn[0:P, kc * 128:(kc + 1) * 128], ident[0:P, 0:P]),
                       ["xn", "ident"], [pk])
                for i in range(4):
                    kc = c4 * 4 + i
                    src = pst[:, i * 128:i * 128 + P]
                    dst = actT[:, kc, r0:r0 + P]
                    if "gplain" in dbg:
                        cp("act" if i % 2 == 0 else "dve", dst, src, [pk], [("actT", blk)])
                    elif (i % 2 == 0 or "gact" in dbg) and "gdve" not in dbg:
                        ACT(lambda e, src=src, dst=dst, kc=kc: e.activation(dst, src, AF.Copy, scale=gvec[:, kc:kc + 1]),
                            [pk, "gvec"], [("actT", blk)])
                    else:
                        V(lambda e, src=src, dst=dst, kc=kc: e.tensor_scalar(dst, src, gvec[:, kc:kc + 1], 1.0, ALU.mult, ALU.mult),
                          [pk, "gvec"], [("actT", blk)])

    def load_vecs(l):
        dma_slow("sp", vecs[:, 0:1], g_q[l, :].rearrange("(p o) -> p o", o=1), [], ["vecs"])
        dma_slow("sp", vecs[:, 1:2], g_k[l, :].rearrange("(p o) -> p o", o=1), [], ["vecs"])
        dma_tp("sp", vecs[:, 2:10], g_oa[l, :], 8, [], ["vecs"])
        for j, src in enumerate([g_os, g_op, D_skip, b_glu, pool_scale]):
            dma_slow("sp", vecs[:, 10 + 4 * j:14 + 4 * j], src[l, :].rearrange("(t p) -> p t", p=128), [], ["vecs"])
        V(lambda e: e.tensor_scalar(vecs[:, 30:34], vecs[:, 22:26], -1.0, 1.0, ALU.mult, ALU.mult), ["vecs"], ["vecs"])
        if "novecx" in dbg:
            return
        dma("pool", wglu[:, :, :], w_glu[l, :, :].rearrange("(c p) m -> p c m", p=128), [], ["wglu"])
        dma("pool", wpool[:, :, :], w_pool[l, :, :, :].rearrange("g c d -> c g d"), [], ["wpool"])
        dma("sp", sinkrow[0:1, 0:8], sinks[l:l + 1, :], [], ["sinkrow"])
        ACT(lambda e: e.activation(sinkrow[0:1, 8:16], sinkrow[0:1, 0:8], AF.Exp), ["sinkrow"], ["sinkrow"])
        V(lambda e: e.tensor_copy(esink[0:1, :, :], sinkrow[0:1, 8:16].unsqueeze(2).to_broadcast([1, 8, 128])),
          ["sinkrow"], ["esink"])

    def qk_finish(ps, n, c0, dst, gcol, wkey, fkey):
        ACT(lambda e: e.copy(qraw[:, 0:n], ps[:, 0:n]), [wkey], ["qraw"])
        G(lambda e: e.tensor_tensor(sqb[:, 0:n], qraw[:, 0:n], qraw[:, 0:n], ALU.mult), ["qraw"], ["sqb"])
        PE(lambda e: e.matmul(psB[0][:, 0:n], ones_b[:, 0, :], sqb[:, 0:n], start=True, stop=True),
           ["sqb", "ones"], ["psB0"])
        ACT(lambda e: e.activation(rtmp[:, 0:n], psB[0][:, 0:n], AF.Ln, bias=EPS, scale=1.0), ["psB0"], ["rtmp"])
        ACT(lambda e: e.activation(rtmp[:, 0:n], rtmp[:, 0:n], AF.Exp, scale=-0.5), ["rtmp"], ["rtmp"])
        V(lambda e: e.scalar_tensor_tensor(dst, qraw[:, 0:n], vecs[:, gcol:gcol + 1], rtmp[:, 0:n], ALU.mult, ALU.mult),
          ["qraw", "rtmp", "vecs"], [wkey + "_d"])
        PE(lambda e: e.matmul(psB[1][0:32, 0:n], prot[:, :], dst, start=True, stop=True), [wkey + "_d", "prot"], ["psB1"])
        V(lambda e: e.tensor_tensor(rtmp2[0:32, 0:n], psB[1][0:32, 0:n], rope[:, 1, c0:c0 + n], ALU.mult),
          ["psB1", "rope"], ["rtmp2"])
        V(lambda e: e.tensor_tensor(rtmp[0:32, 0:n], dst[0:32], rope[:, 0, c0:c0 + n], ALU.mult),
          [wkey + "_d", "rope", "rtmp"], ["rtmp"])
        V(lambda e: e.tensor_tensor(dst[0:32], rtmp[0:32, 0:n], rtmp2[0:32, 0:n], ALU.add),
          ["rtmp", "rtmp2", wkey + "_d"], [wkey + "_d", fkey])

    def w_in_phase(l):
        order = [3, 4, 0, 1, 2]
        for oi, ti in enumerate(order):
            bi = oi % 2
            wkey = load_w(w_tile(w_in, l, 0, ti * 512), bi)
            wb = wbuf[bi]
            for gi, (c0, c1) in enumerate(CG):
                n = c1 - c0
                for mt in range(4):
                    if ti == 2 and mt >= 2:
                        break
                    ps = psA[mt]
                    pk = f"psA{mt}"
                    for kc in range(16):
                        PE(lambda e, ps=ps, kc=kc, mt=mt, wb=wb, c0=c0, c1=c1, n=n:
                           e.matmul(ps[:, 0:n], wb[:, kc, mt * 128:(mt + 1) * 128], actT[:, kc, c0:c1],
                                    start=(kc == 0), stop=(kc == 15)),
                           [wkey] + actT_reads(c0, c1), [pk])
                    if ti in (0, 1):
                        h = ti * 4 + mt
                        qk_finish(ps, n, c0, qT[:, h, c0:c1], 0, pk, ("qT", h, gi))
                    elif ti == 2:
                        qk_finish(ps, n, c0, kT[:, mt, c0:c1], 1, pk, ("kT", mt, gi))
                    elif ti == 3:
                        ACT(lambda e, ps=ps, mt=mt, c0=c0, c1=c1, n=n: e.copy(uT[:, mt, c0:c1], ps[:, 0:n]),
                            [pk], [("uT", gi)])
                    else:
                        V(lambda e, ps=ps, mt=mt, c0=c0, c1=c1, n=n:
                          e.tensor_copy(xpT[:, mt, 16 + c0:16 + c1], ps[:, 0:n]), [pk], [("xpT", gi)])
            if ti == 2:
                for blk in range(NBLK):
                    P = 128 if blk < NB else 32
                    r0 = blk * 128
                    ps = psA[2 + blk % 2]
                    pk = f"psA{2 + blk % 2}"
                    for kc in range(16):
                        PE(lambda e, ps=ps, kc=kc, wb=wb, r0=r0, P=P:
                           e.matmul(ps[0:P, 0:256], actT[:, kc, r0:r0 + P], wb[:, kc, 256:512],
                                    start=(kc == 0), stop=(kc == 15)),
                           [wkey, ("actT", blk)], [pk])
                    ACT(lambda e, ps=ps, blk=blk, P=P: e.copy(vbf[0:P, blk, :], ps[0:P, 0:256]), [pk], [("vbf", blk)])

    SMI = dict(rho=0, aC_r=1, aC_i=2, aL_r=3, aL_i=4, a1_r=5, a1_i=6, f_r=7, f_i=8, are=9, aim=10, dt=11, dre=12, th=13,
               t0=14, t1=15, t2=16, t3=17)

    def smv(name):
        return sm[:, SMI[name], :]

    def ssm_prep(l):
        SK = ["z_re", "z_im", "k_re", "k_im"]
        dma_tp("sp", smv("are"), A_re[l, :], 16, [], ["sm_in"])
        dma_tp("sp", smv("aim"), A_im[l, :], 16, [], ["sm_in"])
        ld = log_dt[l, :].rearrange("(t h) -> h t", h=2)
        dma_slow("sp", sm[0:64, SMI["dt"], :], ld[0:1, :].partition_broadcast(64), [], ["sm_in"])
        dma_slow("sp", sm[64:128, SMI["dt"], :], ld[1:2, :].partition_broadcast(64), [], ["sm_in"])
        ACT(lambda e: e.activation(smv("dt"), smv("dt"), AF.Exp), ["sm_in"], ["sm_dt"])
        V(lambda e: e.tensor_tensor(smv("dre"), smv("dt"), smv("are"), ALU.mult), ["sm_dt", "sm_in"], ["sm_dre"])
        V(lambda e: e.tensor_tensor(smv("th"), smv("dt"), smv("aim"), ALU.mult), ["sm_dt", "sm_in"], ["sm_th"])
        ACT(lambda e: e.activation(smv("rho"), smv("dre"), AF.Exp), ["sm_dre"], ["sm_rho"])
        W65 = 16 * (CL + 1)
        jt = scr[:, 0:W65].rearrange("p (a b) -> p a b", a=16)
        ang = scr[:, W65:2 * W65].rearrange("p (a b) -> p a b", a=16)
        rp = scr[:, 2 * W65:3 * W65].rearrange("p (a b) -> p a b", a=16)
        tq = scr[:, 0:W65]
        angf = scr[:, W65:2 * W65]
        dma("sp", scr[:, 0:W65], c_jtab[:, :], [], SK + ["W1"])
        V(lambda e: e.tensor_tensor(ang, jt, smv("th").unsqueeze(2).to_broadcast([128, 16, CL + 1]), ALU.mult),
          SK + ["sm_th"], SK)
        V(lambda e: e.tensor_tensor(rp, jt, smv("dre").unsqueeze(2).to_broadcast([128, 16, CL + 1]), ALU.mult),
          SK + ["sm_dre"], SK)
        ACT(lambda e: e.activation(scr[:, 2 * W65:3 * W65], scr[:, 2 * W65:3 * W65], AF.Exp), SK, SK)
        tqi = tq.bitcast(I32)
        V(lambda e: e.tensor_scalar(tq, angf, 1.0 / (2 * PI), 1.0, ALU.mult, ALU.mult), SK, SK)
        V(lambda e: e.tensor_copy(tqi, tq), SK, SK)
        V(lambda e: e.tensor_copy(tq, tqi), SK, SK)
        V(lambda e: e.scalar_tensor_tensor(angf, tq, -2 * PI, angf, ALU.mult, ALU.add), SK, SK)
        V(lambda e: e.tensor_scalar(angf, angf, -PI, PI, ALU.max, ALU.min), SK, SK)
        ACT(lambda e: e.activation(sinT[:, :, :].rearrange("p a b -> p (a b)"), angf, AF.Sin), SK, ["sinT"])
        ACT(lambda e: e.activation(tq, angf, AF.Abs), SK, SK)
        ACT(lambda e: e.activation(cosT[:, :, :].rearrange("p a b -> p (a b)"), tq, AF.Sin, bias=PI / 2, scale=-1.0),
            SK, ["cosT"])
        V(lambda e: e.tensor_tensor(smv("a1_r"), rp[:, :, 1], cosT[:, :, 1], ALU.mult), SK + ["cosT"], ["sm_a1"])
        V(lambda e: e.tensor_tensor(smv("a1_i"), rp[:, :, 1], sinT[:, :, 1], ALU.mult), SK + ["sinT"], ["sm_a1"])
        V(lambda e: e.tensor_tensor(smv("aC_r"), rp[:, :, CL], cosT[:, :, CL], ALU.mult), SK + ["cosT"], ["sm_aC"])
        V(lambda e: e.tensor_tensor(smv("aC_i"), rp[:, :, CL], sinT[:, :, CL], ALU.mult), SK + ["sinT"], ["sm_aC"])
        V(lambda e: e.tensor_copy(Dk[:, :, :], smv("rho").unsqueeze(2).to_broadcast([128, 16, CL])), ["sm_rho"], ["Dk"])
        V(lambda e: e.memset(Dk[:, :, 0:1], 0.0), ["Dk"], ["Dk"])
        V(lambda e: e.tensor_copy(Dk16[:, :, :], smv("rho").unsqueeze(2).to_broadcast([128, 16, 16])), ["sm_rho"], ["Dk16"])
        V(lambda e: e.memset(Dk16[:, :, 0:1], 0.0), ["Dk16"], ["Dk16"])
        G(lambda e: e.tensor_copy(smv("aL_r"), smv("aC_r")), ["sm_aC"], ["sm_aL"])
        G(lambda e: e.tensor_copy(smv("aL_i"), smv("aC_i")), ["sm_aC"], ["sm_aL"])
        for _ in range(NSQ):
            tt("pool", smv("t0"), smv("aL_r"), smv("aL_r"), ALU.mult, ["sm_aL"], ["sm_t0"])
            tt("pool", smv("t1"), smv("aL_i"), smv("aL_i"), ALU.mult, ["sm_aL"], ["sm_t1"])
            tt("pool", smv("t2"), smv("aL_r"), smv("aL_i"), ALU.mult, ["sm_aL"], ["sm_t2"])
            tt("pool", smv("aL_r"), smv("t0"), smv("t1"), ALU.subtract, ["sm_t0", "sm_t1", "sm_aL"], ["sm_aL"])
            tt("pool", smv("aL_i"), smv("t2"), smv("t2"), ALU.add, ["sm_t2", "sm_aL"], ["sm_aL"])
        tt("dve", smv("t0"), smv("are"), smv("are"), ALU.mult, ["sm_in", "sm_t0"], ["sm_t0"])
        tt("dve", smv("t1"), smv("aim"), smv("aim"), ALU.mult, ["sm_in", "sm_t1"], ["sm_t1"])
        tt("dve", smv("t0"), smv("t0"), smv("t1"), ALU.add, ["sm_t0", "sm_t1"], ["sm_t0"])
        V(lambda e: e.reciprocal(smv("t0"), smv("t0")), ["sm_t0"], ["sm_t0"])
        V(lambda e: e.tensor_scalar(smv("t1"), smv("a1_r"), -1.0, 1.0, ALU.add, ALU.mult), ["sm_a1", "sm_t1"], ["sm_t1"])
        tt("dve", smv("t2"), smv("t1"), smv("are"), ALU.mult, ["sm_t1", "sm_in", "sm_t2"], ["sm_t2"])
        tt("dve", smv("t3"), smv("a1_i"), smv("aim"), ALU.mult, ["sm_a1", "sm_in"], ["sm_t3"])
        tt("dve", smv("t2"), smv("t2"), smv("t3"), ALU.add, ["sm_t2", "sm_t3"], ["sm_t2"])
        tt("dve", smv("f_r"), smv("t2"), smv("t0"), ALU.mult, ["sm_t2", "sm_t0"], ["sm_f"])
        tt("dve", smv("t2"), smv("a1_i"), smv("are"), ALU.mult, ["sm_a1", "sm_in", "sm_t2"], ["sm_t2"])
        tt("dve", smv("t3"), smv("t1"), smv("aim"), ALU.mult, ["sm_t1", "sm_in", "sm_t3"], ["sm_t3"])
        tt("dve", smv("t2"), smv("t2"), smv("t3"), ALU.subtract, ["sm_t2", "sm_t3"], ["sm_t2"])
        tt("dve", smv("f_i"), smv("t2"), smv("t0"), ALU.mult, ["sm_t2", "sm_t0"], ["sm_f"])
        Bs_r = z_re[:, 0:256].rearrange("p (a b) -> p a b", a=16)
        Bs_i = z_re[:, 256:512].rearrange("p (a b) -> p a b", a=16)
        Bb_r = z_re[:, 512:768].rearrange("p (a b) -> p a b", a=16)
        Bb_i = z_re[:, 768:1024].rearrange("p (a b) -> p a b", a=16)
        Bt = z_im[:, 0:256].rearrange("p (a b) -> p a b", a=16)
        Mp = [k_re.bitcast(BF16), k_im.bitcast(BF16)]
        dma("sp", Bs_r, B_re[l, :, :].rearrange("(t p) c -> p t c", p=128), [], SK)
        dma("sp", Bs_i, B_im[l, :, :].rearrange("(t p) c -> p t c", p=128), [], SK)
        fr = smv("f_r").unsqueeze(2).to_broadcast([128, 16, 16])
        fi = smv("f_i").unsqueeze(2).to_broadcast([128, 16, 16])
        tt("dve", Bb_r, Bs_r, fr, ALU.mult, SK + ["sm_f"], SK)
        tt("dve", Bt, Bs_i, fi, ALU.mult, SK + ["sm_f"], SK)
        tt("dve", Bb_r, Bb_r, Bt, ALU.subtract, SK, SK)
        tt("dve", Bb_i, Bs_i, fr, ALU.mult, SK + ["sm_f"], SK)
        tt("dve", Bt, Bs_r, fi, ALU.mult, SK + ["sm_f"], SK)
        tt("dve", Bb_i, Bb_i, Bt, ALU.add, SK, SK)
        for ri, Bb in enumerate((Bb_r, Bb_i)):
            V(lambda e, ri=ri: e.memset(Mp[ri], 0.0), SK, SK)
            for h in range(2):
                base = Mp[ri][64 * h:64 * h + 64, 0:1]
                for a in range(4):
                    dst = bass.AP(base.tensor, base.offset + 16 * h + 512 * a, [[base.ap[0][0], 64], [160, 4], [1, 16]])
                    src = Bb[64 * h:64 * h + 64, 4 * a:4 * a + 4, :]
                    cp("dve", dst, src, SK, SK)
        for ri in range(2):
            for t4 in range(4):
                pk = "psT"
                pst = psT[:, (t4 % 2) * 512:(t4 % 2 + 1) * 512]
                for i in range(4):
                    t = t4 * 4 + i
                    PE(lambda e, pst=pst, i=i, t=t, ri=ri: e.transpose(pst[:, i * 128:(i + 1) * 128],
                                                                       Mp[ri][:, t * 128:(t + 1) * 128], ident[:, :]),
                       SK + ["ident"], [pk])
                dst = BbT[:, ri * 16 + t4 * 4:ri * 16 + t4 * 4 + 4, :]
                V(lambda e, dst=dst, pst=pst: e.tensor_copy(dst, pst.rearrange("p (a b) -> p a b", a=4)),
                  [pk], ["BbT", "corr"])
        for ri, Csrc in enumerate((C_re, C_im)):
            Zf = scr[0:32, 0:2048].rearrange("p (a b) -> p a b", a=16)
            V(lambda e, Zf=Zf: e.memset(Zf, 0.0), SK, SK)
            cv_ = Csrc[l, :, :, :].rearrange("(t h) c n -> h c t n", h=2)
            dma("sp", Zf[0:16, :, 0:64], cv_[0, :, :, :], [], SK)
            dma("sp", Zf[16:32, :, 64:128], cv_[1, :, :, :], [], SK)
            CTp = CTp_r if ri == 0 else CTp_ni
            ms("dve", CTp[:, :, :], 0.0, [], ["CTp"])
            base = CTp[:, 0, 0:1]
            for t4 in range(4):
                ps = psB[2]
                for i in range(4):
                    t = t4 * 4 + i
                    tr(ps[:, i * 32:(i + 1) * 32], Zf[:, t, :], identf[0:32, 0:32], SK + ["identf"], ["psB2"])
                dst = bass.AP(base.tensor, base.offset + 512 * t4, [[base.ap[0][0], 128], [160, 4], [1, 32]])
                srcv = ps[:, 0:128].rearrange("p (a b) -> p a b", a=4)
                if ri == 0:
                    cp("dve", dst, srcv, ["psB2", "CTp"], ["CTp"])
                else:
                    ts("dve", dst, srcv, -1.0, None, ALU.mult, None, ["psB2", "CTp"], ["CTp"])

    def ssm_chunk(c):
        c0 = c * CL
        gi = [i for i, (a, b) in enumerate(CG) if a <= c0 < b][0]
        xr = psX[:, 0:2, :].rearrange("p a b -> p (a b)")
        xi = psX[:, 2:4, :].rearrange("p a b -> p (a b)")
        for ri in range(2):
            for t in range(16):
                mm(psX[:, 2 * ri + t // 8, (t % 8) * CL:(t % 8 + 1) * CL], BbT[:, ri * 16 + t, :], uT[:, t // 4, c0:c0 + CL],
                   True, True, ["BbT", ("uT", gi)], [f"psA{2 * ri}", f"psA{2 * ri + 1}"])
        C3 = cosT[:, :, 0:CL]; S3 = sinT[:, :, 0:CL]
        v3 = lambda ap: ap.rearrange("p (a b) -> p a b", a=16)
        XR = ["psA0", "psA1"]; XI = ["psA2", "psA3"]
        tt("dve", v3(z_re), v3(xr), C3, ALU.mult, XR + ["cosT", "z_re"], ["z_re"])
        tt("dve", v3(k_re), v3(xi), S3, ALU.mult, XI + ["sinT", "k_re"], ["k_re"])
        tt("dve", z_re, z_re, k_re, ALU.add, ["z_re", "k_re"], ["z_re"])
        tt("dve", v3(z_im), v3(xi), C3, ALU.mult, XI + ["cosT", "z_im"], ["z_im"])
        tt("dve", v3(k_im), v3(xr), S3, ALU.mult, XR + ["sinT", "k_im"], ["k_im"])
        tt("dve", z_im, z_im, k_im, ALU.subtract, ["z_im", "k_im"], ["z_im"])
        sr = scur[:, 0:16]; si = scur[:, 16:32]
        tt("dve", smv("t0"), smv("a1_r"), sr, ALU.mult, ["sm_a1", "scur", "sm_t0"], ["sm_t0"])
        tt("dve", smv("t1"), smv("a1_i"), si, ALU.mult, ["sm_a1", "scur", "sm_t1"], ["sm_t1"])
        tt("dve", smv("t0"), smv("t0"), smv("t1"), ALU.subtract, ["sm_t0", "sm_t1"], ["sm_t0"])
        tt("dve", v3(z_re)[:, :, 0], v3(z_re)[:, :, 0], smv("t0"), ALU.add, ["z_re", "sm_t0"], ["z_re"])
        tt("dve", smv("t2"), smv("a1_r"), si, ALU.mult, ["sm_a1", "scur", "sm_t2"], ["sm_t2"])
        tt("dve", smv("t3"), smv("a1_i"), sr, ALU.mult, ["sm_a1", "scur", "sm_t3"], ["sm_t3"])
        tt("dve", smv("t2"), smv("t2"), smv("t3"), ALU.add, ["sm_t2", "sm_t3"], ["sm_t2"])
        tt("dve", v3(z_im)[:, :, 0], v3(z_im)[:, :, 0], smv("t2"), ALU.add, ["z_im", "sm_t2"], ["z_im"])
        dk = Dk[:, :, :].rearrange("p a b -> p (a b)")
        scan(k_re, dk, z_re, ["Dk", "z_re", "k_re"], ["k_re"])
        scan(k_im, dk, z_im, ["Dk", "z_im", "k_im"], ["k_im"])
        tt("dve", v3(z_re), v3(k_re), C3, ALU.mult, ["k_re", "cosT", "z_re"], ["z_re"])
        tt("dve", v3(z_im), v3(k_im), S3, ALU.mult, ["k_im", "sinT", "z_im"], ["z_im"])
        tt("dve", z_re, z_re, z_im, ALU.subtract, ["z_re", "z_im"], ["z_re"])
        tt("dve", v3(z_im), v3(k_im), C3, ALU.mult, ["k_im", "cosT", "z_im"], ["z_im"])
        tt("dve", v3(k_re), v3(k_re), S3, ALU.mult, ["k_re", "sinT"], ["k_re"])
        tt("dve", z_im, z_im, k_re, ALU.add, ["z_im", "k_re"], ["z_im"])
        cp("dve", scur[:, 0:16], v3(z_re)[:, :, CL - 1], ["z_re", "scur"], ["scur"])
        cp("dve", scur[:, 16:32], v3(z_im)[:, :, CL - 1], ["z_im", "scur"], ["scur"])
        for ct in range(4):
            for i in range(4):
                t = ct * 4 + i
                mm(psB[2][:, ct * CL:(ct + 1) * CL], CTp_r[:, t, :], v3(z_re)[:, t, :], (i == 0), False, ["CTp", "z_re"], ["psB2"])
                mm(psB[2][:, ct * CL:(ct + 1) * CL], CTp_ni[:, t, :], v3(z_im)[:, t, :], False, (i == 3), ["CTp", "z_im"], ["psB2"])
        cp("act", actT[:, 8:12, c0:c0 + CL], psB[2][:, 0:4 * CL].rearrange("p (a b) -> p a b", a=4), ["psB2"], [("ysm", c)])

    def ssm_E(l):
        SK = ["z_re", "z_im", "k_re", "k_im"]
        giE = len(CG) - 1
        for ri in range(2):
            for t in range(16):
                PE(lambda e, ri=ri, t=t: e.matmul(psX[:, 2 * ri + t // 8, (t % 8) * CL:(t % 8) * CL + 32],
                                                  BbT[:, ri * 16 + t, :], uT[:, t // 4, E0:E0 + 32], start=True, stop=True),
                   ["BbT", ("uT", giE)], [f"psA{2 * ri}", f"psA{2 * ri + 1}"])
        xr = psX[:, 0:2, :].rearrange("p a b -> p (a b)").rearrange("p (a b) -> p a b", a=16)
        xi = psX[:, 2:4, :].rearrange("p a b -> p (a b)").rearrange("p (a b) -> p a b", a=16)
        XR = ["psA0", "psA1"]; XI = ["psA2", "psA3"]
        C3 = cosT[:, :, 0:16]; S3 = sinT[:, :, 0:16]
        m3 = lambda ap: ap[:, 0:256].rearrange("p (a b) -> p a b", a=16)
        zr = z_re[:, 0:256]; zi = z_im[:, 0:256]; kr = k_re[:, 0:256]; kim = k_im[:, 0:256]
        tt("dve", m3(z_re), xr[:, :, 0:16], C3, ALU.mult, XR + ["cosT", "z_re"], ["z_re"])
        tt("dve", m3(k_re), xi[:, :, 0:16], S3, ALU.mult, XI + ["sinT", "k_re"], ["k_re"])
        tt("dve", zr, zr, kr, ALU.add, ["z_re", "k_re"], ["z_re"])
        tt("dve", m3(z_im), xi[:, :, 0:16], C3, ALU.mult, XI + ["cosT", "z_im"], ["z_im"])
        tt("dve", m3(k_im), xr[:, :, 0:16], S3, ALU.mult, XR + ["sinT", "k_im"], ["k_im"])
        tt("dve", zi, zi, kim, ALU.subtract, ["z_im", "k_im"], ["z_im"])
        dk = Dk16[:, :, :].rearrange("p a b -> p (a b)")
        V(lambda e: e.tensor_tensor_scan(kr, dk, zr, 0.0, ALU.mult, ALU.add), ["Dk16", "z_re", "k_re"], ["k_re"])
        V(lambda e: e.tensor_tensor_scan(kim, dk, zi, 0.0, ALU.mult, ALU.add), ["Dk16", "z_im", "k_im"], ["k_im"])
        ms("dve", hE_re, 0.0, ["rtmp"], ["rtmp", "hE"])
        ms("dve", hE_im, 0.0, ["rtmp2"], ["rtmp2", "hE"])
        tt("dve", m3(z_re), m3(k_re), C3, ALU.mult, ["k_re", "cosT", "z_re"], ["z_re"])
        tt("dve", m3(z_im), m3(k_im), S3, ALU.mult, ["k_im", "sinT", "z_im"], ["z_im"])
        tt("dve", hE_re[:, :, 0:16], m3(z_re), m3(z_im), ALU.subtract, ["z_re", "z_im", "hE"], ["hE"])
        tt("dve", Gst[:, NCH, 0:16], m3(z_re)[:, :, 15], m3(z_im)[:, :, 15], ALU.subtract, ["z_re", "z_im"], [("G", NCH)])
        tt("dve", m3(z_re), m3(k_im), C3, ALU.mult, ["k_im", "cosT", "z_re"], ["z_re"])
        tt("dve", m3(z_im), m3(k_re), S3, ALU.mult, ["k_re", "sinT", "z_im"], ["z_im"])
        tt("dve", hE_im[:, :, 0:16], m3(z_re), m3(z_im), ALU.add, ["z_re", "z_im", "hE"], ["hE"])
        tt("dve", Gst[:, NCH, 16:32], m3(z_re)[:, :, 15], m3(z_im)[:, :, 15], ALU.add, ["z_re", "z_im"], [("G", NCH)])
        dma("sp", sttok[:, 0, :], st_re[l, :, :], [], ["sttok"])
        dma("sp", sttok[:, 1, :], st_im[l, :, :], [], ["sttok"])
        for ri in range(2):
            PE(lambda e, ri=ri: e.transpose(psB[2][:, ri * 64:(ri + 1) * 64], sttok[:, ri, :], identf[0:64, 0:64]),
               ["sttok", "identf"], ["psB2"])
        V(lambda e: e.tensor_copy(h0s[:, :, :], psB[2][:, 0:128].rearrange("p (a b) -> p a b", a=2)), ["psB2"], ["h0s"])
        h0r = h0s[:, 0, :].rearrange("p (s t) -> p s t", s=4); h0i = h0s[:, 1, :].rearrange("p (s t) -> p s t", s=4)
        hsr = hs[:, 0, :].rearrange("p (s t) -> p s t", s=4); hsi = hs[:, 1, :].rearrange("p (s t) -> p s t", s=4)
        a1r = smv("a1_r").unsqueeze(1).to_broadcast([128, 4, 16]); a1i = smv("a1_i").unsqueeze(1).to_broadcast([128, 4, 16])
        t0 = z_re[:, 0:64].rearrange("p (s t) -> p s t", s=4); t1 = z_im[:, 0:64].rearrange("p (s t) -> p s t", s=4)
        xrs = xr[:, :, 16:20].rearrange("p t s -> p s t"); xis = xi[:, :, 16:20].rearrange("p t s -> p s t")
        tt("dve", t0, h0r, a1r, ALU.mult, ["h0s", "sm_a1", "z_re"], ["z_re"])
        tt("dve", t1, h0i, a1i, ALU.mult, ["h0s", "sm_a1", "z_im"], ["z_im"])
        tt("dve", t0, t0, t1, ALU.subtract, ["z_re", "z_im"], ["z_re"])
        tt("dve", hsr, t0, xrs, ALU.add, ["z_re"] + XR, ["hs"])
        tt("dve", t0, h0i, a1r, ALU.mult, ["h0s", "sm_a1", "z_re"], ["z_re"])
        tt("dve", t1, h0r, a1i, ALU.mult, ["h0s", "sm_a1", "z_im"], ["z_im"])
        tt("dve", t0, t0, t1, ALU.add, ["z_re", "z_im"], ["z_re"])
        tt("dve", hsi, t0, xis, ALU.add, ["z_re"] + XI, ["hs"])
        V(lambda e: e.tensor_copy(hE_re[:, :, 16:20], hsr.rearrange("p s t -> p t s")), ["hs", "hE"], ["hE"])
        V(lambda e: e.tensor_copy(hE_im[:, :, 16:20], hsi.rearrange("p s t -> p t s")), ["hs", "hE"], ["hE"])
        for ri in range(2):
            PE(lambda e, ri=ri: e.transpose(psB[2][0:64, ri * 128:(ri + 1) * 128], hs[:, ri, :], identf[:, :]),
               ["hs", "identf"], ["psB2"])
        V(lambda e: e.tensor_copy(outst[0:64, 0:256], psB[2][0:64, 0:256]), ["psB2"], ["outst"])
        for ri in range(2 if QS["q"] == 0 else 0):
            dma("sp", o_sssm[l, ri, :, :], outst[0:64, ri * 128:(ri + 1) * 128], ["outst"], [("out", "sssm", l, ri)])
        for ct in range(4):
            for i in range(4):
                t = ct * 4 + i
                PE(lambda e, ct=ct, t=t, i=i: e.matmul(psB[2][:, ct * 32:(ct + 1) * 32], CTp_r[:, t, :], hE_re[:, t, :],
                                                       start=(i == 0), stop=False), ["CTp", "hE", "rtmp"], ["psB2"])
                PE(lambda e, ct=ct, t=t, i=i: e.matmul(psB[2][:, ct * 32:(ct + 1) * 32], CTp_ni[:, t, :], hE_im[:, t, :],
                                                       start=False, stop=(i == 3)), ["CTp", "hE", "rtmp2"], ["psB2"])
        ACT(lambda e: e.copy(actT[:, 8:12, E0:E0 + 32], psB[2][:, 0:128].rearrange("p (a b) -> p a b", a=4)),
            ["psB2"], [("ysm", "E")])

    def cmuladd(dst_r, dst_i, a_r, a_i, s_r, s_i, g_r, g_i, rd, wr):
        tt("pool", smv("t0"), a_r, s_r, ALU.mult, rd + ["sm_t0"], ["sm_t0"])
        tt("pool", smv("t1"), a_i, s_i, ALU.mult, rd + ["sm_t1"], ["sm_t1"])
        tt("pool", smv("t2"), a_r, s_i, ALU.mult, rd + ["sm_t2"], ["sm_t2"])
        tt("pool", smv("t3"), a_i, s_r, ALU.mult, rd + ["sm_t3"], ["sm_t3"])
        tt("pool", smv("t0"), smv("t0"), smv("t1"), ALU.subtract, ["sm_t0", "sm_t1"], ["sm_t0"])
        tt("pool", smv("t2"), smv("t2"), smv("t3"), ALU.add, ["sm_t2", "sm_t3"], ["sm_t2"])
        if g_r is not None:
            tt("pool", dst_r, smv("t0"), g_r, ALU.add, rd + ["sm_t0"], wr)
            tt("pool", dst_i, smv("t2"), g_i, ALU.add, rd + ["sm_t2"], wr)
        else:
            G(lambda e: e.tensor_copy(dst_r, smv("t0")), rd + ["sm_t0"], wr)
            G(lambda e: e.tensor_copy(dst_i, smv("t2")), rd + ["sm_t2"], wr)

    def state_init(l, q):
        if q == 0:
            cp("dve", scur[:, :], Gst[:, NCH, :], [("G", NCH), "scur"], ["scur"])
        else:
            cp("dve", scur[:, :], sfin[:, l, :], [("sfin", l), "scur"], ["scur"])

    def local_state(l, q):
        giE = len(CG) - 1
        hk = hkL[l]; hv = hvL[l]
        if q == 0:
            ms("dve", hk[:, :, :], 0.0, [], [("hk", l)])
            cp("dve", hk[:, :, 0:16], kT[:, :, E0:E0 + 16], [("kT", 0, giE), ("kT", 1, giE), ("hk", l)], [("hk", l)])
            ms("dve", hv[:, :], 0.0, [], [("hv", l)])
            cp("dve", hv[0:16, :], vbf[0:16, NB, :], [("vbf", NB), ("hv", l)], [("hv", l)])
            cp("dve", xpT[:, :, 0:16], xpT[:, :, 16 + E0:16 + E0 + 16], [("xpT", giE)], [("xpT", "halo")])
        else:
            cp("dve", xpT[:, :, 0:16], phL[l][:, :, :], [("ph", l)], [("xpT", "halo")])
        cp("dve", sfin[:, l, :], scur[:, :], ["scur", ("sfin", l)], [("sfin", l)])
        for ri in range(2):
            tr(psB[2][0:16, ri * 128:(ri + 1) * 128], scur[:, ri * 16:(ri + 1) * 16], identf[:, :], ["scur", "identf"], ["psB2"])
        cp("dve", outst[0:16, 256:512], psB[2][0:16, 0:256], ["psB2", "outst2"], ["outst2"])
        for ri in range(2):
            dma("sp", o_pssm[l, ri, :, :], outst[0:16, 256 + ri * 128:256 + (ri + 1) * 128], ["outst2"], [("out", "pssm", l, ri)])

    def save_halo(l):
        giL = gi_of(L - 128)
        cp("dve", hkL[l][:, :, :], kT[:, :, L - 128:L], [("kT", 0, giL), ("kT", 1, giL), ("hk", l)], [("hk", l)])
        cp("dve", hvL[l][:, :], vbf[:, NB - 1, :], [("vbf", NB - 1), ("hv", l)], [("hv", l)])
        cp("dve", phL[l][:, :, :], xpT[:, :, 16 + L - 16:16 + L], [("xpT", gi_of(L - 16)), ("ph", l)], [("ph", l)])

    def gi_of(col):
        return [i for i, (a, b) in enumerate(CG) if a <= col < b][0]

    SC = float(128 ** -0.5)

    def attn_block(blk, step):
        r0 = blk * 128
        gi = gi_of(r0)

        def one(kvh):
            i2 = (step * 2 + kvh) % 2
            pa, pb = psA[2 * i2], psA[2 * i2 + 1]
            ka, kb_ = f"psA{2 * i2}", f"psA{2 * i2 + 1}"
            eb = ebuf[i2]; ek = ("ebuf", i2)
            qv = qT[:, 4 * kvh:4 * kvh + 4, r0:r0 + 128]
            qk_ = [("qT", 4 * kvh + g, gi) for g in range(4)]
            if blk == 0:
                l_ = QS["l"]
                kprev = hkL[l_][:, kvh, :]; vprev = hvL[l_][:, kvh * 128:(kvh + 1) * 128]
                mprev = masks[:, 2 if QS["q"] == 0 else 1, :]
                rprev = [("hk", l_)]; rvprev = [("hv", l_)]
            else:
                kprev = kT[:, kvh, r0 - 128:r0]; vprev = vbf[:, blk - 1, kvh * 128:(kvh + 1) * 128]; mprev = masks[:, 1, :]
                rprev = [("kT", kvh, gi_of(r0 - 128))]; rvprev = [("vbf", blk - 1)]
            PE(lambda e: e.matmul(pa[:, :], kprev, qv, start=True, stop=False), rprev + qk_, [ka])
            PE(lambda e: e.matmul(pa[:, :], ident[:, :], mprev, start=False, stop=True), ["ident", "masks"], [ka])
            PE(lambda e: e.matmul(pb[:, :], kT[:, kvh, r0:r0 + 128], qv, start=True, stop=False), [("kT", kvh, gi)] + qk_, [kb_])
            PE(lambda e: e.matmul(pb[:, :], ident[:, :], masks[:, 0, :], start=False, stop=True), ["ident", "masks"], [kb_])
            ACT(lambda e: e.activation(eb[:, 0, :], pa[:, :], AF.Exp, scale=SC), [ka], [ek])
            ACT(lambda e: e.activation(eb[:, 1, :], pb[:, :], AF.Exp, scale=SC), [kb_], [ek])
            PE(lambda e: e.matmul(psB[0][:, :], vprev, eb[:, 0, :], start=True, stop=False), rvprev + [ek], ["psB0"])
            PE(lambda e: e.matmul(psB[0][:, :], vbf[:, blk, kvh * 128:(kvh + 1) * 128], eb[:, 1, :], start=False, stop=True),
               [("vbf", blk), ek], ["psB0"])
            PE(lambda e: e.matmul(psB[1][:, :], ones_b[:, 3, :], eb[:, 0, :], start=True, stop=False), ["ones", ek], ["psB1"])
            PE(lambda e: e.matmul(psB[1][:, :], ones_b[:, 3, :], eb[:, 1, :], start=False, stop=False), ["ones", ek], ["psB1"])
            PE(lambda e: e.matmul(psB[1][:, :], ones_b[0:1, 3, :], esink[0:1, 4 * kvh:4 * kvh + 4, :], start=False, stop=True),
               ["ones", "esink"], ["psB1"])
            V(lambda e: e.reciprocal(qraw[:, :], psB[1][:, :]), ["psB1", "qraw"], ["qraw"])
            V(lambda e: e.tensor_tensor(qv, psB[0][:, :].rearrange("p (a b) -> p a b", a=4),
                                        qraw[:, :].rearrange("p (a b) -> p a b", a=4), ALU.mult),
              ["psB0", "qraw"], [("qT", 4 * kvh + g, gi) for g in range(4)])
        for kvh in range(2):
            one(kvh)

    def attn_E():
        giE = len(CG) - 1

        def one(kvh):
            pa = psA[2 * kvh]; ka = f"psA{2 * kvh}"
            eb = ebuf[kvh]; ek = ("ebuf", kvh)
            qv = qT[:, 4 * kvh:4 * kvh + 4, E0:E0 + 32]
            qk_ = [("qT", 4 * kvh + g, giE) for g in range(4)]
            PE(lambda e: e.matmul(pa[0:32, 0:128], kT[:, kvh, E0:E0 + 32], qv, start=True, stop=False), [("kT", kvh, giE)] + qk_, [ka])
            PE(lambda e: e.matmul(pa[0:32, 0:128], ident[0:32, 0:32], maskE[:, :], start=False, stop=True), ["ident", "maskE"], [ka])
            ACT(lambda e: e.activation(eb[0:32, 0, 0:128], pa[0:32, 0:128], AF.Exp, scale=SC), [ka], [ek])
            PE(lambda e: e.matmul(psB[0][:, 0:128], vbf[0:32, NB, kvh * 128:(kvh + 1) * 128], eb[0:32, 0, 0:128], start=True, stop=True),
               [("vbf", NB), ek], ["psB0"])
            PE(lambda e: e.matmul(psB[1][:, 0:128], ones_b[0:32, 3, :], eb[0:32, 0, 0:128], start=True, stop=False), ["ones", ek], ["psB1"])
            PE(lambda e: e.matmul(psB[1][:, 0:128], ones_b[0:1, 3, :], esink[0:1, 4 * kvh:4 * kvh + 4, 0:32], start=False, stop=True),
               ["ones", "esink"], ["psB1"])
            V(lambda e: e.reciprocal(qraw[:, 0:128], psB[1][:, 0:128]), ["psB1", "qraw"], ["qraw"])
            V(lambda e: e.tensor_tensor(qT[:, 4 * kvh:4 * kvh + 4, E0:E0 + 16],
                                        psB[0][:, 0:128].rearrange("p (a b) -> p a b", a=4)[:, :, 0:16],
                                        qraw[:, 0:128].rearrange("p (a b) -> p a b", a=4)[:, :, 0:16], ALU.mult),
              ["psB0", "qraw"], [("qTm", kvh)])
        for kvh in range(2):
            one(kvh)

    def attn_samples(l):
        giE = len(CG) - 1
        for s_ in range(4):
            dma("pool", kctok[:, :], cache_k[l, s_, :, :], [], ["kctok"])
            dma("pool", vc[:, :], cache_v[l, s_, :, :], [], ["vc"])
            if QS["q"] == 0:
                dma("sp", o_sk[l, s_, 0:127, :], cache_k[l, s_, 1:128, :], [], [("out", "sk", l, s_)])
                dma("sp", o_sv[l, s_, 0:127, :], cache_v[l, s_, 1:128, :], [], [("out", "sv", l, s_)])
                dma("sp", o_spool[l, s_, 0:14, :], st_pool[l, s_, 1:15, :], [], [("out", "sp", l, s_)])
            col = E0 + 16 + s_
            for kvh in range(2):
                pa = psA[2 * kvh]; ka = f"psA{2 * kvh}"
                qs = qT[:, 4 * kvh:4 * kvh + 4, col]
                qk_ = [("qT", 4 * kvh + g, giE) for g in range(4)]
                PE(lambda e, kvh=kvh: e.transpose(psT[:, 0:128], kctok[:, kvh * 128:(kvh + 1) * 128], ident[:, :]),
                   ["kctok", "ident"], ["psT"])
                V(lambda e: e.tensor_copy(kcT[:, :], psT[:, 0:128]), ["psT"], ["kcT"])
                PE(lambda e, pa=pa, qs=qs: e.matmul(pa[:, 0:4], kcT[:, :], qs, start=True, stop=True), ["kcT"] + qk_, [ka])
                PE(lambda e, pa=pa, qs=qs, kvh=kvh: e.matmul(pa[0:32, 4:8], kT[:, kvh, E0:E0 + 32], qs, start=True, stop=False),
                   [("kT", kvh, giE)] + qk_, [ka])
                PE(lambda e, pa=pa, s_=s_: e.matmul(pa[0:32, 4:8], ident[0:32, 0:32], maskS[:, s_, :], start=False, stop=True),
                   ["ident", "maskS"], [ka])
                ACT(lambda e, pa=pa: e.activation(es[:, 0:4], pa[:, 0:4], AF.Exp, scale=SC), [ka], ["es"])
                ACT(lambda e, pa=pa: e.activation(es[0:32, 4:8], pa[0:32, 4:8], AF.Exp, scale=SC), [ka], ["es"])
                PE(lambda e, kvh=kvh: e.matmul(psB[0][:, 0:4], vc[:, kvh * 128:(kvh + 1) * 128], es[:, 0:4], start=True, stop=False),
                   ["vc", "es"], ["psB0"])
                PE(lambda e, kvh=kvh: e.matmul(psB[0][:, 0:4], vbf[0:32, NB, kvh * 128:(kvh + 1) * 128], es[0:32, 4:8],
                                               start=False, stop=True), [("vbf", NB), "es"], ["psB0"])
                PE(lambda e: e.matmul(psB[1][:, 0:4], ones_b[:, 3, :], es[:, 0:4], start=True, stop=False), ["ones", "es"], ["psB1"])
                PE(lambda e: e.matmul(psB[1][:, 0:4], ones_b[0:32, 3, :], es[0:32, 4:8], start=False, stop=False), ["ones", "es"], ["psB1"])
                PE(lambda e, kvh=kvh: e.matmul(psB[1][:, 0:4], ones_b[0:1, 3, :], esink[0:1, 4 * kvh:4 * kvh + 4, 0],
                                               start=False, stop=True), ["ones", "esink"], ["psB1"])
                V(lambda e: e.reciprocal(qraw[:, 0:4], psB[1][:, 0:4]), ["psB1", "qraw"], ["qraw"])
                V(lambda e, qs=qs: e.tensor_tensor(qs, psB[0][:, 0:4], qraw[:, 0:4], ALU.mult), ["psB0", "qraw"], [("qTs", s_, kvh)])

    def kv_outputs(l):
        giL = gi_of(L - 128); giE = len(CG) - 1
        for kvh in range(2):
            PE(lambda e, kvh=kvh: e.transpose(psT[:, 512 + kvh * 128:512 + (kvh + 1) * 128], kT[:, kvh, L - 128:L], ident[:, :]),
               [("kT", kvh, giL), "ident"], ["psT"])
        V(lambda e: e.tensor_copy(outst[:, 0:256], psT[:, 512:768]), ["psT", "outst"], ["outst"])
        dma("sp", o_pk[l, :, :], outst[:, 0:256], ["outst"], [("out", "pk", l)])
        V(lambda e: e.tensor_copy(outst[:, 256:512], vbf[:, NB - 1, :]), [("vbf", NB - 1), "outst2"], ["outst2"])
        dma("sp", o_pv[l, :, :], outst[:, 256:512], ["outst2"], [("out", "pv", l)])
        for kvh in range(2):
            PE(lambda e, kvh=kvh: e.transpose(psT[0:32, 512 + kvh * 128:512 + (kvh + 1) * 128], kT[:, kvh, E0:E0 + 32], ident[:, :]),
               [("kT", kvh, giE), "ident"], ["psT"])
        V(lambda e: e.tensor_copy(outst[0:32, 0:256], psT[0:32, 512:768]), ["psT", "outst"], ["outst"])
        V(lambda e: e.tensor_copy(outst[0:32, 256:512], vbf[0:32, NB, :]), [("vbf", NB), "outst2"], ["outst2"])
        for s_ in range(4 if QS["q"] == 0 else 0):
            dma("sp", o_sk[l, s_, 127:128, :], outst[16 + s_:17 + s_, 0:256], ["outst"], [("out", "sk2", l, s_)])
            dma("sp", o_sv[l, s_, 127:128, :], outst[16 + s_:17 + s_, 256:512], ["outst2"], [("out", "sv2", l, s_)])

    def pool_mixer(l):
        SKA = ["z_re", "z_im"]; SKB = ["k_re", "k_im"]
        W_ = 16 + L
        pa = scr[:, 0:W_]; pb = scr[:, 2048:2048 + W_]
        giE = len(CG) - 1
        allx = [("xpT", i) for i in range(len(CG))] + [("xpT", "halo")]
        for g in range(4):
            PE(lambda e, g=g: e.transpose(psT[0:32, g * 128:(g + 1) * 128], xpT[:, g, 16 + L - 32:16 + L], ident[:, :]),
               allx + ["ident"], ["psT"])
        V(lambda e: e.tensor_copy(outst[0:32, :], psT[0:32, 0:512]), ["psT", "outst", "outst2"], ["outst", "outst2"])
        dma("sp", o_ppool[l, :, :], outst[17:32, :], ["outst", "outst2"], [("out", "ppool", l)])
        for g in range(4):
            PE(lambda e, g=g: e.transpose(psT[0:32, 512 + g * 128:512 + (g + 1) * 128], xpT[:, g, 16 + E0:16 + E0 + 32], ident[:, :]),
               allx + ["ident"], ["psT"])
        V(lambda e: e.tensor_copy(outst[32:64, :], psT[0:32, 512:1024]) if False else e.tensor_copy(outst[0:32, :], psT[0:32, 512:1024]),
          ["psT", "outst", "outst2"], ["outst", "outst2"])
        for s_ in range(4 if QS["q"] == 0 else 0):
            dma("sp", o_spool[l, s_, 14:15, :], outst[16 + s_:17 + s_, :], ["outst", "outst2"], [("out", "sp2", l, s_)])
        for g in range(4):
            w = 2 ** (g + 1)
            src = xpT[:, g, 0:W_]
            cur = None
            sh = 1
            bufs = [pa, pb]
            bkeys = [SKA, SKB]
            bi = 0
            for k_ in range(g + 1):
                out = bufs[bi]; ok = bkeys[bi]
                if cur is None:
                    tt("pool", out[:, sh:W_], src[:, sh:W_], src[:, 0:W_ - sh], ALU.add, allx + ok, ok)
                else:
                    tt("pool", out[:, sh:W_], cur[:, sh:W_], cur[:, 0:W_ - sh], ALU.add, bkeys[1 - bi] + ok, ok)
                cur = out; ck = ok
                sh *= 2
                bi = 1 - bi
            for gi, (c0, c1) in enumerate(CG):
                c1p = min(c1, L)
                n = c1p - c0
                V(lambda e, cur=cur, c0=c0, n=n, g=g, w=w: e.scalar_tensor_tensor(
                    sqb[:, 0:n], cur[:, 16 + c0:16 + c0 + n], 1.0 / w, xpT[:, g, 16 + c0:16 + c0 + n], ALU.mult, ALU.subtract),
                  ck + allx + ["sqb"], ["sqb"])
                if gi == giE:
                    pm = pmeta[:, g, :]; pm2 = pmeta2[:, g, :]
                    V(lambda e, pm=pm: e.memset(pm, 0.0), [], ["pmeta"])
                    V(lambda e, pm=pm, g=g: e.tensor_copy(pm[:, 16:32], xpT[:, g, 16 + E0:16 + E0 + 16]), allx + ["pmeta"], ["pmeta"])
                    a_, b_ = pm, pm2
                    sh2 = 1
                    for k_ in range(g + 1):
                        V(lambda e, a_=a_, b_=b_: e.tensor_copy(b_[:, 0:16], a_[:, 0:16]), ["pmeta"], ["pmeta"])
                        V(lambda e, a_=a_, b_=b_, sh2=sh2: e.tensor_tensor(b_[:, 16:32], a_[:, 16:32], a_[:, 16 - sh2:32 - sh2], ALU.add),
                          ["pmeta"], ["pmeta"])
                        a_, b_ = b_, a_
                        sh2 *= 2
                    V(lambda e, a_=a_, g=g: e.tensor_tensor(a_[:, 16:32], a_[:, 16:32], pinvE[:, g, :], ALU.mult), ["pmeta", "pinvE"], ["pmeta"])
                    V(lambda e, a_=a_, g=g, n=n: e.tensor_tensor(sqb[:, n:n + 16], a_[:, 16:32], xpT[:, g, 16 + E0:16 + E0 + 16], ALU.subtract),
                      ["pmeta", "sqb"] + allx, ["sqb"])
                    for s_ in range(4):
                        dma("pool", stp[0:15, :], st_pool[l, s_, :, :], [], ["stp"])
                        PE(lambda e, g=g, s_=s_: e.matmul(psB[2][:, s_:s_ + 1], stp[0:16, g * 128:(g + 1) * 128], psel[0:16, g:g + 1],
                                                          start=True, stop=True), ["stp", "psel"], ["psB2"])
                    xs4 = xpT[:, g, 16 + E0 + 16:16 + E0 + 20]
                    V(lambda e, xs4=xs4: e.tensor_tensor(small[:, 4:8], psB[2][:, 0:4], xs4, ALU.add), ["psB2"] + allx, ["small4"])
                    V(lambda e, xs4=xs4, n=n, w=w: e.scalar_tensor_tensor(sqb[:, n + 16:n + 20], small[:, 4:8], 1.0 / w, xs4,
                                                                         ALU.mult, ALU.subtract), ["small4", "sqb"] + allx, ["sqb"])
                    V(lambda e, n=n: e.memset(sqb[:, n + 20:n + 32], 0.0), ["sqb"], ["sqb"])
                    n = n + 32
                PE(lambda e, g=g, n=n: e.matmul(psB[0][:, 0:n], wpool[:, g, :], sqb[:, 0:n], start=True, stop=True),
                   ["wpool", "sqb"], ["psB0"])
                ACT(lambda e, g=g, n=n, c0=c0: e.activation(actT[:, 12 + g, c0:c0 + n], psB[0][:, 0:n], AF.Copy, scale=vecs[:, 26 + g:27 + g]),
                    ["psB0", "vecs"], [("plT", g, gi)])

    GC = float(2.0 * np.sqrt(2.0 / np.pi))

    def ssm_post():
        for gi, (c0, c1) in enumerate(CG):
            n = c1 - c0
            ysk = [("ysm", c) for c in range(NCH) if c0 <= c * CL < c1] + ([("ysm", "E")] if c1 > L else [])
            for ct in range(4):
                yv = actT[:, 8 + ct, c0:c1]
                uv = uT[:, ct, c0:c1]
                V(lambda e, yv=yv, uv=uv, ct=ct, n=n: e.scalar_tensor_tensor(qraw[:, 0:n], uv, vecs[:, 18 + ct:19 + ct], yv, ALU.mult, ALU.add),
                  ysk + [("uT", gi), "vecs", "qraw"], ["qraw"])
                tt("pool", rtmp2[:, 0:n], qraw[:, 0:n], qraw[:, 0:n], ALU.mult, ["qraw", "rtmp2"], ["rtmp2"])
                G(lambda e, n=n: e.tensor_scalar(rtmp2[:, 0:n], rtmp2[:, 0:n], 0.044715, 1.0, ALU.mult, ALU.add), ["rtmp2"], ["rtmp2"])
                tt("pool", rtmp2[:, 0:n], rtmp2[:, 0:n], qraw[:, 0:n], ALU.mult, ["qraw", "rtmp2"], ["rtmp2"])
                ACT(lambda e, n=n: e.activation(rtmp2[:, 0:n], rtmp2[:, 0:n], AF.Exp, scale=-GC), ["rtmp2"], ["rtmp2"])
                G(lambda e, n=n: e.tensor_scalar(rtmp2[:, 0:n], rtmp2[:, 0:n], 1.0, 1.0, ALU.add, ALU.mult), ["rtmp2"], ["rtmp2"])
                V(lambda e, n=n: e.reciprocal(rtmp2[:, 0:n], rtmp2[:, 0:n]), ["rtmp2"], ["rtmp2"])
                V(lambda e, uv=uv, n=n: e.tensor_tensor(uv, qraw[:, 0:n], rtmp2[:, 0:n], ALU.mult), ["qraw", "rtmp2", ("uT", gi)], [("uT", gi)])
            for co in range(4):
                for ci in range(4):
                    PE(lambda e, co=co, ci=ci, c0=c0, c1=c1, n=n: e.matmul(psA[co][:, 0:n], wglu[:, ci, co * 128:(co + 1) * 128],
                                                                          uT[:, ci, c0:c1], start=(ci == 0), stop=(ci == 3)),
                       ["wglu", ("uT", gi)], [f"psA{co}"])
                ACT(lambda e, co=co, n=n: e.activation(rtmp[:, 0:n], psA[co][:, 0:n], AF.Exp, bias=vecs[:, 30 + co:31 + co], scale=-1.0),
                    [f"psA{co}", "vecs", "rtmp"], ["rtmp"])
                G(lambda e, n=n: e.tensor_scalar(rtmp[:, 0:n], rtmp[:, 0:n], 1.0, 1.0, ALU.add, ALU.mult), ["rtmp"], ["rtmp"])
                V(lambda e, n=n: e.reciprocal(rtmp[:, 0:n], rtmp[:, 0:n]), ["rtmp"], ["rtmp"])
                V(lambda e, co=co, c0=c0, c1=c1, n=n: e.tensor_tensor(actT[:, 8 + co, c0:c1], uT[:, co, c0:c1], rtmp[:, 0:n], ALU.mult),
                  ["rtmp", ("uT", gi)] + ysk, [("smT", co, gi)])

    def out_norms():
        for gi, (c0, c1) in enumerate(CG):
            n = c1 - c0
            groups = [
                ([(qT[:, h, c0:c1], [("qT", h, gi), ("qTm", h // 4)] + [("qTs", s_, h // 4) for s_ in range(4)]) for h in range(8)], 1, 2, 0),
                ([(actT[:, 8 + t, c0:c1], [("smT", t, gi)]) for t in range(4)], 2, 10, 8),
                ([(actT[:, 12 + t, c0:c1], [("plT", t, gi)]) for t in range(4)], 2, 14, 12),
            ]
            for tiles, oi, gcol, slot0 in groups:
                for i, (src, rk) in enumerate(tiles):
                    tt("pool", sqb[:, 0:n], src, src, ALU.mult, rk + ["sqb"], ["sqb"])
                    PE(lambda e, i=i, oi=oi, n=n, last=(i == len(tiles) - 1): e.matmul(psB[0][:, 0:n], ones_b[:, oi, :], sqb[:, 0:n],
                                                                                     start=(i == 0), stop=last), ["sqb", "ones"], ["psB0"])
                ACT(lambda e, n=n: e.activation(rtmp[:, 0:n], psB[0][:, 0:n], AF.Ln, bias=EPS, scale=1.0), ["psB0", "rtmp"], ["rtmp"])
                ACT(lambda e, n=n: e.activation(rtmp[:, 0:n], rtmp[:, 0:n], AF.Exp, scale=-0.5), ["rtmp"], ["rtmp"])
                for i, (src, rk) in enumerate(tiles):
                    dst = actT[:, slot0 + i, c0:c1]
                    V(lambda e, src=src, dst=dst, i=i, gcol=gcol, n=n: e.scalar_tensor_tensor(dst, src, vecs[:, gcol + i:gcol + i + 1],
                                                                                             rtmp[:, 0:n], ALU.mult, ALU.mult),
                      rk + ["rtmp", "vecs"], [("actT", b) for b in blk_of_cols(c0, c1)])

    def resid_pass(wt, l, rows0, nk, lhs_of, lhs_reads, xsrc, xdst, outkey):
        for ni in range(4):
            wkey = load_w(w_tile(wt, l, rows0, ni * 512), ni % 2)
            wb = wbuf[ni % 2]
            for blk in range(NBLK if QS["q"] == 0 else NB):
                P = 128 if blk < NB else 32
                r0 = blk * 128
                j = blk % 3
                ps = psB[j]; pk = f"psB{j}"
                for kc in range(nk):
                    PE(lambda e, ps=ps, kc=kc, wb=wb, r0=r0, P=P: e.matmul(ps[0:P, :], lhs_of(kc, r0, P), wb[:, kc, :],
                                                                          start=(kc == 0), stop=(kc == nk - 1)),
                       [wkey] + lhs_reads(blk), [pk])
                xj = blk % 4
                dr = drow(blk)
                dma("sp", xio[xj][0:P, :], xsrc[dr:dr + P, ni * 512:(ni + 1) * 512], [xdk(blk)], [("xio", xj)])
                V(lambda e, ps=ps, xj=xj, P=P: e.tensor_tensor(xio[xj][0:P, :], ps[0:P, :], xio[xj][0:P, :], ALU.add),
                  [pk, ("xio", xj)], [("xio", xj)])
                wr = [xdk(blk)] + ([("out", outkey, QS["q"], blk, ni)] if outkey else [])
                if blk < NB or QS["q"] == 0:
                    dma("sp", xdst[dr:dr + P, ni * 512:(ni + 1) * 512], xio[xj][0:P, :], [("xio", xj)], wr)

    def ffn_phase(l, last):
        for hq in range(4):
            for t4 in range(4):
                wkey = load_w(w_tile(w_ff1, l, 0, hq * 2048 + t4 * 512), t4 % 2)
                wb = wbuf[t4 % 2]
                for gi, (c0, c1) in enumerate(CG):
                    n = c1 - c0
                    for mt in range(4):
                        ps = psA[mt]; pk = f"psA{mt}"
                        for kc in range(16):
                            PE(lambda e, ps=ps, kc=kc, mt=mt, wb=wb, c0=c0, c1=c1, n=n:
                               e.matmul(ps[:, 0:n], wb[:, kc, mt * 128:(mt + 1) * 128], actT[:, kc, c0:c1],
                                        start=(kc == 0), stop=(kc == 15)), [wkey] + actT_reads(c0, c1), [pk])
                        rt = relu_t[mt % 2]; rk = f"relu{mt % 2}"
                        ACT(lambda e, ps=ps, rt=rt, n=n: e.activation(rt[:, 0:n], ps[:, 0:n], AF.Relu), [pk], [rk])
                        m = t4 * 4 + mt
                        tt("pool", hidT[:, m, c0:c1], rt[:, 0:n], rt[:, 0:n], ALU.mult, [rk, "hidfence"], [("hid", m, gi)])
            resid_pass(w_ff2, l, hq * 2048, 16, lambda kc, r0, P: hidT[:, kc, r0:r0 + P],
                       lambda blk: [("hid", m, gi) for m in range(16) for gi in range(len(CG))
                                    if blk in blk_of_cols(*CG[gi])],
                       xs, y_out if (last and hq == 3) else xs, "y" if (last and hq == 3) else None)

    PROJ_PRED = lambda k: isinstance(k, tuple) and k[0] in ("qT", "kT", "vbf", "uT", "qTm", "qTs", "hid")
    SKK = ["z_re", "z_im", "k_re", "k_im"]
    STG = ["w_in", "chunks", "ssmE", "kvout", "state", "attnE", "samples", "corr", "attn0", "pool", "post", "mix", "w_out", "ffn"]

    def reached(name):
        return stop in STG and STG.index(stop) < STG.index(name)

    def layer_pass(q, l, first, last):
        QS["q"] = q; QS["l"] = l
        xsrc = x_in if l == 0 else xs
        S.fence(lambda k: isinstance(k, tuple) and k[0] == "ebuf", ["xn"])
        norm_to_actT(xsrc, g_mix, l)
        S.fence(lambda k: k == "xn", [("ebuf", 0), ("ebuf", 1)])
        S.fence(PROJ_PRED, ["projfence"])
        w_in_phase(l)
        if reached("chunks"): return
        S.fence(lambda k: k in ("W1",), SKK)
        if q == 0:
            ssm_E(l)
        state_init(l, q)
        nblk_done = 1
        for c in range(NCH):
            ssm_chunk(c)
            if c % 2 == 1 and nblk_done < NB:
                attn_block(nblk_done, nblk_done)
                nblk_done += 1
        while nblk_done < NB:
            attn_block(nblk_done, nblk_done)
            nblk_done += 1
        if reached("kvout"): return
        kv_outputs(l)
        if reached("state"): return
        local_state(l, q)
        if reached("attnE"): return
        if q == 0:
            attn_E()
        if reached("samples"): return
        if q == 0:
            attn_samples(l)
        tap("ysm", actT[:, 8:12, :], BF16, [k for k in S.last_w])
        if reached("attn0"): return
        attn_block(0, 1)
        tap("aT", projbuf[:, 0:8 * T], BF16, [k for k in S.last_w])
        if reached("pool"): return
        pool_mixer(l)
        save_halo(l)
        tap("ysm2", actT[:, 8:12, :], BF16, [k for k in S.last_w])
        if reached("post"): return
        ssm_post()
        tap("premix", actT[:, 8:16, :], BF16, [k for k in S.last_w])
        if reached("mix"): return
        out_norms()
        tap("mixT", actT[:, :, :], BF16, [k for k in S.last_w])
        if reached("w_out"): return
        S.fence(lambda k: k in SKK, ["W1"])
        S.fence(lambda k: isinstance(k, tuple) and k[0] == "ebuf", ["xn"])
        direct = (not ffw) and last
        resid_pass(w_out, l, 0, 16, lambda kc, r0, P: actT[:, kc, r0:r0 + P], lambda blk: [("actT", blk)],
                   xsrc, y_out if direct else xs, "y" if direct else None)
        if reached("ffn") or not ffw: return
        norm_to_actT(xs, g_ffn, l)
        S.fence(PROJ_PRED, ["hidfence"])
        ffn_phase(l, last)

    PROJ_PRED = lambda k: isinstance(k, tuple) and k[0] in ("qT", "kT", "vbf", "uT", "qTm", "qTs", "hid")
    for l in range(NL):
        load_vecs(l)
        S.fence(lambda k: k == "W1", SKK)
        ssm_prep(l)
        S.fence(lambda k: k in SKK, ["W1"])
        for q in range(NQ):
            dma("sp", rope[:, 0, :], c_rope[q, 0, :, :], [], ["rope"])
            dma("sp", rope[:, 1, :], c_rope[q, 1, :, :], [], ["rope"])
            layer_pass(q, l, q == 0, l == NL - 1)

    S.op("sp", lambda e: e.nop(), reads=[k for k in S.last_w.keys() if isinstance(k, tuple) and k[0] == "out"], writes=[])
    S.emit()
    return nc


def _consts(NB, NQ=4):
    T = NB * 128 + 32
    E0 = NB * 128
    L = NB * 128
    inv = (np.float32(500000.0) ** (-np.arange(0, 32, 2, dtype=np.float32) / np.float32(32))).astype(np.float32)
    ropes = []
    for q in range(NQ):
        pos = np.zeros(T, np.float32)
        pos[:L] = 16 + q * L + np.arange(L)
        pos[E0:E0 + 16] = np.arange(16)
        pos[E0 + 16:E0 + 20] = 16384
        ang = (pos[:, None] * inv[None, :]).astype(np.float32)
        cos = np.cos(ang).astype(np.float32).T
        sin = np.sin(ang).astype(np.float32).T
        ropes.append(np.stack([np.concatenate([cos, cos], 0), np.concatenate([sin, sin], 0)], 0))
    c_rope = np.stack(ropes, 0).astype(np.float32)
    j = np.arange(128)[:, None]
    i = np.arange(128)[None, :]
    md = np.where(j <= i, 0.0, NEG)
    mp = np.where(j >= i, 0.0, NEG)
    mp0 = np.where((j < 16) & (j >= i - 112), 0.0, NEG)
    c_mask = np.stack([np.tile(m, (1, 4)) for m in (md, mp, mp0)], 0).astype(np.float32)
    jE = np.arange(32)[:, None]
    iE = np.arange(32)[None, :]
    mE = np.where((jE < 16) & (iE < 16) & (jE <= iE), 0.0, NEG)
    c_maskE = np.tile(mE, (1, 4)).astype(np.float32)
    c_maskS = np.full((4, 32, 4), NEG, np.float32)
    for s in range(4):
        c_maskS[s, 16 + s, :] = 0.0
    prot = np.zeros((128, 32), np.float32)
    for m in range(16):
        prot[m + 16, m] = -1.0
        prot[m, m + 16] = 1.0
    sel = np.zeros((128, 24), np.float32)
    jt = np.tile(np.arange(CL + 1, dtype=np.float32), 16)
    c_jtab = np.tile(jt[None, :], (128, 1)).astype(np.float32)
    pinv = np.zeros((4, 16), np.float32)
    for g, w in enumerate((2, 4, 8, 16)):
        pinv[g, :] = 1.0 / np.minimum(w, np.arange(16) + 1)
    c_pinv = np.tile(pinv.reshape(1, -1), (128, 1)).astype(np.float32)
    psel = np.zeros((16, 4), np.float32)
    for g, w in enumerate((2, 4, 8, 16)):
        psel[15 - (w - 1):15, g] = 1.0
    return dict(c_rope=c_rope, c_mask=c_mask, c_maskE=c_maskE, c_maskS=c_maskS, c_prot=prot,
                c_ident=np.eye(128, dtype=np.float32), c_sel=sel, c_jtab=c_jtab, c_pinv=c_pinv, c_psel=psel)


def prep_inputs(inp, NB, NL=2, ffw=True, NQ=4):
    L = NB * 128
    T = L + 32
    TT = NQ * L + 32
    cst = _consts(NB, NQ)
    f = lambda a: np.ascontiguousarray(np.asarray(a, dtype=np.float32))
    shared = dict(
        w_in=f(inp["w_in"][:NL]), w_out=f(inp["w_out"][:NL]),
        g_mix=f(inp["g_mix"]), g_ffn=f(inp["g_ffn"]), g_q=f(inp["g_q"]), g_k=f(inp["g_k"]), sinks=f(inp["sinks"]),
        A_re=f(inp["A_re"]).reshape(2, 2048), A_im=f(inp["A_im"]).reshape(2, 2048), log_dt=f(inp["log_dt"]),
        B_re=f(inp["B_re"]).reshape(2, 2048, 16), B_im=f(inp["B_im"]).reshape(2, 2048, 16),
        C_re=f(inp["C_re"]), C_im=f(inp["C_im"]), D_skip=f(inp["D_skip"]), w_glu=f(inp["w_glu"]), b_glu=f(inp["b_glu"]),
        w_pool=f(inp["w_pool"]), pool_scale=f(inp["pool_scale"]), g_out_attn=f(inp["g_out_attn"]),
        g_out_ssm=f(inp["g_out_ssm"]), g_out_pool=f(inp["g_out_pool"]),
    )
    if ffw:
        shared.update(w_ff1=f(inp["w_ff1"][:NL]), w_ff2=f(inp["w_ff2"][:NL]))
    xp = f(inp["x_prompt"]); xsm = f(inp["x_sample"]); meta = f(inp["meta_tokens"])
    ck = f(inp["cache_k"]).reshape(2, 32, 128, 256); cv = f(inp["cache_v"]).reshape(2, 32, 128, 256)
    sre = f(inp["state_ssm_re"]).reshape(2, 32, 16, 128); sim = f(inp["state_ssm_im"]).reshape(2, 32, 16, 128)
    spool = f(inp["state_pool"])
    maps = []
    for r in range(NCORES):
        b = r % 2
        x_in = np.zeros((TT, D), np.float32)
        x_in[:NQ * L] = xp[b, :NQ * L]
        x_in[NQ * L:NQ * L + 16] = meta
        x_in[NQ * L + 16:NQ * L + 20] = xsm[4 * r:4 * r + 4, 0]
        m = dict(shared)
        m.update(cst)
        m.update(x_in=x_in, cache_k=np.ascontiguousarray(ck[:, 4 * r:4 * r + 4]),
                 cache_v=np.ascontiguousarray(cv[:, 4 * r:4 * r + 4]),
                 st_re=np.ascontiguousarray(sre[:, 4 * r:4 * r + 4]).reshape(2, 64, 128),
                 st_im=np.ascontiguousarray(sim[:, 4 * r:4 * r + 4]).reshape(2, 64, 128),
                 st_pool=np.ascontiguousarray(spool[:, 4 * r:4 * r + 4]))
        maps.append(m)
    return maps


_NC_CACHE = {}


def _run(inp, n_cores=NCORES):
    SEQ = np.asarray(inp["x_prompt"]).shape[1]
    NQ = 4
    NB = SEQ // (NQ * 128)
    L = NB * 128
    key = (NB,)
    if key not in _NC_CACHE:
        _NC_CACHE[key] = build(NB, NL=2, ffw=True, NQ=NQ)
    nc = _NC_CACHE[key]
    maps = prep_inputs(inp, NB, NL=2, ffw=True, NQ=NQ)[:n_cores]
    res = run_bass_kernel_spmd(nc, maps, core_ids=list(range(n_cores)))
    R = res.results
    f32 = lambda a: np.asarray(a, dtype=np.float32)
    nb = min(2, n_cores)
    y_prompt = np.stack([f32(R[b]["y_out"])[:NQ * L] for b in range(nb)], 0)
    y_sample = np.concatenate([f32(R[r]["y_out"])[NQ * L + 16:NQ * L + 20] for r in range(n_cores)], 0)[:, None, :]
    pk = np.stack([f32(R[b]["o_pk"]).reshape(2, 128, 2, 128) for b in range(nb)], 1)
    pv = np.stack([f32(R[b]["o_pv"]).reshape(2, 128, 2, 128) for b in range(nb)], 1)
    pre = np.stack([f32(R[b]["o_pssm"])[:, 0].reshape(2, 32, 64) for b in range(nb)], 1)
    pim = np.stack([f32(R[b]["o_pssm"])[:, 1].reshape(2, 32, 64) for b in range(nb)], 1)
    ppool = np.stack([f32(R[b]["o_ppool"]) for b in range(nb)], 1)
    sk = np.concatenate([f32(R[r]["o_sk"]).reshape(2, 4, 128, 2, 128) for r in range(n_cores)], 1)
    sv = np.concatenate([f32(R[r]["o_sv"]).reshape(2, 4, 128, 2, 128) for r in range(n_cores)], 1)
    sre = np.concatenate([f32(R[r]["o_sssm"])[:, 0].reshape(2, 4, 32, 64) for r in range(n_cores)], 1)
    sim = np.concatenate([f32(R[r]["o_sssm"])[:, 1].reshape(2, 4, 32, 64) for r in range(n_cores)], 1)
    spool = np.concatenate([f32(R[r]["o_spool"]) for r in range(n_cores)], 1)
    return (y_prompt, y_sample, pk, pv, pre, pim, ppool, sk, sv, sre, sim, spool)


def kernel(**inputs):
    return _run(inputs, NCORES)
```

```python
import numpy as np
import concourse.bass as bass
import concourse.mybir as mybir
from concourse.bass_utils import run_bass_kernel_spmd

F32 = mybir.dt.float32
BF16 = mybir.dt.bfloat16
I32 = mybir.dt.int32
ALU = mybir.AluOpType
AF = mybir.ActivationFunctionType

D = 2048
NQ = 1024
NKV = 256
SSMW = 512
POOLW = 512
INW = 2560
DFF = 8192
NCORES = 8
CL = 64
EPS = 1e-6
NEG = -30000.0
PI = float(np.pi)


class Sched:
    ENGS = ["pe", "act", "dve", "pool", "sp"]

    def __init__(self, nc, n_sp_slots=60, n_pool_slots=24):
        self.nc = nc
        self.ops = []
        self.by_eng = {e: [] for e in self.ENGS}
        self.last_w = {}
        self.readers = {}
        self.nslots = {"sp": n_sp_slots, "pool": n_pool_slots}
        self.ndma = {"sp": 0, "pool": 0}

    def op(self, eng, fn, reads=(), writes=(), dma=False, cc=False):
        oid = len(self.ops)
        deps = set()
        for r in reads:
            w = self.last_w.get(r)
            if w is not None:
                deps.add(w)
        for r in writes:
            w = self.last_w.get(r)
            if w is not None:
                deps.add(w)
            for rid in self.readers.get(r, {}).values():
                deps.add(rid)
        o = dict(id=oid, eng=eng, fn=fn, deps=deps, dma=dma, marked=False)
        if cc:
            o["dma"] = True
            dma = True
            self.ncc = getattr(self, "ncc", 0) + 1
            o["q"] = "cc"
            o["slot"] = 0
            o["val"] = self.ncc
        elif dma:
            k = self.ndma[eng]
            self.ndma[eng] += 1
            o["q"] = eng
            o["slot"] = k % self.nslots[eng]
            o["val"] = 16 * (k // self.nslots[eng] + 1)
        self.ops.append(o)
        self.by_eng[eng].append(o)
        for r in reads:
            self.readers.setdefault(r, {})[("d", oid) if dma else eng] = oid
        for r in writes:
            self.last_w[r] = oid
            self.readers[r] = {}
        return oid

    def fence(self, pred, newkeys, eng="sp", extra_reads=()):
        keys = [k for k in set(list(self.last_w.keys()) + list(self.readers.keys())) if pred(k)]
        if getattr(self, "nofence", False):
            return None
        return self.op(eng, lambda e: e.nop(), reads=list(extra_reads), writes=keys + list(newkeys))

    def emit(self):
        nc = self.nc
        ops = self.ops
        for o in ops:
            for p in o["deps"]:
                po = ops[p]
                if po["dma"]:
                    continue
                if po["eng"] == "pe" and o["eng"] == "pe":
                    continue
                po["marked"] = True
        cum = {}
        cnt = {e: 0 for e in self.ENGS}
        for e in self.ENGS:
            for o in self.by_eng[e]:
                if o["marked"] and not o["dma"]:
                    cnt[e] += 1
                o["cum"] = cnt[e]
        esem = {e: nc.alloc_semaphore("s_" + e) for e in self.ENGS}
        dsem = {q: [nc.alloc_semaphore(f"d_{q}_{i}") for i in range(self.nslots[q])] for q in ("sp", "pool")}
        dsem["cc"] = [nc.alloc_semaphore("d_cc")]
        handles = {"pe": nc.tensor, "act": nc.scalar, "dve": nc.vector, "pool": nc.gpsimd, "sp": nc.sync}

        def run(eng, e):
            waited = {}
            for o in self.by_eng[eng]:
                need = {}
                for p in o["deps"]:
                    po = ops[p]
                    if po["dma"]:
                        key = ("d", po["q"], po["slot"])
                        v = po["val"]
                    else:
                        if po["eng"] == "pe" and eng == "pe":
                            continue
                        key = ("e", po["eng"])
                        v = po["cum"]
                    if v > need.get(key, 0):
                        need[key] = v
                if o["dma"] and o["q"] != "cc":
                    if o["val"] > 16:
                        key = ("d", o["q"], o["slot"])
                        need[key] = max(need.get(key, 0), o["val"] - 16)
                for key, v in need.items():
                    if waited.get(key, 0) >= v:
                        continue
                    waited[key] = v
                    sem = esem[key[1]] if key[0] == "e" else dsem[key[1]][key[2]]
                    e.wait_ge(sem, v)
                ins = o["fn"](e)
                if o["dma"]:
                    ins.then_inc(dsem[o["q"]][o["slot"]], 1 if o["q"] == "cc" else 16)
                elif o["marked"]:
                    ins.then_inc(esem[eng], 1)

        with nc.Block() as block:
            @block.tensor
            def _(e):
                run("pe", e)

            @block.scalar
            def _(e):
                run("act", e)

            @block.vector
            def _(e):
                run("dve", e)

            @block.gpsimd
            def _(e):
                run("pool", e)

            @block.sync
            def _(e):
                run("sp", e)


def col_groups(T):
    n = (T + 511) // 512
    nb = (T - 32) // 128
    per = [nb // n + (1 if i < nb % n else 0) for i in range(n)]
    gs = []
    c = 0
    for i, p in enumerate(per):
        w = p * 128 + (32 if i == n - 1 else 0)
        gs.append((c, c + w))
        c += w
    assert c == T
    return gs


def build(NB, NL=2, dbg=(), stop=None, ffw=True, NQ=4):
    T = NB * 128 + 32
    L = NB * 128
    E0 = NB * 128
    NBLK = NB + 1
    CG = col_groups(T)
    NCH = 2 * NB
    NSQ = int(np.log2(NCH))
    assert 2 ** NSQ == NCH
    nc = bass.Bass("TRN2", target_bir_lowering=False)
    nc.allow_low_precision("bf16 matmul operands by design (reference tolerance measured for bf16)")
    S = Sched(nc)
    S.nofence = 'nofence' in dbg
    A = nc.alloc_sbuf_tensor

    def din(name, shape, dt=F32):
        return nc.dram_tensor(name, list(shape), dt, kind="ExternalInput")

    def dout(name, shape, dt=F32):
        return nc.dram_tensor(name, list(shape), dt, kind="ExternalOutput")

    TT = NQ * L + 32
    x_in = din("x_in", [TT, D])
    w_in = din("w_in", [NL, D, INW]); w_out = din("w_out", [NL, D, D])
    if ffw:
        w_ff1 = din("w_ff1", [NL, D, DFF]); w_ff2 = din("w_ff2", [NL, DFF, D])
    g_mix = din("g_mix", [2, D]); g_ffn = din("g_ffn", [2, D])
    g_q = din("g_q", [2, 128]); g_k = din("g_k", [2, 128]); sinks = din("sinks", [2, 8])
    A_re = din("A_re", [2, 2048]); A_im = din("A_im", [2, 2048]); log_dt = din("log_dt", [2, 32])
    B_re = din("B_re", [2, 2048, 16]); B_im = din("B_im", [2, 2048, 16])
    C_re = din("C_re", [2, 32, 16, 64]); C_im = din("C_im", [2, 32, 16, 64])
    D_skip = din("D_skip", [2, 512]); w_glu = din("w_glu", [2, 512, 512]); b_glu = din("b_glu", [2, 512])
    w_pool = din("w_pool", [2, 4, 128, 128]); pool_scale = din("pool_scale", [2, 512])
    g_oa = din("g_out_attn", [2, 1024]); g_os = din("g_out_ssm", [2, 512]); g_op = din("g_out_pool", [2, 512])
    cache_k = din("cache_k", [2, 4, 128, 256]); cache_v = din("cache_v", [2, 4, 128, 256])
    st_re = din("st_re", [2, 64, 128]); st_im = din("st_im", [2, 64, 128])
    st_pool = din("st_pool", [2, 4, 15, 512])
    c_rope = din("c_rope", [NQ, 2, 32, T])
    c_mask = din("c_mask", [3, 128, 512])
    c_maskE = din("c_maskE", [32, 128])
    c_maskS = din("c_maskS", [4, 32, 4])
    c_prot = din("c_prot", [128, 32]); c_ident = din("c_ident", [128, 128])
    c_sel = din("c_sel", [128, 24])
    c_jtab = din("c_jtab", [128, 16 * (CL + 1)])
    c_pinv = din("c_pinv", [128, 4 * 16])
    c_psel = din("c_psel", [16, 4])

    xs = nc.dram_tensor("xs_scratch", [TT, D], F32)
    y_out = dout("y_out", [TT, D])
    o_pk = dout("o_pk", [2, 128, 256]); o_pv = dout("o_pv", [2, 128, 256])
    o_pssm = dout("o_pssm", [2, 2, 16, 128]); o_ppool = dout("o_ppool", [2, 15, 512])
    o_sk = dout("o_sk", [2, 4, 128, 256]); o_sv = dout("o_sv", [2, 4, 128, 256])
    o_sssm = dout("o_sssm", [2, 2, 64, 128]); o_spool = dout("o_spool", [2, 4, 15, 512])
    XW = 640
    xch_in = nc.dram_tensor("xch_in", [128, XW], F32)
    xch_out = nc.dram_tensor("xch_out", [NCORES * 128, XW], F32)

    actT = A("actT", [128, 16, T], BF16)
    PROJW = 8 * T + 2 * T + NBLK * 256 + 4 * T
    HIDW = 16 * T
    projbuf = A("projbuf", [128, max(PROJW, HIDW)], BF16)
    o_ = 0
    qT = projbuf[:, o_:o_ + 8 * T].rearrange("p (h t) -> p h t", h=8); o_ += 8 * T
    kT = projbuf[:, o_:o_ + 2 * T].rearrange("p (h t) -> p h t", h=2); o_ += 2 * T
    vbf = projbuf[:, o_:o_ + NBLK * 256].rearrange("p (b c) -> p b c", b=NBLK); o_ += NBLK * 256
    uT = projbuf[:, o_:o_ + 4 * T].rearrange("p (h t) -> p h t", h=4); o_ += 4 * T
    hidT = projbuf[:, 0:HIDW].rearrange("p (m t) -> p m t", m=16)
    xpT = A("xpT", [128, 4, 16 + T], BF16)
    wbuf = [A(f"wbuf{i}", [128, 16, 512], BF16) for i in range(2)]
    xblk = A("xblk", [128, D], F32)
    xn = A("xn", [128, D], BF16)
    ident = A("ident", [128, 128], BF16); identf = A("identf", [128, 128], F32)
    ones_b = A("ones_b", [128, 4, 128], BF16)
    prot = A("prot", [128, 32], BF16)
    rope = A("rope", [32, 2, T], F32)
    masks = A("masks", [128, 3, 512], BF16)
    maskE = A("maskE", [32, 128], BF16); maskS = A("maskS", [32, 4, 4], BF16)
    sel = A("sel", [128, 24], F32)
    vecs = A("vecs", [128, 48], F32)
    gvec = A("gvec", [128, 16], F32)
    qraw = A("qraw", [128, 512], F32); sqb = A("sqb", [128, 512], BF16)
    rtmp = A("rtmp", [128, 512], F32); rtmp2 = A("rtmp2", [128, 512], F32)
    small = A("small", [128, 16], F32)
    relu_t = [A(f"relu_t{i}", [128, 512], BF16) for i in range(2)]
    cosT = A("cosT", [128, 16, CL + 1], F32); sinT = A("sinT", [128, 16, CL + 1], F32)
    Dk = A("Dk", [128, 16, CL], F32); Dk16 = A("Dk16", [128, 16, 16], F32)
    sm = A("sm", [128, 28, 16], F32)
    BbT = A("BbT", [128, 32, 128], BF16)
    CTp_r = A("CTp_r", [128, 16, 128], F32); CTp_ni = A("CTp_ni", [128, 16, 128], F32)
    scur = A("scur", [128, 32], F32)
    Gst = A("Gst", [128, NCH + 1, 32], F32)
    sst = A("sst", [128, NCH + 2, 32], F32)
    esink = A("esink", [1, 8, 128], BF16); sinkrow = A("sinkrow", [1, 16], F32)
    hkL = [A(f"hk{i}", [128, 2, 128], BF16) for i in range(2)]; hvL = [A(f"hv{i}", [128, 256], BF16) for i in range(2)]
    phL = [A(f"ph{i}", [128, 4, 16], BF16) for i in range(2)]; sfin = A("sfin", [128, 2, 32], F32)
    kctok = A("kctok", [128, 256], BF16); kcT = A("kcT", [128, 128], BF16); vc = A("vc", [128, 256], BF16)
    es = A("es", [128, 8], BF16)
    h0s = A("h0s", [128, 2, 64], F32); hs = A("hs", [128, 2, 64], F32)
    sttok = A("sttok", [64, 2, 128], F32)
    stp = A("stp", [16, 512], BF16); psel = A("psel", [16, 4], BF16)
    wglu = A("wglu", [128, 4, 512], BF16); wpool = A("wpool", [128, 4, 128], BF16)
    pmeta = A("pmeta", [128, 4, 32], F32); pmeta2 = A("pmeta2", [128, 4, 32], F32); pinvE = A("pinvE", [128, 4, 16], F32)
    outst = A("outst", [128, 512], F32)

    scr = wbuf[1][:, :, :].rearrange("p a b -> p (a b)").bitcast(F32)
    z_re = scr[:, 0:1024]; z_im = scr[:, 1024:2048]; k_re = scr[:, 2048:3072]; k_im = scr[:, 3072:4096]
    hE_re = rtmp[:, :].rearrange("p (a b) -> p a b", a=16)
    hE_im = rtmp2[:, :].rearrange("p (a b) -> p a b", a=16)
    xst = xblk[:, 0:XW]; stage = xblk[:, 640:640 + 576]; acc = xblk[:, 1280:1280 + 576]
    ebuf = [xn[:, i * 1024:(i + 1) * 1024].rearrange("p (a b) -> p a b", a=2) for i in range(2)]
    XBK = [("xio", j) for j in range(4)]
    xio = [xblk[:, j * 512:(j + 1) * 512] for j in range(4)]

    if "psep" in dbg:
        psA = [nc.alloc_psum_tensor(f"psA{i}", [128, 512], F32) for i in range(4)]
        psX = None
    else:
        psX = nc.alloc_psum_tensor("psX", [128, 4, 512], F32)
        psA = [psX[:, i, :] for i in range(4)]
    psB = [nc.alloc_psum_tensor(f"psB{i}", [128, 512], F32) for i in range(3)]
    psT = nc.alloc_psum_tensor("psT", [128, 1024], BF16)
    psT_alt = psB[2][:, :].bitcast(BF16)

    S._taps = []
    QS = {"q": 0, "l": 0}

    def dma(q, out, in_, reads, writes):
        return S.op(q, lambda e: e.dma_start(out=out, in_=in_), reads=reads, writes=writes, dma=True)

    def dma_slow(q, out, in_, reads, writes):
        return S.op(q, lambda e: e.dma_start(out=out, in_=in_, allow_slow_non_contiguous=True),
                    reads=reads, writes=writes, dma=True)

    def dma_tp(q, out, src_flat, nt, reads, writes):
        v = src_flat.rearrange("(t p) -> p t", p=128)
        for t0 in range(0, nt, 4):
            t1 = min(nt, t0 + 4)
            dma_slow(q, out[:, t0:t1], v[:, t0:t1], reads, writes)

    def tap(name, ap, dt, reads):
        if name not in dbg:
            return
        t = dout("dbg_%s_%d%d" % (name, QS["q"], QS["l"]), list(ap.shape), dt)
        full = t.ap()
        S.op("sp", lambda e: e.dma_start(out=full, in_=ap), reads=reads, writes=[("out", "dbg", name, QS["q"], QS["l"])], dma=True)

    def V(fn, reads, writes):
        return S.op("dve", fn, reads, writes)

    def G(fn, reads, writes):
        return S.op("pool", fn, reads, writes)

    def ACT(fn, reads, writes):
        return S.op("act", fn, reads, writes)

    def PE(fn, reads, writes):
        return S.op("pe", fn, reads, writes)

    def tt(eng, out, a, b, op, reads, writes):
        return S.op(eng, lambda e: e.tensor_tensor(out, a, b, op), reads, writes)

    def mm(out, lhsT, rhs, start, stop, reads, writes):
        return S.op("pe", lambda e: e.matmul(out, lhsT, rhs, start=start, stop=stop), reads, writes)

    def tr(out, in_, idn, reads, writes):
        return S.op("pe", lambda e: e.transpose(out, in_, idn), reads, writes)

    def act(out, in_, func, reads, writes, **kw):
        return S.op("act", lambda e: e.activation(out, in_, func, **kw), reads, writes)

    def cp(eng, out, in_, reads, writes):
        if eng == "act":
            return S.op("act", lambda e: e.copy(out, in_), reads, writes)
        return S.op(eng, lambda e: e.tensor_copy(out, in_), reads, writes)

    def ts(eng, out, in0, s1, s2, op0, op1, reads, writes):
        if op1 is None:
            return S.op(eng, lambda e: e.tensor_scalar(out, in0, s1, 1.0, op0, ALU.mult), reads, writes)
        return S.op(eng, lambda e: e.tensor_scalar(out, in0, s1, s2, op0, op1), reads, writes)

    def stt(out, in0, sc, in1, op0, op1, reads, writes):
        return S.op("dve", lambda e: e.scalar_tensor_tensor(out, in0, sc, in1, op0, op1), reads, writes)

    def ms(eng, out, val, reads, writes):
        return S.op(eng, lambda e: e.memset(out, val), reads, writes)

    def rcp(out, in_, reads, writes):
        return S.op("dve", lambda e: e.reciprocal(out, in_), reads, writes)

    def scan(out, d0, d1, reads, writes):
        return S.op("dve", lambda e: e.tensor_tensor_scan(out, d0, d1, 0.0, ALU.mult, ALU.add), reads, writes)

    dma("pool", ident[:, :], c_ident[:, :], [], ["ident"])
    dma("sp", identf[:, :], c_ident[:, :], [], ["identf"])
    dma("pool", prot[:, :], c_prot[:, :], [], ["prot"])
    for i in range(3):
        dma("pool", masks[:, i, :], c_mask[i, :, :], [], ["masks"])
    dma("pool", maskE[:, :], c_maskE[:, :], [], ["maskE"])
    if "noconst" not in dbg:
        for s_ in range(4):
            dma("pool", maskS[:, s_, :], c_maskS[s_, :, :], [], ["maskS"])
        dma("pool", psel[:, :], c_psel[:, :], [], ["psel"])
    dma("sp", sel[:, :], c_sel[:, :], [], ["sel"])
    dma("sp", pinvE[:, :, :], c_pinv[:, :].rearrange("p (a b) -> p a b", a=4), [], ["pinvE"])
    for i, v in enumerate([1.0 / 128, 1.0 / 1024, 1.0 / 512, 1.0]):
        G(lambda e, i=i, v=v: e.memset(ones_b[:, i, :], v), [], ["ones"])
    G(lambda e: e.memset(stp[:, :], 0.0), [], ["stp"])

    def load_w(src_ap, i):
        key = f"W{i}"
        S.op("pool", lambda e: e.dma_start(out=wbuf[i][:, :, :], in_=src_ap), reads=[], writes=[key], dma=True)
        return key

    def w_tile(wt, l, rows0, col0):
        return wt[l, rows0:rows0 + 2048, col0:col0 + 512].rearrange("(c p) m -> p c m", p=128)

    def drow(blk):
        return QS["q"] * L + blk * 128 if blk < NB else NQ * L

    def xdk(blk):
        return ("xd", QS["q"], blk) if blk < NB else ("xd", "E")

    def actT_reads(c0, c1):
        return [("actT", b) for b in range(NBLK) if b * 128 < c1 and min((b + 1) * 128, T) > c0]

    def blk_of_cols(c0, c1):
        return [b for b in range(NBLK) if b * 128 < c1 and min((b + 1) * 128, T) > c0]

    def norm_to_actT(xsrc, gsrc, l):
        dma_tp("sp", gvec, gsrc[l, :], 16, [], ["gvec"])
        for blk in range(NBLK if QS["q"] == 0 else NB):
            P = 128 if blk < NB else 32
            r0 = blk * 128
            dr = drow(blk)
            dma("sp", xblk[0:P, :], xsrc[dr:dr + P, :], [xdk(blk)], XBK)
            V(lambda e, P=P: e.memset(small[0:P, 0:1], 0.0), [], ["small0"])
            ACT(lambda e, P=P: e.activation(xn[0:P, :], xblk[0:P, :], AF.Square, accum_out=small[0:P, 0:1]),
                XBK + ["small0"], ["xn", "small0"])
            ACT(lambda e, P=P: e.activation(small[0:P, 1:2], small[0:P, 0:1], AF.Ln, bias=EPS, scale=1.0 / D),
                ["small0"], ["small1"])
            ACT(lambda e, P=P: e.activation(small[0:P, 2:3], small[0:P, 1:2], AF.Exp, scale=-0.5),
                ["small1"], ["small2"])
            if "n2" in dbg:
                V(lambda e, P=P: e.tensor_scalar(xn[0:P, :], xblk[0:P, :], small[0:P, 2:3], 1.0, ALU.mult, ALU.mult),
                  XBK + ["small2", "xn"], ["xn"])
            elif "n3" in dbg:
                ACT(lambda e, P=P: e.activation(xn[0:P, :], xblk[0:P, :], AF.Copy, scale=small[0:P, 2:3]),
                    XBK + ["small2", "xn"], ["xn"])
            elif "n4" in dbg:
                pass
            else:
                V(lambda e, P=P: e.tensor_scalar(xn[0:P, :], xblk[0:P, :], small[0:P, 2:3], 1.0, ALU.mult, ALU.mult),
                  XBK + ["small2", "xn"], ["xn"])
            for c4 in range(4 if "n1" not in dbg else 0):
                pk = ("psT" if c4 % 2 == 0 else "psB2")
                pst = psT[:, 0:512] if c4 % 2 == 0 else psT_alt[:, 0:512]
                for i in range(4):
                    kc = c4 * 4 + i
                    PE(lambda e, P=P, kc=kc, i=i, pst=pst: e.transpose(pst[:, i * 128:i * 128 + P],
                                                                       xn[0:P, kc * 128:(kc + 1) * 128], ident[0:P, 0:P]),
                       ["xn", "ident"], [pk])
                for i in range(4):
                    kc = c4 * 4 + i
                    src = pst[:, i * 128:i * 128 + P]
                    dst = actT[:, kc, r0:r0 + P]
                    if "gplain" in dbg:
                        cp("act" if i % 2 == 0 else "dve", dst, src, [pk], [("actT", blk)])
                    elif (i % 2 == 0 or "gact" in dbg) and "gdve" not in dbg:
                        ACT(lambda e, src=src, dst=dst, kc=kc: e.activation(dst, src, AF.Copy, scale=gvec[:, kc:kc + 1]),
                            [pk, "gvec"], [("actT", blk)])
                    else:
                        V(lambda e, src=src, dst=dst, kc=kc: e.tensor_scalar(dst, src, gvec[:, kc:kc + 1], 1.0, ALU.mult, ALU.mult),
                          [pk, "gvec"], [("actT", blk)])

    def load_vecs(l):
        dma_slow("sp", vecs[:, 0:1], g_q[l, :].rearrange("(p o) -> p o", o=1), [], ["vecs"])
        dma_slow("sp", vecs[:, 1:2], g_k[l, :].rearrange("(p o) -> p o", o=1), [], ["vecs"])
        dma_tp("sp", vecs[:, 2:10], g_oa[l, :], 8, [], ["vecs"])
        for j, src in enumerate([g_os, g_op, D_skip, b_glu, pool_scale]):
            dma_slow("sp", vecs[:, 10 + 4 * j:14 + 4 * j], src[l, :].rearrange("(t p) -> p t", p=128), [], ["vecs"])
        V(lambda e: e.tensor_scalar(vecs[:, 30:34], vecs[:, 22:26], -1.0, 1.0, ALU.mult, ALU.mult), ["vecs"], ["vecs"])
        if "novecx" in dbg:
            return
        dma("pool", wglu[:, :, :], w_glu[l, :, :].rearrange("(c p) m -> p c m", p=128), [], ["wglu"])
        dma("pool", wpool[:, :, :], w_pool[l, :, :, :].rearrange("g c d -> c g d"), [], ["wpool"])
        dma("sp", sinkrow[0:1, 0:8], sinks[l:l + 1, :], [], ["sinkrow"])
        ACT(lambda e: e.activation(sinkrow[0:1, 8:16], sinkrow[0:1, 0:8], AF.Exp), ["sinkrow"], ["sinkrow"])
        V(lambda e: e.tensor_copy(esink[0:1, :, :], sinkrow[0:1, 8:16].unsqueeze(2).to_broadcast([1, 8, 128])),
          ["sinkrow"], ["esink"])

    def qk_finish(ps, n, c0, dst, gcol, wkey, fkey):
        ACT(lambda e: e.copy(qraw[:, 0:n], ps[:, 0:n]), [wkey], ["qraw"])
        G(lambda e: e.tensor_tensor(sqb[:, 0:n], qraw[:, 0:n], qraw[:, 0:n], ALU.mult), ["qraw"], ["sqb"])
        PE(lambda e: e.matmul(psB[0][:, 0:n], ones_b[:, 0, :], sqb[:, 0:n], start=True, stop=True),
           ["sqb", "ones"], ["psB0"])
        ACT(lambda e: e.activation(rtmp[:, 0:n], psB[0][:, 0:n], AF.Ln, bias=EPS, scale=1.0), ["psB0"], ["rtmp"])
        ACT(lambda e: e.activation(rtmp[:, 0:n], rtmp[:, 0:n], AF.Exp, scale=-0.5), ["rtmp"], ["rtmp"])
        V(lambda e: e.scalar_tensor_tensor(dst, qraw[:, 0:n], vecs[:, gcol:gcol + 1], rtmp[:, 0:n], ALU.mult, ALU.mult),
          ["qraw", "rtmp", "vecs"], [wkey + "_d"])
        PE(lambda e: e.matmul(psB[1][0:32, 0:n], prot[:, :], dst, start=True, stop=True), [wkey + "_d", "prot"], ["psB1"])
        V(lambda e: e.tensor_tensor(rtmp2[0:32, 0:n], psB[1][0:32, 0:n], rope[:, 1, c0:c0 + n], ALU.mult),
          ["psB1", "rope"], ["rtmp2"])
        V(lambda e: e.tensor_tensor(rtmp[0:32, 0:n], dst[0:32], rope[:, 0, c0:c0 + n], ALU.mult),
          [wkey + "_d", "rope", "rtmp"], ["rtmp"])
        V(lambda e: e.tensor_tensor(dst[0:32], rtmp[0:32, 0:n], rtmp2[0:32, 0:n], ALU.add),
          ["rtmp", "rtmp2", wkey + "_d"], [wkey + "_d", fkey])

    def w_in_phase(l):
        order = [3, 4, 0, 1, 2]
        for oi, ti in enumerate(order):
            bi = oi % 2
            wkey = load_w(w_tile(w_in, l, 0, ti * 512), bi)
            wb = wbuf[bi]
            for gi, (c0, c1) in enumerate(CG):
                n = c1 - c0
                for mt in range(4):
                    if ti == 2 and mt >= 2:
                        break
                    ps = psA[mt]
                    pk = f"psA{mt}"
                    for kc in range(16):
                        PE(lambda e, ps=ps, kc=kc, mt=mt, wb=wb, c0=c0, c1=c1, n=n:
                           e.matmul(ps[:, 0:n], wb[:, kc, mt * 128:(mt + 1) * 128], actT[:, kc, c0:c1],
                                    start=(kc == 0), stop=(kc == 15)),
                           [wkey] + actT_reads(c0, c1), [pk])
                    if ti in (0, 1):
                        h = ti * 4 + mt
                        qk_finish(ps, n, c0, qT[:, h, c0:c1], 0, pk, ("qT", h, gi))
                    elif ti == 2:
                        qk_finish(ps, n, c0, kT[:, mt, c0:c1], 1, pk, ("kT", mt, gi))
                    elif ti == 3:
                        ACT(lambda e, ps=ps, mt=mt, c0=c0, c1=c1, n=n: e.copy(uT[:, mt, c0:c1], ps[:, 0:n]),
                            [pk], [("uT", gi)])
                    else:
                        V(lambda e, ps=ps, mt=mt, c0=c0, c1=c1, n=n:
                          e.tensor_copy(xpT[:, mt, 16 + c0:16 + c1], ps[:, 0:n]), [pk], [("xpT", gi)])
            if ti == 2:
                for blk in range(NBLK):
                    P = 128 if blk < NB else 32
                    r0 = blk * 128
                    ps = psA[2 + blk % 2]
                    pk = f"psA{2 + blk % 2}"
                    for kc in range(16):
                        PE(lambda e, ps=ps, kc=kc, wb=wb, r0=r0, P=P:
                           e.matmul(ps[0:P, 0:256], actT[:, kc, r0:r0 + P], wb[:, kc, 256:512],
                                    start=(kc == 0), stop=(kc == 15)),
                           [wkey, ("actT", blk)], [pk])
                    ACT(lambda e, ps=ps, blk=blk, P=P: e.copy(vbf[0:P, blk, :], ps[0:P, 0:256]), [pk], [("vbf", blk)])

    SMI = dict(rho=0, aC_r=1, aC_i=2, aL_r=3, aL_i=4, a1_r=5, a1_i=6, f_r=7, f_i=8, are=9, aim=10, dt=11, dre=12, th=13,
               t0=14, t1=15, t2=16, t3=17)

    def smv(name):
        return sm[:, SMI[name], :]

    def ssm_prep(l):
        SK = ["z_re", "z_im", "k_re", "k_im"]
        dma_tp("sp", smv("are"), A_re[l, :], 16, [], ["sm_in"])
        dma_tp("sp", smv("aim"), A_im[l, :], 16, [], ["sm_in"])
        ld = log_dt[l, :].rearrange("(t h) -> h t", h=2)
        dma_slow("sp", sm[0:64, SMI["dt"], :], ld[0:1, :].partition_broadcast(64), [], ["sm_in"])
        dma_slow("sp", sm[64:128, SMI["dt"], :], ld[1:2, :].partition_broadcast(64), [], ["sm_in"])
        ACT(lambda e: e.activation(smv("dt"), smv("dt"), AF.Exp), ["sm_in"], ["sm_dt"])
        V(lambda e: e.tensor_tensor(smv("dre"), smv("dt"), smv("are"), ALU.mult), ["sm_dt", "sm_in"], ["sm_dre"])
        V(lambda e: e.tensor_tensor(smv("th"), smv("dt"), smv("aim"), ALU.mult), ["sm_dt", "sm_in"], ["sm_th"])
        ACT(lambda e: e.activation(smv("rho"), smv("dre"), AF.Exp), ["sm_dre"], ["sm_rho"])
        W65 = 16 * (CL + 1)
        jt = scr[:, 0:W65].rearrange("p (a b) -> p a b", a=16)
        ang = scr[:, W65:2 * W65].rearrange("p (a b) -> p a b", a=16)
        rp = scr[:, 2 * W65:3 * W65].rearrange("p (a b) -> p a b", a=16)
        tq = scr[:, 0:W65]
        angf = scr[:, W65:2 * W65]
        dma("sp", scr[:, 0:W65], c_jtab[:, :], [], SK + ["W1"])
        V(lambda e: e.tensor_tensor(ang, jt, smv("th").unsqueeze(2).to_broadcast([128, 16, CL + 1]), ALU.mult),
          SK + ["sm_th"], SK)
        V(lambda e: e.tensor_tensor(rp, jt, smv("dre").unsqueeze(2).to_broadcast([128, 16, CL + 1]), ALU.mult),
          SK + ["sm_dre"], SK)
        ACT(lambda e: e.activation(scr[:, 2 * W65:3 * W65], scr[:, 2 * W65:3 * W65], AF.Exp), SK, SK)
        tqi = tq.bitcast(I32)
        V(lambda e: e.tensor_scalar(tq, angf, 1.0 / (2 * PI), 1.0, ALU.mult, ALU.mult), SK, SK)
        V(lambda e: e.tensor_copy(tqi, tq), SK, SK)
        V(lambda e: e.tensor_copy(tq, tqi), SK, SK)
        V(lambda e: e.scalar_tensor_tensor(angf, tq, -2 * PI, angf, ALU.mult, ALU.add), SK, SK)
        V(lambda e: e.tensor_scalar(angf, angf, -PI, PI, ALU.max, ALU.min), SK, SK)
        ACT(lambda e: e.activation(sinT[:, :, :].rearrange("p a b -> p (a b)"), angf, AF.Sin), SK, ["sinT"])
        ACT(lambda e: e.activation(tq, angf, AF.Abs), SK, SK)
        ACT(lambda e: e.activation(cosT[:, :, :].rearrange("p a b -> p (a b)"), tq, AF.Sin, bias=PI / 2, scale=-1.0),
            SK, ["cosT"])
        V(lambda e: e.tensor_tensor(smv("a1_r"), rp[:, :, 1], cosT[:, :, 1], ALU.mult), SK + ["cosT"], ["sm_a1"])
        V(lambda e: e.tensor_tensor(smv("a1_i"), rp[:, :, 1], sinT[:, :, 1], ALU.mult), SK + ["sinT"], ["sm_a1"])
        V(lambda e: e.tensor_tensor(smv("aC_r"), rp[:, :, CL], cosT[:, :, CL], ALU.mult), SK + ["cosT"], ["sm_aC"])
        V(lambda e: e.tensor_tensor(smv("aC_i"), rp[:, :, CL], sinT[:, :, CL], ALU.mult), SK + ["sinT"], ["sm_aC"])
        V(lambda e: e.tensor_copy(Dk[:, :, :], smv("rho").unsqueeze(2).to_broadcast([128, 16, CL])), ["sm_rho"], ["Dk"])
        V(lambda e: e.memset(Dk[:, :, 0:1], 0.0), ["Dk"], ["Dk"])
        V(lambda e: e.tensor_copy(Dk16[:, :, :], smv("rho").unsqueeze(2).to_broadcast([128, 16, 16])), ["sm_rho"], ["Dk16"])
        V(lambda e: e.memset(Dk16[:, :, 0:1], 0.0), ["Dk16"], ["Dk16"])
        G(lambda e: e.tensor_copy(smv("aL_r"), smv("aC_r")), ["sm_aC"], ["sm_aL"])
        G(lambda e: e.tensor_copy(smv("aL_i"), smv("aC_i")), ["sm_aC"], ["sm_aL"])
        for _ in range(NSQ):
            tt("pool", smv("t0"), smv("aL_r"), smv("aL_r"), ALU.mult, ["sm_aL"], ["sm_t0"])
            tt("pool", smv("t1"), smv("aL_i"), smv("aL_i"), ALU.mult, ["sm_aL"], ["sm_t1"])
            tt("pool", smv("t2"), smv("aL_r"), smv("aL_i"), ALU.mult, ["sm_aL"], ["sm_t2"])
            tt("pool", smv("aL_r"), smv("t0"), smv("t1"), ALU.subtract, ["sm_t0", "sm_t1", "sm_aL"], ["sm_aL"])
            tt("pool", smv("aL_i"), smv("t2"), smv("t2"), ALU.add, ["sm_t2", "sm_aL"], ["sm_aL"])
        tt("dve", smv("t0"), smv("are"), smv("are"), ALU.mult, ["sm_in", "sm_t0"], ["sm_t0"])
        tt("dve", smv("t1"), smv("aim"), smv("aim"), ALU.mult, ["sm_in", "sm_t1"], ["sm_t1"])
        tt("dve", smv("t0"), smv("t0"), smv("t1"), ALU.add, ["sm_t0", "sm_t1"], ["sm_t0"])
        V(lambda e: e.reciprocal(smv("t0"), smv("t0")), ["sm_t0"], ["sm_t0"])
        V(lambda e: e.tensor_scalar(smv("t1"), smv("a1_r"), -1.0, 1.0, ALU.add, ALU.mult), ["sm_a1", "sm_t1"], ["sm_t1"])
        tt("dve", smv("t2"), smv("t1"), smv("are"), ALU.mult, ["sm_t1", "sm_in", "sm_t2"], ["sm_t2"])
        tt("dve", smv("t3"), smv("a1_i"), smv("aim"), ALU.mult, ["sm_a1", "sm_in"], ["sm_t3"])
        tt("dve", smv("t2"), smv("t2"), smv("t3"), ALU.add, ["sm_t2", "sm_t3"], ["sm_t2"])
        tt("dve", smv("f_r"), smv("t2"), smv("t0"), ALU.mult, ["sm_t2", "sm_t0"], ["sm_f"])
        tt("dve", smv("t2"), smv("a1_i"), smv("are"), ALU.mult, ["sm_a1", "sm_in", "sm_t2"], ["sm_t2"])
        tt("dve", smv("t3"), smv("t1"), smv("aim"), ALU.mult, ["sm_t1", "sm_in", "sm_t3"], ["sm_t3"])
        tt("dve", smv("t2"), smv("t2"), smv("t3"), ALU.subtract, ["sm_t2", "sm_t3"], ["sm_t2"])
        tt("dve", smv("f_i"), smv("t2"), smv("t0"), ALU.mult, ["sm_t2", "sm_t0"], ["sm_f"])
        Bs_r = z_re[:, 0:256].rearrange("p (a b) -> p a b", a=16)
        Bs_i = z_re[:, 256:512].rearrange("p (a b) -> p a b", a=16)
        Bb_r = z_re[:, 512:768].rearrange("p (a b) -> p a b", a=16)
        Bb_i = z_re[:, 768:1024].rearrange("p (a b) -> p a b", a=16)
        Bt = z_im[:, 0:256].rearrange("p (a b) -> p a b", a=16)
        Mp = [k_re.bitcast(BF16), k_im.bitcast(BF16)]
        dma("sp", Bs_r, B_re[l, :, :].rearrange("(t p) c -> p t c", p=128), [], SK)
        dma("sp", Bs_i, B_im[l, :, :].rearrange("(t p) c -> p t c", p=128), [], SK)
        fr = smv("f_r").unsqueeze(2).to_broadcast([128, 16, 16])
        fi = smv("f_i").unsqueeze(2).to_broadcast([128, 16, 16])
        tt("dve", Bb_r, Bs_r, fr, ALU.mult, SK + ["sm_f"], SK)
        tt("dve", Bt, Bs_i, fi, ALU.mult, SK + ["sm_f"], SK)
        tt("dve", Bb_r, Bb_r, Bt, ALU.subtract, SK, SK)
        tt("dve", Bb_i, Bs_i, fr, ALU.mult, SK + ["sm_f"], SK)
        tt("dve", Bt, Bs_r, fi, ALU.mult, SK + ["sm_f"], SK)
        tt("dve", Bb_i, Bb_i, Bt, ALU.add, SK, SK)
        for ri, Bb in enumerate((Bb_r, Bb_i)):
            V(lambda e, ri=ri: e.memset(Mp[ri], 0.0), SK, SK)
            for h in range(2):
                base = Mp[ri][64 * h:64 * h + 64, 0:1]
                for a in range(4):
                    dst = bass.AP(base.tensor, base.offset + 16 * h + 512 * a, [[base.ap[0][0], 64], [160, 4], [1, 16]])
                    src = Bb[64 * h:64 * h + 64, 4 * a:4 * a + 4, :]
                    cp("dve", dst, src, SK, SK)
        for ri in range(2):
            for t4 in range(4):
                pk = "psT"
                pst = psT[:, (t4 % 2) * 512:(t4 % 2 + 1) * 512]
                for i in range(4):
                    t = t4 * 4 + i
                    PE(lambda e, pst=pst, i=i, t=t, ri=ri: e.transpose(pst[:, i * 128:(i + 1) * 128],
                                                                       Mp[ri][:, t * 128:(t + 1) * 128], ident[:, :]),
                       SK + ["ident"], [pk])
                dst = BbT[:, ri * 16 + t4 * 4:ri * 16 + t4 * 4 + 4, :]
                V(lambda e, dst=dst, pst=pst: e.tensor_copy(dst, pst.rearrange("p (a b) -> p a b", a=4)),
                  [pk], ["BbT", "corr"])
        for ri, Csrc in enumerate((C_re, C_im)):
            Zf = scr[0:32, 0:2048].rearrange("p (a b) -> p a b", a=16)
            V(lambda e, Zf=Zf: e.memset(Zf, 0.0), SK, SK)
            cv_ = Csrc[l, :, :, :].rearrange("(t h) c n -> h c t n", h=2)
            dma("sp", Zf[0:16, :, 0:64], cv_[0, :, :, :], [], SK)
            dma("sp", Zf[16:32, :, 64:128], cv_[1, :, :, :], [], SK)
            CTp = CTp_r if ri == 0 else CTp_ni
            ms("dve", CTp[:, :, :], 0.0, [], ["CTp"])
            base = CTp[:, 0, 0:1]
            for t4 in range(4):
                ps = psB[2]
                for i in range(4):
                    t = t4 * 4 + i
                    tr(ps[:, i * 32:(i + 1) * 32], Zf[:, t, :], identf[0:32, 0:32], SK + ["identf"], ["psB2"])
                dst = bass.AP(base.tensor, base.offset + 512 * t4, [[base.ap[0][0], 128], [160, 4], [1, 32]])
                srcv = ps[:, 0:128].rearrange("p (a b) -> p a b", a=4)
                if ri == 0:
                    cp("dve", dst, srcv, ["psB2", "CTp"], ["CTp"])
                else:
                    ts("dve", dst, srcv, -1.0, None, ALU.mult, None, ["psB2", "CTp"], ["CTp"])

    def ssm_chunk(c):
        c0 = c * CL
        gi = [i for i, (a, b) in enumerate(CG) if a <= c0 < b][0]
        HW = 8 * CL
        for ri in range(2):
            for t in range(16):
                mm(psX[:, 2 * ri + t // 8, (t % 8) * CL:(t % 8 + 1) * CL], BbT[:, ri * 16 + t, :], uT[:, t // 4, c0:c0 + CL],
                   True, True, ["BbT", ("uT", gi)], [f"psA{2 * ri + t // 8}"])

        def H(hh):
            d = {}
            sl = slice(hh * HW, (hh + 1) * HW)
            d["zr"] = z_re[:, sl]; d["zi"] = z_im[:, sl]; d["kr"] = k_re[:, sl]; d["ki"] = k_im[:, sl]
            for k in ("zr", "zi", "kr", "ki"):
                d[k + "3"] = d[k].rearrange("p (a b) -> p a b", a=8)
            d["xr3"] = psX[:, hh, :].rearrange("p (a b) -> p a b", a=8)
            d["xi3"] = psX[:, 2 + hh, :].rearrange("p (a b) -> p a b", a=8)
            d["XR"] = [f"psA{hh}"]; d["XI"] = [f"psA{2 + hh}"]
            d["C3"] = cosT[:, 8 * hh:8 * hh + 8, 0:CL]; d["S3"] = sinT[:, 8 * hh:8 * hh + 8, 0:CL]
            d["dk"] = Dk[:, 8 * hh:8 * hh + 8, :].rearrange("p a b -> p (a b)")
            ts_ = slice(8 * hh, 8 * hh + 8)
            d["sr"] = scur[:, 8 * hh:8 * hh + 8]; d["si"] = scur[:, 16 + 8 * hh:16 + 8 * hh + 8]
            d["a1r"] = smv("a1_r")[:, ts_]; d["a1i"] = smv("a1_i")[:, ts_]
            d["t"] = [smv(f"t{k}")[:, ts_] for k in range(4)]
            d["K"] = {"zr": ["z_re", ("zr", hh)], "zi": ["z_im", ("zi", hh)], "kr": ["k_re", ("kr", hh)], "ki": ["k_im", ("ki", hh)],
                      "sc": [("scur", hh)], "t": [[f"sm_t{k}", ("smt", k, hh)] for k in range(4)]}
            d["W"] = {"zr": [("zr", hh)], "zi": [("zi", hh)], "kr": [("kr", hh)], "ki": [("ki", hh)], "sc": [("scur", hh)],
                      "t": [[("smt", k, hh)] for k in range(4)]}
            return d
        HS = [H(0), H(1)]

        def both(fn):
            for d in HS:
                fn(d)
        both(lambda d: tt("dve", d["zr3"], d["xr3"], d["C3"], ALU.mult, d["XR"] + ["cosT"] + d["K"]["zr"], d["W"]["zr"]))
        both(lambda d: tt("dve", d["kr3"], d["xi3"], d["S3"], ALU.mult, d["XI"] + ["sinT"] + d["K"]["kr"], d["W"]["kr"]))
        both(lambda d: tt("dve", d["zr"], d["zr"], d["kr"], ALU.add, d["K"]["zr"] + d["K"]["kr"], d["W"]["zr"]))
        both(lambda d: tt("dve", d["zi3"], d["xi3"], d["C3"], ALU.mult, d["XI"] + ["cosT"] + d["K"]["zi"], d["W"]["zi"]))
        both(lambda d: tt("dve", d["ki3"], d["xr3"], d["S3"], ALU.mult, d["XR"] + ["sinT"] + d["K"]["ki"], d["W"]["ki"]))
        both(lambda d: tt("dve", d["zi"], d["zi"], d["ki"], ALU.subtract, d["K"]["zi"] + d["K"]["ki"], d["W"]["zi"]))
        both(lambda d: tt("dve", d["t"][0], d["a1r"], d["sr"], ALU.mult, ["sm_a1"] + d["K"]["sc"] + d["K"]["t"][0], d["W"]["t"][0]))
        both(lambda d: tt("dve", d["t"][1], d["a1i"], d["si"], ALU.mult, ["sm_a1"] + d["K"]["sc"] + d["K"]["t"][1], d["W"]["t"][1]))
        both(lambda d: tt("dve", d["t"][0], d["t"][0], d["t"][1], ALU.subtract, d["K"]["t"][0] + d["K"]["t"][1], d["W"]["t"][0]))
        both(lambda d: tt("dve", d["zr3"][:, :, 0], d["zr3"][:, :, 0], d["t"][0], ALU.add, d["K"]["zr"] + d["K"]["t"][0], d["W"]["zr"]))
        both(lambda d: tt("dve", d["t"][2], d["a1r"], d["si"], ALU.mult, ["sm_a1"] + d["K"]["sc"] + d["K"]["t"][2], d["W"]["t"][2]))
        both(lambda d: tt("dve", d["t"][3], d["a1i"], d["sr"], ALU.mult, ["sm_a1"] + d["K"]["sc"] + d["K"]["t"][3], d["W"]["t"][3]))
        both(lambda d: tt("dve", d["t"][2], d["t"][2], d["t"][3], ALU.add, d["K"]["t"][2] + d["K"]["t"][3], d["W"]["t"][2]))
        both(lambda d: tt("dve", d["zi3"][:, :, 0], d["zi3"][:, :, 0], d["t"][2], ALU.add, d["K"]["zi"] + d["K"]["t"][2], d["W"]["zi"]))
        both(lambda d: scan(d["kr"], d["dk"], d["zr"], ["Dk"] + d["K"]["zr"] + d["K"]["kr"], d["W"]["kr"]))
        both(lambda d: scan(d["ki"], d["dk"], d["zi"], ["Dk"] + d["K"]["zi"] + d["K"]["ki"], d["W"]["ki"]))
        both(lambda d: tt("dve", d["zr3"], d["kr3"], d["C3"], ALU.mult, d["K"]["kr"] + ["cosT"] + d["K"]["zr"], d["W"]["zr"]))
        both(lambda d: tt("dve", d["zi3"], d["ki3"], d["S3"], ALU.mult, d["K"]["ki"] + ["sinT"] + d["K"]["zi"], d["W"]["zi"]))
        both(lambda d: tt("dve", d["zr"], d["zr"], d["zi"], ALU.subtract, d["K"]["zr"] + d["K"]["zi"], d["W"]["zr"]))
        both(lambda d: tt("dve", d["zi3"], d["ki3"], d["C3"], ALU.mult, d["K"]["ki"] + ["cosT"] + d["K"]["zi"], d["W"]["zi"]))
        both(lambda d: tt("dve", d["kr3"], d["kr3"], d["S3"], ALU.mult, d["K"]["kr"] + ["sinT"], d["W"]["kr"]))
        both(lambda d: tt("dve", d["zi"], d["zi"], d["kr"], ALU.add, d["K"]["zi"] + d["K"]["kr"], d["W"]["zi"]))
        both(lambda d: cp("dve", d["sr"], d["zr3"][:, :, CL - 1], d["K"]["zr"] + d["K"]["sc"], d["W"]["sc"]))
        both(lambda d: cp("dve", d["si"], d["zi3"][:, :, CL - 1], d["K"]["zi"] + d["K"]["sc"], d["W"]["sc"]))
        for ct in range(4):
            d = HS[ct // 2]
            for i in range(4):
                tl = (ct % 2) * 4 + i
                t = ct * 4 + i
                mm(psB[2][:, ct * CL:(ct + 1) * CL], CTp_r[:, t, :], d["zr3"][:, tl, :], (i == 0), False, ["CTp"] + d["K"]["zr"], ["psB2"])
                mm(psB[2][:, ct * CL:(ct + 1) * CL], CTp_ni[:, t, :], d["zi3"][:, tl, :], False, (i == 3), ["CTp"] + d["K"]["zi"], ["psB2"])
        cp("act", actT[:, 8:12, c0:c0 + CL], psB[2][:, 0:4 * CL].rearrange("p (a b) -> p a b", a=4), ["psB2"], [("ysm", c)])

    def ssm_E(l):
        SK = ["z_re", "z_im", "k_re", "k_im"]
        giE = len(CG) - 1
        for ri in range(2):
            for t in range(16):
                PE(lambda e, ri=ri, t=t: e.matmul(psX[:, 2 * ri + t // 8, (t % 8) * CL:(t % 8) * CL + 32],
                                                  BbT[:, ri * 16 + t, :], uT[:, t // 4, E0:E0 + 32], start=True, stop=True),
                   ["BbT", ("uT", giE)], [f"psA{2 * ri}", f"psA{2 * ri + 1}"])
        xr = psX[:, 0:2, :].rearrange("p a b -> p (a b)").rearrange("p (a b) -> p a b", a=16)
        xi = psX[:, 2:4, :].rearrange("p a b -> p (a b)").rearrange("p (a b) -> p a b", a=16)
        XR = ["psA0", "psA1"]; XI = ["psA2", "psA3"]
        C3 = cosT[:, :, 0:16]; S3 = sinT[:, :, 0:16]
        m3 = lambda ap: ap[:, 0:256].rearrange("p (a b) -> p a b", a=16)
        zr = z_re[:, 0:256]; zi = z_im[:, 0:256]; kr = k_re[:, 0:256]; kim = k_im[:, 0:256]
        tt("dve", m3(z_re), xr[:, :, 0:16], C3, ALU.mult, XR + ["cosT", "z_re"], ["z_re"])
        tt("dve", m3(k_re), xi[:, :, 0:16], S3, ALU.mult, XI + ["sinT", "k_re"], ["k_re"])
        tt("dve", zr, zr, kr, ALU.add, ["z_re", "k_re"], ["z_re"])
        tt("dve", m3(z_im), xi[:, :, 0:16], C3, ALU.mult, XI + ["cosT", "z_im"], ["z_im"])
        tt("dve", m3(k_im), xr[:, :, 0:16], S3, ALU.mult, XR + ["sinT", "k_im"], ["k_im"])
        tt("dve", zi, zi, kim, ALU.subtract, ["z_im", "k_im"], ["z_im"])
        dk = Dk16[:, :, :].rearrange("p a b -> p (a b)")
        V(lambda e: e.tensor_tensor_scan(kr, dk, zr, 0.0, ALU.mult, ALU.add), ["Dk16", "z_re", "k_re"], ["k_re"])
        V(lambda e: e.tensor_tensor_scan(kim, dk, zi, 0.0, ALU.mult, ALU.add), ["Dk16", "z_im", "k_im"], ["k_im"])
        ms("dve", hE_re, 0.0, ["rtmp"], ["rtmp", "hE"])
        ms("dve", hE_im, 0.0, ["rtmp2"], ["rtmp2", "hE"])
        tt("dve", m3(z_re), m3(k_re), C3, ALU.mult, ["k_re", "cosT", "z_re"], ["z_re"])
        tt("dve", m3(z_im), m3(k_im), S3, ALU.mult, ["k_im", "sinT", "z_im"], ["z_im"])
        tt("dve", hE_re[:, :, 0:16], m3(z_re), m3(z_im), ALU.subtract, ["z_re", "z_im", "hE"], ["hE"])
        tt("dve", Gst[:, NCH, 0:16], m3(z_re)[:, :, 15], m3(z_im)[:, :, 15], ALU.subtract, ["z_re", "z_im"], [("G", NCH)])
        tt("dve", m3(z_re), m3(k_im), C3, ALU.mult, ["k_im", "cosT", "z_re"], ["z_re"])
        tt("dve", m3(z_im), m3(k_re), S3, ALU.mult, ["k_re", "sinT", "z_im"], ["z_im"])
        tt("dve", hE_im[:, :, 0:16], m3(z_re), m3(z_im), ALU.add, ["z_re", "z_im", "hE"], ["hE"])
        tt("dve", Gst[:, NCH, 16:32], m3(z_re)[:, :, 15], m3(z_im)[:, :, 15], ALU.add, ["z_re", "z_im"], [("G", NCH)])
        dma("sp", sttok[:, 0, :], st_re[l, :, :], [], ["sttok"])
        dma("sp", sttok[:, 1, :], st_im[l, :, :], [], ["sttok"])
        for ri in range(2):
            PE(lambda e, ri=ri: e.transpose(psB[2][:, ri * 64:(ri + 1) * 64], sttok[:, ri, :], identf[0:64, 0:64]),
               ["sttok", "identf"], ["psB2"])
        V(lambda e: e.tensor_copy(h0s[:, :, :], psB[2][:, 0:128].rearrange("p (a b) -> p a b", a=2)), ["psB2"], ["h0s"])
        h0r = h0s[:, 0, :].rearrange("p (s t) -> p s t", s=4); h0i = h0s[:, 1, :].rearrange("p (s t) -> p s t", s=4)
        hsr = hs[:, 0, :].rearrange("p (s t) -> p s t", s=4); hsi = hs[:, 1, :].rearrange("p (s t) -> p s t", s=4)
        a1r = smv("a1_r").unsqueeze(1).to_broadcast([128, 4, 16]); a1i = smv("a1_i").unsqueeze(1).to_broadcast([128, 4, 16])
        t0 = z_re[:, 0:64].rearrange("p (s t) -> p s t", s=4); t1 = z_im[:, 0:64].rearrange("p (s t) -> p s t", s=4)
        xrs = xr[:, :, 16:20].rearrange("p t s -> p s t"); xis = xi[:, :, 16:20].rearrange("p t s -> p s t")
        tt("dve", t0, h0r, a1r, ALU.mult, ["h0s", "sm_a1", "z_re"], ["z_re"])
        tt("dve", t1, h0i, a1i, ALU.mult, ["h0s", "sm_a1", "z_im"], ["z_im"])
        tt("dve", t0, t0, t1, ALU.subtract, ["z_re", "z_im"], ["z_re"])
        tt("dve", hsr, t0, xrs, ALU.add, ["z_re"] + XR, ["hs"])
        tt("dve", t0, h0i, a1r, ALU.mult, ["h0s", "sm_a1", "z_re"], ["z_re"])
        tt("dve", t1, h0r, a1i, ALU.mult, ["h0s", "sm_a1", "z_im"], ["z_im"])
        tt("dve", t0, t0, t1, ALU.add, ["z_re", "z_im"], ["z_re"])
        tt("dve", hsi, t0, xis, ALU.add, ["z_re"] + XI, ["hs"])
        V(lambda e: e.tensor_copy(hE_re[:, :, 16:20], hsr.rearrange("p s t -> p t s")), ["hs", "hE"], ["hE"])
        V(lambda e: e.tensor_copy(hE_im[:, :, 16:20], hsi.rearrange("p s t -> p t s")), ["hs", "hE"], ["hE"])
        for ri in range(2):
            PE(lambda e, ri=ri: e.transpose(psB[2][0:64, ri * 128:(ri + 1) * 128], hs[:, ri, :], identf[:, :]),
               ["hs", "identf"], ["psB2"])
        V(lambda e: e.tensor_copy(outst[0:64, 0:256], psB[2][0:64, 0:256]), ["psB2"], ["outst"])
        for ri in range(2 if QS["q"] == 0 else 0):
            dma("sp", o_sssm[l, ri, :, :], outst[0:64, ri * 128:(ri + 1) * 128], ["outst"], [("out", "sssm", l, ri)])
        for ct in range(4):
            for i in range(4):
                t = ct * 4 + i
                PE(lambda e, ct=ct, t=t, i=i: e.matmul(psB[2][:, ct * 32:(ct + 1) * 32], CTp_r[:, t, :], hE_re[:, t, :],
                                                       start=(i == 0), stop=False), ["CTp", "hE", "rtmp"], ["psB2"])
                PE(lambda e, ct=ct, t=t, i=i: e.matmul(psB[2][:, ct * 32:(ct + 1) * 32], CTp_ni[:, t, :], hE_im[:, t, :],
                                                       start=False, stop=(i == 3)), ["CTp", "hE", "rtmp2"], ["psB2"])
        ACT(lambda e: e.copy(actT[:, 8:12, E0:E0 + 32], psB[2][:, 0:128].rearrange("p (a b) -> p a b", a=4)),
            ["psB2"], [("ysm", "E")])

    def cmuladd(dst_r, dst_i, a_r, a_i, s_r, s_i, g_r, g_i, rd, wr):
        tt("pool", smv("t0"), a_r, s_r, ALU.mult, rd + ["sm_t0"], ["sm_t0"])
        tt("pool", smv("t1"), a_i, s_i, ALU.mult, rd + ["sm_t1"], ["sm_t1"])
        tt("pool", smv("t2"), a_r, s_i, ALU.mult, rd + ["sm_t2"], ["sm_t2"])
        tt("pool", smv("t3"), a_i, s_r, ALU.mult, rd + ["sm_t3"], ["sm_t3"])
        tt("pool", smv("t0"), smv("t0"), smv("t1"), ALU.subtract, ["sm_t0", "sm_t1"], ["sm_t0"])
        tt("pool", smv("t2"), smv("t2"), smv("t3"), ALU.add, ["sm_t2", "sm_t3"], ["sm_t2"])
        if g_r is not None:
            tt("pool", dst_r, smv("t0"), g_r, ALU.add, rd + ["sm_t0"], wr)
            tt("pool", dst_i, smv("t2"), g_i, ALU.add, rd + ["sm_t2"], wr)
        else:
            G(lambda e: e.tensor_copy(dst_r, smv("t0")), rd + ["sm_t0"], wr)
            G(lambda e: e.tensor_copy(dst_i, smv("t2")), rd + ["sm_t2"], wr)

    def state_init(l, q):
        if q == 0:
            cp("dve", scur[:, :], Gst[:, NCH, :], [("G", NCH), ("scur", 0), ("scur", 1)], [("scur", 0), ("scur", 1)])
        else:
            cp("dve", scur[:, :], sfin[:, l, :], [("sfin", l), ("scur", 0), ("scur", 1)], [("scur", 0), ("scur", 1)])

    def local_state(l, q):
        giE = len(CG) - 1
        hk = hkL[l]; hv = hvL[l]
        if q == 0:
            ms("dve", hk[:, :, :], 0.0, [], [("hk", l)])
            cp("dve", hk[:, :, 0:16], kT[:, :, E0:E0 + 16], [("kT", 0, giE), ("kT", 1, giE), ("hk", l)], [("hk", l)])
            ms("dve", hv[:, :], 0.0, [], [("hv", l)])
            cp("dve", hv[0:16, :], vbf[0:16, NB, :], [("vbf", NB), ("hv", l)], [("hv", l)])
            cp("dve", xpT[:, :, 0:16], xpT[:, :, 16 + E0:16 + E0 + 16], [("xpT", giE)], [("xpT", "halo")])
        else:
            cp("dve", xpT[:, :, 0:16], phL[l][:, :, :], [("ph", l)], [("xpT", "halo")])
        cp("dve", sfin[:, l, :], scur[:, :], [("scur", 0), ("scur", 1), ("sfin", l)], [("sfin", l)])
        for ri in range(2):
            tr(psB[2][0:16, ri * 128:(ri + 1) * 128], scur[:, ri * 16:(ri + 1) * 16], identf[:, :], [("scur", 0), ("scur", 1), "identf"], ["psB2"])
        cp("dve", outst[0:16, 256:512], psB[2][0:16, 0:256], ["psB2", "outst2"], ["outst2"])
        for ri in range(2):
            dma("sp", o_pssm[l, ri, :, :], outst[0:16, 256 + ri * 128:256 + (ri + 1) * 128], ["outst2"], [("out", "pssm", l, ri)])

    def save_halo(l):
        giL = gi_of(L - 128)
        cp("dve", hkL[l][:, :, :], kT[:, :, L - 128:L], [("kT", 0, giL), ("kT", 1, giL), ("hk", l)], [("hk", l)])
        cp("dve", hvL[l][:, :], vbf[:, NB - 1, :], [("vbf", NB - 1), ("hv", l)], [("hv", l)])
        cp("dve", phL[l][:, :, :], xpT[:, :, 16 + L - 16:16 + L], [("xpT", gi_of(L - 16)), ("ph", l)], [("ph", l)])

    def gi_of(col):
        return [i for i, (a, b) in enumerate(CG) if a <= col < b][0]

    SC = float(128 ** -0.5)

    def attn_block(blk, step):
        r0 = blk * 128
        gi = gi_of(r0)

        def one(kvh):
            i2 = (step * 2 + kvh) % 2
            pa, pb = psA[2 * i2], psA[2 * i2 + 1]
            ka, kb_ = f"psA{2 * i2}", f"psA{2 * i2 + 1}"
            eb = ebuf[i2]; ek = ("ebuf", i2)
            qv = qT[:, 4 * kvh:4 * kvh + 4, r0:r0 + 128]
            qk_ = [("qT", 4 * kvh + g, gi) for g in range(4)]
            if blk == 0:
                l_ = QS["l"]
                kprev = hkL[l_][:, kvh, :]; vprev = hvL[l_][:, kvh * 128:(kvh + 1) * 128]
                mprev = masks[:, 2 if QS["q"] == 0 else 1, :]
                rprev = [("hk", l_)]; rvprev = [("hv", l_)]
            else:
                kprev = kT[:, kvh, r0 - 128:r0]; vprev = vbf[:, blk - 1, kvh * 128:(kvh + 1) * 128]; mprev = masks[:, 1, :]
                rprev = [("kT", kvh, gi_of(r0 - 128))]; rvprev = [("vbf", blk - 1)]
            PE(lambda e: e.matmul(pa[:, :], kprev, qv, start=True, stop=False), rprev + qk_, [ka])
            PE(lambda e: e.matmul(pa[:, :], ident[:, :], mprev, start=False, stop=True), ["ident", "masks"], [ka])
            PE(lambda e: e.matmul(pb[:, :], kT[:, kvh, r0:r0 + 128], qv, start=True, stop=False), [("kT", kvh, gi)] + qk_, [kb_])
            PE(lambda e: e.matmul(pb[:, :], ident[:, :], masks[:, 0, :], start=False, stop=True), ["ident", "masks"], [kb_])
            ACT(lambda e: e.activation(eb[:, 0, :], pa[:, :], AF.Exp, scale=SC), [ka], [ek])
            ACT(lambda e: e.activation(eb[:, 1, :], pb[:, :], AF.Exp, scale=SC), [kb_], [ek])
            PE(lambda e: e.matmul(psB[0][:, :], vprev, eb[:, 0, :], start=True, stop=False), rvprev + [ek], ["psB0"])
            PE(lambda e: e.matmul(psB[0][:, :], vbf[:, blk, kvh * 128:(kvh + 1) * 128], eb[:, 1, :], start=False, stop=True),
               [("vbf", blk), ek], ["psB0"])
            PE(lambda e: e.matmul(psB[1][:, :], ones_b[:, 3, :], eb[:, 0, :], start=True, stop=False), ["ones", ek], ["psB1"])
            PE(lambda e: e.matmul(psB[1][:, :], ones_b[:, 3, :], eb[:, 1, :], start=False, stop=False), ["ones", ek], ["psB1"])
            PE(lambda e: e.matmul(psB[1][:, :], ones_b[0:1, 3, :], esink[0:1, 4 * kvh:4 * kvh + 4, :], start=False, stop=True),
               ["ones", "esink"], ["psB1"])
            V(lambda e: e.reciprocal(qraw[:, :], psB[1][:, :]), ["psB1", "qraw"], ["qraw"])
            V(lambda e: e.tensor_tensor(qv, psB[0][:, :].rearrange("p (a b) -> p a b", a=4),
                                        qraw[:, :].rearrange("p (a b) -> p a b", a=4), ALU.mult),
              ["psB0", "qraw"], [("qT", 4 * kvh + g, gi) for g in range(4)])
        for kvh in range(2):
            one(kvh)

    def attn_E():
        giE = len(CG) - 1

        def one(kvh):
            pa = psA[2 * kvh]; ka = f"psA{2 * kvh}"
            eb = ebuf[kvh]; ek = ("ebuf", kvh)
            qv = qT[:, 4 * kvh:4 * kvh + 4, E0:E0 + 32]
            qk_ = [("qT", 4 * kvh + g, giE) for g in range(4)]
            PE(lambda e: e.matmul(pa[0:32, 0:128], kT[:, kvh, E0:E0 + 32], qv, start=True, stop=False), [("kT", kvh, giE)] + qk_, [ka])
            PE(lambda e: e.matmul(pa[0:32, 0:128], ident[0:32, 0:32], maskE[:, :], start=False, stop=True), ["ident", "maskE"], [ka])
            ACT(lambda e: e.activation(eb[0:32, 0, 0:128], pa[0:32, 0:128], AF.Exp, scale=SC), [ka], [ek])
            PE(lambda e: e.matmul(psB[0][:, 0:128], vbf[0:32, NB, kvh * 128:(kvh + 1) * 128], eb[0:32, 0, 0:128], start=True, stop=True),
               [("vbf", NB), ek], ["psB0"])
            PE(lambda e: e.matmul(psB[1][:, 0:128], ones_b[0:32, 3, :], eb[0:32, 0, 0:128], start=True, stop=False), ["ones", ek], ["psB1"])
            PE(lambda e: e.matmul(psB[1][:, 0:128], ones_b[0:1, 3, :], esink[0:1, 4 * kvh:4 * kvh + 4, 0:32], start=False, stop=True),
               ["ones", "esink"], ["psB1"])
            V(lambda e: e.reciprocal(qraw[:, 0:128], psB[1][:, 0:128]), ["psB1", "qraw"], ["qraw"])
            V(lambda e: e.tensor_tensor(qT[:, 4 * kvh:4 * kvh + 4, E0:E0 + 16],
                                        psB[0][:, 0:128].rearrange("p (a b) -> p a b", a=4)[:, :, 0:16],
                                        qraw[:, 0:128].rearrange("p (a b) -> p a b", a=4)[:, :, 0:16], ALU.mult),
              ["psB0", "qraw"], [("qTm", kvh)])
        for kvh in range(2):
            one(kvh)

    def attn_samples(l):
        giE = len(CG) - 1
        for s_ in range(4):
            dma("pool", kctok[:, :], cache_k[l, s_, :, :], [], ["kctok"])
            dma("pool", vc[:, :], cache_v[l, s_, :, :], [], ["vc"])
            if QS["q"] == 0:
                dma("sp", o_sk[l, s_, 0:127, :], cache_k[l, s_, 1:128, :], [], [("out", "sk", l, s_)])
                dma("sp", o_sv[l, s_, 0:127, :], cache_v[l, s_, 1:128, :], [], [("out", "sv", l, s_)])
                dma("sp", o_spool[l, s_, 0:14, :], st_pool[l, s_, 1:15, :], [], [("out", "sp", l, s_)])
            col = E0 + 16 + s_
            for kvh in range(2):
                pa = psA[2 * kvh]; ka = f"psA{2 * kvh}"
                qs = qT[:, 4 * kvh:4 * kvh + 4, col]
                qk_ = [("qT", 4 * kvh + g, giE) for g in range(4)]
                PE(lambda e, kvh=kvh: e.transpose(psT[:, 0:128], kctok[:, kvh * 128:(kvh + 1) * 128], ident[:, :]),
                   ["kctok", "ident"], ["psT"])
                V(lambda e: e.tensor_copy(kcT[:, :], psT[:, 0:128]), ["psT"], ["kcT"])
                PE(lambda e, pa=pa, qs=qs: e.matmul(pa[:, 0:4], kcT[:, :], qs, start=True, stop=True), ["kcT"] + qk_, [ka])
                PE(lambda e, pa=pa, qs=qs, kvh=kvh: e.matmul(pa[0:32, 4:8], kT[:, kvh, E0:E0 + 32], qs, start=True, stop=False),
                   [("kT", kvh, giE)] + qk_, [ka])
                PE(lambda e, pa=pa, s_=s_: e.matmul(pa[0:32, 4:8], ident[0:32, 0:32], maskS[:, s_, :], start=False, stop=True),
                   ["ident", "maskS"], [ka])
                ACT(lambda e, pa=pa: e.activation(es[:, 0:4], pa[:, 0:4], AF.Exp, scale=SC), [ka], ["es"])
                ACT(lambda e, pa=pa: e.activation(es[0:32, 4:8], pa[0:32, 4:8], AF.Exp, scale=SC), [ka], ["es"])
                PE(lambda e, kvh=kvh: e.matmul(psB[0][:, 0:4], vc[:, kvh * 128:(kvh + 1) * 128], es[:, 0:4], start=True, stop=False),
                   ["vc", "es"], ["psB0"])
                PE(lambda e, kvh=kvh: e.matmul(psB[0][:, 0:4], vbf[0:32, NB, kvh * 128:(kvh + 1) * 128], es[0:32, 4:8],
                                               start=False, stop=True), [("vbf", NB), "es"], ["psB0"])
                PE(lambda e: e.matmul(psB[1][:, 0:4], ones_b[:, 3, :], es[:, 0:4], start=True, stop=False), ["ones", "es"], ["psB1"])
                PE(lambda e: e.matmul(psB[1][:, 0:4], ones_b[0:32, 3, :], es[0:32, 4:8], start=False, stop=False), ["ones", "es"], ["psB1"])
                PE(lambda e, kvh=kvh: e.matmul(psB[1][:, 0:4], ones_b[0:1, 3, :], esink[0:1, 4 * kvh:4 * kvh + 4, 0],
                                               start=False, stop=True), ["ones", "esink"], ["psB1"])
                V(lambda e: e.reciprocal(qraw[:, 0:4], psB[1][:, 0:4]), ["psB1", "qraw"], ["qraw"])
                V(lambda e, qs=qs: e.tensor_tensor(qs, psB[0][:, 0:4], qraw[:, 0:4], ALU.mult), ["psB0", "qraw"], [("qTs", s_, kvh)])

    def kv_outputs(l):
        giL = gi_of(L - 128); giE = len(CG) - 1
        for kvh in range(2):
            PE(lambda e, kvh=kvh: e.transpose(psT[:, 512 + kvh * 128:512 + (kvh + 1) * 128], kT[:, kvh, L - 128:L], ident[:, :]),
               [("kT", kvh, giL), "ident"], ["psT"])
        V(lambda e: e.tensor_copy(outst[:, 0:256], psT[:, 512:768]), ["psT", "outst"], ["outst"])
        dma("sp", o_pk[l, :, :], outst[:, 0:256], ["outst"], [("out", "pk", l)])
        V(lambda e: e.tensor_copy(outst[:, 256:512], vbf[:, NB - 1, :]), [("vbf", NB - 1), "outst2"], ["outst2"])
        dma("sp", o_pv[l, :, :], outst[:, 256:512], ["outst2"], [("out", "pv", l)])
        for kvh in range(2):
            PE(lambda e, kvh=kvh: e.transpose(psT[0:32, 512 + kvh * 128:512 + (kvh + 1) * 128], kT[:, kvh, E0:E0 + 32], ident[:, :]),
               [("kT", kvh, giE), "ident"], ["psT"])
        V(lambda e: e.tensor_copy(outst[0:32, 0:256], psT[0:32, 512:768]), ["psT", "outst"], ["outst"])
        V(lambda e: e.tensor_copy(outst[0:32, 256:512], vbf[0:32, NB, :]), [("vbf", NB), "outst2"], ["outst2"])
        for s_ in range(4 if QS["q"] == 0 else 0):
            dma("sp", o_sk[l, s_, 127:128, :], outst[16 + s_:17 + s_, 0:256], ["outst"], [("out", "sk2", l, s_)])
            dma("sp", o_sv[l, s_, 127:128, :], outst[16 + s_:17 + s_, 256:512], ["outst2"], [("out", "sv2", l, s_)])

    def pool_mixer(l):
        SKA = ["z_re", "z_im"]; SKB = ["k_re", "k_im"]
        W_ = 16 + L
        pa = scr[:, 0:W_]; pb = scr[:, 2048:2048 + W_]
        giE = len(CG) - 1
        allx = [("xpT", i) for i in range(len(CG))] + [("xpT", "halo")]
        for g in range(4):
            PE(lambda e, g=g: e.transpose(psT[0:32, g * 128:(g + 1) * 128], xpT[:, g, 16 + L - 32:16 + L], ident[:, :]),
               allx + ["ident"], ["psT"])
        V(lambda e: e.tensor_copy(outst[0:32, :], psT[0:32, 0:512]), ["psT", "outst", "outst2"], ["outst", "outst2"])
        dma("sp", o_ppool[l, :, :], outst[17:32, :], ["outst", "outst2"], [("out", "ppool", l)])
        for g in range(4):
            PE(lambda e, g=g: e.transpose(psT[0:32, 512 + g * 128:512 + (g + 1) * 128], xpT[:, g, 16 + E0:16 + E0 + 32], ident[:, :]),
               allx + ["ident"], ["psT"])
        V(lambda e: e.tensor_copy(outst[32:64, :], psT[0:32, 512:1024]) if False else e.tensor_copy(outst[0:32, :], psT[0:32, 512:1024]),
          ["psT", "outst", "outst2"], ["outst", "outst2"])
        for s_ in range(4 if QS["q"] == 0 else 0):
            dma("sp", o_spool[l, s_, 14:15, :], outst[16 + s_:17 + s_, :], ["outst", "outst2"], [("out", "sp2", l, s_)])
        for g in range(4):
            w = 2 ** (g + 1)
            src = xpT[:, g, 0:W_]
            cur = None
            sh = 1
            bufs = [pa, pb]
            bkeys = [SKA, SKB]
            bi = 0
            for k_ in range(g + 1):
                out = bufs[bi]; ok = bkeys[bi]
                if cur is None:
                    tt("pool", out[:, sh:W_], src[:, sh:W_], src[:, 0:W_ - sh], ALU.add, allx + ok, ok)
                else:
                    tt("pool", out[:, sh:W_], cur[:, sh:W_], cur[:, 0:W_ - sh], ALU.add, bkeys[1 - bi] + ok, ok)
                cur = out; ck = ok
                sh *= 2
                bi = 1 - bi
            for gi, (c0, c1) in enumerate(CG):
                c1p = min(c1, L)
                n = c1p - c0
                V(lambda e, cur=cur, c0=c0, n=n, g=g, w=w: e.scalar_tensor_tensor(
                    sqb[:, 0:n], cur[:, 16 + c0:16 + c0 + n], 1.0 / w, xpT[:, g, 16 + c0:16 + c0 + n], ALU.mult, ALU.subtract),
                  ck + allx + ["sqb"], ["sqb"])
                if gi == giE:
                    pm = pmeta[:, g, :]; pm2 = pmeta2[:, g, :]
                    V(lambda e, pm=pm: e.memset(pm, 0.0), [], ["pmeta"])
                    V(lambda e, pm=pm, g=g: e.tensor_copy(pm[:, 16:32], xpT[:, g, 16 + E0:16 + E0 + 16]), allx + ["pmeta"], ["pmeta"])
                    a_, b_ = pm, pm2
                    sh2 = 1
                    for k_ in range(g + 1):
                        V(lambda e, a_=a_, b_=b_: e.tensor_copy(b_[:, 0:16], a_[:, 0:16]), ["pmeta"], ["pmeta"])
                        V(lambda e, a_=a_, b_=b_, sh2=sh2: e.tensor_tensor(b_[:, 16:32], a_[:, 16:32], a_[:, 16 - sh2:32 - sh2], ALU.add),
                          ["pmeta"], ["pmeta"])
                        a_, b_ = b_, a_
                        sh2 *= 2
                    V(lambda e, a_=a_, g=g: e.tensor_tensor(a_[:, 16:32], a_[:, 16:32], pinvE[:, g, :], ALU.mult), ["pmeta", "pinvE"], ["pmeta"])
                    V(lambda e, a_=a_, g=g, n=n: e.tensor_tensor(sqb[:, n:n + 16], a_[:, 16:32], xpT[:, g, 16 + E0:16 + E0 + 16], ALU.subtract),
                      ["pmeta", "sqb"] + allx, ["sqb"])
                    for s_ in range(4):
                        dma("pool", stp[0:15, :], st_pool[l, s_, :, :], [], ["stp"])
                        PE(lambda e, g=g, s_=s_: e.matmul(psB[2][:, s_:s_ + 1], stp[0:16, g * 128:(g + 1) * 128], psel[0:16, g:g + 1],
                                                          start=True, stop=True), ["stp", "psel"], ["psB2"])
                    xs4 = xpT[:, g, 16 + E0 + 16:16 + E0 + 20]
                    V(lambda e, xs4=xs4: e.tensor_tensor(small[:, 4:8], psB[2][:, 0:4], xs4, ALU.add), ["psB2"] + allx, ["small4"])
                    V(lambda e, xs4=xs4, n=n, w=w: e.scalar_tensor_tensor(sqb[:, n + 16:n + 20], small[:, 4:8], 1.0 / w, xs4,
                                                                         ALU.mult, ALU.subtract), ["small4", "sqb"] + allx, ["sqb"])
                    V(lambda e, n=n: e.memset(sqb[:, n + 20:n + 32], 0.0), ["sqb"], ["sqb"])
                    n = n + 32
                PE(lambda e, g=g, n=n: e.matmul(psB[0][:, 0:n], wpool[:, g, :], sqb[:, 0:n], start=True, stop=True),
                   ["wpool", "sqb"], ["psB0"])
                ACT(lambda e, g=g, n=n, c0=c0: e.activation(actT[:, 12 + g, c0:c0 + n], psB[0][:, 0:n], AF.Copy, scale=vecs[:, 26 + g:27 + g]),
                    ["psB0", "vecs"], [("plT", g, gi)])

    GC = float(2.0 * np.sqrt(2.0 / np.pi))

    def ssm_post():
        for gi, (c0, c1) in enumerate(CG):
            n = c1 - c0
            ysk = [("ysm", c) for c in range(NCH) if c0 <= c * CL < c1] + ([("ysm", "E")] if c1 > L else [])
            for ct in range(4):
                yv = actT[:, 8 + ct, c0:c1]
                uv = uT[:, ct, c0:c1]
                V(lambda e, yv=yv, uv=uv, ct=ct, n=n: e.scalar_tensor_tensor(qraw[:, 0:n], uv, vecs[:, 18 + ct:19 + ct], yv, ALU.mult, ALU.add),
                  ysk + [("uT", gi), "vecs", "qraw"], ["qraw"])
                tt("pool", rtmp2[:, 0:n], qraw[:, 0:n], qraw[:, 0:n], ALU.mult, ["qraw", "rtmp2"], ["rtmp2"])
                G(lambda e, n=n: e.tensor_scalar(rtmp2[:, 0:n], rtmp2[:, 0:n], 0.044715, 1.0, ALU.mult, ALU.add), ["rtmp2"], ["rtmp2"])
                tt("pool", rtmp2[:, 0:n], rtmp2[:, 0:n], qraw[:, 0:n], ALU.mult, ["qraw", "rtmp2"], ["rtmp2"])
                G(lambda e, n=n: e.tensor_scalar(rtmp2[:, 0:n], rtmp2[:, 0:n], -30.0, 1.0, ALU.max, ALU.mult), ["rtmp2"], ["rtmp2"])
                ACT(lambda e, n=n: e.activation(rtmp2[:, 0:n], rtmp2[:, 0:n], AF.Exp, scale=-GC), ["rtmp2"], ["rtmp2"])
                G(lambda e, n=n: e.tensor_scalar(rtmp2[:, 0:n], rtmp2[:, 0:n], 1.0, 1.0, ALU.add, ALU.mult), ["rtmp2"], ["rtmp2"])
                V(lambda e, n=n: e.reciprocal(rtmp2[:, 0:n], rtmp2[:, 0:n]), ["rtmp2"], ["rtmp2"])
                V(lambda e, uv=uv, n=n: e.tensor_tensor(uv, qraw[:, 0:n], rtmp2[:, 0:n], ALU.mult), ["qraw", "rtmp2", ("uT", gi)], [("uT", gi)])
            for co in range(4):
                for ci in range(4):
                    PE(lambda e, co=co, ci=ci, c0=c0, c1=c1, n=n: e.matmul(psA[co][:, 0:n], wglu[:, ci, co * 128:(co + 1) * 128],
                                                                          uT[:, ci, c0:c1], start=(ci == 0), stop=(ci == 3)),
                       ["wglu", ("uT", gi)], [f"psA{co}"])
                ACT(lambda e, co=co, n=n: e.activation(rtmp[:, 0:n], psA[co][:, 0:n], AF.Exp, bias=vecs[:, 30 + co:31 + co], scale=-1.0),
                    [f"psA{co}", "vecs", "rtmp"], ["rtmp"])
                G(lambda e, n=n: e.tensor_scalar(rtmp[:, 0:n], rtmp[:, 0:n], 1.0, 1.0, ALU.add, ALU.mult), ["rtmp"], ["rtmp"])
                V(lambda e, n=n: e.reciprocal(rtmp[:, 0:n], rtmp[:, 0:n]), ["rtmp"], ["rtmp"])
                V(lambda e, co=co, c0=c0, c1=c1, n=n: e.tensor_tensor(actT[:, 8 + co, c0:c1], uT[:, co, c0:c1], rtmp[:, 0:n], ALU.mult),
                  ["rtmp", ("uT", gi)] + ysk, [("smT", co, gi)])

    def out_norms():
        for gi, (c0, c1) in enumerate(CG):
            n = c1 - c0
            groups = [
                ([(qT[:, h, c0:c1], [("qT", h, gi), ("qTm", h // 4)] + [("qTs", s_, h // 4) for s_ in range(4)]) for h in range(8)], 1, 2, 0),
                ([(actT[:, 8 + t, c0:c1], [("smT", t, gi)]) for t in range(4)], 2, 10, 8),
                ([(actT[:, 12 + t, c0:c1], [("plT", t, gi)]) for t in range(4)], 2, 14, 12),
            ]
            for tiles, oi, gcol, slot0 in groups:
                for i, (src, rk) in enumerate(tiles):
                    tt("pool", sqb[:, 0:n], src, src, ALU.mult, rk + ["sqb"], ["sqb"])
                    PE(lambda e, i=i, oi=oi, n=n, last=(i == len(tiles) - 1): e.matmul(psB[0][:, 0:n], ones_b[:, oi, :], sqb[:, 0:n],
                                                                                     start=(i == 0), stop=last), ["sqb", "ones"], ["psB0"])
                ACT(lambda e, n=n: e.activation(rtmp[:, 0:n], psB[0][:, 0:n], AF.Ln, bias=EPS, scale=1.0), ["psB0", "rtmp"], ["rtmp"])
                ACT(lambda e, n=n: e.activation(rtmp[:, 0:n], rtmp[:, 0:n], AF.Exp, scale=-0.5), ["rtmp"], ["rtmp"])
                for i, (src, rk) in enumerate(tiles):
                    dst = actT[:, slot0 + i, c0:c1]
                    V(lambda e, src=src, dst=dst, i=i, gcol=gcol, n=n: e.scalar_tensor_tensor(dst, src, vecs[:, gcol + i:gcol + i + 1],
                                                                                             rtmp[:, 0:n], ALU.mult, ALU.mult),
                      rk + ["rtmp", "vecs"], [("actT", b) for b in blk_of_cols(c0, c1)])

    def resid_pass(wt, l, rows0, nk, lhs_of, lhs_reads, xsrc, xdst, outkey):
        for ni in range(4):
            wkey = load_w(w_tile(wt, l, rows0, ni * 512), ni % 2)
            wb = wbuf[ni % 2]
            for blk in range(NBLK if QS["q"] == 0 else NB):
                P = 128 if blk < NB else 32
                r0 = blk * 128
                j = blk % 3
                ps = psB[j]; pk = f"psB{j}"
                for kc in range(nk):
                    PE(lambda e, ps=ps, kc=kc, wb=wb, r0=r0, P=P: e.matmul(ps[0:P, :], lhs_of(kc, r0, P), wb[:, kc, :],
                                                                          start=(kc == 0), stop=(kc == nk - 1)),
                       [wkey] + lhs_reads(blk), [pk])
                xj = blk % 4
                dr = drow(blk)
                dma("sp", xio[xj][0:P, :], xsrc[dr:dr + P, ni * 512:(ni + 1) * 512], [xdk(blk)], [("xio", xj)])
                V(lambda e, ps=ps, xj=xj, P=P: e.tensor_tensor(xio[xj][0:P, :], ps[0:P, :], xio[xj][0:P, :], ALU.add),
                  [pk, ("xio", xj)], [("xio", xj)])
                wr = [xdk(blk)] + ([("out", outkey, QS["q"], blk, ni)] if outkey else [])
                if blk < NB or QS["q"] == 0:
                    dma("sp", xdst[dr:dr + P, ni * 512:(ni + 1) * 512], xio[xj][0:P, :], [("xio", xj)], wr)

    def ffn_phase(l, last):
        for hq in range(4):
            for t4 in range(4):
                wkey = load_w(w_tile(w_ff1, l, 0, hq * 2048 + t4 * 512), t4 % 2)
                wb = wbuf[t4 % 2]
                for gi, (c0, c1) in enumerate(CG):
                    n = c1 - c0
                    for mt in range(4):
                        ps = psA[mt]; pk = f"psA{mt}"
                        for kc in range(16):
                            PE(lambda e, ps=ps, kc=kc, mt=mt, wb=wb, c0=c0, c1=c1, n=n:
                               e.matmul(ps[:, 0:n], wb[:, kc, mt * 128:(mt + 1) * 128], actT[:, kc, c0:c1],
                                        start=(kc == 0), stop=(kc == 15)), [wkey] + actT_reads(c0, c1), [pk])
                        rt = relu_t[mt % 2]; rk = f"relu{mt % 2}"
                        ACT(lambda e, ps=ps, rt=rt, n=n: e.activation(rt[:, 0:n], ps[:, 0:n], AF.Relu), [pk], [rk])
                        m = t4 * 4 + mt
                        tt("pool", hidT[:, m, c0:c1], rt[:, 0:n], rt[:, 0:n], ALU.mult, [rk, "hidfence"], [("hid", m, gi)])
            resid_pass(w_ff2, l, hq * 2048, 16, lambda kc, r0, P: hidT[:, kc, r0:r0 + P],
                       lambda blk: [("hid", m, gi) for m in range(16) for gi in range(len(CG))
                                    if blk in blk_of_cols(*CG[gi])],
                       xs, y_out if (last and hq == 3) else xs, "y" if (last and hq == 3) else None)

    PROJ_PRED = lambda k: isinstance(k, tuple) and k[0] in ("qT", "kT", "vbf", "uT", "qTm", "qTs", "hid")
    SKK = ["z_re", "z_im", "k_re", "k_im"]
    STG = ["w_in", "chunks", "ssmE", "kvout", "state", "attnE", "samples", "corr", "attn0", "pool", "post", "mix", "w_out", "ffn"]

    def reached(name):
        return stop in STG and STG.index(stop) < STG.index(name)

    def layer_pass(q, l, first, last):
        QS["q"] = q; QS["l"] = l
        xsrc = x_in if l == 0 else xs
        S.fence(lambda k: isinstance(k, tuple) and k[0] == "ebuf", ["xn"])
        norm_to_actT(xsrc, g_mix, l)
        S.fence(lambda k: k == "xn", [("ebuf", 0), ("ebuf", 1)])
        S.fence(PROJ_PRED, ["projfence"])
        w_in_phase(l)
        if reached("chunks"): return
        S.fence(lambda k: k in ("W1",), SKK)
        if q == 0:
            ssm_E(l)
        state_init(l, q)
        nblk_done = 1
        for c in range(NCH):
            ssm_chunk(c)
            if c % 2 == 1 and nblk_done < NB:
                attn_block(nblk_done, nblk_done)
                nblk_done += 1
        while nblk_done < NB:
            attn_block(nblk_done, nblk_done)
            nblk_done += 1
        if reached("kvout"): return
        kv_outputs(l)
        if reached("state"): return
        local_state(l, q)
        if reached("attnE"): return
        if q == 0:
            attn_E()
        if reached("samples"): return
        if q == 0:
            attn_samples(l)
        tap("ysm", actT[:, 8:12, :], BF16, [k for k in S.last_w])
        if reached("attn0"): return
        attn_block(0, 1)
        tap("aT", projbuf[:, 0:8 * T], BF16, [k for k in S.last_w])
        if reached("pool"): return
        pool_mixer(l)
        save_halo(l)
        tap("ysm2", actT[:, 8:12, :], BF16, [k for k in S.last_w])
        if reached("post"): return
        ssm_post()
        tap("premix", actT[:, 8:16, :], BF16, [k for k in S.last_w])
        if reached("mix"): return
        out_norms()
        tap("mixT", actT[:, :, :], BF16, [k for k in S.last_w])
        if reached("w_out"): return
        S.fence(lambda k: k in SKK, ["W1"])
        S.fence(lambda k: isinstance(k, tuple) and k[0] == "ebuf", ["xn"])
        direct = (not ffw) and last
        resid_pass(w_out, l, 0, 16, lambda kc, r0, P: actT[:, kc, r0:r0 + P], lambda blk: [("actT", blk)],
                   xsrc, y_out if direct else xs, "y" if direct else None)
        if reached("ffn") or not ffw: return
        norm_to_actT(xs, g_ffn, l)
        S.fence(PROJ_PRED, ["hidfence"])
        ffn_phase(l, last)

    PROJ_PRED = lambda k: isinstance(k, tuple) and k[0] in ("qT", "kT", "vbf", "uT", "qTm", "qTs", "hid")
    for l in range(NL):
        load_vecs(l)
        S.fence(lambda k: k == "W1", SKK)
        ssm_prep(l)
        S.fence(lambda k: k in SKK, ["W1"])
        for q in range(NQ):
            dma("sp", rope[:, 0, :], c_rope[q, 0, :, :], [], ["rope"])
            dma("sp", rope[:, 1, :], c_rope[q, 1, :, :], [], ["rope"])
            layer_pass(q, l, q == 0, l == NL - 1)

    S.op("sp", lambda e: e.nop(), reads=[k for k in S.last_w.keys() if isinstance(k, tuple) and k[0] == "out"], writes=[])
    S.emit()
    return nc


def _consts(NB, NQ=4):
    T = NB * 128 + 32
    E0 = NB * 128
    L = NB * 128
    inv = (np.float32(500000.0) ** (-np.arange(0, 32, 2, dtype=np.float32) / np.float32(32))).astype(np.float32)
    ropes = []
    for q in range(NQ):
        pos = np.zeros(T, np.float32)
        pos[:L] = 16 + q * L + np.arange(L)
        pos[E0:E0 + 16] = np.arange(16)
        pos[E0 + 16:E0 + 20] = 16384
        ang = (pos[:, None] * inv[None, :]).astype(np.float32)
        cos = np.cos(ang).astype(np.float32).T
        sin = np.sin(ang).astype(np.float32).T
        ropes.append(np.stack([np.concatenate([cos, cos], 0), np.concatenate([sin, sin], 0)], 0))
    c_rope = np.stack(ropes, 0).astype(np.float32)
    j = np.arange(128)[:, None]
    i = np.arange(128)[None, :]
    md = np.where(j <= i, 0.0, NEG)
    mp = np.where(j >= i, 0.0, NEG)
    mp0 = np.where((j < 16) & (j >= i - 112), 0.0, NEG)
    c_mask = np.stack([np.tile(m, (1, 4)) for m in (md, mp, mp0)], 0).astype(np.float32)
    jE = np.arange(32)[:, None]
    iE = np.arange(32)[None, :]
    mE = np.where((jE < 16) & (iE < 16) & (jE <= iE), 0.0, NEG)
    c_maskE = np.tile(mE, (1, 4)).astype(np.float32)
    c_maskS = np.full((4, 32, 4), NEG, np.float32)
    for s in range(4):
        c_maskS[s, 16 + s, :] = 0.0
    prot = np.zeros((128, 32), np.float32)
    for m in range(16):
        prot[m + 16, m] = -1.0
        prot[m, m + 16] = 1.0
    sel = np.zeros((128, 24), np.float32)
    jt = np.tile(np.arange(CL + 1, dtype=np.float32), 16)
    c_jtab = np.tile(jt[None, :], (128, 1)).astype(np.float32)
    pinv = np.zeros((4, 16), np.float32)
    for g, w in enumerate((2, 4, 8, 16)):
        pinv[g, :] = 1.0 / np.minimum(w, np.arange(16) + 1)
    c_pinv = np.tile(pinv.reshape(1, -1), (128, 1)).astype(np.float32)
    psel = np.zeros((16, 4), np.float32)
    for g, w in enumerate((2, 4, 8, 16)):
        psel[15 - (w - 1):15, g] = 1.0
    return dict(c_rope=c_rope, c_mask=c_mask, c_maskE=c_maskE, c_maskS=c_maskS, c_prot=prot,
                c_ident=np.eye(128, dtype=np.float32), c_sel=sel, c_jtab=c_jtab, c_pinv=c_pinv, c_psel=psel)


def prep_inputs(inp, NB, NL=2, ffw=True, NQ=4):
    L = NB * 128
    T = L + 32
    TT = NQ * L + 32
    cst = _consts(NB, NQ)
    f = lambda a: np.ascontiguousarray(np.asarray(a, dtype=np.float32))
    shared = dict(
        w_in=f(inp["w_in"][:NL]), w_out=f(inp["w_out"][:NL]),
        g_mix=f(inp["g_mix"]), g_ffn=f(inp["g_ffn"]), g_q=f(inp["g_q"]), g_k=f(inp["g_k"]), sinks=f(inp["sinks"]),
        A_re=f(inp["A_re"]).reshape(2, 2048), A_im=f(inp["A_im"]).reshape(2, 2048), log_dt=f(inp["log_dt"]),
        B_re=f(inp["B_re"]).reshape(2, 2048, 16), B_im=f(inp["B_im"]).reshape(2, 2048, 16),
        C_re=f(inp["C_re"]), C_im=f(inp["C_im"]), D_skip=f(inp["D_skip"]), w_glu=f(inp["w_glu"]), b_glu=f(inp["b_glu"]),
        w_pool=f(inp["w_pool"]), pool_scale=f(inp["pool_scale"]), g_out_attn=f(inp["g_out_attn"]),
        g_out_ssm=f(inp["g_out_ssm"]), g_out_pool=f(inp["g_out_pool"]),
    )
    if ffw:
        shared.update(w_ff1=f(inp["w_ff1"][:NL]), w_ff2=f(inp["w_ff2"][:NL]))
    xp = f(inp["x_prompt"]); xsm = f(inp["x_sample"]); meta = f(inp["meta_tokens"])
    ck = f(inp["cache_k"]).reshape(2, 32, 128, 256); cv = f(inp["cache_v"]).reshape(2, 32, 128, 256)
    sre = f(inp["state_ssm_re"]).reshape(2, 32, 16, 128); sim = f(inp["state_ssm_im"]).reshape(2, 32, 16, 128)
    spool = f(inp["state_pool"])
    maps = []
    for r in range(NCORES):
        b = r % 2
        x_in = np.zeros((TT, D), np.float32)
        x_in[:NQ * L] = xp[b, :NQ * L]
        x_in[NQ * L:NQ * L + 16] = meta
        x_in[NQ * L + 16:NQ * L + 20] = xsm[4 * r:4 * r + 4, 0]
        m = dict(shared)
        m.update(cst)
        m.update(x_in=x_in, cache_k=np.ascontiguousarray(ck[:, 4 * r:4 * r + 4]),
                 cache_v=np.ascontiguousarray(cv[:, 4 * r:4 * r + 4]),
                 st_re=np.ascontiguousarray(sre[:, 4 * r:4 * r + 4]).reshape(2, 64, 128),
                 st_im=np.ascontiguousarray(sim[:, 4 * r:4 * r + 4]).reshape(2, 64, 128),
                 st_pool=np.ascontiguousarray(spool[:, 4 * r:4 * r + 4]))
        maps.append(m)
    return maps


_NC_CACHE = {}


def _run(inp, n_cores=NCORES):
    SEQ = np.asarray(inp["x_prompt"]).shape[1]
    NQ = 4
    NB = SEQ // (NQ * 128)
    L = NB * 128
    key = (NB,)
    if key not in _NC_CACHE:
        _NC_CACHE[key] = build(NB, NL=2, ffw=True, NQ=NQ)
    nc = _NC_CACHE[key]
    maps = prep_inputs(inp, NB, NL=2, ffw=True, NQ=NQ)[:n_cores]
    res = run_bass_kernel_spmd(nc, maps, core_ids=list(range(n_cores)))
    R = res.results
    f32 = lambda a: np.asarray(a, dtype=np.float32)
    nb = min(2, n_cores)
    y_prompt = np.stack([f32(R[b]["y_out"])[:NQ * L] for b in range(nb)], 0)
    y_sample = np.concatenate([f32(R[r]["y_out"])[NQ * L + 16:NQ * L + 20] for r in range(n_cores)], 0)[:, None, :]
    pk = np.stack([f32(R[b]["o_pk"]).reshape(2, 128, 2, 128) for b in range(nb)], 1)
    pv = np.stack([f32(R[b]["o_pv"]).reshape(2, 128, 2, 128) for b in range(nb)], 1)
    pre = np.stack([f32(R[b]["o_pssm"])[:, 0].reshape(2, 32, 64) for b in range(nb)], 1)
    pim = np.stack([f32(R[b]["o_pssm"])[:, 1].reshape(2, 32, 64) for b in range(nb)], 1)
    ppool = np.stack([f32(R[b]["o_ppool"]) for b in range(nb)], 1)
    sk = np.concatenate([f32(R[r]["o_sk"]).reshape(2, 4, 128, 2, 128) for r in range(n_cores)], 1)
    sv = np.concatenate([f32(R[r]["o_sv"]).reshape(2, 4, 128, 2, 128) for r in range(n_cores)], 1)
    sre = np.concatenate([f32(R[r]["o_sssm"])[:, 0].reshape(2, 4, 32, 64) for r in range(n_cores)], 1)
    sim = np.concatenate([f32(R[r]["o_sssm"])[:, 1].reshape(2, 4, 32, 64) for r in range(n_cores)], 1)
    spool = np.concatenate([f32(R[r]["o_spool"]) for r in range(n_cores)], 1)
    return (y_prompt, y_sample, pk, pv, pre, pim, ppool, sk, sv, sre, sim, spool)


def kernel(**inputs):
    return _run(inputs, NCORES)
```

```python
import numpy as np
import concourse.bass as bass
import concourse.mybir as mybir
from concourse.bass_utils import run_bass_kernel_spmd

F32 = mybir.dt.float32
BF16 = mybir.dt.bfloat16
I32 = mybir.dt.int32
ALU = mybir.AluOpType
AF = mybir.ActivationFunctionType

D = 2048
NQ = 1024
NKV = 256
SSMW = 512
POOLW = 512
INW = 2560
DFF = 8192
NCORES = 8
CL = 64
EPS = 1e-6
NEG = -30000.0
PI = float(np.pi)


class Sched:
    ENGS = ["pe", "act", "dve", "pool", "sp"]

    def __init__(self, nc, n_sp_slots=60, n_pool_slots=24):
        self.nc = nc
        self.ops = []
        self.by_eng = {e: [] for e in self.ENGS}
        self.last_w = {}
        self.readers = {}
        self.nslots = {"sp": n_sp_slots, "pool": n_pool_slots}
        self.ndma = {"sp": 0, "pool": 0}

    def op(self, eng, fn, reads=(), writes=(), dma=False, cc=False):
        oid = len(self.ops)
        deps = set()
        for r in reads:
            w = self.last_w.get(r)
            if w is not None:
                deps.add(w)
        for r in writes:
            w = self.last_w.get(r)
            if w is not None:
                deps.add(w)
            for rid in self.readers.get(r, {}).values():
                deps.add(rid)
        o = dict(id=oid, eng=eng, fn=fn, deps=deps, dma=dma, marked=False)
        if cc:
            o["dma"] = True
            dma = True
            self.ncc = getattr(self, "ncc", 0) + 1
            o["q"] = "cc"
            o["slot"] = 0
            o["val"] = self.ncc
        elif dma:
            k = self.ndma[eng]
            self.ndma[eng] += 1
            o["q"] = eng
            o["slot"] = k % self.nslots[eng]
            o["val"] = 16 * (k // self.nslots[eng] + 1)
        self.ops.append(o)
        self.by_eng[eng].append(o)
        for r in reads:
            self.readers.setdefault(r, {})[("d", oid) if dma else eng] = oid
        for r in writes:
            self.last_w[r] = oid
            self.readers[r] = {}
        return oid

    def fence(self, pred, newkeys, eng="sp", extra_reads=()):
        keys = [k for k in set(list(self.last_w.keys()) + list(self.readers.keys())) if pred(k)]
        if getattr(self, "nofence", False):
            return None
        return self.op(eng, lambda e: e.nop(), reads=list(extra_reads), writes=keys + list(newkeys))

    def emit(self):
        nc = self.nc
        ops = self.ops
        for o in ops:
            for p in o["deps"]:
                po = ops[p]
                if po["dma"]:
                    continue
                if po["eng"] == "pe" and o["eng"] == "pe":
                    continue
                po["marked"] = True
        cum = {}
        cnt = {e: 0 for e in self.ENGS}
        for e in self.ENGS:
            for o in self.by_eng[e]:
                if o["marked"] and not o["dma"]:
                    cnt[e] += 1
                o["cum"] = cnt[e]
        esem = {e: nc.alloc_semaphore("s_" + e) for e in self.ENGS}
        dsem = {q: [nc.alloc_semaphore(f"d_{q}_{i}") for i in range(self.nslots[q])] for q in ("sp", "pool")}
        dsem["cc"] = [nc.alloc_semaphore("d_cc")]
        handles = {"pe": nc.tensor, "act": nc.scalar, "dve": nc.vector, "pool": nc.gpsimd, "sp": nc.sync}

        def run(eng, e):
            waited = {}
            for o in self.by_eng[eng]:
                need = {}
                for p in o["deps"]:
                    po = ops[p]
                    if po["dma"]:
                        key = ("d", po["q"], po["slot"])
                        v = po["val"]
                    else:
                        if po["eng"] == "pe" and eng == "pe":
                            continue
                        key = ("e", po["eng"])
                        v = po["cum"]
                    if v > need.get(key, 0):
                        need[key] = v
                if o["dma"] and o["q"] != "cc":
                    if o["val"] > 16:
                        key = ("d", o["q"], o["slot"])
                        need[key] = max(need.get(key, 0), o["val"] - 16)
                for key, v in need.items():
                    if waited.get(key, 0) >= v:
                        continue
                    waited[key] = v
                    sem = esem[key[1]] if key[0] == "e" else dsem[key[1]][key[2]]
                    e.wait_ge(sem, v)
                ins = o["fn"](e)
                if o["dma"]:
                    ins.then_inc(dsem[o["q"]][o["slot"]], 1 if o["q"] == "cc" else 16)
                elif o["marked"]:
                    ins.then_inc(esem[eng], 1)

        with nc.Block() as block:
            @block.tensor
            def _(e):
                run("pe", e)

            @block.scalar
            def _(e):
                run("act", e)

            @block.vector
            def _(e):
                run("dve", e)

            @block.gpsimd
            def _(e):
                run("pool", e)

            @block.sync
            def _(e):
                run("sp", e)


def col_groups(T):
    n = (T + 511) // 512
    nb = (T - 32) // 128
    per = [nb // n + (1 if i < nb % n else 0) for i in range(n)]
    gs = []
    c = 0
    for i, p in enumerate(per):
        w = p * 128 + (32 if i == n - 1 else 0)
        gs.append((c, c + w))
        c += w
    assert c == T
    return gs


def build(NB, NL=2, dbg=(), stop=None, ffw=True, NQ=4):
    T = NB * 128 + 32
    L = NB * 128
    E0 = NB * 128
    NBLK = NB + 1
    CG = col_groups(T)
    CG0 = list(CG)
    CG1 = [(c, min(c + 512, L)) for c in range(0, L, 512)]
    NCH = 2 * NB
    NSQ = int(np.log2(NCH))
    assert 2 ** NSQ == NCH
    nc = bass.Bass("TRN2", target_bir_lowering=False)
    nc.allow_low_precision("bf16 matmul operands by design (reference tolerance measured for bf16)")
    S = Sched(nc)
    S.nofence = 'nofence' in dbg
    A = nc.alloc_sbuf_tensor

    def din(name, shape, dt=F32):
        return nc.dram_tensor(name, list(shape), dt, kind="ExternalInput")

    def dout(name, shape, dt=F32):
        return nc.dram_tensor(name, list(shape), dt, kind="ExternalOutput")

    TT = NQ * L + 32
    x_in = din("x_in", [TT, D])
    w_in = din("w_in", [NL, D, INW]); w_out = din("w_out", [NL, D, D])
    if ffw:
        w_ff1 = din("w_ff1", [NL, D, DFF]); w_ff2 = din("w_ff2", [NL, DFF, D])
    g_mix = din("g_mix", [2, D]); g_ffn = din("g_ffn", [2, D])
    g_q = din("g_q", [2, 128]); g_k = din("g_k", [2, 128]); sinks = din("sinks", [2, 8])
    A_re = din("A_re", [2, 2048]); A_im = din("A_im", [2, 2048]); log_dt = din("log_dt", [2, 32])
    B_re = din("B_re", [2, 2048, 16]); B_im = din("B_im", [2, 2048, 16])
    C_re = din("C_re", [2, 32, 16, 64]); C_im = din("C_im", [2, 32, 16, 64])
    D_skip = din("D_skip", [2, 512]); w_glu = din("w_glu", [2, 512, 512]); b_glu = din("b_glu", [2, 512])
    w_pool = din("w_pool", [2, 4, 128, 128]); pool_scale = din("pool_scale", [2, 512])
    g_oa = din("g_out_attn", [2, 1024]); g_os = din("g_out_ssm", [2, 512]); g_op = din("g_out_pool", [2, 512])
    cache_k = din("cache_k", [2, 4, 128, 256]); cache_v = din("cache_v", [2, 4, 128, 256])
    st_re = din("st_re", [2, 64, 128]); st_im = din("st_im", [2, 64, 128])
    st_pool = din("st_pool", [2, 4, 15, 512])
    c_rope = din("c_rope", [NQ, 2, 32, T])
    c_mask = din("c_mask", [3, 128, 512])
    c_maskE = din("c_maskE", [32, 128])
    c_maskS = din("c_maskS", [4, 32, 4])
    c_prot = din("c_prot", [128, 32]); c_ident = din("c_ident", [128, 128])
    c_sel = din("c_sel", [128, 24])
    c_jtab = din("c_jtab", [128, 16 * (CL + 1)])
    c_pinv = din("c_pinv", [128, 4 * 16])
    c_psel = din("c_psel", [16, 4])

    xs = nc.dram_tensor("xs_scratch", [TT, D], F32)
    y_out = dout("y_out", [TT, D])
    o_pk = dout("o_pk", [2, 128, 256]); o_pv = dout("o_pv", [2, 128, 256])
    o_pssm = dout("o_pssm", [2, 2, 16, 128]); o_ppool = dout("o_ppool", [2, 15, 512])
    o_sk = dout("o_sk", [2, 4, 128, 256]); o_sv = dout("o_sv", [2, 4, 128, 256])
    o_sssm = dout("o_sssm", [2, 2, 64, 128]); o_spool = dout("o_spool", [2, 4, 15, 512])
    XW = 640
    xch_in = nc.dram_tensor("xch_in", [128, XW], F32)
    xch_out = nc.dram_tensor("xch_out", [NCORES * 128, XW], F32)

    actT = A("actT", [128, 16, T], BF16)
    PROJW = 8 * T + 2 * T + NBLK * 256 + 4 * T
    HIDW = 16 * T
    projbuf = A("projbuf", [128, max(PROJW, HIDW)], BF16)
    o_ = 0
    qT = projbuf[:, o_:o_ + 8 * T].rearrange("p (h t) -> p h t", h=8); o_ += 8 * T
    kT = projbuf[:, o_:o_ + 2 * T].rearrange("p (h t) -> p h t", h=2); o_ += 2 * T
    vbf = projbuf[:, o_:o_ + NBLK * 256].rearrange("p (b c) -> p b c", b=NBLK); o_ += NBLK * 256
    uT = projbuf[:, o_:o_ + 4 * T].rearrange("p (h t) -> p h t", h=4); o_ += 4 * T
    hidT = projbuf[:, 0:HIDW].rearrange("p (m t) -> p m t", m=16)
    xpT = A("xpT", [128, 4, 16 + T], BF16)
    wbuf = [A(f"wbuf{i}", [128, 16, 512], BF16) for i in range(2)]
    xblk = A("xblk", [128, D], F32)
    xn = A("xn", [128, D], BF16)
    ident = A("ident", [128, 128], BF16); identf = A("identf", [128, 128], F32)
    ones_b = A("ones_b", [128, 4, 128], BF16)
    prot = A("prot", [128, 32], BF16)
    rope = A("rope", [32, 2, T], F32)
    masks = A("masks", [128, 3, 512], BF16)
    maskE = A("maskE", [32, 128], BF16); maskS = A("maskS", [32, 4, 4], BF16)
    sel = A("sel", [128, 24], F32)
    vecs = A("vecs", [128, 48], F32)
    gvec = A("gvec", [128, 16], F32)
    qraw = A("qraw", [128, 512], F32); sqb = A("sqb", [128, 512], BF16)
    rtmp = A("rtmp", [128, 512], F32); rtmp2 = A("rtmp2", [128, 512], F32)
    small = A("small", [128, 16], F32)
    relu_t = [A(f"relu_t{i}", [128, 512], BF16) for i in range(2)]
    cosT = A("cosT", [128, 16, CL + 1], F32); sinT = A("sinT", [128, 16, CL + 1], F32)
    Dk = A("Dk", [128, 16, CL], F32); Dk16 = A("Dk16", [128, 16, 16], F32)
    sm = A("sm", [128, 28, 16], F32)
    BbT = A("BbT", [128, 32, 128], BF16)
    CTp_r = A("CTp_r", [128, 16, 128], F32); CTp_ni = A("CTp_ni", [128, 16, 128], F32)
    scur = A("scur", [128, 32], F32)
    Gst = A("Gst", [128, NCH + 1, 32], F32)
    sst = A("sst", [128, NCH + 2, 32], F32)
    esink = A("esink", [1, 8, 128], BF16); sinkrow = A("sinkrow", [1, 16], F32)
    hkL = [A(f"hk{i}", [128, 2, 128], BF16) for i in range(2)]; hvL = [A(f"hv{i}", [128, 256], BF16) for i in range(2)]
    phL = [A(f"ph{i}", [128, 4, 16], BF16) for i in range(2)]; sfin = A("sfin", [128, 2, 32], F32)
    kctok = A("kctok", [128, 256], BF16); kcT = A("kcT", [128, 128], BF16); vc = A("vc", [128, 256], BF16)
    es = A("es", [128, 8], BF16)
    h0s = A("h0s", [128, 2, 64], F32); hs = A("hs", [128, 2, 64], F32)
    sttok = A("sttok", [64, 2, 128], F32)
    stp = A("stp", [16, 512], BF16); psel = A("psel", [16, 4], BF16)
    wglu = A("wglu", [128, 4, 512], BF16); wpool = A("wpool", [128, 4, 128], BF16)
    pmeta = A("pmeta", [128, 4, 32], F32); pmeta2 = A("pmeta2", [128, 4, 32], F32); pinvE = A("pinvE", [128, 4, 16], F32)
    outst = A("outst", [128, 512], F32)

    scr = wbuf[1][:, :, :].rearrange("p a b -> p (a b)").bitcast(F32)
    z_re = scr[:, 0:1024]; z_im = scr[:, 1024:2048]; k_re = scr[:, 2048:3072]; k_im = scr[:, 3072:4096]
    hE_re = rtmp[:, :].rearrange("p (a b) -> p a b", a=16)
    hE_im = rtmp2[:, :].rearrange("p (a b) -> p a b", a=16)
    xst = xblk[:, 0:XW]; stage = xblk[:, 640:640 + 576]; acc = xblk[:, 1280:1280 + 576]
    ebuf = [xn[:, i * 1024:(i + 1) * 1024].rearrange("p (a b) -> p a b", a=2) for i in range(2)]
    XBK = [("xio", j) for j in range(4)]
    xio = [xblk[:, j * 512:(j + 1) * 512] for j in range(4)]

    if "psep" in dbg:
        psA = [nc.alloc_psum_tensor(f"psA{i}", [128, 512], F32) for i in range(4)]
        psX = None
    else:
        psX = nc.alloc_psum_tensor("psX", [128, 4, 512], F32)
        psA = [psX[:, i, :] for i in range(4)]
    psB = [nc.alloc_psum_tensor(f"psB{i}", [128, 512], F32) for i in range(3)]
    psT = nc.alloc_psum_tensor("psT", [128, 1024], BF16)
    psT_alt = psB[2][:, :].bitcast(BF16)

    S._taps = []
    QS = {"q": 0, "l": 0}

    def dma(q, out, in_, reads, writes):
        return S.op(q, lambda e: e.dma_start(out=out, in_=in_), reads=reads, writes=writes, dma=True)

    def dma_slow(q, out, in_, reads, writes):
        return S.op(q, lambda e: e.dma_start(out=out, in_=in_, allow_slow_non_contiguous=True),
                    reads=reads, writes=writes, dma=True)

    def dma_tp(q, out, src_flat, nt, reads, writes):
        v = src_flat.rearrange("(t p) -> p t", p=128)
        for t0 in range(0, nt, 4):
            t1 = min(nt, t0 + 4)
            dma_slow(q, out[:, t0:t1], v[:, t0:t1], reads, writes)

    def tap(name, ap, dt, reads):
        if name not in dbg:
            return
        t = dout("dbg_%s_%d%d" % (name, QS["q"], QS["l"]), list(ap.shape), dt)
        full = t.ap()
        S.op("sp", lambda e: e.dma_start(out=full, in_=ap), reads=reads, writes=[("out", "dbg", name, QS["q"], QS["l"])], dma=True)

    def V(fn, reads, writes):
        return S.op("dve", fn, reads, writes)

    def G(fn, reads, writes):
        return S.op("pool", fn, reads, writes)

    def ACT(fn, reads, writes):
        return S.op("act", fn, reads, writes)

    def PE(fn, reads, writes):
        return S.op("pe", fn, reads, writes)

    def tt(eng, out, a, b, op, reads, writes):
        return S.op(eng, lambda e: e.tensor_tensor(out, a, b, op), reads, writes)

    def mm(out, lhsT, rhs, start, stop, reads, writes):
        return S.op("pe", lambda e: e.matmul(out, lhsT, rhs, start=start, stop=stop), reads, writes)

    def tr(out, in_, idn, reads, writes):
        return S.op("pe", lambda e: e.transpose(out, in_, idn), reads, writes)

    def act(out, in_, func, reads, writes, **kw):
        return S.op("act", lambda e: e.activation(out, in_, func, **kw), reads, writes)

    def cp(eng, out, in_, reads, writes):
        if eng == "act":
            return S.op("act", lambda e: e.copy(out, in_), reads, writes)
        return S.op(eng, lambda e: e.tensor_copy(out, in_), reads, writes)

    def ts(eng, out, in0, s1, s2, op0, op1, reads, writes):
        if op1 is None:
            return S.op(eng, lambda e: e.tensor_scalar(out, in0, s1, 1.0, op0, ALU.mult), reads, writes)
        return S.op(eng, lambda e: e.tensor_scalar(out, in0, s1, s2, op0, op1), reads, writes)

    def stt(out, in0, sc, in1, op0, op1, reads, writes):
        return S.op("dve", lambda e: e.scalar_tensor_tensor(out, in0, sc, in1, op0, op1), reads, writes)

    def ms(eng, out, val, reads, writes):
        return S.op(eng, lambda e: e.memset(out, val), reads, writes)

    def rcp(out, in_, reads, writes):
        return S.op("dve", lambda e: e.reciprocal(out, in_), reads, writes)

    def scan(out, d0, d1, reads, writes):
        return S.op("dve", lambda e: e.tensor_tensor_scan(out, d0, d1, 0.0, ALU.mult, ALU.add), reads, writes)

    dma("pool", ident[:, :], c_ident[:, :], [], ["ident"])
    dma("sp", identf[:, :], c_ident[:, :], [], ["identf"])
    dma("pool", prot[:, :], c_prot[:, :], [], ["prot"])
    for i in range(3):
        dma("pool", masks[:, i, :], c_mask[i, :, :], [], ["masks"])
    dma("pool", maskE[:, :], c_maskE[:, :], [], ["maskE"])
    if "noconst" not in dbg:
        for s_ in range(4):
            dma("pool", maskS[:, s_, :], c_maskS[s_, :, :], [], ["maskS"])
        dma("pool", psel[:, :], c_psel[:, :], [], ["psel"])
    dma("sp", sel[:, :], c_sel[:, :], [], ["sel"])
    dma("sp", pinvE[:, :, :], c_pinv[:, :].rearrange("p (a b) -> p a b", a=4), [], ["pinvE"])
    for i, v in enumerate([1.0 / 128, 1.0 / 1024, 1.0 / 512, 1.0]):
        G(lambda e, i=i, v=v: e.memset(ones_b[:, i, :], v), [], ["ones"])
    G(lambda e: e.memset(stp[:, :], 0.0), [], ["stp"])

    def load_w(src_ap, i):
        key = f"W{i}"
        S.op("pool", lambda e: e.dma_start(out=wbuf[i][:, :, :], in_=src_ap), reads=[], writes=[key], dma=True)
        return key

    def w_tile(wt, l, rows0, col0):
        return wt[l, rows0:rows0 + 2048, col0:col0 + 512].rearrange("(c p) m -> p c m", p=128)

    def drow(blk):
        return QS["q"] * L + blk * 128 if blk < NB else NQ * L

    def xdk(blk):
        return ("xd", QS["q"], blk) if blk < NB else ("xd", "E")

    def actT_reads(c0, c1):
        return [("actT", b) for b in range(NBLK) if b * 128 < c1 and min((b + 1) * 128, T) > c0]

    def blk_of_cols(c0, c1):
        return [b for b in range(NBLK) if b * 128 < c1 and min((b + 1) * 128, T) > c0]

    def norm_to_actT(xsrc, gsrc, l):
        dma_tp("sp", gvec, gsrc[l, :], 16, [], ["gvec"])
        for blk in range(NBLK if QS["q"] == 0 else NB):
            P = 128 if blk < NB else 32
            r0 = blk * 128
            dr = drow(blk)
            dma("sp", xblk[0:P, :], xsrc[dr:dr + P, :], [xdk(blk)], XBK)
            V(lambda e, P=P: e.memset(small[0:P, 0:1], 0.0), [], ["small0"])
            ACT(lambda e, P=P: e.activation(xn[0:P, :], xblk[0:P, :], AF.Square, accum_out=small[0:P, 0:1]),
                XBK + ["small0"], ["xn", "small0"])
            ACT(lambda e, P=P: e.activation(small[0:P, 1:2], small[0:P, 0:1], AF.Ln, bias=EPS, scale=1.0 / D),
                ["small0"], ["small1"])
            ACT(lambda e, P=P: e.activation(small[0:P, 2:3], small[0:P, 1:2], AF.Exp, scale=-0.5),
                ["small1"], ["small2"])
            if "n2" in dbg:
                V(lambda e, P=P: e.tensor_scalar(xn[0:P, :], xblk[0:P, :], small[0:P, 2:3], 1.0, ALU.mult, ALU.mult),
                  XBK + ["small2", "xn"], ["xn"])
            elif "n3" in dbg:
                ACT(lambda e, P=P: e.activation(xn[0:P, :], xblk[0:P, :], AF.Copy, scale=small[0:P, 2:3]),
                    XBK + ["small2", "xn"], ["xn"])
            elif "n4" in dbg:
                pass
            else:
                V(lambda e, P=P: e.tensor_scalar(xn[0:P, :], xblk[0:P, :], small[0:P, 2:3], 1.0, ALU.mult, ALU.mult),
                  XBK + ["small2", "xn"], ["xn"])
            for c4 in range(4 if "n1" not in dbg else 0):
                pk = ("psT" if c4 % 2 == 0 else "psB2")
                pst = psT[:, 0:512] if c4 % 2 == 0 else psT_alt[:, 0:512]
                for i in range(4):
                    kc = c4 * 4 + i
                    PE(lambda e, P=P, kc=kc, i=i, pst=pst: e.transpose(pst[:, i * 128:i * 128 + P],
                                                                       xn[0:P, kc * 128:(kc + 1) * 128], ident[0:P, 0:P]),
                       ["xn", "ident"], [pk])
                for i in range(4):
                    kc = c4 * 4 + i
                    src = pst[:, i * 128:i * 128 + P]
                    dst = actT[:, kc, r0:r0 + P]
                    if "gplain" in dbg:
                        cp("act" if i % 2 == 0 else "dve", dst, src, [pk], [("actT", blk)])
                    elif (i % 2 == 0 or "gact" in dbg) and "gdve" not in dbg:
                        ACT(lambda e, src=src, dst=dst, kc=kc: e.activation(dst, src, AF.Copy, scale=gvec[:, kc:kc + 1]),
                            [pk, "gvec"], [("actT", blk)])
                    else:
                        V(lambda e, src=src, dst=dst, kc=kc: e.tensor_scalar(dst, src, gvec[:, kc:kc + 1], 1.0, ALU.mult, ALU.mult),
                          [pk, "gvec"], [("actT", blk)])

    def load_vecs(l):
        dma_slow("sp", vecs[:, 0:1], g_q[l, :].rearrange("(p o) -> p o", o=1), [], ["vecs"])
        dma_slow("sp", vecs[:, 1:2], g_k[l, :].rearrange("(p o) -> p o", o=1), [], ["vecs"])
        dma_tp("sp", vecs[:, 2:10], g_oa[l, :], 8, [], ["vecs"])
        for j, src in enumerate([g_os, g_op, D_skip, b_glu, pool_scale]):
            dma_slow("sp", vecs[:, 10 + 4 * j:14 + 4 * j], src[l, :].rearrange("(t p) -> p t", p=128), [], ["vecs"])
        V(lambda e: e.tensor_scalar(vecs[:, 30:34], vecs[:, 22:26], -1.0, 1.0, ALU.mult, ALU.mult), ["vecs"], ["vecs"])
        if "novecx" in dbg:
            return
        dma("pool", wglu[:, :, :], w_glu[l, :, :].rearrange("(c p) m -> p c m", p=128), [], ["wglu"])
        dma("pool", wpool[:, :, :], w_pool[l, :, :, :].rearrange("g c d -> c g d"), [], ["wpool"])
        dma("sp", sinkrow[0:1, 0:8], sinks[l:l + 1, :], [], ["sinkrow"])
        ACT(lambda e: e.activation(sinkrow[0:1, 8:16], sinkrow[0:1, 0:8], AF.Exp), ["sinkrow"], ["sinkrow"])
        V(lambda e: e.tensor_copy(esink[0:1, :, :], sinkrow[0:1, 8:16].unsqueeze(2).to_broadcast([1, 8, 128])),
          ["sinkrow"], ["esink"])

    def qk_finish(ps, n, c0, dst, gcol, wkey, fkey):
        ACT(lambda e: e.copy(qraw[:, 0:n], ps[:, 0:n]), [wkey], ["qraw"])
        G(lambda e: e.tensor_tensor(sqb[:, 0:n], qraw[:, 0:n], qraw[:, 0:n], ALU.mult), ["qraw"], ["sqb"])
        PE(lambda e: e.matmul(psB[0][:, 0:n], ones_b[:, 0, :], sqb[:, 0:n], start=True, stop=True),
           ["sqb", "ones"], ["psB0"])
        ACT(lambda e: e.activation(rtmp[:, 0:n], psB[0][:, 0:n], AF.Ln, bias=EPS, scale=1.0), ["psB0"], ["rtmp"])
        ACT(lambda e: e.activation(rtmp[:, 0:n], rtmp[:, 0:n], AF.Exp, scale=-0.5), ["rtmp"], ["rtmp"])
        V(lambda e: e.scalar_tensor_tensor(dst, qraw[:, 0:n], vecs[:, gcol:gcol + 1], rtmp[:, 0:n], ALU.mult, ALU.mult),
          ["qraw", "rtmp", "vecs"], [wkey + "_d"])
        PE(lambda e: e.matmul(psB[1][0:32, 0:n], prot[:, :], dst, start=True, stop=True), [wkey + "_d", "prot"], ["psB1"])
        V(lambda e: e.tensor_tensor(rtmp2[0:32, 0:n], psB[1][0:32, 0:n], rope[:, 1, c0:c0 + n], ALU.mult),
          ["psB1", "rope"], ["rtmp2"])
        V(lambda e: e.tensor_tensor(rtmp[0:32, 0:n], dst[0:32], rope[:, 0, c0:c0 + n], ALU.mult),
          [wkey + "_d", "rope", "rtmp"], ["rtmp"])
        V(lambda e: e.tensor_tensor(dst[0:32], rtmp[0:32, 0:n], rtmp2[0:32, 0:n], ALU.add),
          ["rtmp", "rtmp2", wkey + "_d"], [wkey + "_d", fkey])

    def w_in_phase(l):
        order = [3, 4, 0, 1, 2]
        for oi, ti in enumerate(order):
            bi = oi % 2
            wkey = load_w(w_tile(w_in, l, 0, ti * 512), bi)
            wb = wbuf[bi]
            for gi, (c0, c1) in enumerate(CG):
                n = c1 - c0
                for mt in range(4):
                    if ti == 2 and mt >= 2:
                        break
                    ps = psA[mt]
                    pk = f"psA{mt}"
                    for kc in range(16):
                        PE(lambda e, ps=ps, kc=kc, mt=mt, wb=wb, c0=c0, c1=c1, n=n:
                           e.matmul(ps[:, 0:n], wb[:, kc, mt * 128:(mt + 1) * 128], actT[:, kc, c0:c1],
                                    start=(kc == 0), stop=(kc == 15)),
                           [wkey] + actT_reads(c0, c1), [pk])
                    if ti in (0, 1):
                        h = ti * 4 + mt
                        qk_finish(ps, n, c0, qT[:, h, c0:c1], 0, pk, ("qT", h, gi))
                    elif ti == 2:
                        qk_finish(ps, n, c0, kT[:, mt, c0:c1], 1, pk, ("kT", mt, gi))
                    elif ti == 3:
                        ACT(lambda e, ps=ps, mt=mt, c0=c0, c1=c1, n=n: e.copy(uT[:, mt, c0:c1], ps[:, 0:n]),
                            [pk], [("uT", gi)])
                    else:
                        V(lambda e, ps=ps, mt=mt, c0=c0, c1=c1, n=n:
                          e.tensor_copy(xpT[:, mt, 16 + c0:16 + c1], ps[:, 0:n]), [pk], [("xpT", gi)])
            if ti == 2:
                for blk in range(NBLK):
                    P = 128 if blk < NB else 32
                    r0 = blk * 128
                    ps = psA[2 + blk % 2]
                    pk = f"psA{2 + blk % 2}"
                    for kc in range(16):
                        PE(lambda e, ps=ps, kc=kc, wb=wb, r0=r0, P=P:
                           e.matmul(ps[0:P, 0:256], actT[:, kc, r0:r0 + P], wb[:, kc, 256:512],
                                    start=(kc == 0), stop=(kc == 15)),
                           [wkey, ("actT", blk)], [pk])
                    ACT(lambda e, ps=ps, blk=blk, P=P: e.copy(vbf[0:P, blk, :], ps[0:P, 0:256]), [pk], [("vbf", blk)])

    SMI = dict(rho=0, aC_r=1, aC_i=2, aL_r=3, aL_i=4, a1_r=5, a1_i=6, f_r=7, f_i=8, are=9, aim=10, dt=11, dre=12, th=13,
               t0=14, t1=15, t2=16, t3=17)

    def smv(name):
        return sm[:, SMI[name], :]

    def ssm_prep(l):
        SK = ["z_re", "z_im", "k_re", "k_im"]
        dma_tp("sp", smv("are"), A_re[l, :], 16, [], ["sm_in"])
        dma_tp("sp", smv("aim"), A_im[l, :], 16, [], ["sm_in"])
        ld = log_dt[l, :].rearrange("(t h) -> h t", h=2)
        dma_slow("sp", sm[0:64, SMI["dt"], :], ld[0:1, :].partition_broadcast(64), [], ["sm_in"])
        dma_slow("sp", sm[64:128, SMI["dt"], :], ld[1:2, :].partition_broadcast(64), [], ["sm_in"])
        ACT(lambda e: e.activation(smv("dt"), smv("dt"), AF.Exp), ["sm_in"], ["sm_dt"])
        V(lambda e: e.tensor_tensor(smv("dre"), smv("dt"), smv("are"), ALU.mult), ["sm_dt", "sm_in"], ["sm_dre"])
        V(lambda e: e.tensor_tensor(smv("th"), smv("dt"), smv("aim"), ALU.mult), ["sm_dt", "sm_in"], ["sm_th"])
        ACT(lambda e: e.activation(smv("rho"), smv("dre"), AF.Exp), ["sm_dre"], ["sm_rho"])
        W65 = 16 * (CL + 1)
        jt = scr[:, 0:W65].rearrange("p (a b) -> p a b", a=16)
        ang = scr[:, W65:2 * W65].rearrange("p (a b) -> p a b", a=16)
        rp = scr[:, 2 * W65:3 * W65].rearrange("p (a b) -> p a b", a=16)
        tq = scr[:, 0:W65]
        angf = scr[:, W65:2 * W65]
        dma("sp", scr[:, 0:W65], c_jtab[:, :], [], SK + ["W1"])
        V(lambda e: e.tensor_tensor(ang, jt, smv("th").unsqueeze(2).to_broadcast([128, 16, CL + 1]), ALU.mult),
          SK + ["sm_th"], SK)
        V(lambda e: e.tensor_tensor(rp, jt, smv("dre").unsqueeze(2).to_broadcast([128, 16, CL + 1]), ALU.mult),
          SK + ["sm_dre"], SK)
        ACT(lambda e: e.activation(scr[:, 2 * W65:3 * W65], scr[:, 2 * W65:3 * W65], AF.Exp), SK, SK)
        tqi = tq.bitcast(I32)
        V(lambda e: e.tensor_scalar(tq, angf, 1.0 / (2 * PI), 1.0, ALU.mult, ALU.mult), SK, SK)
        V(lambda e: e.tensor_copy(tqi, tq), SK, SK)
        V(lambda e: e.tensor_copy(tq, tqi), SK, SK)
        V(lambda e: e.scalar_tensor_tensor(angf, tq, -2 * PI, angf, ALU.mult, ALU.add), SK, SK)
        V(lambda e: e.tensor_scalar(angf, angf, -PI, PI, ALU.max, ALU.min), SK, SK)
        ACT(lambda e: e.activation(sinT[:, :, :].rearrange("p a b -> p (a b)"), angf, AF.Sin), SK, ["sinT"])
        ACT(lambda e: e.activation(tq, angf, AF.Abs), SK, SK)
        ACT(lambda e: e.activation(cosT[:, :, :].rearrange("p a b -> p (a b)"), tq, AF.Sin, bias=PI / 2, scale=-1.0),
            SK, ["cosT"])
        V(lambda e: e.tensor_tensor(smv("a1_r"), rp[:, :, 1], cosT[:, :, 1], ALU.mult), SK + ["cosT"], ["sm_a1"])
        V(lambda e: e.tensor_tensor(smv("a1_i"), rp[:, :, 1], sinT[:, :, 1], ALU.mult), SK + ["sinT"], ["sm_a1"])
        V(lambda e: e.tensor_tensor(smv("aC_r"), rp[:, :, CL], cosT[:, :, CL], ALU.mult), SK + ["cosT"], ["sm_aC"])
        V(lambda e: e.tensor_tensor(smv("aC_i"), rp[:, :, CL], sinT[:, :, CL], ALU.mult), SK + ["sinT"], ["sm_aC"])
        V(lambda e: e.tensor_copy(Dk[:, :, :], smv("rho").unsqueeze(2).to_broadcast([128, 16, CL])), ["sm_rho"], ["Dk"])
        V(lambda e: e.memset(Dk[:, :, 0:1], 0.0), ["Dk"], ["Dk"])
        V(lambda e: e.tensor_copy(Dk16[:, :, :], smv("rho").unsqueeze(2).to_broadcast([128, 16, 16])), ["sm_rho"], ["Dk16"])
        V(lambda e: e.memset(Dk16[:, :, 0:1], 0.0), ["Dk16"], ["Dk16"])
        G(lambda e: e.tensor_copy(smv("aL_r"), smv("aC_r")), ["sm_aC"], ["sm_aL"])
        G(lambda e: e.tensor_copy(smv("aL_i"), smv("aC_i")), ["sm_aC"], ["sm_aL"])
        for _ in range(NSQ):
            tt("pool", smv("t0"), smv("aL_r"), smv("aL_r"), ALU.mult, ["sm_aL"], ["sm_t0"])
            tt("pool", smv("t1"), smv("aL_i"), smv("aL_i"), ALU.mult, ["sm_aL"], ["sm_t1"])
            tt("pool", smv("t2"), smv("aL_r"), smv("aL_i"), ALU.mult, ["sm_aL"], ["sm_t2"])
            tt("pool", smv("aL_r"), smv("t0"), smv("t1"), ALU.subtract, ["sm_t0", "sm_t1", "sm_aL"], ["sm_aL"])
            tt("pool", smv("aL_i"), smv("t2"), smv("t2"), ALU.add, ["sm_t2", "sm_aL"], ["sm_aL"])
        tt("dve", smv("t0"), smv("are"), smv("are"), ALU.mult, ["sm_in", "sm_t0"], ["sm_t0"])
        tt("dve", smv("t1"), smv("aim"), smv("aim"), ALU.mult, ["sm_in", "sm_t1"], ["sm_t1"])
        tt("dve", smv("t0"), smv("t0"), smv("t1"), ALU.add, ["sm_t0", "sm_t1"], ["sm_t0"])
        V(lambda e: e.reciprocal(smv("t0"), smv("t0")), ["sm_t0"], ["sm_t0"])
        V(lambda e: e.tensor_scalar(smv("t1"), smv("a1_r"), -1.0, 1.0, ALU.add, ALU.mult), ["sm_a1", "sm_t1"], ["sm_t1"])
        tt("dve", smv("t2"), smv("t1"), smv("are"), ALU.mult, ["sm_t1", "sm_in", "sm_t2"], ["sm_t2"])
        tt("dve", smv("t3"), smv("a1_i"), smv("aim"), ALU.mult, ["sm_a1", "sm_in"], ["sm_t3"])
        tt("dve", smv("t2"), smv("t2"), smv("t3"), ALU.add, ["sm_t2", "sm_t3"], ["sm_t2"])
        tt("dve", smv("f_r"), smv("t2"), smv("t0"), ALU.mult, ["sm_t2", "sm_t0"], ["sm_f"])
        tt("dve", smv("t2"), smv("a1_i"), smv("are"), ALU.mult, ["sm_a1", "sm_in", "sm_t2"], ["sm_t2"])
        tt("dve", smv("t3"), smv("t1"), smv("aim"), ALU.mult, ["sm_t1", "sm_in", "sm_t3"], ["sm_t3"])
        tt("dve", smv("t2"), smv("t2"), smv("t3"), ALU.subtract, ["sm_t2", "sm_t3"], ["sm_t2"])
        tt("dve", smv("f_i"), smv("t2"), smv("t0"), ALU.mult, ["sm_t2", "sm_t0"], ["sm_f"])
        Bs_r = z_re[:, 0:256].rearrange("p (a b) -> p a b", a=16)
        Bs_i = z_re[:, 256:512].rearrange("p (a b) -> p a b", a=16)
        Bb_r = z_re[:, 512:768].rearrange("p (a b) -> p a b", a=16)
        Bb_i = z_re[:, 768:1024].rearrange("p (a b) -> p a b", a=16)
        Bt = z_im[:, 0:256].rearrange("p (a b) -> p a b", a=16)
        Mp = [k_re.bitcast(BF16), k_im.bitcast(BF16)]
        dma("sp", Bs_r, B_re[l, :, :].rearrange("(t p) c -> p t c", p=128), [], SK)
        dma("sp", Bs_i, B_im[l, :, :].rearrange("(t p) c -> p t c", p=128), [], SK)
        fr = smv("f_r").unsqueeze(2).to_broadcast([128, 16, 16])
        fi = smv("f_i").unsqueeze(2).to_broadcast([128, 16, 16])
        tt("dve", Bb_r, Bs_r, fr, ALU.mult, SK + ["sm_f"], SK)
        tt("dve", Bt, Bs_i, fi, ALU.mult, SK + ["sm_f"], SK)
        tt("dve", Bb_r, Bb_r, Bt, ALU.subtract, SK, SK)
        tt("dve", Bb_i, Bs_i, fr, ALU.mult, SK + ["sm_f"], SK)
        tt("dve", Bt, Bs_r, fi, ALU.mult, SK + ["sm_f"], SK)
        tt("dve", Bb_i, Bb_i, Bt, ALU.add, SK, SK)
        for ri, Bb in enumerate((Bb_r, Bb_i)):
            V(lambda e, ri=ri: e.memset(Mp[ri], 0.0), SK, SK)
            for h in range(2):
                base = Mp[ri][64 * h:64 * h + 64, 0:1]
                for a in range(4):
                    dst = bass.AP(base.tensor, base.offset + 16 * h + 512 * a, [[base.ap[0][0], 64], [160, 4], [1, 16]])
                    src = Bb[64 * h:64 * h + 64, 4 * a:4 * a + 4, :]
                    cp("dve", dst, src, SK, SK)
        for ri in range(2):
            for t4 in range(4):
                pk = "psT"
                pst = psT[:, (t4 % 2) * 512:(t4 % 2 + 1) * 512]
                for i in range(4):
                    t = t4 * 4 + i
                    PE(lambda e, pst=pst, i=i, t=t, ri=ri: e.transpose(pst[:, i * 128:(i + 1) * 128],
                                                                       Mp[ri][:, t * 128:(t + 1) * 128], ident[:, :]),
                       SK + ["ident"], [pk])
                dst = BbT[:, ri * 16 + t4 * 4:ri * 16 + t4 * 4 + 4, :]
                V(lambda e, dst=dst, pst=pst: e.tensor_copy(dst, pst.rearrange("p (a b) -> p a b", a=4)),
                  [pk], ["BbT", "corr"])
        for ri, Csrc in enumerate((C_re, C_im)):
            Zf = scr[0:32, 0:2048].rearrange("p (a b) -> p a b", a=16)
            V(lambda e, Zf=Zf: e.memset(Zf, 0.0), SK, SK)
            cv_ = Csrc[l, :, :, :].rearrange("(t h) c n -> h c t n", h=2)
            dma("sp", Zf[0:16, :, 0:64], cv_[0, :, :, :], [], SK)
            dma("sp", Zf[16:32, :, 64:128], cv_[1, :, :, :], [], SK)
            CTp = CTp_r if ri == 0 else CTp_ni
            ms("dve", CTp[:, :, :], 0.0, [], ["CTp"])
            base = CTp[:, 0, 0:1]
            for t4 in range(4):
                ps = psB[2]
                for i in range(4):
                    t = t4 * 4 + i
                    tr(ps[:, i * 32:(i + 1) * 32], Zf[:, t, :], identf[0:32, 0:32], SK + ["identf"], ["psB2"])
                dst = bass.AP(base.tensor, base.offset + 512 * t4, [[base.ap[0][0], 128], [160, 4], [1, 32]])
                srcv = ps[:, 0:128].rearrange("p (a b) -> p a b", a=4)
                if ri == 0:
                    cp("dve", dst, srcv, ["psB2", "CTp"], ["CTp"])
                else:
                    ts("dve", dst, srcv, -1.0, None, ALU.mult, None, ["psB2", "CTp"], ["CTp"])

    def ssm_chunk(c):
        c0 = c * CL
        gi = [i for i, (a, b) in enumerate(CG) if a <= c0 < b][0]
        xr = psX[:, 0:2, :].rearrange("p a b -> p (a b)")
        xi = psX[:, 2:4, :].rearrange("p a b -> p (a b)")
        for ri in range(2):
            for t in range(16):
                mm(psX[:, 2 * ri + t // 8, (t % 8) * CL:(t % 8 + 1) * CL], BbT[:, ri * 16 + t, :], uT[:, t // 4, c0:c0 + CL],
                   True, True, ["BbT", ("uT", gi)], [f"psA{2 * ri}", f"psA{2 * ri + 1}"])
        C3 = cosT[:, :, 0:CL]; S3 = sinT[:, :, 0:CL]
        v3 = lambda ap: ap.rearrange("p (a b) -> p a b", a=16)
        XR = ["psA0", "psA1"]; XI = ["psA2", "psA3"]
        tt("dve", v3(z_re), v3(xr), C3, ALU.mult, XR + ["cosT", "z_re"], ["z_re"])
        tt("dve", v3(k_re), v3(xi), S3, ALU.mult, XI + ["sinT", "k_re"], ["k_re"])
        tt("dve", z_re, z_re, k_re, ALU.add, ["z_re", "k_re"], ["z_re"])
        tt("dve", v3(z_im), v3(xi), C3, ALU.mult, XI + ["cosT", "z_im"], ["z_im"])
        tt("dve", v3(k_im), v3(xr), S3, ALU.mult, XR + ["sinT", "k_im"], ["k_im"])
        tt("dve", z_im, z_im, k_im, ALU.subtract, ["z_im", "k_im"], ["z_im"])
        sr = scur[:, 0:16]; si = scur[:, 16:32]
        tt("dve", smv("t0"), smv("a1_r"), sr, ALU.mult, ["sm_a1", "scur", "sm_t0"], ["sm_t0"])
        tt("dve", smv("t1"), smv("a1_i"), si, ALU.mult, ["sm_a1", "scur", "sm_t1"], ["sm_t1"])
        tt("dve", smv("t0"), smv("t0"), smv("t1"), ALU.subtract, ["sm_t0", "sm_t1"], ["sm_t0"])
        tt("dve", v3(z_re)[:, :, 0], v3(z_re)[:, :, 0], smv("t0"), ALU.add, ["z_re", "sm_t0"], ["z_re"])
        tt("dve", smv("t2"), smv("a1_r"), si, ALU.mult, ["sm_a1", "scur", "sm_t2"], ["sm_t2"])
        tt("dve", smv("t3"), smv("a1_i"), sr, ALU.mult, ["sm_a1", "scur", "sm_t3"], ["sm_t3"])
        tt("dve", smv("t2"), smv("t2"), smv("t3"), ALU.add, ["sm_t2", "sm_t3"], ["sm_t2"])
        tt("dve", v3(z_im)[:, :, 0], v3(z_im)[:, :, 0], smv("t2"), ALU.add, ["z_im", "sm_t2"], ["z_im"])
        dk = Dk[:, :, :].rearrange("p a b -> p (a b)")
        scan(k_re, dk, z_re, ["Dk", "z_re", "k_re"], ["k_re"])
        scan(k_im, dk, z_im, ["Dk", "z_im", "k_im"], ["k_im"])
        tt("dve", v3(z_re), v3(k_re), C3, ALU.mult, ["k_re", "cosT", "z_re"], ["z_re"])
        tt("dve", v3(z_im), v3(k_im), S3, ALU.mult, ["k_im", "sinT", "z_im"], ["z_im"])
        tt("dve", z_re, z_re, z_im, ALU.subtract, ["z_re", "z_im"], ["z_re"])
        tt("dve", v3(z_im), v3(k_im), C3, ALU.mult, ["k_im", "cosT", "z_im"], ["z_im"])
        tt("dve", v3(k_re), v3(k_re), S3, ALU.mult, ["k_re", "sinT"], ["k_re"])
        tt("dve", z_im, z_im, k_re, ALU.add, ["z_im", "k_re"], ["z_im"])
        cp("dve", scur[:, 0:16], v3(z_re)[:, :, CL - 1], ["z_re", "scur"], ["scur"])
        cp("dve", scur[:, 16:32], v3(z_im)[:, :, CL - 1], ["z_im", "scur"], ["scur"])
        for ct in range(4):
            for i in range(4):
                t = ct * 4 + i
                mm(psB[2][:, ct * CL:(ct + 1) * CL], CTp_r[:, t, :], v3(z_re)[:, t, :], (i == 0), False, ["CTp", "z_re"], ["psB2"])
                mm(psB[2][:, ct * CL:(ct + 1) * CL], CTp_ni[:, t, :], v3(z_im)[:, t, :], False, (i == 3), ["CTp", "z_im"], ["psB2"])
        cp("act", actT[:, 8:12, c0:c0 + CL], psB[2][:, 0:4 * CL].rearrange("p (a b) -> p a b", a=4), ["psB2"], [("ysm", c)])

    def ssm_E(l):
        SK = ["z_re", "z_im", "k_re", "k_im"]
        giE = len(CG) - 1
        for ri in range(2):
            for t in range(16):
                PE(lambda e, ri=ri, t=t: e.matmul(psX[:, 2 * ri + t // 8, (t % 8) * CL:(t % 8) * CL + 32],
                                                  BbT[:, ri * 16 + t, :], uT[:, t // 4, E0:E0 + 32], start=True, stop=True),
                   ["BbT", ("uT", giE)], [f"psA{2 * ri}", f"psA{2 * ri + 1}"])
        xr = psX[:, 0:2, :].rearrange("p a b -> p (a b)").rearrange("p (a b) -> p a b", a=16)
        xi = psX[:, 2:4, :].rearrange("p a b -> p (a b)").rearrange("p (a b) -> p a b", a=16)
        XR = ["psA0", "psA1"]; XI = ["psA2", "psA3"]
        C3 = cosT[:, :, 0:16]; S3 = sinT[:, :, 0:16]
        m3 = lambda ap: ap[:, 0:256].rearrange("p (a b) -> p a b", a=16)
        zr = z_re[:, 0:256]; zi = z_im[:, 0:256]; kr = k_re[:, 0:256]; kim = k_im[:, 0:256]
        tt("dve", m3(z_re), xr[:, :, 0:16], C3, ALU.mult, XR + ["cosT", "z_re"], ["z_re"])
        tt("dve", m3(k_re), xi[:, :, 0:16], S3, ALU.mult, XI + ["sinT", "k_re"], ["k_re"])
        tt("dve", zr, zr, kr, ALU.add, ["z_re", "k_re"], ["z_re"])
        tt("dve", m3(z_im), xi[:, :, 0:16], C3, ALU.mult, XI + ["cosT", "z_im"], ["z_im"])
        tt("dve", m3(k_im), xr[:, :, 0:16], S3, ALU.mult, XR + ["sinT", "k_im"], ["k_im"])
        tt("dve", zi, zi, kim, ALU.subtract, ["z_im", "k_im"], ["z_im"])
        dk = Dk16[:, :, :].rearrange("p a b -> p (a b)")
        V(lambda e: e.tensor_tensor_scan(kr, dk, zr, 0.0, ALU.mult, ALU.add), ["Dk16", "z_re", "k_re"], ["k_re"])
        V(lambda e: e.tensor_tensor_scan(kim, dk, zi, 0.0, ALU.mult, ALU.add), ["Dk16", "z_im", "k_im"], ["k_im"])
        ms("dve", hE_re, 0.0, ["rtmp"], ["rtmp", "hE"])
        ms("dve", hE_im, 0.0, ["rtmp2"], ["rtmp2", "hE"])
        tt("dve", m3(z_re), m3(k_re), C3, ALU.mult, ["k_re", "cosT", "z_re"], ["z_re"])
        tt("dve", m3(z_im), m3(k_im), S3, ALU.mult, ["k_im", "sinT", "z_im"], ["z_im"])
        tt("dve", hE_re[:, :, 0:16], m3(z_re), m3(z_im), ALU.subtract, ["z_re", "z_im", "hE"], ["hE"])
        tt("dve", Gst[:, NCH, 0:16], m3(z_re)[:, :, 15], m3(z_im)[:, :, 15], ALU.subtract, ["z_re", "z_im"], [("G", NCH)])
        tt("dve", m3(z_re), m3(k_im), C3, ALU.mult, ["k_im", "cosT", "z_re"], ["z_re"])
        tt("dve", m3(z_im), m3(k_re), S3, ALU.mult, ["k_re", "sinT", "z_im"], ["z_im"])
        tt("dve", hE_im[:, :, 0:16], m3(z_re), m3(z_im), ALU.add, ["z_re", "z_im", "hE"], ["hE"])
        tt("dve", Gst[:, NCH, 16:32], m3(z_re)[:, :, 15], m3(z_im)[:, :, 15], ALU.add, ["z_re", "z_im"], [("G", NCH)])
        dma("sp", sttok[:, 0, :], st_re[l, :, :], [], ["sttok"])
        dma("sp", sttok[:, 1, :], st_im[l, :, :], [], ["sttok"])
        for ri in range(2):
            PE(lambda e, ri=ri: e.transpose(psB[2][:, ri * 64:(ri + 1) * 64], sttok[:, ri, :], identf[0:64, 0:64]),
               ["sttok", "identf"], ["psB2"])
        V(lambda e: e.tensor_copy(h0s[:, :, :], psB[2][:, 0:128].rearrange("p (a b) -> p a b", a=2)), ["psB2"], ["h0s"])
        h0r = h0s[:, 0, :].rearrange("p (s t) -> p s t", s=4); h0i = h0s[:, 1, :].rearrange("p (s t) -> p s t", s=4)
        hsr = hs[:, 0, :].rearrange("p (s t) -> p s t", s=4); hsi = hs[:, 1, :].rearrange("p (s t) -> p s t", s=4)
        a1r = smv("a1_r").unsqueeze(1).to_broadcast([128, 4, 16]); a1i = smv("a1_i").unsqueeze(1).to_broadcast([128, 4, 16])
        t0 = z_re[:, 0:64].rearrange("p (s t) -> p s t", s=4); t1 = z_im[:, 0:64].rearrange("p (s t) -> p s t", s=4)
        xrs = xr[:, :, 16:20].rearrange("p t s -> p s t"); xis = xi[:, :, 16:20].rearrange("p t s -> p s t")
        tt("dve", t0, h0r, a1r, ALU.mult, ["h0s", "sm_a1", "z_re"], ["z_re"])
        tt("dve", t1, h0i, a1i, ALU.mult, ["h0s", "sm_a1", "z_im"], ["z_im"])
        tt("dve", t0, t0, t1, ALU.subtract, ["z_re", "z_im"], ["z_re"])
        tt("dve", hsr, t0, xrs, ALU.add, ["z_re"] + XR, ["hs"])
        tt("dve", t0, h0i, a1r, ALU.mult, ["h0s", "sm_a1", "z_re"], ["z_re"])
        tt("dve", t1, h0r, a1i, ALU.mult, ["h0s", "sm_a1", "z_im"], ["z_im"])
        tt("dve", t0, t0, t1, ALU.add, ["z_re", "z_im"], ["z_re"])
        tt("dve", hsi, t0, xis, ALU.add, ["z_re"] + XI, ["hs"])
        V(lambda e: e.tensor_copy(hE_re[:, :, 16:20], hsr.rearrange("p s t -> p t s")), ["hs", "hE"], ["hE"])
        V(lambda e: e.tensor_copy(hE_im[:, :, 16:20], hsi.rearrange("p s t -> p t s")), ["hs", "hE"], ["hE"])
        for ri in range(2):
            PE(lambda e, ri=ri: e.transpose(psB[2][0:64, ri * 128:(ri + 1) * 128], hs[:, ri, :], identf[:, :]),
               ["hs", "identf"], ["psB2"])
        V(lambda e: e.tensor_copy(outst[0:64, 0:256], psB[2][0:64, 0:256]), ["psB2"], ["outst"])
        for ri in range(2 if QS["q"] == 0 else 0):
            dma("sp", o_sssm[l, ri, :, :], outst[0:64, ri * 128:(ri + 1) * 128], ["outst"], [("out", "sssm", l, ri)])
        for ct in range(4):
            for i in range(4):
                t = ct * 4 + i
                PE(lambda e, ct=ct, t=t, i=i: e.matmul(psB[2][:, ct * 32:(ct + 1) * 32], CTp_r[:, t, :], hE_re[:, t, :],
                                                       start=(i == 0), stop=False), ["CTp", "hE", "rtmp"], ["psB2"])
                PE(lambda e, ct=ct, t=t, i=i: e.matmul(psB[2][:, ct * 32:(ct + 1) * 32], CTp_ni[:, t, :], hE_im[:, t, :],
                                                       start=False, stop=(i == 3)), ["CTp", "hE", "rtmp2"], ["psB2"])
        ACT(lambda e: e.copy(actT[:, 8:12, E0:E0 + 32], psB[2][:, 0:128].rearrange("p (a b) -> p a b", a=4)),
            ["psB2"], [("ysm", "E")])

    def cmuladd(dst_r, dst_i, a_r, a_i, s_r, s_i, g_r, g_i, rd, wr):
        tt("pool", smv("t0"), a_r, s_r, ALU.mult, rd + ["sm_t0"], ["sm_t0"])
        tt("pool", smv("t1"), a_i, s_i, ALU.mult, rd + ["sm_t1"], ["sm_t1"])
        tt("pool", smv("t2"), a_r, s_i, ALU.mult, rd + ["sm_t2"], ["sm_t2"])
        tt("pool", smv("t3"), a_i, s_r, ALU.mult, rd + ["sm_t3"], ["sm_t3"])
        tt("pool", smv("t0"), smv("t0"), smv("t1"), ALU.subtract, ["sm_t0", "sm_t1"], ["sm_t0"])
        tt("pool", smv("t2"), smv("t2"), smv("t3"), ALU.add, ["sm_t2", "sm_t3"], ["sm_t2"])
        if g_r is not None:
            tt("pool", dst_r, smv("t0"), g_r, ALU.add, rd + ["sm_t0"], wr)
            tt("pool", dst_i, smv("t2"), g_i, ALU.add, rd + ["sm_t2"], wr)
        else:
            G(lambda e: e.tensor_copy(dst_r, smv("t0")), rd + ["sm_t0"], wr)
            G(lambda e: e.tensor_copy(dst_i, smv("t2")), rd + ["sm_t2"], wr)

    def state_init(l, q):
        if q == 0:
            cp("dve", scur[:, :], Gst[:, NCH, :], [("G", NCH), "scur"], ["scur"])
        else:
            cp("dve", scur[:, :], sfin[:, l, :], [("sfin", l), "scur"], ["scur"])

    def local_state(l, q):
        giE = len(CG) - 1
        hk = hkL[l]; hv = hvL[l]
        if q == 0:
            ms("dve", hk[:, :, :], 0.0, [], [("hk", l)])
            cp("dve", hk[:, :, 0:16], kT[:, :, E0:E0 + 16], [("kT", 0, giE), ("kT", 1, giE), ("hk", l)], [("hk", l)])
            ms("dve", hv[:, :], 0.0, [], [("hv", l)])
            cp("dve", hv[0:16, :], vbf[0:16, NB, :], [("vbf", NB), ("hv", l)], [("hv", l)])
            cp("dve", xpT[:, :, 0:16], xpT[:, :, 16 + E0:16 + E0 + 16], [("xpT", giE)], [("xpT", "halo")])
        else:
            cp("dve", xpT[:, :, 0:16], phL[l][:, :, :], [("ph", l)], [("xpT", "halo")])
        cp("dve", sfin[:, l, :], scur[:, :], ["scur", ("sfin", l)], [("sfin", l)])
        for ri in range(2):
            tr(psB[2][0:16, ri * 128:(ri + 1) * 128], scur[:, ri * 16:(ri + 1) * 16], identf[:, :], ["scur", "identf"], ["psB2"])
        cp("dve", outst[0:16, 256:512], psB[2][0:16, 0:256], ["psB2", "outst2"], ["outst2"])
        for ri in range(2):
            dma("sp", o_pssm[l, ri, :, :], outst[0:16, 256 + ri * 128:256 + (ri + 1) * 128], ["outst2"], [("out", "pssm", l, ri)])

    def save_halo(l):
        giL = gi_of(L - 128)
        cp("dve", hkL[l][:, :, :], kT[:, :, L - 128:L], [("kT", 0, giL), ("kT", 1, giL), ("hk", l)], [("hk", l)])
        cp("dve", hvL[l][:, :], vbf[:, NB - 1, :], [("vbf", NB - 1), ("hv", l)], [("hv", l)])
        cp("dve", phL[l][:, :, :], xpT[:, :, 16 + L - 16:16 + L], [("xpT", gi_of(L - 16)), ("ph", l)], [("ph", l)])

    def gi_of(col):
        return [i for i, (a, b) in enumerate(CG) if a <= col < b][0]

    SC = float(128 ** -0.5)

    def attn_block(blk, step):
        r0 = blk * 128
        gi = gi_of(r0)

        def one(kvh):
            i2 = (step * 2 + kvh) % 2
            pa, pb = psA[2 * i2], psA[2 * i2 + 1]
            ka, kb_ = f"psA{2 * i2}", f"psA{2 * i2 + 1}"
            eb = ebuf[i2]; ek = ("ebuf", i2)
            qv = qT[:, 4 * kvh:4 * kvh + 4, r0:r0 + 128]
            qk_ = [("qT", 4 * kvh + g, gi) for g in range(4)]
            if blk == 0:
                l_ = QS["l"]
                kprev = hkL[l_][:, kvh, :]; vprev = hvL[l_][:, kvh * 128:(kvh + 1) * 128]
                mprev = masks[:, 2 if QS["q"] == 0 else 1, :]
                rprev = [("hk", l_)]; rvprev = [("hv", l_)]
            else:
                kprev = kT[:, kvh, r0 - 128:r0]; vprev = vbf[:, blk - 1, kvh * 128:(kvh + 1) * 128]; mprev = masks[:, 1, :]
                rprev = [("kT", kvh, gi_of(r0 - 128))]; rvprev = [("vbf", blk - 1)]
            PE(lambda e: e.matmul(pa[:, :], kprev, qv, start=True, stop=False), rprev + qk_, [ka])
            PE(lambda e: e.matmul(pa[:, :], ident[:, :], mprev, start=False, stop=True), ["ident", "masks"], [ka])
            PE(lambda e: e.matmul(pb[:, :], kT[:, kvh, r0:r0 + 128], qv, start=True, stop=False), [("kT", kvh, gi)] + qk_, [kb_])
            PE(lambda e: e.matmul(pb[:, :], ident[:, :], masks[:, 0, :], start=False, stop=True), ["ident", "masks"], [kb_])
            ACT(lambda e: e.activation(eb[:, 0, :], pa[:, :], AF.Exp, scale=SC), [ka], [ek])
            ACT(lambda e: e.activation(eb[:, 1, :], pb[:, :], AF.Exp, scale=SC), [kb_], [ek])
            PE(lambda e: e.matmul(psB[0][:, :], vprev, eb[:, 0, :], start=True, stop=False), rvprev + [ek], ["psB0"])
            PE(lambda e: e.matmul(psB[0][:, :], vbf[:, blk, kvh * 128:(kvh + 1) * 128], eb[:, 1, :], start=False, stop=True),
               [("vbf", blk), ek], ["psB0"])
            PE(lambda e: e.matmul(psB[1][:, :], ones_b[:, 3, :], eb[:, 0, :], start=True, stop=False), ["ones", ek], ["psB1"])
            PE(lambda e: e.matmul(psB[1][:, :], ones_b[:, 3, :], eb[:, 1, :], start=False, stop=False), ["ones", ek], ["psB1"])
            PE(lambda e: e.matmul(psB[1][:, :], ones_b[0:1, 3, :], esink[0:1, 4 * kvh:4 * kvh + 4, :], start=False, stop=True),
               ["ones", "esink"], ["psB1"])
            V(lambda e: e.reciprocal(qraw[:, :], psB[1][:, :]), ["psB1", "qraw"], ["qraw"])
            V(lambda e: e.tensor_tensor(qv, psB[0][:, :].rearrange("p (a b) -> p a b", a=4),
                                        qraw[:, :].rearrange("p (a b) -> p a b", a=4), ALU.mult),
              ["psB0", "qraw"], [("qT", 4 * kvh + g, gi) for g in range(4)])
        for kvh in range(2):
            one(kvh)

    def attn_E():
        giE = len(CG) - 1

        def one(kvh):
            pa = psA[2 * kvh]; ka = f"psA{2 * kvh}"
            eb = ebuf[kvh]; ek = ("ebuf", kvh)
            qv = qT[:, 4 * kvh:4 * kvh + 4, E0:E0 + 32]
            qk_ = [("qT", 4 * kvh + g, giE) for g in range(4)]
            PE(lambda e: e.matmul(pa[0:32, 0:128], kT[:, kvh, E0:E0 + 32], qv, start=True, stop=False), [("kT", kvh, giE)] + qk_, [ka])
            PE(lambda e: e.matmul(pa[0:32, 0:128], ident[0:32, 0:32], maskE[:, :], start=False, stop=True), ["ident", "maskE"], [ka])
            ACT(lambda e: e.activation(eb[0:32, 0, 0:128], pa[0:32, 0:128], AF.Exp, scale=SC), [ka], [ek])
            PE(lambda e: e.matmul(psB[0][:, 0:128], vbf[0:32, NB, kvh * 128:(kvh + 1) * 128], eb[0:32, 0, 0:128], start=True, stop=True),
               [("vbf", NB), ek], ["psB0"])
            PE(lambda e: e.matmul(psB[1][:, 0:128], ones_b[0:32, 3, :], eb[0:32, 0, 0:128], start=True, stop=False), ["ones", ek], ["psB1"])
            PE(lambda e: e.matmul(psB[1][:, 0:128], ones_b[0:1, 3, :], esink[0:1, 4 * kvh:4 * kvh + 4, 0:32], start=False, stop=True),
               ["ones", "esink"], ["psB1"])
            V(lambda e: e.reciprocal(qraw[:, 0:128], psB[1][:, 0:128]), ["psB1", "qraw"], ["qraw"])
            V(lambda e: e.tensor_tensor(qT[:, 4 * kvh:4 * kvh + 4, E0:E0 + 16],
                                        psB[0][:, 0:128].rearrange("p (a b) -> p a b", a=4)[:, :, 0:16],
                                        qraw[:, 0:128].rearrange("p (a b) -> p a b", a=4)[:, :, 0:16], ALU.mult),
              ["psB0", "qraw"], [("qTm", kvh)])
        for kvh in range(2):
            one(kvh)

    def attn_samples(l):
        giE = len(CG) - 1
        for s_ in range(4):
            dma("pool", kctok[:, :], cache_k[l, s_, :, :], [], ["kctok"])
            dma("pool", vc[:, :], cache_v[l, s_, :, :], [], ["vc"])
            if QS["q"] == 0:
                dma("sp", o_sk[l, s_, 0:127, :], cache_k[l, s_, 1:128, :], [], [("out", "sk", l, s_)])
                dma("sp", o_sv[l, s_, 0:127, :], cache_v[l, s_, 1:128, :], [], [("out", "sv", l, s_)])
                dma("sp", o_spool[l, s_, 0:14, :], st_pool[l, s_, 1:15, :], [], [("out", "sp", l, s_)])
            col = E0 + 16 + s_
            for kvh in range(2):
                pa = psA[2 * kvh]; ka = f"psA{2 * kvh}"
                qs = qT[:, 4 * kvh:4 * kvh + 4, col]
                qk_ = [("qT", 4 * kvh + g, giE) for g in range(4)]
                PE(lambda e, kvh=kvh: e.transpose(psT[:, 0:128], kctok[:, kvh * 128:(kvh + 1) * 128], ident[:, :]),
                   ["kctok", "ident"], ["psT"])
                V(lambda e: e.tensor_copy(kcT[:, :], psT[:, 0:128]), ["psT"], ["kcT"])
                PE(lambda e, pa=pa, qs=qs: e.matmul(pa[:, 0:4], kcT[:, :], qs, start=True, stop=True), ["kcT"] + qk_, [ka])
                PE(lambda e, pa=pa, qs=qs, kvh=kvh: e.matmul(pa[0:32, 4:8], kT[:, kvh, E0:E0 + 32], qs, start=True, stop=False),
                   [("kT", kvh, giE)] + qk_, [ka])
                PE(lambda e, pa=pa, s_=s_: e.matmul(pa[0:32, 4:8], ident[0:32, 0:32], maskS[:, s_, :], start=False, stop=True),
                   ["ident", "maskS"], [ka])
                ACT(lambda e, pa=pa: e.activation(es[:, 0:4], pa[:, 0:4], AF.Exp, scale=SC), [ka], ["es"])
                ACT(lambda e, pa=pa: e.activation(es[0:32, 4:8], pa[0:32, 4:8], AF.Exp, scale=SC), [ka], ["es"])
                PE(lambda e, kvh=kvh: e.matmul(psB[0][:, 0:4], vc[:, kvh * 128:(kvh + 1) * 128], es[:, 0:4], start=True, stop=False),
                   ["vc", "es"], ["psB0"])
                PE(lambda e, kvh=kvh: e.matmul(psB[0][:, 0:4], vbf[0:32, NB, kvh * 128:(kvh + 1) * 128], es[0:32, 4:8],
                                               start=False, stop=True), [("vbf", NB), "es"], ["psB0"])
                PE(lambda e: e.matmul(psB[1][:, 0:4], ones_b[:, 3, :], es[:, 0:4], start=True, stop=False), ["ones", "es"], ["psB1"])
                PE(lambda e: e.matmul(psB[1][:, 0:4], ones_b[0:32, 3, :], es[0:32, 4:8], start=False, stop=False), ["ones", "es"], ["psB1"])
                PE(lambda e, kvh=kvh: e.matmul(psB[1][:, 0:4], ones_b[0:1, 3, :], esink[0:1, 4 * kvh:4 * kvh + 4, 0],
                                               start=False, stop=True), ["ones", "esink"], ["psB1"])
                V(lambda e: e.reciprocal(qraw[:, 0:4], psB[1][:, 0:4]), ["psB1", "qraw"], ["qraw"])
                V(lambda e, qs=qs: e.tensor_tensor(qs, psB[0][:, 0:4], qraw[:, 0:4], ALU.mult), ["psB0", "qraw"], [("qTs", s_, kvh)])

    def kv_outputs(l):
        giL = gi_of(L - 128); giE = len(CG) - 1
        for kvh in range(2):
            PE(lambda e, kvh=kvh: e.transpose(psT[:, 512 + kvh * 128:512 + (kvh + 1) * 128], kT[:, kvh, L - 128:L], ident[:, :]),
               [("kT", kvh, giL), "ident"], ["psT"])
        V(lambda e: e.tensor_copy(outst[:, 0:256], psT[:, 512:768]), ["psT", "outst"], ["outst"])
        dma("sp", o_pk[l, :, :], outst[:, 0:256], ["outst"], [("out", "pk", l)])
        V(lambda e: e.tensor_copy(outst[:, 256:512], vbf[:, NB - 1, :]), [("vbf", NB - 1), "outst2"], ["outst2"])
        dma("sp", o_pv[l, :, :], outst[:, 256:512], ["outst2"], [("out", "pv", l)])
        for kvh in range(2):
            PE(lambda e, kvh=kvh: e.transpose(psT[0:32, 512 + kvh * 128:512 + (kvh + 1) * 128], kT[:, kvh, E0:E0 + 32], ident[:, :]),
               [("kT", kvh, giE), "ident"], ["psT"])
        V(lambda e: e.tensor_copy(outst[0:32, 0:256], psT[0:32, 512:768]), ["psT", "outst"], ["outst"])
        V(lambda e: e.tensor_copy(outst[0:32, 256:512], vbf[0:32, NB, :]), [("vbf", NB), "outst2"], ["outst2"])
        for s_ in range(4 if QS["q"] == 0 else 0):
            dma("sp", o_sk[l, s_, 127:128, :], outst[16 + s_:17 + s_, 0:256], ["outst"], [("out", "sk2", l, s_)])
            dma("sp", o_sv[l, s_, 127:128, :], outst[16 + s_:17 + s_, 256:512], ["outst2"], [("out", "sv2", l, s_)])

    def pool_mixer(l):
        SKA = ["z_re", "z_im"]; SKB = ["k_re", "k_im"]
        W_ = 16 + L
        pa = scr[:, 0:W_]; pb = scr[:, 2048:2048 + W_]
        giE = len(CG) - 1
        allx = [("xpT", i) for i in range(len(CG))] + [("xpT", "halo")]
        for g in range(4):
            PE(lambda e, g=g: e.transpose(psT[0:32, g * 128:(g + 1) * 128], xpT[:, g, 16 + L - 32:16 + L], ident[:, :]),
               allx + ["ident"], ["psT"])
        V(lambda e: e.tensor_copy(outst[0:32, :], psT[0:32, 0:512]), ["psT", "outst", "outst2"], ["outst", "outst2"])
        dma("sp", o_ppool[l, :, :], outst[17:32, :], ["outst", "outst2"], [("out", "ppool", l)])
        for g in range(4):
            PE(lambda e, g=g: e.transpose(psT[0:32, 512 + g * 128:512 + (g + 1) * 128], xpT[:, g, 16 + E0:16 + E0 + 32], ident[:, :]),
               allx + ["ident"], ["psT"])
        V(lambda e: e.tensor_copy(outst[32:64, :], psT[0:32, 512:1024]) if False else e.tensor_copy(outst[0:32, :], psT[0:32, 512:1024]),
          ["psT", "outst", "outst2"], ["outst", "outst2"])
        for s_ in range(4 if QS["q"] == 0 else 0):
            dma("sp", o_spool[l, s_, 14:15, :], outst[16 + s_:17 + s_, :], ["outst", "outst2"], [("out", "sp2", l, s_)])
        for g in range(4):
            w = 2 ** (g + 1)
            src = xpT[:, g, 0:W_]
            cur = None
            sh = 1
            bufs = [pa, pb]
            bkeys = [SKA, SKB]
            bi = 0
            for k_ in range(g + 1):
                out = bufs[bi]; ok = bkeys[bi]
                if cur is None:
                    tt("pool", out[:, sh:W_], src[:, sh:W_], src[:, 0:W_ - sh], ALU.add, allx + ok, ok)
                else:
                    tt("pool", out[:, sh:W_], cur[:, sh:W_], cur[:, 0:W_ - sh], ALU.add, bkeys[1 - bi] + ok, ok)
                cur = out; ck = ok
                sh *= 2
                bi = 1 - bi
            for gi, (c0, c1) in enumerate(CG):
                c1p = min(c1, L)
                n = c1p - c0
                V(lambda e, cur=cur, c0=c0, n=n, g=g, w=w: e.scalar_tensor_tensor(
                    sqb[:, 0:n], cur[:, 16 + c0:16 + c0 + n], 1.0 / w, xpT[:, g, 16 + c0:16 + c0 + n], ALU.mult, ALU.subtract),
                  ck + allx + ["sqb"], ["sqb"])
                if c1 > L:
                    pm = pmeta[:, g, :]; pm2 = pmeta2[:, g, :]
                    V(lambda e, pm=pm: e.memset(pm, 0.0), [], ["pmeta"])
                    V(lambda e, pm=pm, g=g: e.tensor_copy(pm[:, 16:32], xpT[:, g, 16 + E0:16 + E0 + 16]), allx + ["pmeta"], ["pmeta"])
                    a_, b_ = pm, pm2
                    sh2 = 1
                    for k_ in range(g + 1):
                        V(lambda e, a_=a_, b_=b_: e.tensor_copy(b_[:, 0:16], a_[:, 0:16]), ["pmeta"], ["pmeta"])
                        V(lambda e, a_=a_, b_=b_, sh2=sh2: e.tensor_tensor(b_[:, 16:32], a_[:, 16:32], a_[:, 16 - sh2:32 - sh2], ALU.add),
                          ["pmeta"], ["pmeta"])
                        a_, b_ = b_, a_
                        sh2 *= 2
                    V(lambda e, a_=a_, g=g: e.tensor_tensor(a_[:, 16:32], a_[:, 16:32], pinvE[:, g, :], ALU.mult), ["pmeta", "pinvE"], ["pmeta"])
                    V(lambda e, a_=a_, g=g, n=n: e.tensor_tensor(sqb[:, n:n + 16], a_[:, 16:32], xpT[:, g, 16 + E0:16 + E0 + 16], ALU.subtract),
                      ["pmeta", "sqb"] + allx, ["sqb"])
                    for s_ in range(4):
                        dma("pool", stp[0:15, :], st_pool[l, s_, :, :], [], ["stp"])
                        PE(lambda e, g=g, s_=s_: e.matmul(psB[2][:, s_:s_ + 1], stp[0:16, g * 128:(g + 1) * 128], psel[0:16, g:g + 1],
                                                          start=True, stop=True), ["stp", "psel"], ["psB2"])
                    xs4 = xpT[:, g, 16 + E0 + 16:16 + E0 + 20]
                    V(lambda e, xs4=xs4: e.tensor_tensor(small[:, 4:8], psB[2][:, 0:4], xs4, ALU.add), ["psB2"] + allx, ["small4"])
                    V(lambda e, xs4=xs4, n=n, w=w: e.scalar_tensor_tensor(sqb[:, n + 16:n + 20], small[:, 4:8], 1.0 / w, xs4,
                                                                         ALU.mult, ALU.subtract), ["small4", "sqb"] + allx, ["sqb"])
                    V(lambda e, n=n: e.memset(sqb[:, n + 20:n + 32], 0.0), ["sqb"], ["sqb"])
                    n = n + 32
                PE(lambda e, g=g, n=n: e.matmul(psB[0][:, 0:n], wpool[:, g, :], sqb[:, 0:n], start=True, stop=True),
                   ["wpool", "sqb"], ["psB0"])
                ACT(lambda e, g=g, n=n, c0=c0: e.activation(actT[:, 12 + g, c0:c0 + n], psB[0][:, 0:n], AF.Copy, scale=vecs[:, 26 + g:27 + g]),
                    ["psB0", "vecs"], [("plT", g, gi)])

    GC = float(2.0 * np.sqrt(2.0 / np.pi))

    def ssm_post():
        for gi, (c0, c1) in enumerate(CG):
            n = c1 - c0
            ysk = [("ysm", c) for c in range(NCH) if c0 <= c * CL < c1] + ([("ysm", "E")] if c1 > L else [])
            for ct in range(4):
                yv = actT[:, 8 + ct, c0:c1]
                uv = uT[:, ct, c0:c1]
                V(lambda e, yv=yv, uv=uv, ct=ct, n=n: e.scalar_tensor_tensor(qraw[:, 0:n], uv, vecs[:, 18 + ct:19 + ct], yv, ALU.mult, ALU.add),
                  ysk + [("uT", gi), "vecs", "qraw"], ["qraw"])
                tt("pool", rtmp2[:, 0:n], qraw[:, 0:n], qraw[:, 0:n], ALU.mult, ["qraw", "rtmp2"], ["rtmp2"])
                G(lambda e, n=n: e.tensor_scalar(rtmp2[:, 0:n], rtmp2[:, 0:n], 0.044715, 1.0, ALU.mult, ALU.add), ["rtmp2"], ["rtmp2"])
                tt("pool", rtmp2[:, 0:n], rtmp2[:, 0:n], qraw[:, 0:n], ALU.mult, ["qraw", "rtmp2"], ["rtmp2"])
                G(lambda e, n=n: e.tensor_scalar(rtmp2[:, 0:n], rtmp2[:, 0:n], -30.0, 1.0, ALU.max, ALU.mult), ["rtmp2"], ["rtmp2"])
                ACT(lambda e, n=n: e.activation(rtmp2[:, 0:n], rtmp2[:, 0:n], AF.Exp, scale=-GC), ["rtmp2"], ["rtmp2"])
                G(lambda e, n=n: e.tensor_scalar(rtmp2[:, 0:n], rtmp2[:, 0:n], 1.0, 1.0, ALU.add, ALU.mult), ["rtmp2"], ["rtmp2"])
                V(lambda e, n=n: e.reciprocal(rtmp2[:, 0:n], rtmp2[:, 0:n]), ["rtmp2"], ["rtmp2"])
                V(lambda e, uv=uv, n=n: e.tensor_tensor(uv, qraw[:, 0:n], rtmp2[:, 0:n], ALU.mult), ["qraw", "rtmp2", ("uT", gi)], [("uT", gi)])
            for co in range(4):
                for ci in range(4):
                    PE(lambda e, co=co, ci=ci, c0=c0, c1=c1, n=n: e.matmul(psA[co][:, 0:n], wglu[:, ci, co * 128:(co + 1) * 128],
                                                                          uT[:, ci, c0:c1], start=(ci == 0), stop=(ci == 3)),
                       ["wglu", ("uT", gi)], [f"psA{co}"])
                ACT(lambda e, co=co, n=n: e.activation(rtmp[:, 0:n], psA[co][:, 0:n], AF.Exp, bias=vecs[:, 30 + co:31 + co], scale=-1.0),
                    [f"psA{co}", "vecs", "rtmp"], ["rtmp"])
                G(lambda e, n=n: e.tensor_scalar(rtmp[:, 0:n], rtmp[:, 0:n], 1.0, 1.0, ALU.add, ALU.mult), ["rtmp"], ["rtmp"])
                V(lambda e, n=n: e.reciprocal(rtmp[:, 0:n], rtmp[:, 0:n]), ["rtmp"], ["rtmp"])
                V(lambda e, co=co, c0=c0, c1=c1, n=n: e.tensor_tensor(actT[:, 8 + co, c0:c1], uT[:, co, c0:c1], rtmp[:, 0:n], ALU.mult),
                  ["rtmp", ("uT", gi)] + ysk, [("smT", co, gi)])

    def out_norms():
        for gi, (c0, c1) in enumerate(CG):
            n = c1 - c0
            groups = [
                ([(qT[:, h, c0:c1], [("qT", h, gi), ("qTm", h // 4)] + [("qTs", s_, h // 4) for s_ in range(4)]) for h in range(8)], 1, 2, 0),
                ([(actT[:, 8 + t, c0:c1], [("smT", t, gi)]) for t in range(4)], 2, 10, 8),
                ([(actT[:, 12 + t, c0:c1], [("plT", t, gi)]) for t in range(4)], 2, 14, 12),
            ]
            for tiles, oi, gcol, slot0 in groups:
                for i, (src, rk) in enumerate(tiles):
                    tt("pool", sqb[:, 0:n], src, src, ALU.mult, rk + ["sqb"], ["sqb"])
                    PE(lambda e, i=i, oi=oi, n=n, last=(i == len(tiles) - 1): e.matmul(psB[0][:, 0:n], ones_b[:, oi, :], sqb[:, 0:n],
                                                                                     start=(i == 0), stop=last), ["sqb", "ones"], ["psB0"])
                ACT(lambda e, n=n: e.activation(rtmp[:, 0:n], psB[0][:, 0:n], AF.Ln, bias=EPS, scale=1.0), ["psB0", "rtmp"], ["rtmp"])
                ACT(lambda e, n=n: e.activation(rtmp[:, 0:n], rtmp[:, 0:n], AF.Exp, scale=-0.5), ["rtmp"], ["rtmp"])
                for i, (src, rk) in enumerate(tiles):
                    dst = actT[:, slot0 + i, c0:c1]
                    V(lambda e, src=src, dst=dst, i=i, gcol=gcol, n=n: e.scalar_tensor_tensor(dst, src, vecs[:, gcol + i:gcol + i + 1],
                                                                                             rtmp[:, 0:n], ALU.mult, ALU.mult),
                      rk + ["rtmp", "vecs"], [("actT", b) for b in blk_of_cols(c0, c1)])

    def resid_pass(wt, l, rows0, nk, lhs_of, lhs_reads, xsrc, xdst, outkey):
        for ni in range(4):
            wkey = load_w(w_tile(wt, l, rows0, ni * 512), ni % 2)
            wb = wbuf[ni % 2]
            for blk in range(NBLK if QS["q"] == 0 else NB):
                P = 128 if blk < NB else 32
                r0 = blk * 128
                j = blk % 3
                ps = psB[j]; pk = f"psB{j}"
                for kc in range(nk):
                    PE(lambda e, ps=ps, kc=kc, wb=wb, r0=r0, P=P: e.matmul(ps[0:P, :], lhs_of(kc, r0, P), wb[:, kc, :],
                                                                          start=(kc == 0), stop=(kc == nk - 1)),
                       [wkey] + lhs_reads(blk), [pk])
                xj = blk % 4
                dr = drow(blk)
                dma("sp", xio[xj][0:P, :], xsrc[dr:dr + P, ni * 512:(ni + 1) * 512], [xdk(blk)], [("xio", xj)])
                V(lambda e, ps=ps, xj=xj, P=P: e.tensor_tensor(xio[xj][0:P, :], ps[0:P, :], xio[xj][0:P, :], ALU.add),
                  [pk, ("xio", xj)], [("xio", xj)])
                wr = [xdk(blk)] + ([("out", outkey, QS["q"], blk, ni)] if outkey else [])
                if blk < NB or QS["q"] == 0:
                    dma("sp", xdst[dr:dr + P, ni * 512:(ni + 1) * 512], xio[xj][0:P, :], [("xio", xj)], wr)

    def ffn_phase(l, last):
        for hq in range(4):
            for t4 in range(4):
                wkey = load_w(w_tile(w_ff1, l, 0, hq * 2048 + t4 * 512), t4 % 2)
                wb = wbuf[t4 % 2]
                for gi, (c0, c1) in enumerate(CG):
                    n = c1 - c0
                    for mt in range(4):
                        ps = psA[mt]; pk = f"psA{mt}"
                        for kc in range(16):
                            PE(lambda e, ps=ps, kc=kc, mt=mt, wb=wb, c0=c0, c1=c1, n=n:
                               e.matmul(ps[:, 0:n], wb[:, kc, mt * 128:(mt + 1) * 128], actT[:, kc, c0:c1],
                                        start=(kc == 0), stop=(kc == 15)), [wkey] + actT_reads(c0, c1), [pk])
                        rt = relu_t[mt % 2]; rk = f"relu{mt % 2}"
                        ACT(lambda e, ps=ps, rt=rt, n=n: e.activation(rt[:, 0:n], ps[:, 0:n], AF.Relu), [pk], [rk])
                        m = t4 * 4 + mt
                        tt("pool", hidT[:, m, c0:c1], rt[:, 0:n], rt[:, 0:n], ALU.mult, [rk, "hidfence"], [("hid", m, gi)])
            resid_pass(w_ff2, l, hq * 2048, 16, lambda kc, r0, P: hidT[:, kc, r0:r0 + P],
                       lambda blk: [("hid", m, gi) for m in range(16) for gi in range(len(CG))
                                    if blk in blk_of_cols(*CG[gi])],
                       xs, y_out if (last and hq == 3) else xs, "y" if (last and hq == 3) else None)

    PROJ_PRED = lambda k: isinstance(k, tuple) and k[0] in ("qT", "kT", "vbf", "uT", "qTm", "qTs", "hid")
    SKK = ["z_re", "z_im", "k_re", "k_im"]
    STG = ["w_in", "chunks", "ssmE", "kvout", "state", "attnE", "samples", "corr", "attn0", "pool", "post", "mix", "w_out", "ffn"]

    def reached(name):
        return stop in STG and STG.index(stop) < STG.index(name)

    def layer_pass(q, l, first, last):
        QS["q"] = q; QS["l"] = l
        CG[:] = CG0 if q == 0 else CG1
        xsrc = x_in if l == 0 else xs
        S.fence(lambda k: isinstance(k, tuple) and k[0] == "ebuf", ["xn"])
        norm_to_actT(xsrc, g_mix, l)
        S.fence(lambda k: k == "xn", [("ebuf", 0), ("ebuf", 1)])
        S.fence(PROJ_PRED, ["projfence"])
        w_in_phase(l)
        if reached("chunks"): return
        S.fence(lambda k: k in ("W1",), SKK)
        if q == 0:
            ssm_E(l)
        state_init(l, q)
        nblk_done = 1
        for c in range(NCH):
            ssm_chunk(c)
            if c % 2 == 1 and nblk_done < NB:
                attn_block(nblk_done, nblk_done)
                nblk_done += 1
        while nblk_done < NB:
            attn_block(nblk_done, nblk_done)
            nblk_done += 1
        if reached("kvout"): return
        kv_outputs(l)
        if reached("state"): return
        local_state(l, q)
        if reached("attnE"): return
        if q == 0:
            attn_E()
        if reached("samples"): return
        if q == 0:
            attn_samples(l)
        tap("ysm", actT[:, 8:12, :], BF16, [k for k in S.last_w])
        if reached("attn0"): return
        attn_block(0, 1)
        tap("aT", projbuf[:, 0:8 * T], BF16, [k for k in S.last_w])
        if reached("pool"): return
        pool_mixer(l)
        save_halo(l)
        tap("ysm2", actT[:, 8:12, :], BF16, [k for k in S.last_w])
        if reached("post"): return
        ssm_post()
        tap("premix", actT[:, 8:16, :], BF16, [k for k in S.last_w])
        if reached("mix"): return
        out_norms()
        tap("mixT", actT[:, :, :], BF16, [k for k in S.last_w])
        if reached("w_out"): return
        S.fence(lambda k: k in SKK, ["W1"])
        S.fence(lambda k: isinstance(k, tuple) and k[0] == "ebuf", ["xn"])
        direct = (not ffw) and last
        resid_pass(w_out, l, 0, 16, lambda kc, r0, P: actT[:, kc, r0:r0 + P], lambda blk: [("actT", blk)],
                   xsrc, y_out if direct else xs, "y" if direct else None)
        if reached("ffn") or not ffw: return
        norm_to_actT(xs, g_ffn, l)
        S.fence(PROJ_PRED, ["hidfence"])
        ffn_phase(l, last)

    PROJ_PRED = lambda k: isinstance(k, tuple) and k[0] in ("qT", "kT", "vbf", "uT", "qTm", "qTs", "hid")
    for l in range(NL):
        load_vecs(l)
        S.fence(lambda k: k == "W1", SKK)
        ssm_prep(l)
        S.fence(lambda k: k in SKK, ["W1"])
        for q in range(NQ):
            dma("sp", rope[:, 0, :], c_rope[q, 0, :, :], [], ["rope"])
            dma("sp", rope[:, 1, :], c_rope[q, 1, :, :], [], ["rope"])
            layer_pass(q, l, q == 0, l == NL - 1)

    S.op("sp", lambda e: e.nop(), reads=[k for k in S.last_w.keys() if isinstance(k, tuple) and k[0] == "out"], writes=[])
    S.emit()
    return nc


def _consts(NB, NQ=4):
    T = NB * 128 + 32
    E0 = NB * 128
    L = NB * 128
    inv = (np.float32(500000.0) ** (-np.arange(0, 32, 2, dtype=np.float32) / np.float32(32))).astype(np.float32)
    ropes = []
    for q in range(NQ):
        pos = np.zeros(T, np.float32)
        pos[:L] = 16 + q * L + np.arange(L)
        pos[E0:E0 + 16] = np.arange(16)
        pos[E0 + 16:E0 + 20] = 16384
        ang = (pos[:, None] * inv[None, :]).astype(np.float32)
        cos = np.cos(ang).astype(np.float32).T
        sin = np.sin(ang).astype(np.float32).T
        ropes.append(np.stack([np.concatenate([cos, cos], 0), np.concatenate([sin, sin], 0)], 0))
    c_rope = np.stack(ropes, 0).astype(np.float32)
    j = np.arange(128)[:, None]
    i = np.arange(128)[None, :]
    md = np.where(j <= i, 0.0, NEG)
    mp = np.where(j >= i, 0.0, NEG)
    mp0 = np.where((j < 16) & (j >= i - 112), 0.0, NEG)
    c_mask = np.stack([np.tile(m, (1, 4)) for m in (md, mp, mp0)], 0).astype(np.float32)
    jE = np.arange(32)[:, None]
    iE = np.arange(32)[None, :]
    mE = np.where((jE < 16) & (iE < 16) & (jE <= iE), 0.0, NEG)
    c_maskE = np.tile(mE, (1, 4)).astype(np.float32)
    c_maskS = np.full((4, 32, 4), NEG, np.float32)
    for s in range(4):
        c_maskS[s, 16 + s, :] = 0.0
    prot = np.zeros((128, 32), np.float32)
    for m in range(16):
        prot[m + 16, m] = -1.0
        prot[m, m + 16] = 1.0
    sel = np.zeros((128, 24), np.float32)
    jt = np.tile(np.arange(CL + 1, dtype=np.float32), 16)
    c_jtab = np.tile(jt[None, :], (128, 1)).astype(np.float32)
    pinv = np.zeros((4, 16), np.float32)
    for g, w in enumerate((2, 4, 8, 16)):
        pinv[g, :] = 1.0 / np.minimum(w, np.arange(16) + 1)
    c_pinv = np.tile(pinv.reshape(1, -1), (128, 1)).astype(np.float32)
    psel = np.zeros((16, 4), np.float32)
    for g, w in enumerate((2, 4, 8, 16)):
        psel[15 - (w - 1):15, g] = 1.0
    return dict(c_rope=c_rope, c_mask=c_mask, c_maskE=c_maskE, c_maskS=c_maskS, c_prot=prot,
                c_ident=np.eye(128, dtype=np.float32), c_sel=sel, c_jtab=c_jtab, c_pinv=c_pinv, c_psel=psel)


def prep_inputs(inp, NB, NL=2, ffw=True, NQ=4):
    L = NB * 128
    T = L + 32
    TT = NQ * L + 32
    cst = _consts(NB, NQ)
    f = lambda a: np.ascontiguousarray(np.asarray(a, dtype=np.float32))
    shared = dict(
        w_in=f(inp["w_in"][:NL]), w_out=f(inp["w_out"][:NL]),
        g_mix=f(inp["g_mix"]), g_ffn=f(inp["g_ffn"]), g_q=f(inp["g_q"]), g_k=f(inp["g_k"]), sinks=f(inp["sinks"]),
        A_re=f(inp["A_re"]).reshape(2, 2048), A_im=f(inp["A_im"]).reshape(2, 2048), log_dt=f(inp["log_dt"]),
        B_re=f(inp["B_re"]).reshape(2, 2048, 16), B_im=f(inp["B_im"]).reshape(2, 2048, 16),
        C_re=f(inp["C_re"]), C_im=f(inp["C_im"]), D_skip=f(inp["D_skip"]), w_glu=f(inp["w_glu"]), b_glu=f(inp["b_glu"]),
        w_pool=f(inp["w_pool"]), pool_scale=f(inp["pool_scale"]), g_out_attn=f(inp["g_out_attn"]),
        g_out_ssm=f(inp["g_out_ssm"]), g_out_pool=f(inp["g_out_pool"]),
    )
    if ffw:
        shared.update(w_ff1=f(inp["w_ff1"][:NL]), w_ff2=f(inp["w_ff2"][:NL]))
    xp = f(inp["x_prompt"]); xsm = f(inp["x_sample"]); meta = f(inp["meta_tokens"])
    ck = f(inp["cache_k"]).reshape(2, 32, 128, 256); cv = f(inp["cache_v"]).reshape(2, 32, 128, 256)
    sre = f(inp["state_ssm_re"]).reshape(2, 32, 16, 128); sim = f(inp["state_ssm_im"]).reshape(2, 32, 16, 128)
    spool = f(inp["state_pool"])
    maps = []
    for r in range(NCORES):
        b = r % 2
        x_in = np.zeros((TT, D), np.float32)
        x_in[:NQ * L] = xp[b, :NQ * L]
        x_in[NQ * L:NQ * L + 16] = meta
        x_in[NQ * L + 16:NQ * L + 20] = xsm[4 * r:4 * r + 4, 0]
        m = dict(shared)
        m.update(cst)
        m.update(x_in=x_in, cache_k=np.ascontiguousarray(ck[:, 4 * r:4 * r + 4]),
                 cache_v=np.ascontiguousarray(cv[:, 4 * r:4 * r + 4]),
                 st_re=np.ascontiguousarray(sre[:, 4 * r:4 * r + 4]).reshape(2, 64, 128),
                 st_im=np.ascontiguousarray(sim[:, 4 * r:4 * r + 4]).reshape(2, 64, 128),
                 st_pool=np.ascontiguousarray(spool[:, 4 * r:4 * r + 4]))
        maps.append(m)
    return maps


_NC_CACHE = {}


def _run(inp, n_cores=NCORES):
    SEQ = np.asarray(inp["x_prompt"]).shape[1]
    NQ = 4
    NB = SEQ // (NQ * 128)
    L = NB * 128
    key = (NB,)
    if key not in _NC_CACHE:
        _NC_CACHE[key] = build(NB, NL=2, ffw=True, NQ=NQ)
    nc = _NC_CACHE[key]
    maps = prep_inputs(inp, NB, NL=2, ffw=True, NQ=NQ)[:n_cores]
    res = run_bass_kernel_spmd(nc, maps, core_ids=list(range(n_cores)))
    R = res.results
    f32 = lambda a: np.asarray(a, dtype=np.float32)
    nb = min(2, n_cores)
    y_prompt = np.stack([f32(R[b]["y_out"])[:NQ * L] for b in range(nb)], 0)
    y_sample = np.concatenate([f32(R[r]["y_out"])[NQ * L + 16:NQ * L + 20] for r in range(n_cores)], 0)[:, None, :]
    pk = np.stack([f32(R[b]["o_pk"]).reshape(2, 128, 2, 128) for b in range(nb)], 1)
    pv = np.stack([f32(R[b]["o_pv"]).reshape(2, 128, 2, 128) for b in range(nb)], 1)
    pre = np.stack([f32(R[b]["o_pssm"])[:, 0].reshape(2, 32, 64) for b in range(nb)], 1)
    pim = np.stack([f32(R[b]["o_pssm"])[:, 1].reshape(2, 32, 64) for b in range(nb)], 1)
    ppool = np.stack([f32(R[b]["o_ppool"]) for b in range(nb)], 1)
    sk = np.concatenate([f32(R[r]["o_sk"]).reshape(2, 4, 128, 2, 128) for r in range(n_cores)], 1)
    sv = np.concatenate([f32(R[r]["o_sv"]).reshape(2, 4, 128, 2, 128) for r in range(n_cores)], 1)
    sre = np.concatenate([f32(R[r]["o_sssm"])[:, 0].reshape(2, 4, 32, 64) for r in range(n_cores)], 1)
    sim = np.concatenate([f32(R[r]["o_sssm"])[:, 1].reshape(2, 4, 32, 64) for r in range(n_cores)], 1)
    spool = np.concatenate([f32(R[r]["o_spool"]) for r in range(n_cores)], 1)
    return (y_prompt, y_sample, pk, pv, pre, pim, ppool, sk, sv, sre, sim, spool)


def kernel(**inputs):
    return _run(inputs, NCORES)
```
